# Optimizing a Trainium2 kernel written in Bass

```python
import math
import jax, jax.numpy as jnp
from jax import lax
import numpy as np

D_MODEL = 2048
BATCH = 4
SEQ = 2048
DEPTH = 1
DEC_BATCH = 128
DEC_SEQ = 1
PAST_LEN = 16384
PAGE_SIZE = 128

D_RNN = D_MODEL // 2
RNN_HEADS = 8
RNN_HD = D_RNN // RNN_HEADS
CONV_W = 4
LRU_C = 8.0
D_MLSTM = D_MODEL // 2
MLSTM_HEADS = 4
MLSTM_HD = D_MLSTM // MLSTM_HEADS
CHUNK = 128
N_MEM = 256
X_HEADS = 4
X_HD = D_MODEL // X_HEADS
D_FF = 4 * D_MODEL
EPS = 1e-6
OFF_RX = 0
OFF_RG = OFF_RX + D_RNN
OFF_MU = OFF_RG + D_RNN
OFF_MV = OFF_MU + D_MLSTM
OFF_MO = OFF_MV + D_MLSTM
OFF_MI = OFF_MO + D_MLSTM
OFF_MF = OFF_MI + MLSTM_HEADS
D_IN = OFF_MF + MLSTM_HEADS

kernel_name = "hymba_rglru_mlstm_memxattn_step"


def rmsnorm(x, g):
    xf = x.astype(jnp.float32)
    y = xf * lax.rsqrt(jnp.mean(xf * xf, axis=-1, keepdims=True) + EPS) * g.astype(jnp.float32)
    return y.astype(x.dtype)


def causal_conv(x, buf, w, b):
    T = x.shape[1]
    xp = jnp.concatenate([buf.astype(x.dtype), x], axis=1)
    y = b.astype(x.dtype)
    for j in range(CONV_W):
        y = y + w[j].astype(x.dtype) * xp[:, j:j + T]
    return y, xp[:, -(CONV_W - 1):]


def rglru(xc, h0, w_a, b_a, w_x, b_x, lam):
    B, T, _ = xc.shape
    xf = xc.astype(jnp.float32)
    xh = xf.reshape(B, T, RNN_HEADS, RNN_HD)
    r = jax.nn.sigmoid(jnp.einsum('bthi,hij->bthj', xh, w_a.astype(jnp.float32)) + b_a).reshape(B, T, D_RNN)
    i = jax.nn.sigmoid(jnp.einsum('bthi,hij->bthj', xh, w_x.astype(jnp.float32)) + b_x).reshape(B, T, D_RNN)
    log_a = -LRU_C * r * jax.nn.softplus(-lam.astype(jnp.float32))
    a = jnp.exp(log_a)
    u = jnp.sqrt(-jnp.expm1(2.0 * log_a)) * (i * xf)

    def step(h, au):
        a_t, u_t = au
        h = a_t * h + u_t
        return h, h

    hT, hs = lax.scan(step, h0.astype(jnp.float32), (a.swapaxes(0, 1), u.swapaxes(0, 1)))
    return hs.swapaxes(0, 1), hT


def to_chunks(a, nc, L):
    B, H = a.shape[:2]
    return jnp.moveaxis(a.reshape(B, H, nc, L, *a.shape[3:]), 2, 0)


def mlstm_chunkwise(q, k, v, ig, lf, C0, n0, m0):
    B, H, T, Dh = q.shape
    L = CHUNK if T % CHUNK == 0 else T
    nc = T // L
    causal = jnp.tril(jnp.ones((L, L), dtype=bool))

    def step(carry, inp):
        C, n, m = carry
        qc, kc, vc, ic, fc = inp
        b = jnp.cumsum(fc, axis=-1)
        dmat = ic[..., None, :] + b[..., :, None] - b[..., None, :]
        dmat = jnp.where(causal, dmat, -jnp.inf)
        inter = b + m[..., None]
        m_t = jnp.maximum(inter, jnp.max(dmat, axis=-1))
        w_inter = jnp.exp(inter - m_t)
        s = jnp.einsum('bhtd,bhsd->bhts', qc, kc) * jnp.exp(dmat - m_t[..., None])
        num = w_inter[..., None] * jnp.einsum('bhvk,bhtk->bhtv', C, qc) + jnp.einsum('bhts,bhsv->bhtv', s, vc)
        den = w_inter * jnp.einsum('bhk,bhtk->bht', n, qc) + jnp.sum(s, axis=-1)
        h = num / jnp.maximum(jnp.abs(den), jnp.exp(-m_t))[..., None]
        m_new = m_t[..., -1]
        g_state = jnp.exp(b[..., -1] + m - m_new)
        g_in = jnp.exp(ic + b[..., -1:] - b - m_new[..., None])
        C_new = g_state[..., None, None] * C + jnp.einsum('bhs,bhsv,bhsk->bhvk', g_in, vc, kc)
        n_new = g_state[..., None] * n + jnp.einsum('bhs,bhsk->bhk', g_in, kc)
        return (C_new, n_new, m_new), h

    xs = (to_chunks(q, nc, L), to_chunks(k, nc, L), to_chunks(v, nc, L), to_chunks(ig, nc, L), to_chunks(lf, nc, L))
    init = (C0.astype(jnp.float32), n0.astype(jnp.float32), m0.astype(jnp.float32))
    (C, n, m), hs = lax.scan(step, init, xs)
    hs = jnp.moveaxis(hs, 0, 2).reshape(B, H, T, Dh)
    return hs, C, n, m


def mixer(h, rg_h, rg_conv, C, n, m, ml_conv, w):
    B, T, _ = h.shape
    z = h @ w['w_in']
    xr = z[..., OFF_RX:OFF_RG]
    gr = z[..., OFF_RG:OFF_MU]
    u = z[..., OFF_MU:OFF_MV]
    v = z[..., OFF_MV:OFF_MO]
    og = z[..., OFF_MO:OFF_MI]
    ig_pre = z[..., OFF_MI:OFF_MF]
    fg_pre = z[..., OFF_MF:D_IN]
    xc, rg_conv_new = causal_conv(xr, rg_conv, w['conv_rnn_w'], w['conv_rnn_b'])
    hs, rg_h_new = rglru(xc, rg_h, w['lru_wa'], w['lru_ba'], w['lru_wx'], w['lru_bx'], w['lru_lambda'])
    y_rnn = rmsnorm(hs * jax.nn.gelu(gr.astype(jnp.float32)), w['g_rnn_out'])
    uc, ml_conv_new = causal_conv(u, ml_conv, w['conv_ml_w'], w['conv_ml_b'])
    uc = jax.nn.silu(uc.astype(jnp.float32)).reshape(B, T, MLSTM_HEADS, MLSTM_HD)
    q = jnp.einsum('bthi,hij->bhtj', uc, w['ml_wq'].astype(jnp.float32))
    k = jnp.einsum('bthi,hij->bhtj', uc, w['ml_wk'].astype(jnp.float32)) * (MLSTM_HD ** -0.5)
    vv = v.astype(jnp.float32).reshape(B, T, MLSTM_HEADS, MLSTM_HD).transpose(0, 2, 1, 3)
    ig = (ig_pre.astype(jnp.float32) + w['ml_bi']).transpose(0, 2, 1)
    lf = jax.nn.log_sigmoid(fg_pre.astype(jnp.float32) + w['ml_bf']).transpose(0, 2, 1)
    hm, C_new, n_new, m_new = mlstm_chunkwise(q, k, vv, ig, lf, C, n, m)
    hm = hm.transpose(0, 2, 1, 3)
    o = jax.nn.sigmoid(og.astype(jnp.float32)).reshape(B, T, MLSTM_HEADS, MLSTM_HD)
    y_ml = (rmsnorm(hm, w['g_ml_out']) * o).reshape(B, T, D_MLSTM)
    y = jnp.concatenate([y_rnn, y_ml], axis=-1).astype(h.dtype) @ w['w_out']
    new = (rg_h_new.astype(rg_h.dtype), rg_conv_new.astype(rg_conv.dtype), C_new.astype(C.dtype),
           n_new.astype(n.dtype), m_new.astype(m.dtype), ml_conv_new.astype(ml_conv.dtype))
    return y, new


def mem_kv(mem, w):
    B = mem.shape[0]
    mn = rmsnorm(mem, w['g_mem'])
    mk = (mn @ w['w_mk']).reshape(B, N_MEM, X_HEADS, X_HD)
    mv = (mn @ w['w_mv']).reshape(B, N_MEM, X_HEADS, X_HD)
    return mk, mv


def cross_attn(h, mk, mv, w):
    B, T, _ = h.shape
    q = (h @ w['w_cq']).reshape(B, T, X_HEADS, X_HD)
    s = jnp.einsum('bthd,bnhd->bhtn', q, mk.astype(q.dtype)).astype(jnp.float32) * (X_HD ** -0.5)
    p = jax.nn.softmax(s, axis=-1)
    o = jnp.einsum('bhtn,bnhd->bthd', p.astype(h.dtype), mv.astype(h.dtype)).reshape(B, T, D_MODEL)
    return o @ w['w_co']


def block(x, rg_h, rg_conv, C, n, m, ml_conv, mk, mv, w):
    y, new = mixer(rmsnorm(x, w['g_mix']), rg_h, rg_conv, C, n, m, ml_conv, w)
    x = x + y.astype(x.dtype)
    x = x + cross_attn(rmsnorm(x, w['g_xattn']), mk, mv, w).astype(x.dtype)
    hf = rmsnorm(x, w['g_ffn'])
    x = x + (jnp.square(jax.nn.relu(hf @ w['w_up'])) @ w['w_down']).astype(x.dtype)
    return x, new


def setup_inputs(seed: int = 0) -> dict:
    key = jax.random.key(seed)
    ks = iter(jax.random.split(key, 64))

    def nrm(shape, scale=1.0):
        return jax.random.normal(next(ks), shape, jnp.float32) * scale

    def gain(shape):
        return 1.0 + nrm(shape, 0.01)

    Dp = DEPTH
    lam_a = jax.random.uniform(next(ks), (Dp, D_RNN), jnp.float32, 0.9, 0.999)
    return {
        'x_prompt': nrm((BATCH, SEQ, D_MODEL)),
        'x_sample': nrm((DEC_BATCH, DEC_SEQ, D_MODEL)),
        'mem_prompt': nrm((BATCH, N_MEM, D_MODEL)),
        'state_rglru_h': nrm((Dp, DEC_BATCH, D_RNN), 0.5),
        'state_rglru_conv': nrm((Dp, DEC_BATCH, CONV_W - 1, D_RNN)),
        'state_mlstm_C': nrm((Dp, DEC_BATCH, MLSTM_HEADS, MLSTM_HD, MLSTM_HD), 0.1),
        'state_mlstm_n': nrm((Dp, DEC_BATCH, MLSTM_HEADS, MLSTM_HD), 0.1),
        'state_mlstm_m': nrm((Dp, DEC_BATCH, MLSTM_HEADS)),
        'state_mlstm_conv': nrm((Dp, DEC_BATCH, CONV_W - 1, D_MLSTM)),
        'cache_mem_k': nrm((Dp, DEC_BATCH, N_MEM, X_HEADS, X_HD)),
        'cache_mem_v': nrm((Dp, DEC_BATCH, N_MEM, X_HEADS, X_HD)),
        'g_mix': gain((Dp, D_MODEL)),
        'w_in': nrm((Dp, D_MODEL, D_IN), D_MODEL ** -0.5),
        'conv_rnn_w': nrm((Dp, CONV_W, D_RNN), CONV_W ** -0.5),
        'conv_rnn_b': nrm((Dp, D_RNN), 0.01),
        'lru_wa': nrm((Dp, RNN_HEADS, RNN_HD, RNN_HD), RNN_HD ** -0.5),
        'lru_ba': nrm((Dp, RNN_HEADS, RNN_HD), 0.01),
        'lru_wx': nrm((Dp, RNN_HEADS, RNN_HD, RNN_HD), RNN_HD ** -0.5),
        'lru_bx': nrm((Dp, RNN_HEADS, RNN_HD), 0.01),
        'lru_lambda': jnp.log(lam_a) - jnp.log1p(-lam_a),
        'g_rnn_out': gain((Dp, D_RNN)),
        'conv_ml_w': nrm((Dp, CONV_W, D_MLSTM), CONV_W ** -0.5),
        'conv_ml_b': nrm((Dp, D_MLSTM), 0.01),
        'ml_wq': nrm((Dp, MLSTM_HEADS, MLSTM_HD, MLSTM_HD), MLSTM_HD ** -0.5),
        'ml_wk': nrm((Dp, MLSTM_HEADS, MLSTM_HD, MLSTM_HD), MLSTM_HD ** -0.5),
        'ml_bi': nrm((Dp, MLSTM_HEADS), 0.1),
        'ml_bf': jnp.linspace(3.0, 6.0, MLSTM_HEADS, dtype=jnp.float32)[None, :] + nrm((Dp, MLSTM_HEADS), 0.1),
        'g_ml_out': gain((Dp, MLSTM_HD)),
        'w_out': nrm((Dp, D_MODEL, D_MODEL), D_MODEL ** -0.5),
        'g_xattn': gain((Dp, D_MODEL)),
        'g_mem': gain((Dp, D_MODEL)),
        'w_cq': nrm((Dp, D_MODEL, D_MODEL), D_MODEL ** -0.5),
        'w_mk': nrm((Dp, D_MODEL, D_MODEL), D_MODEL ** -0.5),
        'w_mv': nrm((Dp, D_MODEL, D_MODEL), D_MODEL ** -0.5),
        'w_co': nrm((Dp, D_MODEL, D_MODEL), D_MODEL ** -0.5),
        'g_ffn': gain((Dp, D_MODEL)),
        'w_up': nrm((Dp, D_MODEL, D_FF), D_MODEL ** -0.5),
        'w_down': nrm((Dp, D_FF, D_MODEL), D_FF ** -0.5),
        'g_final': gain((D_MODEL,)),
    }


def reference(x_prompt, x_sample, mem_prompt, state_rglru_h, state_rglru_conv, state_mlstm_C,
              state_mlstm_n, state_mlstm_m, state_mlstm_conv, cache_mem_k, cache_mem_v,
              g_mix, w_in, conv_rnn_w, conv_rnn_b, lru_wa, lru_ba, lru_wx, lru_bx, lru_lambda,
              g_rnn_out, conv_ml_w, conv_ml_b, ml_wq, ml_wk, ml_bi, ml_bf, g_ml_out, w_out,
              g_xattn, g_mem, w_cq, w_mk, w_mv, w_co, g_ffn, w_up, w_down, g_final):
    B = x_prompt.shape[0]
    dt = x_prompt.dtype
    xp, xs = x_prompt, x_sample
    p_out = [[] for _ in range(8)]
    s_out = [[] for _ in range(6)]
    for l in range(DEPTH):
        w = dict(g_mix=g_mix[l], w_in=w_in[l], conv_rnn_w=conv_rnn_w[l], conv_rnn_b=conv_rnn_b[l],
                 lru_wa=lru_wa[l], lru_ba=lru_ba[l], lru_wx=lru_wx[l], lru_bx=lru_bx[l],
                 lru_lambda=lru_lambda[l], g_rnn_out=g_rnn_out[l], conv_ml_w=conv_ml_w[l],
                 conv_ml_b=conv_ml_b[l], ml_wq=ml_wq[l], ml_wk=ml_wk[l], ml_bi=ml_bi[l], ml_bf=ml_bf[l],
                 g_ml_out=g_ml_out[l], w_out=w_out[l], g_xattn=g_xattn[l], g_mem=g_mem[l],
                 w_cq=w_cq[l], w_mk=w_mk[l], w_mv=w_mv[l], w_co=w_co[l], g_ffn=g_ffn[l],
                 w_up=w_up[l], w_down=w_down[l])
        z_h = jnp.zeros((B, D_RNN), dt)
        z_rc = jnp.zeros((B, CONV_W - 1, D_RNN), dt)
        z_C = jnp.zeros((B, MLSTM_HEADS, MLSTM_HD, MLSTM_HD), dt)
        z_n = jnp.zeros((B, MLSTM_HEADS, MLSTM_HD), dt)
        z_m = jnp.zeros((B, MLSTM_HEADS), dt)
        z_mc = jnp.zeros((B, CONV_W - 1, D_MLSTM), dt)
        mk_p, mv_p = mem_kv(mem_prompt, w)
        xp, new_p = block(xp, z_h, z_rc, z_C, z_n, z_m, z_mc, mk_p, mv_p, w)
        for j, a in enumerate(new_p):
            p_out[j].append(a)
        p_out[6].append(mk_p)
        p_out[7].append(mv_p)
        xs, new_s = block(xs, state_rglru_h[l], state_rglru_conv[l], state_mlstm_C[l], state_mlstm_n[l],
                          state_mlstm_m[l], state_mlstm_conv[l], cache_mem_k[l], cache_mem_v[l], w)
        for j, a in enumerate(new_s):
            s_out[j].append(a)
    y_prompt = rmsnorm(xp, g_final)
    y_sample = rmsnorm(xs, g_final)
    P = [jnp.stack(a, axis=0) for a in p_out]
    S = [jnp.stack(a, axis=0) for a in s_out]
    return (y_prompt, y_sample, P[0], P[1], P[2], P[3], P[4], P[5], P[6], P[7],
            S[0], S[1], S[2], S[3], S[4], S[5])
```

```python
import numpy as np
from contextlib import ExitStack
import concourse.bass as bass
import concourse.mybir as mybir
from concourse.bass_utils import run_bass_kernel_spmd

F32 = mybir.dt.float32
BF16 = mybir.dt.bfloat16
AF = mybir.ActivationFunctionType
ALU = mybir.AluOpType
AX = mybir.AxisListType

SAME_ENGINE_WAIT = True
EPS = 1e-6
NSLOT = 2

P_GMIX, P_GXA, P_GMEM, P_GFFN, P_GFIN = 0, 16, 32, 48, 64
P_CRW, P_CRB, P_LBA, P_LBX, P_LAM, P_GRN = 80, 112, 120, 128, 136, 144
P_CMW, P_CMB, P_GML2, P_BI, P_BF = 152, 184, 192, 194, 195
NPRM = 196


import os
STOP = int(os.environ.get("KSTOP", "0"))
DEBUG_SITES = bool(int(os.environ.get("KSITES", "0")))
DBG2 = int(os.environ.get("DBG2", "0"))
DBG3 = int(os.environ.get("DBG3", "0"))


class _Stop(Exception):
    pass


class Buf:
    __slots__ = ("name", "w", "r", "dsem", "dcount")

    def __init__(self, name="b"):
        self.name = name
        self.w = None
        self.r = {}
        self.dsem = None
        self.dcount = 0


class K:
    ENG = ("pe", "act", "dve", "pool", "sp")

    def __init__(self, nc, es):
        self.nc = nc
        self.es = es
        self.ops = {e: [] for e in self.ENG}
        self.sem = {e: es.enter_context(nc.semaphore("s_" + e)) for e in self.ENG}
        self.cnt = {e: 0 for e in self.ENG}
        self.known = {e: {} for e in self.ENG}
        self.semobj = {e: self.sem[e] for e in self.ENG}
        self.nd = 0
        self.dbufs = []
        self.nins = 0
        self.dead = False

    def _need(self, eng, reads, writes):
        need = {}

        def add(k, v):
            if need.get(k, 0) < v:
                need[k] = v
        for b in reads:
            if b.w:
                add(*b.w)
        for b in writes:
            if b.w:
                add(*b.w)
            for k, v in b.r.items():
                add(k, v)
        waits = []
        for k, v in need.items():
            if k == eng and (eng == "pe" or not SAME_ENGINE_WAIT):
                continue
            if self.known[eng].get(k, 0) >= v:
                continue
            self.known[eng][k] = v
            waits.append((self.semobj[k], v))
        return waits

    def op(self, eng, fn, reads=(), writes=(), inc=True):
        if self.dead:
            return
        waits = self._need(eng, reads, writes)
        val = self.cnt[eng] + 1
        if inc:
            self.cnt[eng] = val
        for b in reads:
            if b.r.get(eng, 0) < val:
                b.r[eng] = val
        for b in writes:
            b.w = (eng, val)
            b.r = {}
        sem = self.sem[eng]
        self.nins += 1
        if getattr(self, "trace", False):
            print("TRACE", eng, "val", val, "inc", inc, "waits", [(str(s_), v_) for s_, v_ in waits], "reads", [(b.name, b.w) for b in reads], "writes", [b.name for b in writes])
        import sys as _sys
        fr = _sys._getframe(1)
        site = []
        while fr is not None and len(site) < 3:
            site.append(fr.f_lineno)
            fr = fr.f_back
        site = "SITE" + "_".join(map(str, site)) + "_" + getattr(self, "tag", "")

        def run(e, waits=waits, fn=fn, inc=inc, sem=sem, site=site):
            for s, v in waits:
                e.wait_ge(s, v)
            ins = fn(e)
            if DEBUG_SITES:
                ins.annotate(site)
            if inc:
                ins.then_inc(sem, 1)
        self.ops[eng].append(run)

    def _dsem(self, b):
        if b.dsem is None:
            key = "d%d" % self.nd
            self.nd += 1
            b.dsem = key
            self.semobj[key] = self.es.enter_context(self.nc.semaphore(key))
            self.dbufs.append(b)
        return b.dsem

    def dma(self, q, out, in_, reads=(), writes=(), **kw):
        if self.dead:
            return
        waits = self._need(q, reads, writes)
        bl = list(reads) + list(writes)
        assert len(bl) == 1
        b = bl[0]
        kk = self._dsem(b)
        b.dcount += 16
        v = b.dcount
        if reads:
            b.r[kk] = v
        else:
            b.w = (kk, v)
            b.r = {}
        s = self.semobj[kk]
        self.nins += 1

        def run(e, waits=waits, s=s, out=out, in_=in_, kw=kw):
            for ws, wv in waits:
                e.wait_ge(ws, wv)
            e.dma_start(out=out, in_=in_, **kw).then_inc(s, 16)
        self.ops[q].append(run)

    def barrier(self):
        if self.dead:
            return
        tgt = [(e, self.cnt[e]) for e in self.ENG if self.cnt[e] > 0]
        tgt += [(b.dsem, b.dcount) for b in self.dbufs]
        for eng in self.ENG:
            waits = []
            for kk, v in tgt:
                if kk == eng:
                    continue
                if self.known[eng].get(kk, 0) >= v:
                    continue
                self.known[eng][kk] = v
                waits.append((self.semobj[kk], v))

            def run(e, waits=waits):
                for s, v in waits:
                    e.wait_ge(s, v)
            if waits:
                self.ops[eng].append(run)

    def finish(self):
        self.barrier()

    def emit(self):
        nc = self.nc
        with nc.Block() as block:
            @block.tensor
            def _(e):
                for f in self.ops["pe"]:
                    f(e)

            @block.scalar
            def _(e):
                for f in self.ops["act"]:
                    f(e)

            @block.vector
            def _(e):
                for f in self.ops["dve"]:
                    f(e)

            @block.gpsimd
            def _(e):
                for f in self.ops["pool"]:
                    f(e)

            @block.sync
            def _(e):
                for f in self.ops["sp"]:
                    f(e)


class Arena:
    def __init__(self, ap, lo, hi):
        self.ap = ap
        self.n = hi
        self.top = lo

    def alloc(self, shape, dt, parts=128):
        n = 1
        for s in shape:
            n *= s
        esz = 4 if dt == F32 else 2
        words = (n * esz + 3) // 4
        words = (words + 15) // 16 * 16
        off = self.top
        self.top += words
        assert self.top <= self.n, "arena overflow %d > %d" % (self.top, self.n)
        self.peak = max(getattr(self, "peak", 0), self.top)
        v = self.ap[:, off:off + words]
        if dt != F32:
            v = v.bitcast(dt)
        v = v[:, 0:n]
        if len(shape) == 2:
            v = v.rearrange("p (a b) -> p a b", a=shape[0])
        elif len(shape) == 3:
            v = v.rearrange("p (a b c) -> p a b c", a=shape[0], b=shape[1])
        elif len(shape) == 4:
            v = v.rearrange("p (a b c d) -> p a b c d", a=shape[0], b=shape[1], c=shape[2])
        if parts != 128:
            v = v[0:parts]
        return v

    def mark(self):
        return self.top

    def release(self, m):
        self.top = m


def build_program():
    nc = bass.Bass("TRN2", target_bir_lowering=False)

    def DI(name, shape):
        return nc.dram_tensor(name, list(shape), F32, kind="ExternalInput").ap()

    def DO(name, shape):
        return nc.dram_tensor(name, list(shape), F32, kind="ExternalOutput").ap()

    xm_d = DI("xm", [1024, 2048])
    xp_d = DI("xp", [1024, 2048])
    mem_d = DI("mem", [256, 2048])
    mask_d = DI("mask", [128, 1])
    prm_d = DI("prm", [128, NPRM])
    gml_d = DI("gmlrep", [128, 256])
    id_d = DI("ident", [128, 128])
    mneg_d = DI("maskneg", [128, 128])
    sel_d = DI("sel", [4, 4 * 128])
    w_in_d = DI("w_in", [2048, 5128])
    lwa_d = DI("lru_wa", [8, 128, 128])
    lwx_d = DI("lru_wx", [8, 128, 128])
    wq_d = DI("ml_wq", [4, 256, 256])
    wk_d = DI("ml_wk", [4, 256, 256])
    w_out_d = DI("w_out", [2048, 2048])
    w_cq_d = DI("w_cq", [2048, 2048])
    w_mk_d = DI("w_mk", [2048, 2048])
    w_mv_d = DI("w_mv", [2048, 2048])
    w_co_d = DI("w_co", [2048, 2048])
    w_up_d = DI("w_up", [2048, 8192])
    w_dn_d = DI("w_down", [8192, 2048])

    xs_d = DI("xs", [16, 2048])
    sh_d = DI("s_h", [16, 1024])
    src_d = DI("s_rc", [16, 3, 1024])
    sC_d = DI("s_C", [16, 4, 256, 256])
    sn_d = DI("s_n", [16, 4, 256])
    sm_d = DI("s_m", [16, 4])
    smc_d = DI("s_mc", [16, 3, 1024])
    ck_d = DI("ck", [16, 256, 2048])
    cv_d = DI("cv", [16, 256, 2048])
    seltok_d = DI("seltok", [16, 16 * 128])
    cmw_d = DI("cmw_rep", [16, 4, 1024])
    cmb_d = DI("cmb_rep", [16, 1024])
    gb_d = DI("gb_rep", [16, 8])
    ys_d = DO("o_ys", [16, 2048])
    osh_d = DO("o_sh", [128, 8, 16])
    osrc_d = DO("o_src", [128, 8, 3, 16])
    osC_d = DO("o_sC", [16, 4, 256, 256])
    osn_d = DO("o_sn", [16, 4, 256])
    osm_d = DO("o_sm", [16, 4])
    osmc_d = DO("o_smc", [16, 3, 1024])
    y_d = DO("o_y", [1024, 2048])
    oph_d = DO("o_ph", [128, 8])
    oprc_d = DO("o_prc", [128, 8, 3])
    opmc_d = DO("o_pmc", [128, 8, 3])
    opC_d = DO("o_pC", [128, 4, 2, 257])
    opm_d = DO("o_pm", [4, 1])
    omk_d = DO("o_mkT", [128, 16, 256])
    omv_d = DO("o_mv", [256, 2048])

    with ExitStack() as es:
        k = K(nc, es)
        NW = 52992
        ar_t = es.enter_context(nc.sbuf_tensor("arena", [128, NW], F32))
        A = Arena(ar_t, 0, NW)
        PS = es.enter_context(nc.psum_tensor("ps", [128, 8, 512], F32))

        def psb(b):
            return PS[:, b, :].bitcast(BF16)
        pbuf = [Buf("ps%d" % i) for i in range(8)]

        wslot = [A.alloc([16, 512], BF16) for _ in range(NSLOT)]
        wsb = [Buf("ws%d" % i) for i in range(NSLOT)]
        idf = A.alloc([128], F32)[:, :]
        idb = A.alloc([128], BF16)
        onesb = A.alloc([128], BF16)
        onesf = A.alloc([128], F32)
        mneg = A.alloc([128], F32)
        m01 = A.alloc([128], BF16)
        prm = A.alloc([NPRM], F32)
        gml = A.alloc([256], F32)
        maskc = A.alloc([1], F32)
        sel = A.alloc([4 * 128], F32, parts=4)
        off_wqb = A.top
        wqb = A.alloc([4, 2, 256], BF16)
        wkb = A.alloc([4, 2, 256], BF16)
        lwab = A.alloc([8, 128], BF16)
        lwxb = A.alloc([8, 128], BF16)
        C32 = A.alloc([4, 2, 257], F32)
        Cb = A.alloc([2, 257], BF16)
        hcar = A.alloc([8], F32)
        rtail = A.alloc([8, 3], F32)
        mtail = A.alloc([8, 3], F32)
        ccol = A.alloc([8], F32)
        ccol2 = A.alloc([8], F32)
        negbf = A.alloc([1], F32, parts=4)
        st0 = A.alloc([1], F32)
        gcar = A.alloc([4], F32, parts=4)
        b_const = Buf("const")
        yTs = A.alloc([16, 16], BF16)
        ssum_s = A.alloc([16], F32)
        PBASE = A.top
        R0_LO, R0_HI = PBASE, PBASE + 9216
        A0 = Arena(ar_t, R0_LO, R0_HI)
        A = Arena(ar_t, R0_HI, NW)
        print("persistent words", PBASE, "R12 words", NW - R0_HI)
        b_C32, b_Cb, b_hcar, b_rtail, b_mtail, b_gcar = [Buf(n) for n in "C32 Cb hcar rtail mtail gcar".split()]

        def act(out, in_, func, reads, writes, **kw):
            k.op("act", lambda e: e.activation(out, in_, func, **kw), reads=reads, writes=writes)

        def dve(fn, reads, writes):
            k.op("dve", fn, reads=reads, writes=writes)

        def mm(out, lhsT, rhs, start, stop, reads, writes, inc):
            k.op("pe", lambda e: e.matmul(out, lhsT, rhs, start=start, stop=stop), reads=reads, writes=writes, inc=inc)

        def tr(out, in_, ident, reads, writes, inc):
            k.op("pe", lambda e: e.transpose(out, in_, ident), reads=reads, writes=writes, inc=inc)

        def chk(n):
            if STOP == n:
                k.finish()
                k.dead = True
        for dst, src in ((idf, id_d), (mneg, mneg_d), (prm, prm_d), (gml, gml_d), (maskc, mask_d), (sel, sel_d)):
            k.dma("sp", dst, src, writes=[b_const])
        b_cw = Buf("constw")
        k.dma("pool", wqb, wq_d.rearrange("h (c p) n -> p h c n", p=128), writes=[b_cw])
        k.dma("pool", wkb, wk_d.rearrange("h (c p) n -> p h c n", p=128), writes=[b_cw])
        k.dma("pool", lwab, lwa_d.rearrange("h p n -> p h n"), writes=[b_cw])
        k.dma("pool", lwxb, lwx_d.rearrange("h p n -> p h n"), writes=[b_cw])
        k.op("dve", lambda e: e.memset(st0, 0.0), reads=[b_cw, b_const], writes=[b_const])
        dve(lambda e: e.tensor_copy(idb, idf), [b_const], [b_const])
        dve(lambda e: e.memset(onesb, 1.0), [], [b_const])
        dve(lambda e: e.tensor_scalar(m01, mneg, 0.0, None, op0=ALU.is_equal), [b_const], [b_const])
        dve(lambda e: e.memset(onesf, 1.0), [], [b_const])
        dve(lambda e: e.memset(C32, 0.0), [], [b_C32])
        dve(lambda e: e.memset(hcar, 0.0), [], [b_hcar])
        dve(lambda e: e.memset(gcar, 0.0), [], [b_gcar])
        dve(lambda e: e.memset(rtail, 0.0), [], [b_rtail])
        dve(lambda e: e.memset(mtail, 0.0), [], [b_mtail])
        act(ccol, prm[:, P_LAM:P_LAM + 8], AF.Exp, [b_const], [b_const], scale=-1.0)
        act(ccol, ccol, AF.Ln, [b_const], [b_const], bias=1.0)
        dve(lambda e: e.tensor_scalar(ccol2, ccol, -16.0, None, op0=ALU.mult), [b_const], [b_const])
        dve(lambda e: e.tensor_scalar(ccol, ccol, -8.0, None, op0=ALU.mult), [b_const], [b_const])
        dve(lambda e: e.tensor_scalar(negbf, prm[0:4, P_BF:P_BF + 1], -1.0, None, op0=ALU.mult), [b_const], [b_const])

        chk(1)
        wsched = []
        wstate = {"issued": 0, "used": 0, "cnt": [0, 0]}
        wassign = {}
        wflat = [w_.rearrange("p a b -> p (a b)") for w_ in wslot]
        hslot = [wflat[kk // 2][:, (kk % 2) * 4096:(kk % 2 + 1) * 4096].rearrange("p (a b) -> p a b", a=16) for kk in range(4)]
        hsb = [Buf("hs%d" % i) for i in range(4)]

        def wplan(ap, half=False):
            wsched.append((ap, half))

        def wplan256(wd, r0, c0):
            for hh_ in range(2):
                wplan(wd[r0:r0 + 2048, c0 + hh_ * 256:c0 + (hh_ + 1) * 256], True)

        def wissue():
            i = wstate["issued"]
            ap, half = wsched[i]
            nco = ap.shape[1]
            md = 1 if half else 0
            cidx = wstate["cnt"][md]
            wstate["cnt"][md] += 1
            if half:
                sl_, bf_ = hslot[cidx % 4], hsb[cidx % 4]
            else:
                sl_, bf_ = wslot[cidx % 2], wsb[cidx % 2]
            k.dma("pool", sl_[:, :, 0:nco], ap.rearrange("(c p) n -> p c n", p=128), writes=[bf_])
            wassign[i] = (sl_, bf_)
            wstate["issued"] = i + 1

        def wnext():
            i = wstate["used"]
            half = wsched[i][1]
            if wstate["issued"] <= i:
                if i > 0 and wsched[i - 1][1] != half:
                    k.barrier()
                wissue()
            depth = 4 if half else NSLOT
            while wstate["issued"] < min(len(wsched), i + depth) and wsched[wstate["issued"]][1] == half:
                wissue()
            wstate["used"] = i + 1
            return wassign.pop(i)

        def cols(wd, r0, c0, n):
            return wd[r0:r0 + 2048, c0:c0 + n]

        wplan(cols(w_in_d, 0, 5120, 8))
        for c0 in (3072, 3584, 4096, 4608, 2048, 2560, 1024, 1536, 0, 512):
            wplan(cols(w_in_d, 0, c0, 512))
        for ps_ in range(2):
            wplan(cols(w_in_d, 0, 5120, 8))
            for pr in range(2):
                wplan(cols(w_in_d, 0, 3072 + pr * 512, 512))
                if ps_ == 1:
                    wplan(cols(w_in_d, 0, 4096 + pr * 512, 512))
                wplan(cols(w_in_d, 0, 2048 + pr * 512, 512))
            for pr in range(2):
                if ps_ == 1:
                    wplan(cols(w_in_d, 0, 1024 + pr * 512, 512))
                wplan(cols(w_in_d, 0, 0 + pr * 512, 512))
        for j in range(4):
            wplan(cols(w_mk_d, 0, j * 512, 512))
        for j in range(4):
            wplan(cols(w_mv_d, 0, j * 512, 512))
        def plan_post():
            for j in range(4):
                wplan256(w_out_d, 0, j * 512)
            for j in range(4):
                wplan256(w_cq_d, 0, j * 512)
            for j in range(4):
                wplan256(w_co_d, 0, j * 512)
            for g in range(4):
                for j in range(4):
                    wplan256(w_up_d, 0, g * 2048 + j * 512)
                for j in range(4):
                    wplan256(w_dn_d, g * 2048, j * 512)
        plan_post()

        accn = {"i": 0, "banks": [0, 1]}

        def acc_bank():
            bl = accn["banks"]
            b = bl[accn["i"] % len(bl)]
            accn["i"] += 1
            return b

        trn = {"i": 0}

        def tr_bank():
            b = 2 + trn["i"] % 2
            trn["i"] += 1
            return b

        def load_norm(src, T, gcol0, xn, xnb, scratch):
            stg, stgb, xb2, xbb2, junk2, junkb2, st2, stb2 = scratch
            ng = T // 128

            def stage_a(i):
                s2 = i % 2
                junk, junkb, st, stb = junk2[s2], junkb2[s2], st2[s2], stb2[s2]
                k.dma("sp", stg[s2], src[i * 128:(i + 1) * 128, :], writes=[stgb[s2]])
                act(junk, stg[s2], AF.Square, [stgb[s2]], [junkb, stb], accum_out=st[:, 0:1])
                dve(lambda e, st=st: e.tensor_scalar(st[:, 1:2], st[:, 0:1], 1.0 / 2048, EPS, op0=ALU.mult, op1=ALU.add), [stb], [stb])
                act(st[:, 2:3], st[:, 1:2], AF.Sqrt, [stb], [stb])
                dve(lambda e, st=st: e.reciprocal(st[:, 3:4], st[:, 2:3]), [stb], [stb])

            def stage_b(i):
                s2 = i % 2
                xb, xbb, st, stb = xb2[s2], xbb2[s2], st2[s2], stb2[s2]
                dve(lambda e, s2=s2, xb=xb, st=st: e.tensor_scalar(xb, stg[s2], st[:, 3:4], None, op0=ALU.mult), [stb, stgb[s2]], [xbb])
                for hh in range(2):
                    b = tr_bank()
                    pv = psb(b).rearrange("p (a b) -> p a b", a=8)
                    for c in range(8):
                        cc = hh * 8 + c
                        tr(pv[:, c, :], xb[:, cc * 128:(cc + 1) * 128], idb, [xbb, b_const], [pbuf[b]], inc=(c == 7))
                    g = prm[:, gcol0 + hh * 8:gcol0 + hh * 8 + 8].unsqueeze(2).to_broadcast([128, 8, 128])
                    dve(lambda e, pv=pv, g=g, hh=hh, i=i: e.tensor_tensor(xn[:, hh * 8:hh * 8 + 8, i * 128:(i + 1) * 128], pv, g, ALU.mult),
                        [pbuf[b], b_const], [xnb[i]])
            stage_a(0)
            for i in range(ng):
                if i + 1 < ng:
                    stage_a(i + 1)
                stage_b(i)

        def fm_block(slot, sb_, nchunks, xin, xin_bufs, tiles, epi, kc=16):
            for j in range(nchunks):
                for ti, (t0, n) in enumerate(tiles):
                    b = acc_bank()
                    for c in range(kc):
                        mm(PS[:, b, 0:n], slot[:, c, j * 128:(j + 1) * 128], xin[:, c, t0:t0 + n], c == 0, c == kc - 1,
                           [sb_] + xin_bufs(t0, n), [pbuf[b]], inc=(c == kc - 1))
                    epi(j, ti, t0, n, PS[:, b, 0:n], pbuf[b])

        def tm_block(slot, sb_, ncols, xin, xin_bufs, nchunk_tok, epi, kc=16):
            for i in range(nchunk_tok):
                b = acc_bank()
                for c in range(kc):
                    mm(PS[:, b, 0:ncols], xin[:, c, i * 128:(i + 1) * 128], slot[:, c, 0:ncols], c == 0, c == kc - 1,
                       [sb_] + xin_bufs(i * 128, 128), [pbuf[b]], inc=(c == kc - 1))
                epi(i, PS[:, b, 0:ncols], pbuf[b])

        chk(12)
        NS = 16
        mS = A.mark()
        b_yTs_r, b_yTs_m = Buf("yTs_r"), Buf("yTs_m")
        b_ssum_s = Buf("ssum_s")
        xnS = A.alloc([16, NS], BF16)
        b_xnS = Buf("xnS")
        bc = lambda ap, shape: ap.to_broadcast(shape)

        def dv(fn, reads, writes):
            k.op("dve", fn, reads=reads, writes=writes)
        mS1 = A.mark()
        stgS = A.alloc([2048], F32)
        xbS = A.alloc([2048], BF16)
        junkS = A.alloc([2048], BF16)
        stS = A.alloc([4], F32)
        b_stgS, b_l = Buf("stgS"), Buf("l")
        k.dma("sp", stgS[0:16], xs_d, writes=[b_stgS])
        act(junkS[0:16], stgS[0:16], AF.Square, [b_stgS], [b_l], accum_out=stS[0:16, 0:1])
        dv(lambda e: e.tensor_scalar(stS[0:16, 1:2], stS[0:16, 0:1], 1.0 / 2048, EPS, op0=ALU.mult, op1=ALU.add), [b_l], [b_l])
        act(stS[0:16, 2:3], stS[0:16, 1:2], AF.Sqrt, [b_l], [b_l])
        dv(lambda e: e.reciprocal(stS[0:16, 3:4], stS[0:16, 2:3]), [b_l], [b_l])
        dv(lambda e: e.tensor_scalar(xbS[0:16], stgS[0:16], stS[0:16, 3:4], None, op0=ALU.mult), [b_l, b_stgS], [b_l])
        for hh in range(2):
            b = tr_bank()
            pv = psb(b)[:, 0:8 * 16].rearrange("p (a b) -> p a b", a=8)
            for c in range(8):
                cc = hh * 8 + c
                tr(pv[:, c, :], xbS[0:16, cc * 128:(cc + 1) * 128], idb[0:16, 0:16], [b_l, b_const], [pbuf[b]], inc=(c == 7))
            g = prm[:, P_GMIX + hh * 8:P_GMIX + hh * 8 + 8].unsqueeze(2).to_broadcast([128, 8, 16])
            dv(lambda e, pv=pv, g=g, hh=hh: e.tensor_tensor(xnS[:, hh * 8:hh * 8 + 8, :], pv, g, ALU.mult), [pbuf[b], b_const], [b_xnS])
        k.barrier()
        A.release(mS1)

        gz = A.alloc([8], F32)
        v_s = A.alloc([1024], F32)
        og_s = A.alloc([1024], F32)
        u_s = A.alloc([1024], F32)
        gel_s = A.alloc([8, NS], F32)
        xr_s = A.alloc([8, NS], F32)
        b_z = {n_: Buf(n_) for n_ in "gz v og u gel xr".split()}

        def tm_s(ncols, epi):
            slot, sb_ = wnext()
            b = acc_bank()
            for c in range(16):
                mm(PS[0:16, b, 0:ncols], xnS[:, c, :], slot[:, c, 0:ncols], c == 0, c == 15, [sb_, b_xnS], [pbuf[b]], inc=(c == 15))
            epi(PS[0:16, b, 0:ncols], pbuf[b])

        def fm_s(epi):
            slot, sb_ = wnext()
            b = acc_bank()
            for j in range(4):
                for c in range(16):
                    mm(PS[:, b, j * 16:(j + 1) * 16], slot[:, c, j * 128:(j + 1) * 128], xnS[:, c, :], c == 0, c == 15, [sb_, b_xnS], [pbuf[b]],
                       inc=(c == 15 and j == 3))
            epi(PS[:, b, 0:64].rearrange("p (j t) -> p j t", j=4), pbuf[b])
        tm_s(8, lambda acc, ab: act(gz[0:16], acc, AF.Copy, [ab], [b_z["gz"]]))
        for pr in range(2):
            tm_s(512, lambda acc, ab, pr=pr: act(v_s[0:16, pr * 512:(pr + 1) * 512], acc, AF.Copy, [ab], [b_z["v"]]))
        for pr in range(2):
            tm_s(512, lambda acc, ab, pr=pr: act(og_s[0:16, pr * 512:(pr + 1) * 512], acc, AF.Sigmoid, [ab], [b_z["og"]]))
        for pr in range(2):
            tm_s(512, lambda acc, ab, pr=pr: act(u_s[0:16, pr * 512:(pr + 1) * 512], acc, AF.Copy, [ab], [b_z["u"]]))
        for pr in range(2):
            fm_s(lambda acc, ab, pr=pr: act(gel_s[:, pr * 4:pr * 4 + 4, :], acc, AF.Gelu, [ab], [b_z["gel"]]))
        for pr in range(2):
            fm_s(lambda acc, ab, pr=pr: act(xr_s[:, pr * 4:pr * 4 + 4, :], acc, AF.Copy, [ab], [b_z["xr"]]))

        mS2 = A.mark()
        sh_tok = A.alloc([1024], F32)
        src_tok = A.alloc([3, 1024], F32)
        b_sh, b_src = Buf("sh"), Buf("src")
        k.dma("sp", sh_tok[0:16], sh_d, writes=[b_sh])
        k.dma("sp", src_tok[0:16], src_d, writes=[b_src])
        h0T = A.alloc([8, NS], F32)
        bufT = A.alloc([8, 3, NS], F32)
        b_h0T, b_bufT = Buf("h0T"), Buf("bufT")
        b = tr_bank()
        for c in range(8):
            tr(PS[:, b, c * 16:(c + 1) * 16], sh_tok[0:16, c * 128:(c + 1) * 128], idf[0:16, 0:16], [b_sh, b_const], [pbuf[b]], inc=(c == 7))
        dv(lambda e, b=b: e.tensor_copy(h0T, PS[:, b, 0:128].rearrange("p (c t) -> p c t", c=8)), [pbuf[b]], [b_h0T])
        b = tr_bank()
        for c in range(8):
            for j in range(3):
                tr(PS[:, b, (c * 3 + j) * 16:(c * 3 + j + 1) * 16], src_tok[0:16, j, c * 128:(c + 1) * 128], idf[0:16, 0:16], [b_src, b_const], [pbuf[b]],
                   inc=(c == 7 and j == 2))
        dv(lambda e, b=b: e.tensor_copy(bufT, PS[:, b, 0:384].rearrange("p (c j t) -> p c j t", c=8, j=3)), [pbuf[b]], [b_bufT])
        xcS = A.alloc([8, NS], F32)
        tS = A.alloc([8, NS], F32)
        xcbS = A.alloc([8, NS], BF16)
        rS = A.alloc([8, NS], F32)
        iS = A.alloc([8, NS], F32)
        aS = A.alloc([8, NS], F32)
        muS = A.alloc([8, NS], F32)
        hS = A.alloc([8, NS], F32)
        srcN = A.alloc([8, 3, NS], F32)
        b_r = [Buf("r%d" % i) for i in range(10)]
        Wt = lambda tap: prm[:, P_CRW + tap * 8:P_CRW + tap * 8 + 8].unsqueeze(2).to_broadcast([128, 8, NS])
        pbS = lambda col: prm[:, col:col + 8].unsqueeze(2).to_broadcast([128, 8, NS])
        dv(lambda e: e.tensor_tensor(xcS, bufT[:, :, 0, :], Wt(0), ALU.mult), [b_bufT, b_const], [b_r[0]])
        for j in (1, 2):
            dv(lambda e, j=j: e.tensor_tensor(tS, bufT[:, :, j, :], Wt(j), ALU.mult), [b_bufT, b_const], [b_r[1]])
            dv(lambda e: e.tensor_tensor(xcS, xcS, tS, ALU.add), [b_r[0], b_r[1]], [b_r[0]])
        dv(lambda e: e.tensor_tensor(tS, xr_s, Wt(3), ALU.mult), [b_z["xr"], b_const], [b_r[1]])
        dv(lambda e: e.tensor_tensor(xcS, xcS, tS, ALU.add), [b_r[0], b_r[1]], [b_r[0]])
        dv(lambda e: e.tensor_tensor(xcS, xcS, pbS(P_CRB), ALU.add), [b_r[0], b_const], [b_r[0]])
        dv(lambda e: e.tensor_copy(xcbS, xcS), [b_r[0]], [b_r[2]])
        for (W, dst, db, pcol) in ((lwab, rS, b_r[3], P_LBA), (lwxb, iS, b_r[4], P_LBX)):
            b = acc_bank()
            for c in range(8):
                mm(PS[:, b, c * 16:(c + 1) * 16], W[:, c, :], xcbS[:, c, :], True, True, [b_cw, b_r[2]], [pbuf[b]], inc=(c == 7))
            dv(lambda e, b=b, dst=dst, pcol=pcol: e.tensor_tensor(dst, PS[:, b, 0:128].rearrange("p (c t) -> p c t", c=8), pbS(pcol), ALU.add),
               [pbuf[b], b_const], [db])
            act(dst, dst, AF.Sigmoid, [db], [db])
        dv(lambda e: e.tensor_tensor(tS, rS, ccol[:, 0:8].unsqueeze(2).to_broadcast([128, 8, NS]), ALU.mult), [b_r[3], b_const], [b_r[1]])
        act(aS, tS, AF.Exp, [b_r[1]], [b_r[5]])
        act(muS, tS, AF.Exp, [b_r[1]], [b_r[6]], scale=2.0)
        act(muS, muS, AF.Sqrt, [b_r[6]], [b_r[6]], scale=-1.0, bias=1.0)
        dv(lambda e: e.tensor_tensor(iS, iS, xcS, ALU.mult), [b_r[4], b_r[0]], [b_r[4]])
        dv(lambda e: e.tensor_tensor(muS, muS, iS, ALU.mult), [b_r[6], b_r[4]], [b_r[6]])
        dv(lambda e: e.tensor_tensor(hS, aS, h0T, ALU.mult), [b_r[5], b_h0T], [b_r[7]])
        dv(lambda e: e.tensor_tensor(hS, hS, muS, ALU.add), [b_r[7], b_r[6]], [b_r[7]])
        k.dma("sp", osh_d, hS, reads=[b_r[7]])
        dv(lambda e: e.tensor_copy(srcN[:, :, 0:2, :], bufT[:, :, 1:3, :]), [b_bufT], [b_r[8]])
        dv(lambda e: e.tensor_copy(srcN[:, :, 2, :], xr_s), [b_z["xr"], b_r[8]], [b_r[8]])
        k.dma("sp", osrc_d, srcN, reads=[b_r[8]])
        dv(lambda e: e.tensor_tensor(tS, hS, gel_s, ALU.mult), [b_r[7], b_z["gel"]], [b_r[1]])
        dv(lambda e: e.tensor_tensor(yTs[:, 0:8, :], tS, pbS(P_GRN), ALU.mult), [b_r[1], b_const], [b_yTs_r])
        dv(lambda e: e.tensor_tensor(rS, tS, tS, ALU.mult), [b_r[1], b_r[3]], [b_r[3]])
        dv(lambda e: e.tensor_reduce(ssum_s, rS.rearrange("p c t -> p t c"), AX.X, ALU.add), [b_r[3]], [b_ssum_s])
        k.barrier()
        A.release(mS2)

        mS3 = A.mark()
        smc = A0.alloc([3, 1024], F32)
        snt = A.alloc([4, 256], F32)
        smt = A.alloc([4], F32)
        cmw = A0.alloc([4, 1024], F32)
        cmb = A0.alloc([1024], F32)
        gb = A.alloc([8], F32)
        b_in = Buf("sin")
        for dst, srcd in ((smc, smc_d), (snt, sn_d), (smt, sm_d), (cmw, cmw_d), (cmb, cmb_d), (gb, gb_d)):
            k.dma("sp", dst[0:16], srcd, writes=[b_in])
        P16 = slice(0, 16)
        ucp = A.alloc([1024], F32)
        t1 = A.alloc([1024], F32)
        ucbS = A.alloc([1024], BF16)
        ucT = A.alloc([8, NS], BF16)
        q_s = A.alloc([1024], F32)
        k_s = A.alloc([1024], F32)
        G = A.alloc([48], F32)
        b_m = [Buf("m%d" % i) for i in range(16)]
        dv(lambda e: e.tensor_tensor(ucp[P16], smc[P16, 0, :], cmw[P16, 0, :], ALU.mult), [b_in], [b_m[0]])
        for j in (1, 2):
            dv(lambda e, j=j: e.tensor_tensor(t1[P16], smc[P16, j, :], cmw[P16, j, :], ALU.mult), [b_in], [b_m[1]])
            dv(lambda e: e.tensor_tensor(ucp[P16], ucp[P16], t1[P16], ALU.add), [b_m[0], b_m[1]], [b_m[0]])
        dv(lambda e: e.tensor_tensor(t1[P16], u_s[P16], cmw[P16, 3, :], ALU.mult), [b_in, b_z["u"]], [b_m[1]])
        dv(lambda e: e.tensor_tensor(ucp[P16], ucp[P16], t1[P16], ALU.add), [b_m[0], b_m[1]], [b_m[0]])
        dv(lambda e: e.tensor_tensor(ucp[P16], ucp[P16], cmb[P16], ALU.add), [b_m[0], b_in], [b_m[0]])
        act(ucbS[P16], ucp[P16], AF.Silu, [b_m[0]], [b_m[2]])
        k.dma("sp", osmc_d[:, 0:2, :], smc[P16, 1:3, :], reads=[b_in])
        k.dma("sp", osmc_d[:, 2, :], u_s[P16], reads=[b_z["u"]])
        b = tr_bank()
        pv = psb(b)[:, 0:128].rearrange("p (a b) -> p a b", a=8)
        for c in range(8):
            tr(pv[:, c, :], ucbS[P16, c * 128:(c + 1) * 128], idb[0:16, 0:16], [b_m[2], b_const], [pbuf[b]], inc=(c == 7))
        dv(lambda e, pv=pv: e.tensor_copy(ucT, pv), [pbuf[b]], [b_m[4]])
        for h in range(4):
            for (W, dst, db, scl) in ((wqb, q_s, b_m[5], 1.0), (wkb, k_s, b_m[6], 1.0 / 16)):
                b = acc_bank()
                for ic in range(2):
                    mm(PS[0:16, b, 0:256], ucT[:, h * 2 + ic, :], W[:, h, ic, :], ic == 0, ic == 1, [b_cw, b_m[4]], [pbuf[b]], inc=(ic == 1))
                act(dst[P16, h * 256:(h + 1) * 256], PS[0:16, b, 0:256], AF.Copy, [pbuf[b]], [db], scale=scl)
        gG = lambda i: G[P16, i * 4:(i + 1) * 4]
        b_G = Buf("G")
        dv(lambda e: e.tensor_tensor(G[P16, 0:8], gz[P16], gb[P16], ALU.add), [b_z["gz"], b_in], [b_G])
        act(gG(1), gG(1), AF.Exp, [b_G], [b_G], scale=-1.0)
        act(gG(1), gG(1), AF.Ln, [b_G], [b_G], bias=1.0)
        dv(lambda e: e.tensor_tensor(gG(2), smt[P16], gG(1), ALU.subtract), [b_G, b_in], [b_G])
        dv(lambda e: e.tensor_tensor(gG(3), gG(2), gG(0), ALU.max), [b_G], [b_G])
        k.dma("sp", osm_d, gG(3), reads=[b_G])
        dv(lambda e: e.tensor_tensor(gG(4), gG(2), gG(3), ALU.subtract), [b_G], [b_G])
        act(gG(4), gG(4), AF.Exp, [b_G], [b_G])
        dv(lambda e: e.tensor_tensor(gG(5), gG(0), gG(3), ALU.subtract), [b_G], [b_G])
        act(gG(5), gG(5), AF.Exp, [b_G], [b_G])
        act(gG(6), gG(3), AF.Exp, [b_G], [b_G], scale=-1.0)
        v4 = lambda ap: ap.rearrange("p (h d) -> p h d", h=4)
        g4 = lambda i: gG(i).unsqueeze(2).to_broadcast([16, 4, 256])
        dv(lambda e: e.tensor_tensor(t1[P16], q_s[P16], k_s[P16], ALU.mult), [b_m[5], b_m[6]], [b_m[1]])
        dv(lambda e: e.tensor_reduce(gG(7), v4(t1[P16]), AX.X, ALU.add), [b_m[1], b_G], [b_G])
        dv(lambda e: e.tensor_tensor(v4(t1[P16]), v4(q_s[P16]), snt[P16], ALU.mult), [b_m[5], b_in, b_G], [b_m[1]])
        dv(lambda e: e.tensor_reduce(gG(8), v4(t1[P16]), AX.X, ALU.add), [b_m[1], b_G], [b_G])
        dv(lambda e: e.tensor_tensor(gG(9), gG(7), gG(5), ALU.mult), [b_G], [b_G])
        dv(lambda e: e.tensor_tensor(gG(10), gG(4), gG(8), ALU.mult), [b_G], [b_G])
        dv(lambda e: e.tensor_tensor(gG(10), gG(10), gG(9), ALU.add), [b_G], [b_G])
        act(gG(10), gG(10), AF.Abs, [b_G], [b_G])
        dv(lambda e: e.tensor_tensor(gG(10), gG(10), gG(6), ALU.max), [b_G], [b_G])
        dv(lambda e: e.reciprocal(gG(10), gG(10)), [b_G], [b_G])
        nN = A.alloc([4, 256], F32)
        gvS = A.alloc([4, 256], F32)
        dv(lambda e: e.tensor_tensor(nN[P16], snt[P16], g4(4), ALU.mult), [b_in, b_G], [b_m[7]])
        dv(lambda e: e.tensor_tensor(v4(t1[P16]), v4(k_s[P16]), g4(5), ALU.mult), [b_m[6], b_G, b_m[1]], [b_m[1]])
        dv(lambda e: e.tensor_tensor(nN[P16], nN[P16], v4(t1[P16]), ALU.add), [b_m[7], b_m[1]], [b_m[7]])
        k.dma("sp", osn_d, nN[P16], reads=[b_m[7]])
        dv(lambda e: e.tensor_tensor(gvS[P16], v4(v_s[P16]), g4(5), ALU.mult), [b_z["v"], b_G], [b_m[8]])
        Cq = A.alloc([4, 256], F32)
        b_Cq = Buf("Cq")
        selT = A.alloc([16 * 128], F32)
        b_selT = Buf("selT")
        k.dma("sp", selT[P16], seltok_d, writes=[b_selT])
        vT = A.alloc([8, NS], F32)
        CqT = A.alloc([8, NS], F32)
        wgR = A.alloc([NS, 8], F32)
        qR = [A.alloc([1024], F32) for _ in range(2)]
        kR = [A.alloc([1024], F32) for _ in range(2)]
        Ct = [A.alloc([8, 256], F32) for _ in range(2)]
        jk = A.alloc([256], F32)
        tT = [A.alloc([256], F32) for _ in range(2)]
        b_vT, b_CqT, b_wgR, b_jk = [Buf(x) for x in "vT CqT wgR jk".split()]
        b_tT = [Buf("tT0"), Buf("tT1")]
        b_qR, b_kR, b_Ct = [Buf("qR0"), Buf("qR1")], [Buf("kR0"), Buf("kR1")], [Buf("Ct0"), Buf("Ct1")]
        b = tr_bank()
        for c in range(8):
            tr(PS[:, b, c * 16:(c + 1) * 16], v_s[P16, c * 128:(c + 1) * 128], idf[0:16, 0:16], [b_z["v"], b_const], [pbuf[b]], inc=(c == 7))
        dv(lambda e, b=b: e.tensor_copy(vT, PS[:, b, 0:128].rearrange("p (c t) -> p c t", c=8)), [pbuf[b]], [b_vT])
        b = acc_bank()
        for tok in range(NS):
            mm(PS[:, b, tok * 8:(tok + 1) * 8], selT[P16, tok * 128:(tok + 1) * 128], G[P16, 16:24], True, True, [b_selT, b_G], [pbuf[b]], inc=(tok == NS - 1))
        dv(lambda e, b=b: e.tensor_copy(wgR, PS[:, b, 0:128].rearrange("p (t g) -> p t g", t=NS)), [pbuf[b]], [b_wgR])
        for tok in range(NS):
            s2 = tok % 2
            k.dma("sp", Ct[s2], sC_d[tok].rearrange("h (vh p) k -> p (h vh) k", p=128), writes=[b_Ct[s2]])
            for (src, dstR, dbR, sb1) in ((q_s, qR, b_qR, b_m[5]), (k_s, kR, b_kR, b_m[6])):
                for hh in range(2):
                    bb = 4 + (hh if src is q_s else 2 + hh)
                    mm(PS[:, bb, :], selT[P16, tok * 128:(tok + 1) * 128], src[P16, hh * 512:(hh + 1) * 512], True, True, [b_selT, sb1], [pbuf[bb]], inc=True)
                    act(dstR[s2][:, hh * 512:(hh + 1) * 512], PS[:, bb, :], AF.Copy, [pbuf[bb]], [dbR[s2]])
            for hv in range(8):
                h = hv // 2
                dv(lambda e, s2=s2, hv=hv, h=h, tok=tok: e.scalar_tensor_tensor(jk, Ct[s2][:, hv, :], 1.0, qR[s2][:, h * 256:(h + 1) * 256], op0=ALU.mult, op1=ALU.mult,
                                                                               accum_out=CqT[:, hv, tok:tok + 1]),
                   [b_Ct[s2], b_qR[s2], b_CqT], [b_jk, b_CqT])
                k.op("pool", lambda e, s2=s2, hv=hv, h=h, tok=tok: e.tensor_scalar(tT[hv % 2], kR[s2][:, h * 256:(h + 1) * 256], vT[:, hv, tok:tok + 1], wgR[:, tok, 4 + h:5 + h],
                                                                                  op0=ALU.mult, op1=ALU.mult),
                     reads=[b_kR[s2], b_vT, b_wgR], writes=[b_tT[hv % 2]])
                dv(lambda e, s2=s2, hv=hv, h=h, tok=tok: e.scalar_tensor_tensor(Ct[s2][:, hv, :], Ct[s2][:, hv, :], wgR[:, tok, h:h + 1], tT[hv % 2], op0=ALU.mult, op1=ALU.add),
                   [b_Ct[s2], b_wgR, b_tT[hv % 2]], [b_Ct[s2]])
            k.dma("sp", osC_d[tok].rearrange("h (vh p) k -> p (h vh) k", p=128), Ct[s2], reads=[b_Ct[s2]])
        for q4 in range(2):
            b = tr_bank()
            for c in range(4):
                cc = q4 * 4 + c
                tr(PS[0:16, b, c * 128:(c + 1) * 128], CqT[:, cc, :], idf, [b_CqT, b_const], [pbuf[b]], inc=(c == 3))
            dv(lambda e, b=b, q4=q4: e.tensor_copy(Cq[P16, q4 * 2:q4 * 2 + 2, :], PS[0:16, b, :].rearrange("p (h d) -> p h d", h=2)), [pbuf[b]], [b_Cq])
        hN = A.alloc([4, 256], F32)
        dv(lambda e: e.tensor_tensor(hN[P16], Cq[P16], g4(4), ALU.mult), [b_Cq, b_G], [b_m[9]])
        dv(lambda e: e.tensor_tensor(v4(t1[P16]), v4(v_s[P16]), g4(9), ALU.mult), [b_z["v"], b_G, b_m[1]], [b_m[1]])
        dv(lambda e: e.tensor_tensor(hN[P16], hN[P16], v4(t1[P16]), ALU.add), [b_m[9], b_m[1]], [b_m[9]])
        dv(lambda e: e.tensor_tensor(hN[P16], hN[P16], g4(10), ALU.mult), [b_m[9], b_G], [b_m[9]])
        dv(lambda e: e.tensor_tensor(v4(t1[P16]), hN[P16], hN[P16], ALU.mult), [b_m[9], b_m[1]], [b_m[1]])
        dv(lambda e: e.tensor_reduce(gG(11), v4(t1[P16]), AX.X, ALU.add), [b_m[1], b_G], [b_G])
        dv(lambda e: e.tensor_scalar(gG(11), gG(11), 1.0 / 256, EPS, op0=ALU.mult, op1=ALU.add), [b_G], [b_G])
        act(gG(11), gG(11), AF.Sqrt, [b_G], [b_G])
        dv(lambda e: e.reciprocal(gG(11), gG(11)), [b_G], [b_G])
        dv(lambda e: e.tensor_tensor(hN[P16], hN[P16], g4(11), ALU.mult), [b_m[9], b_G], [b_m[9]])
        dv(lambda e: e.tensor_tensor(hN[P16], hN[P16], gml[P16].unsqueeze(1).to_broadcast([16, 4, 256]), ALU.mult), [b_m[9], b_const], [b_m[9]])
        dv(lambda e: e.tensor_tensor(v4(ucbS[P16]), hN[P16], v4(og_s[P16]), ALU.mult), [b_m[9], b_z["og"], b_m[2], b_m[4]], [b_m[2]])
        b = tr_bank()
        pv = psb(b)[:, 0:128].rearrange("p (a b) -> p a b", a=8)
        for c in range(8):
            tr(pv[:, c, :], ucbS[P16, c * 128:(c + 1) * 128], idb[0:16, 0:16], [b_m[2], b_const], [pbuf[b]], inc=(c == 7))
        dv(lambda e, pv=pv: e.tensor_copy(yTs[:, 8:16, :], pv), [pbuf[b]], [b_yTs_m])
        k.barrier()
        A.release(mS3)
        k.barrier()
        A.release(mS)
        A0.release(R0_LO)

        m_mix = A.mark()
        yT = A0.alloc([16, 1024], BF16)
        ssum = A0.alloc([1024], F32)
        xn = A.alloc([16, 1024], BF16)
        xnb = [Buf("xn%d" % i) for i in range(8)]
        yTb = [[Buf("yT%d_%d" % (c, t)) for t in range(2)] for c in range(16)]
        b_ssum = Buf("ssum")
        TILES = [(0, 512), (512, 512)]

        def xn_bufs(t0, n):
            return xnb[t0 // 128:(t0 + n + 127) // 128]

        for ps_ in range(2):
            main = ps_ == 1
            src = xm_d if main else xp_d
            m0 = A.mark()
            stg = [A.alloc([2048], F32) for _ in range(2)]
            scratch = (stg, [Buf("stg0"), Buf("stg1")], [A.alloc([2048], BF16) for _ in range(2)], [Buf("xb0"), Buf("xb1")],
                       [A.alloc([2048], BF16) for _ in range(2)], [Buf("jk0"), Buf("jk1")], [A.alloc([4], F32) for _ in range(2)], [Buf("st0"), Buf("st1")])
            load_norm(src, 1024, P_GMIX, xn, xnb, scratch)
            k.barrier()
            chk(2)
            A.release(m0)

            if main:
                dve(lambda e: e.tensor_scalar(C32, C32, maskc[:, 0:1], None, op0=ALU.mult), [b_C32, b_const], [b_C32])
                dve(lambda e: e.tensor_scalar(hcar, hcar, maskc[:, 0:1], None, op0=ALU.mult), [b_hcar, b_const], [b_hcar])
                dve(lambda e: e.tensor_scalar(gcar, gcar, maskc[0:4, 0:1], None, op0=ALU.mult), [b_gcar, b_const], [b_gcar])
                dve(lambda e: e.tensor_scalar(rtail, rtail, maskc[:, 0:1], None, op0=ALU.mult), [b_rtail, b_const], [b_rtail])
                dve(lambda e: e.tensor_scalar(mtail, mtail, maskc[:, 0:1], None, op0=ALU.mult), [b_mtail, b_const], [b_mtail])
                dve(lambda e: e.memset(ssum, 0.0), [], [b_ssum])
                chk(20)

            m1 = A.mark()
            R_B = A.alloc([1024], F32, parts=4)
            R_A = A.alloc([1024], F32, parts=4)
            R_ig = R_A
            R_M = A.alloc([1024], F32, parts=4)
            R_w = R_B
            R_g = A.alloc([1024], F32, parts=4)
            R_e = A.alloc([1024], F32, parts=4)
            R_s = A.alloc([16], F32, parts=4)
            gcols = A.alloc([8, 4, 4], F32)
            gsrep = A.alloc([4, 8], F32)
            b_rows = Buf("rows")
            b_gcols = Buf("gcols")
            b_gsrep = Buf("gsrep")

            slot, sb_ = wnext()
            for gi in range(2):
                for ti, (t0, n) in enumerate(TILES):
                    b = acc_bank()
                    for c in range(16):
                        mm(PS[0:4, b, 0:n], slot[:, c, gi * 4:gi * 4 + 4], xn[:, c, t0:t0 + n], c == 0, c == 15,
                           [sb_] + xn_bufs(t0, n), [pbuf[b]], inc=(c == 15))
                    if gi == 0:
                        act(R_ig[:, t0:t0 + n], PS[0:4, b, 0:n], AF.Identity, [pbuf[b], b_const], [b_rows], bias=prm[0:4, P_BI:P_BI + 1])
                    else:
                        act(R_e[:, t0:t0 + n], PS[0:4, b, 0:n], AF.Exp, [pbuf[b], b_const], [b_rows], scale=-1.0, bias=negbf[:, 0:1])
            act(R_e, R_e, AF.Ln, [b_rows], [b_rows], bias=1.0)
            dve(lambda e: e.tensor_tensor_scan(R_B, onesf[0:4, 0:1].to_broadcast([4, 1024]), R_e, gcar[:, 0:1], ALU.mult, ALU.subtract),
                [b_rows, b_gcar, b_const], [b_rows])
            dve(lambda e: e.tensor_tensor(R_A, R_ig, R_B, ALU.subtract), [b_rows], [b_rows])
            dve(lambda e: e.tensor_tensor_scan(R_M, onesf[0:4, 0:1].to_broadcast([4, 1024]), R_A, gcar[:, 1:2], ALU.mult, ALU.max),
                [b_rows, b_gcar], [b_rows])
            dve(lambda e: e.tensor_copy(R_s[:, 0:1], gcar[:, 1:2]), [b_gcar, b_rows], [b_rows])
            dve(lambda e: e.tensor_copy(R_s[:, 1:8], R_M[:, 127:896:128]), [b_rows], [b_rows])
            dve(lambda e: e.tensor_copy(R_s[:, 8:16], R_M[:, 127:1024:128]), [b_rows], [b_rows])
            v3 = lambda r: r.rearrange("p (c t) -> p c t", c=8)
            dve(lambda e: e.tensor_tensor(v3(R_g), v3(R_A), R_s[:, 8:16].unsqueeze(2).to_broadcast([4, 8, 128]), ALU.subtract), [b_rows], [b_rows])
            act(R_g, R_g, AF.Exp, [b_rows], [b_rows])
            dve(lambda e: e.tensor_tensor(R_e, R_B, R_M, ALU.add), [b_rows], [b_rows])
            dve(lambda e: e.tensor_copy(gcar[:, 2:3], R_e[:, 1023:1024]), [b_rows, b_gcar], [b_gcar])
            act(R_e, R_e, AF.Exp, [b_rows], [b_rows], scale=-1.0)
            dve(lambda e: e.tensor_copy(gcar[:, 0:1], R_B[:, 1023:1024]), [b_rows, b_gcar], [b_gcar])
            dve(lambda e: e.tensor_copy(gcar[:, 1:2], R_M[:, 1023:1024]), [b_rows, b_gcar], [b_gcar])
            dve(lambda e: e.tensor_tensor(v3(R_w), R_s[:, 0:8].unsqueeze(2).to_broadcast([4, 8, 128]), v3(R_M), ALU.subtract), [b_rows], [b_rows])
            act(R_w, R_w, AF.Exp, [b_rows], [b_rows])
            b = 4
            pgc = PS[:, b, 0:128].rearrange("p (c q h) -> p c q h", c=8, q=4)
            for c in range(8):
                for q, R in enumerate((R_A, R_w, R_e, R_g)):
                    tr(pgc[:, c, q, :], R[:, c * 128:(c + 1) * 128], idf[0:4, 0:4], [b_rows, b_const], [pbuf[b]], inc=(c == 7 and q == 3))
            dve(lambda e: e.tensor_copy(gcols, pgc), [pbuf[b]], [b_gcols])
            b = 5
            for h in range(4):
                mm(PS[:, b, h * 8:h * 8 + 8], sel[:, h * 128:(h + 1) * 128], R_w[:, 127:1024:128], True, True,
                   [b_rows, b_const], [pbuf[b]], inc=(h == 3))
            dve(lambda e: e.tensor_copy(gsrep, PS[:, 5, 0:32].rearrange("p (h c) -> p h c", h=4)), [pbuf[5]], [b_gsrep])
            chk(3)

            m2 = A.mark()
            vtok = A.alloc([8, 2, 257], BF16)
            b_vtok = [Buf("vtok%d" % i) for i in range(8)]
            ogt = A.alloc([8, 512], BF16)
            b_ogt = [Buf("og%d" % i) for i in range(8)]
            off_ub = A.top
            ub = A.alloc([4, 1028], BF16)
            ndv = ar_t[:, off_ub:off_ub + 2056].rearrange("p (a b) -> p a b", a=8)
            b_ub = [Buf("ub%d" % j) for j in range(4)]
            uc = A.alloc([4, 1024], BF16)
            b_uc = [[Buf("uc%d_%d" % (j, t)) for t in range(2)] for j in range(4)]
            diag = A.alloc([4, 128], BF16)
            b_diag = Buf("diag")
            qT = A.alloc([2, 1024], BF16)
            kT = A.alloc([2, 1024], BF16)
            ktok = A.alloc([8, 256], BF16)
            b_qT, b_kT, b_ktok = Buf("qT"), Buf("kT"), Buf("ktok")
            wk1 = A.alloc([257], F32)
            Eh = A.alloc([512], BF16)
            Pb = A.alloc([128], BF16)
            gv = A.alloc([257], BF16)
            ytk4 = A.alloc([4, 256], BF16)
            sm8 = A.alloc([32], F32)
            b_wk = [Buf("wk%d" % i) for i in range(8)]
            for pr in range(2):
                dve(lambda e: e.memset(vtok[:, :, :, 256:257], 1.0), [], b_vtok)
                slot, sb_ = wnext()

                def epi_v(i, acc, ab):
                    act(vtok[:, i, :, 0:256], acc.rearrange("p (h d) -> p h d", h=2), AF.Copy, [ab], [b_vtok[i]])
                tm_block(slot, sb_, 512, xn, xn_bufs, 8, epi_v)
                if main:
                    slot, sb_ = wnext()

                    def epi_og(i, acc, ab):
                        act(ogt[:, i, :], acc, AF.Sigmoid, [ab], [b_ogt[i]])
                    tm_block(slot, sb_, 512, xn, xn_bufs, 8, epi_og)
                slot, sb_ = wnext()
                dve(lambda e: e.memset(ub[:, :, 0:4], 0.0), [], b_ub)
                dve(lambda e, pr=pr: e.tensor_copy(ub[:, :, 1:4], mtail[:, pr * 4:pr * 4 + 4, :]), [b_mtail], b_ub)

                def epi_u(j, ti, t0, n, acc, ab, pr=pr):
                    act(ub[:, j, 4 + t0:4 + t0 + n], acc, AF.Copy, [ab], [b_ub[j]])
                    if ti == 1:
                        dve(lambda e: e.tensor_copy(mtail[:, pr * 4 + j, :], acc[:, n - 3:n]), [ab, b_ub[j]], [b_mtail])
                fm_block(slot, sb_, 4, xn, xn_bufs, TILES, epi_u)
                for j in range(4):
                    cg = pr * 4 + j
                    for tap in range(4):
                        dve(lambda e, tap=tap, cg=cg: e.tensor_scalar(diag[:, tap, :], idf, prm[:, P_CMW + tap * 8 + cg:P_CMW + tap * 8 + cg + 1], None, op0=ALU.mult),
                            [b_const], [b_diag])
                    for ti, (t0, n) in enumerate(TILES):
                        b = acc_bank()
                        for tap in range(4):
                            k.tag = "ps%dpr%dj%dti%dtap%d" % (ps_, pr, j, ti, tap)
                            if j > 0:
                                k.trace = False
                            mm(PS[:, b, 0:n], diag[:, tap, :], ub[:, j, t0 + tap + 1:t0 + tap + 1 + n], tap == 0, tap == 3,
                               [b_diag, b_ub[j]], [pbuf[b]], inc=(tap == 3))
                        act(uc[:, j, t0:t0 + n], PS[:, b, 0:n], AF.Silu, [pbuf[b], b_const], [b_uc[j][ti]], bias=prm[:, P_CMB + cg:P_CMB + cg + 1])
                if main and pr == 0:
                    chk(21)
                for hl in range(2):
                    h = pr * 2 + hl
                    ucb = lambda t0, n, hl=hl: [b_uc[hl * 2 + ic][t0 // 512] for ic in range(2)]
                    if main:
                        for (W, dstT, dbf, scl) in ((wqb, qT, b_qT, 1.0), (wkb, kT, b_kT, 1.0 / 16)):
                            for oc in range(2):
                                for ti, (t0, n) in enumerate(TILES):
                                    b = acc_bank()
                                    for ic in range(2):
                                        mm(PS[:, b, 0:n], W[:, h, ic, oc * 128:(oc + 1) * 128], uc[:, hl * 2 + ic, t0:t0 + n], ic == 0, ic == 1,
                                           [b_const] + ucb(t0, n), [pbuf[b]], inc=(ic == 1))
                                    act(dstT[:, oc, t0:t0 + n], PS[:, b, 0:n], AF.Copy, [pbuf[b]], [dbf], scale=scl)
                    for i in range(8):
                        b = acc_bank()
                        for ic in range(2):
                            mm(PS[:, b, 0:256], uc[:, hl * 2 + ic, i * 128:(i + 1) * 128], wkb[:, h, ic, :], ic == 0, ic == 1,
                               [b_const] + ucb(i * 128, 128), [pbuf[b]], inc=(ic == 1))
                        act(ktok[:, i, :], PS[:, b, 0:256], AF.Copy, [pbuf[b]], [b_ktok], scale=1.0 / 16)
                    if main and h == 0:
                        chk(22)
                    act(Cb, C32[:, h, :, :], AF.Copy, [b_C32], [b_Cb])
                    for i in range(8):
                        cs = slice(i * 128, (i + 1) * 128)
                        gc = lambda q, i=i, h=h: gcols[:, i, q, h:h + 1]
                        if main and i % 4 == 0:
                            mm(PS[:, 4, :], sel[:, h * 128:(h + 1) * 128], R_M[:, i * 128:i * 128 + 512], True, True, [b_rows, b_const], [pbuf[4]], inc=True)
                            for i4 in range(4):
                                act(Eh[:, i4 * 128:(i4 + 1) * 128], PS[:, 4, i4 * 128:(i4 + 1) * 128], AF.Exp, [pbuf[4], b_gcols], [b_wk[1]],
                                    scale=-1.0, bias=gcols[:, i + i4, 0, h:h + 1])
                            dve(lambda e: e.tensor_tensor(Eh.rearrange("p (c t) -> p c t", c=4), Eh.rearrange("p (c t) -> p c t", c=4),
                                                          m01.unsqueeze(1).to_broadcast([128, 4, 128]), ALU.mult), [b_wk[1], b_const], [b_wk[1]])
                        dve(lambda e, i=i, hl=hl, gc=gc: e.tensor_scalar(gv, vtok[:, i, hl, :], gc(3), None, op0=ALU.mult), [b_vtok[i], b_gcols], [b_wk[0]])
                        if main:
                            for dc in range(2):
                                mm(PS[:, 1, 0:128], kT[:, dc, cs], qT[:, dc, cs], dc == 0, dc == 1, [b_kT, b_qT], [pbuf[1]], inc=(dc == 1))
                        mm(PS[:, 7, 0:257], ktok[:, i, 0:128], gv, True, True, [b_ktok, b_wk[0]], [pbuf[7]], inc=True)
                        mm(PS[:, 0, 0:257], ktok[:, i, 128:256], gv, True, True, [b_ktok, b_wk[0]], [pbuf[0]], inc=True)
                        if main:
                            dve(lambda e, i=i: e.tensor_tensor(Pb, PS[:, 1, 0:128], Eh[:, (i % 4) * 128:(i % 4 + 1) * 128], ALU.mult), [pbuf[1], b_wk[1]], [b_wk[3]])
                            mm(PS[:, 5, 0:257], Pb, vtok[:, i, hl, :], True, True, [b_wk[3], b_vtok[i]], [pbuf[5]], inc=True)
                            for kc in range(2):
                                mm(PS[:, 6, 0:257], qT[:, kc, cs], Cb[:, kc, :], kc == 0, kc == 1, [b_qT, b_Cb], [pbuf[6]], inc=(kc == 1))
                            act(wk1, PS[:, 6, 0:257], AF.Identity, [pbuf[6], b_gcols], [b_wk[4]], scale=gc(1))
                            dve(lambda e, i=i: e.tensor_tensor(ndv[:, i, :], wk1, PS[:, 5, 0:257], ALU.add), [b_wk[4], pbuf[5]] + b_ub, b_ub)
                        if main and h == 0 and i == 0:
                            chk(23)
                        dve(lambda e, h=h, i=i: e.scalar_tensor_tensor(C32[:, h, 0, :], C32[:, h, 0, :], gsrep[:, h, i:i + 1], PS[:, 7, 0:257], op0=ALU.mult, op1=ALU.add),
                            [b_C32, b_gsrep, pbuf[7]], [b_C32])
                        dve(lambda e, h=h, i=i: e.scalar_tensor_tensor(C32[:, h, 1, :], C32[:, h, 1, :], gsrep[:, h, i:i + 1], PS[:, 0, 0:257], op0=ALU.mult, op1=ALU.add),
                            [b_C32, b_gsrep, pbuf[0]], [b_C32])
                        if main and i < 7:
                            act(Cb, C32[:, h, :, :], AF.Copy, [b_C32], [b_Cb])
                    if main:
                        ndh = ndv[:, :, 0:256]
                        act(sm8[:, 0:8], ndv[:, :, 256], AF.Abs, b_ub, [b_wk[6]])
                        dve(lambda e, h=h: e.tensor_tensor(sm8[:, 0:8], sm8[:, 0:8], gcols[:, :, 2, h], ALU.max), [b_wk[6], b_gcols], [b_wk[6]])
                        dve(lambda e: e.reciprocal(sm8[:, 8:16], sm8[:, 0:8]), [b_wk[6]], [b_wk[6]])
                        dve(lambda e: e.tensor_tensor(ndh, ndh, sm8[:, 8:16].unsqueeze(2).to_broadcast([128, 8, 256]), ALU.mult), [b_wk[6]] + b_ub, b_ub)
                        for i in range(8):
                            act(wk1[:, 0:256], ndv[:, i, 0:256], AF.Square, b_ub + [b_wk[6]], [b_wk[4], b_wk[6]], accum_out=sm8[:, 16 + i:17 + i])
                        dve(lambda e: e.tensor_scalar(sm8[:, 24:32], sm8[:, 16:24], 1.0 / 256, EPS, op0=ALU.mult, op1=ALU.add), [b_wk[6]], [b_wk[6]])
                        act(sm8[:, 24:32], sm8[:, 24:32], AF.Sqrt, [b_wk[6]], [b_wk[6]])
                        dve(lambda e: e.reciprocal(sm8[:, 24:32], sm8[:, 24:32]), [b_wk[6]], [b_wk[6]])
                        dve(lambda e: e.tensor_tensor(ndh, ndh, sm8[:, 24:32].unsqueeze(2).to_broadcast([128, 8, 256]), ALU.mult), [b_wk[6]] + b_ub, b_ub)
                        dve(lambda e: e.tensor_tensor(ndh, ndh, gml.unsqueeze(1).to_broadcast([128, 8, 256]), ALU.mult), [b_const] + b_ub, b_ub)
                        for half in range(2):
                            dve(lambda e, half=half, hl=hl: e.tensor_tensor(ytk4, ndv[:, half * 4:half * 4 + 4, 0:256], ogt[:, half * 4:half * 4 + 4, hl * 256:(hl + 1) * 256], ALU.mult),
                                b_ub + b_ogt[half * 4:half * 4 + 4], [b_wk[7]])
                            bT = tr_bank()
                            pv = psb(bT)
                            for i4 in range(4):
                                for hf in range(2):
                                    tr(pv[:, (hf * 4 + i4) * 128:(hf * 4 + i4 + 1) * 128], ytk4[:, i4, hf * 128:(hf + 1) * 128], idb, [b_wk[7], b_const], [pbuf[bT]],
                                       inc=(i4 == 3 and hf == 1))
                            for hf in range(2):
                                cgl = 8 + h * 2 + hf
                                act(yT[:, cgl, half * 512:(half + 1) * 512], pv[:, hf * 512:(hf + 1) * 512], AF.Copy, [pbuf[bT]], [yTb[cgl][half]])
                if main and pr == 0:
                    chk(24)
            k.barrier()
            A.release(m2)
            if main:
                chk(25)
                k.dma("sp", opC_d, C32, reads=[b_C32])
                k.dma("sp", opm_d, gcar[:, 2:3], reads=[b_gcar])
                k.dma("sp", opmc_d, mtail, reads=[b_mtail])
            k.barrier()
            A.release(m1)

            chk(4 if not main else 6)
            m2 = A.mark()
            xrb = A.alloc([4, 1028], BF16)
            b_xrb = [Buf("xrb%d" % j) for j in range(4)]
            gel = A.alloc([4, 1024], BF16)
            b_gel = [[Buf("gel") for t in range(2)] for j in range(4)]
            dgr = A.alloc([4, 128], BF16)
            b_dgr = Buf("dgr")
            RW = [dict(xc=A.alloc([1024], F32), xcb=A.alloc([1024], BF16), rr=A.alloc([1024], F32), ii=A.alloc([1024], F32),
                       aa=A.alloc([1024], F32), mu=A.alloc([1024], F32), hh_=A.alloc([1024], F32), bw=[Buf("rw%d" % i) for i in range(8)]) for _ in range(2)]
            for pr in range(2):
                if main:
                    slot, sb_ = wnext()

                    def epi_gr(j, ti, t0, n, acc, ab):
                        act(gel[:, j, t0:t0 + n], acc, AF.Gelu, [ab], [b_gel[j][ti]])
                    fm_block(slot, sb_, 4, xn, xn_bufs, TILES, epi_gr)
                slot, sb_ = wnext()
                dve(lambda e: e.memset(xrb[:, :, 0:4], 0.0), [], b_xrb)
                dve(lambda e, pr=pr: e.tensor_copy(xrb[:, :, 1:4], rtail[:, pr * 4:pr * 4 + 4, :]), [b_rtail], b_xrb)

                def epi_xr(j, ti, t0, n, acc, ab, pr=pr):
                    act(xrb[:, j, 4 + t0:4 + t0 + n], acc, AF.Copy, [ab], [b_xrb[j]])
                    if ti == 1:
                        dve(lambda e: e.tensor_copy(rtail[:, pr * 4 + j, :], acc[:, n - 3:n]), [ab, b_xrb[j]], [b_rtail])
                fm_block(slot, sb_, 4, xn, xn_bufs, TILES, epi_xr)
                for j in range(4):
                    cg = pr * 4 + j
                    rw_ = RW[j % 2]
                    xc, xcb, rr, ii, aa, mu, hh_, bw = rw_['xc'], rw_['xcb'], rw_['rr'], rw_['ii'], rw_['aa'], rw_['mu'], rw_['hh_'], rw_['bw']
                    for tap in range(4):
                        dve(lambda e, tap=tap, cg=cg, xc=xc, xcb=xcb, rr=rr, ii=ii, aa=aa, mu=mu, hh_=hh_: e.tensor_scalar(dgr[:, tap, :], idf, prm[:, P_CRW + tap * 8 + cg:P_CRW + tap * 8 + cg + 1], None, op0=ALU.mult),
                            [b_const], [b_dgr])
                    for ti, (t0, n) in enumerate(TILES):
                        b = acc_bank()
                        for tap in range(4):
                            mm(PS[:, b, 0:n], dgr[:, tap, :], xrb[:, j, t0 + tap + 1:t0 + tap + 1 + n], tap == 0, tap == 3,
                               [b_dgr, b_xrb[j]], [pbuf[b]], inc=(tap == 3))
                        act(xc[:, t0:t0 + n], PS[:, b, 0:n], AF.Identity, [pbuf[b], b_const], [bw[0]], bias=prm[:, P_CRB + cg:P_CRB + cg + 1])
                    dve(lambda e, xc=xc, xcb=xcb, rr=rr, ii=ii, aa=aa, mu=mu, hh_=hh_: e.tensor_copy(xcb, xc), [bw[0]], [bw[1]])
                    for (W, dst, db, pb) in ((lwab, rr, bw[2], P_LBA), (lwxb, ii, bw[3], P_LBX)):
                        for ti, (t0, n) in enumerate(TILES):
                            b = acc_bank()
                            mm(PS[:, b, 0:n], W[:, cg, :], xcb[:, t0:t0 + n], True, True, [b_const, bw[1]], [pbuf[b]], inc=True)
                            act(dst[:, t0:t0 + n], PS[:, b, 0:n], AF.Sigmoid, [pbuf[b], b_const], [db], bias=prm[:, pb + cg:pb + cg + 1])
                    act(aa, rr, AF.Exp, [bw[2], b_const], [bw[4]], scale=ccol[:, cg:cg + 1])
                    act(mu, rr, AF.Exp, [bw[2], b_const], [bw[5]], scale=ccol2[:, cg:cg + 1])
                    act(mu, mu, AF.Sqrt, [bw[5]], [bw[5]], scale=-1.0, bias=1.0)
                    dve(lambda e, xc=xc, xcb=xcb, rr=rr, ii=ii, aa=aa, mu=mu, hh_=hh_: e.tensor_tensor(ii, ii, xc, ALU.mult), [bw[3], bw[0]], [bw[3]])
                    dve(lambda e, xc=xc, xcb=xcb, rr=rr, ii=ii, aa=aa, mu=mu, hh_=hh_: e.tensor_tensor(mu, mu, ii, ALU.mult), [bw[5], bw[3]], [bw[5]])
                    dve(lambda e, cg=cg, xc=xc, xcb=xcb, rr=rr, ii=ii, aa=aa, mu=mu, hh_=hh_: e.tensor_tensor_scan(hh_, aa, mu, hcar[:, cg:cg + 1], ALU.mult, ALU.add), [bw[4], bw[5], b_hcar], [bw[6]])
                    dve(lambda e, cg=cg, xc=xc, xcb=xcb, rr=rr, ii=ii, aa=aa, mu=mu, hh_=hh_: e.tensor_copy(hcar[:, cg:cg + 1], hh_[:, 1023:1024]), [bw[6], b_hcar], [b_hcar])
                    if main:
                        dve(lambda e, j=j, xc=xc, xcb=xcb, rr=rr, ii=ii, aa=aa, mu=mu, hh_=hh_: e.tensor_tensor(hh_, hh_, gel[:, j, :], ALU.mult), [bw[6]] + b_gel[j], [bw[6]])
                        dve(lambda e, cg=cg, xc=xc, xcb=xcb, rr=rr, ii=ii, aa=aa, mu=mu, hh_=hh_: e.tensor_scalar(yT[:, cg, :], hh_, prm[:, P_GRN + cg:P_GRN + cg + 1], None, op0=ALU.mult),
                            [bw[6], b_const], yTb[cg])
                        dve(lambda e, xc=xc, xcb=xcb, rr=rr, ii=ii, aa=aa, mu=mu, hh_=hh_: e.tensor_tensor(rr, hh_, hh_, ALU.mult), [bw[6], bw[2]], [bw[2]])
                        dve(lambda e, xc=xc, xcb=xcb, rr=rr, ii=ii, aa=aa, mu=mu, hh_=hh_: e.tensor_tensor(ssum, ssum, rr, ALU.add), [bw[2], b_ssum], [b_ssum])
            k.barrier()
            A.release(m2)
            chk(5 if not main else 7)
            if main:
                k.dma("sp", oph_d, hcar, reads=[b_hcar])
                k.dma("sp", oprc_d, rtail, reads=[b_rtail])

        k.barrier()
        A.release(m_mix)
        AM = Arena(ar_t, off_wqb, off_wqb + 4096)
        mkT = AM.alloc([16, 256], BF16)
        b_mkT = [Buf("mkT%d" % c) for c in range(16)]
        mvt = AM.alloc([2, 2048], BF16)
        b_mvt = [[Buf("mvt") for jb in range(4)] for nh in range(2)]
        m4 = A.mark()
        mn = A.alloc([16, 256], BF16)
        mnb = [Buf("mn0"), Buf("mn1")]
        m5 = A.mark()
        stg = [A.alloc([2048], F32) for _ in range(2)]
        scratch = (stg, [Buf("stg0"), Buf("stg1")], [A.alloc([2048], BF16) for _ in range(2)], [Buf("xb0"), Buf("xb1")],
                   [A.alloc([2048], BF16) for _ in range(2)], [Buf("jk0"), Buf("jk1")], [A.alloc([4], F32) for _ in range(2)], [Buf("st0"), Buf("st1")])
        chk(30)
        load_norm(mem_d, 256, P_GMEM, mn, mnb, scratch)
        k.barrier()
        chk(31)
        A.release(m5)
        ost = [A.alloc([4, 256], F32) for _ in range(2)]
        ostb = [Buf("ost0"), Buf("ost1")]
        mn_bufs = lambda t0, n: mnb[t0 // 128:(t0 + n + 127) // 128]
        for jb in range(4):
            slot, sb_ = wnext()
            s2 = jb % 2

            def epi_mk(j, ti, t0, n, acc, ab, jb=jb, s2=s2):
                act(ost[s2][:, j, :], acc, AF.Copy, [ab], [ostb[s2]])
                dve(lambda e: e.tensor_copy(mkT[:, jb * 4 + j, :], ost[s2][:, j, :]), [ostb[s2]], [b_mkT[jb * 4 + j]])
            fm_block(slot, sb_, 4, mn, mn_bufs, [(0, 256)], epi_mk)
            k.dma("sp", omk_d[:, jb * 4:jb * 4 + 4, :], ost[s2], reads=[ostb[s2]])
        chk(32)
        ost2 = [ost[0].rearrange("p a b -> p (a b)")[:, 0:512], ost[1].rearrange("p a b -> p (a b)")[:, 0:512]]
        cnt2 = 0
        for jb in range(4):
            slot, sb_ = wnext()

            def epi_mv(i, acc, ab, jb=jb):
                nonlocal cnt2
                s2 = cnt2 % 2
                cnt2 += 1
                act(ost2[s2], acc, AF.Copy, [ab], [ostb[s2]])
                dve(lambda e: e.tensor_copy(mvt[:, i, jb * 512:(jb + 1) * 512], ost2[s2]), [ostb[s2]], [b_mvt[i][jb]])
                k.dma("sp", omv_d[i * 128:(i + 1) * 128, jb * 512:(jb + 1) * 512], ost2[s2], reads=[ostb[s2]])
            tm_block(slot, sb_, 512, mn, mn_bufs, 2, epi_mv)
        k.barrier()
        A.release(m4)

        chk(8)

        def attn_core(n, heads, kfn, kbuf, vfn, vbuf, qc_, qcb_, oT_, oTb_, ET, b_ET, rden, b_rden):
            for hd in heads:
                for nh in range(2):
                    b = acc_bank()
                    for dc in range(4):
                        c = hd * 4 + dc
                        mm(PS[:, b, 0:n], kfn(hd, dc, nh), qc_[:, c, :], dc == 0, dc == 3,
                           [kbuf(hd, dc), qcb_[c]], [pbuf[b]], inc=(dc == 3))
                    act(ET[:, nh, 0:n], PS[:, b, 0:n], AF.Exp, [pbuf[b]], [b_ET], scale=float(512 ** -0.5))
                b = acc_bank()
                for nh in range(2):
                    mm(PS[:, b, 0:n], onesb, ET[:, nh, 0:n], nh == 0, nh == 1, [b_const, b_ET], [pbuf[b]], inc=(nh == 1))
                dve(lambda e, b=b: e.reciprocal(rden[:, 0:n], PS[:, b, 0:n]), [pbuf[b]], [b_rden])
                for dc in range(4):
                    c = hd * 4 + dc
                    b = acc_bank()
                    for nh in range(2):
                        mm(PS[:, b, 0:n], vfn(hd, dc, nh), ET[:, nh, 0:n], nh == 0, nh == 1,
                           [vbuf(hd, dc, nh), b_ET], [pbuf[b]], inc=(nh == 1))
                    dve(lambda e, b=b, c=c: e.tensor_tensor(oT_[:, c, :], PS[:, b, 0:n], rden[:, 0:n], ALU.mult), [pbuf[b], b_rden], [oTb_[c]])

        def post_tile(tiles, tinfo):
            NT = sum(n_ for _, n_ in tiles)
            nmax = max(n_ for _, n_ in tiles)
            accn["banks"] = [0, 1, 4, 5, 6, 7]
            m_tile = A.mark()
            X = A.alloc([16, NT], F32)
            off_hq = A.top
            hq = A.alloc([16, NT], BF16)
            Xb = [[Buf("X%d_%d" % (c, ti)) for ti in range(len(tiles))] for c in range(16)]
            m3 = A.mark()
            if NT >= 512:
                AH = Arena(ar_t, off_hq, off_hq + 16 * NT // 2)
                stg = [AH.alloc([2048], F32) for _ in range(2)]
            else:
                stg = [A.alloc([2048], F32) for _ in range(2)]
            stgb = [Buf("stg0"), Buf("stg1")]
            rstd = A.alloc([NT], F32)
            b_rstd = Buf("rstd")
            tmp = A.alloc([nmax], F32)
            b_tmp = Buf("tmp")
            gi = 0
            for ti, (t0, n) in enumerate(tiles):
                gs = tinfo[ti]["gs"]
                for i in range(n // gs):
                    s2 = gi % 2
                    gi += 1
                    r0 = t0 + i * gs
                    k.dma("sp", stg[s2][0:gs], tinfo[ti]["xsrc"][i * gs:(i + 1) * gs, :], writes=[stgb[s2]])
                    for q4 in range(4):
                        b = tr_bank()
                        for c in range(4):
                            cc = q4 * 4 + c
                            tr(PS[:, b, c * gs:(c + 1) * gs], stg[s2][0:gs, cc * 128:(cc + 1) * 128], idf[0:gs, 0:gs], [stgb[s2], b_const], [pbuf[b]], inc=(c == 3))
                        act(X[:, q4 * 4:q4 * 4 + 4, r0:r0 + gs], PS[:, b, 0:4 * gs].rearrange("p (c t) -> p c t", c=4), AF.Copy,
                            [pbuf[b]], [Xb[q4 * 4 + c][ti] for c in range(4)])
                b = acc_bank()
                mm(PS[:, b, 0:n], onesf, tinfo[ti]["ssum"], True, True, [b_const, tinfo[ti]["b_ssum"]], [pbuf[b]], inc=True)
                act(rstd[:, t0:t0 + n], PS[:, b, 0:n], AF.Sqrt, [pbuf[b]], [b_rstd], scale=1.0 / 1024, bias=EPS)
            dve(lambda e: e.reciprocal(rstd, rstd), [b_rstd], [b_rstd])
            for jb in range(8):
                slot, sb_ = wnext()
                for j in range(2):
                    m = jb * 2 + j
                    for ti, (t0, n) in enumerate(tiles):
                        b1 = acc_bank()
                        for c in range(8):
                            mm(PS[:, b1, 0:n], slot[:, c, j * 128:(j + 1) * 128], tinfo[ti]["yT"][:, c, :], c == 0, c == 7,
                               [sb_] + tinfo[ti]["yT_rb"], [pbuf[b1]], inc=(c == 7))
                        b2 = acc_bank()
                        for c in range(8, 16):
                            mm(PS[:, b2, 0:n], slot[:, c, j * 128:(j + 1) * 128], tinfo[ti]["yT"][:, c, :], c == 8, c == 15,
                               [sb_] + tinfo[ti]["yT_mb"], [pbuf[b2]], inc=(c == 15))
                        dve(lambda e, b1=b1, t0=t0, n=n: e.tensor_tensor(tmp[:, 0:n], PS[:, b1, 0:n], rstd[:, t0:t0 + n], ALU.mult), [pbuf[b1], b_rstd], [b_tmp])
                        dve(lambda e, m=m, b2=b2, t0=t0, n=n: e.tensor_tensor(X[:, m, t0:t0 + n], X[:, m, t0:t0 + n], PS[:, b2, 0:n], ALU.add), [pbuf[b2], Xb[m][ti]], [Xb[m][ti]])
                        dve(lambda e, m=m, t0=t0, n=n: e.tensor_tensor(X[:, m, t0:t0 + n], X[:, m, t0:t0 + n], tmp[:, 0:n], ALU.add), [b_tmp, Xb[m][ti]], [Xb[m][ti]])
            k.barrier()
            A.release(m3)
            A0.release(R0_LO)

            def rmsnorm_fm(gcol0, out, outb):
                mk_ = A.mark()
                mk0 = A0.mark()
                sq = A0.alloc([16, nmax], BF16)
                rs = A.alloc([nmax], F32)
                b_sq, b_rs = Buf("sq"), Buf("rs")
                for ti, (t0, n) in enumerate(tiles):
                    for c in range(16):
                        act(sq[:, c, 0:n], X[:, c, t0:t0 + n], AF.Square, [Xb[c][ti]], [b_sq])
                    b = acc_bank()
                    for c in range(16):
                        mm(PS[:, b, 0:n], onesb, sq[:, c, 0:n], c == 0, c == 15, [b_const, b_sq], [pbuf[b]], inc=(c == 15))
                    act(rs[:, 0:n], PS[:, b, 0:n], AF.Sqrt, [pbuf[b]], [b_rs], scale=1.0 / 2048, bias=EPS)
                    dve(lambda e, n=n: e.reciprocal(rs[:, 0:n], rs[:, 0:n]), [b_rs], [b_rs])
                    for c in range(16):
                        dve(lambda e, c=c, t0=t0, n=n: e.scalar_tensor_tensor(out[:, c, t0:t0 + n], X[:, c, t0:t0 + n], prm[:, gcol0 + c:gcol0 + c + 1], rs[:, 0:n],
                                                                               op0=ALU.mult, op1=ALU.mult),
                            [Xb[c][ti], b_const, b_rs], [outb[c][ti]])
                k.barrier()
                A.release(mk_)
                A0.release(mk0)

            def tb(bl):
                return lambda t0_, n_: [bl[c][[t for t, _ in tiles].index(t0_)] for c in range(16)]
            chk(9)
            hqb = [[Buf("hq") for _ in tiles] for c in range(16)]
            rmsnorm_fm(P_GXA, hq, hqb)
            m6 = A.mark()
            mk0 = A0.mark()
            qc = A0.alloc([16, NT], BF16)
            qcb = [[Buf("qc") for _ in tiles] for c in range(16)]
            for jb in range(8):
                slot, sb_ = wnext()

                def epi_q(j, ti_, t0_, n_, acc, ab, jb=jb):
                    act(qc[:, jb * 2 + j, t0_:t0_ + n_], acc, AF.Copy, [ab], [qcb[jb * 2 + j][ti_]])
                fm_block(slot, sb_, 2, hq, tb(hqb), tiles, epi_q)
            k.barrier()
            oT = hq
            oTb = [[Buf("oT") for _ in tiles] for c in range(16)]
            for ti, (t0, n) in enumerate(tiles):
                tinfo[ti]["attn"](n, qc[:, :, t0:t0 + n], [qcb[c][ti] for c in range(16)], oT[:, :, t0:t0 + n], [oTb[c][ti] for c in range(16)])
            for jb in range(8):
                slot, sb_ = wnext()

                def epi_co(j, ti_, t0_, n_, acc, ab, jb=jb):
                    m = jb * 2 + j
                    dve(lambda e: e.tensor_tensor(X[:, m, t0_:t0_ + n_], X[:, m, t0_:t0_ + n_], acc, ALU.add), [ab, Xb[m][ti_]], [Xb[m][ti_]])
                fm_block(slot, sb_, 2, oT, tb(oTb), tiles, epi_co)
            k.barrier()
            A.release(m6)
            A0.release(mk0)

            chk(10)
            hn = hq
            hnb = [[Buf("hn") for _ in tiles] for c in range(16)]
            rmsnorm_fm(P_GFFN, hn, hnb)
            m6 = A.mark()
            mk0 = A0.mark()
            hG = A0.alloc([16, NT], BF16)
            hGb = [[Buf("hG") for _ in tiles] for c in range(16)]
            rl = [A.alloc([nmax], F32) for _ in range(2)]
            rlb = [Buf("rl0"), Buf("rl1")]
            cnt3 = [0]
            for g in range(4):
                for jb in range(8):
                    slot, sb_ = wnext()

                    def epi_up(j, ti_, t0_, n_, acc, ab, jb=jb):
                        s2 = cnt3[0] % 2
                        cnt3[0] += 1
                        act(rl[s2][:, 0:n_], acc, AF.Relu, [ab], [rlb[s2]])
                        dve(lambda e: e.tensor_tensor(hG[:, jb * 2 + j, t0_:t0_ + n_], rl[s2][:, 0:n_], rl[s2][:, 0:n_], ALU.mult), [rlb[s2]], [hGb[jb * 2 + j][ti_]])
                    fm_block(slot, sb_, 2, hn, tb(hnb), tiles, epi_up)
                for jb in range(8):
                    slot, sb_ = wnext()

                    def epi_dn(j, ti_, t0_, n_, acc, ab, jb=jb):
                        m = jb * 2 + j
                        dve(lambda e: e.tensor_tensor(X[:, m, t0_:t0_ + n_], X[:, m, t0_:t0_ + n_], acc, ALU.add), [ab, Xb[m][ti_]], [Xb[m][ti_]])
                    fm_block(slot, sb_, 2, hG, tb(hGb), tiles, epi_dn)
            k.barrier()
            A.release(m6)
            A0.release(mk0)

            chk(11)
            m7 = A.mark()
            mk0 = A0.mark()
            sq = hq
            rs = A.alloc([nmax], F32)
            yn = [A0.alloc([16, 128], F32) for _ in range(2)]
            ob = [A0.alloc([2048], F32) for _ in range(2)]
            b_sq, b_rs = Buf("sq"), Buf("rs")
            b_yn = [Buf("yn0"), Buf("yn1")]
            obb = [Buf("ob0"), Buf("ob1")]
            gi = 0
            for ti, (t0, n) in enumerate(tiles):
                for c in range(16):
                    act(sq[:, c, t0:t0 + n], X[:, c, t0:t0 + n], AF.Square, [Xb[c][ti]], [b_sq])
                b = acc_bank()
                for c in range(16):
                    mm(PS[:, b, 0:n], onesb, sq[:, c, t0:t0 + n], c == 0, c == 15, [b_const, b_sq], [pbuf[b]], inc=(c == 15))
                act(rs[:, 0:n], PS[:, b, 0:n], AF.Sqrt, [pbuf[b]], [b_rs], scale=1.0 / 2048, bias=EPS)
                dve(lambda e, n=n: e.reciprocal(rs[:, 0:n], rs[:, 0:n]), [b_rs], [b_rs])
                gs = tinfo[ti]["gs"]
                for i in range(n // gs):
                    s2 = gi % 2
                    gi += 1
                    r0 = t0 + i * gs
                    for c in range(16):
                        dve(lambda e, c=c, i=i, s2=s2, r0=r0, gs=gs: e.scalar_tensor_tensor(yn[s2][:, c, 0:gs], X[:, c, r0:r0 + gs], prm[:, P_GFIN + c:P_GFIN + c + 1],
                                                                                             rs[:, i * gs:(i + 1) * gs], op0=ALU.mult, op1=ALU.mult),
                            [Xb[c][ti], b_const, b_rs], [b_yn[s2]])
                    for q4 in range(4):
                        b = tr_bank()
                        for c in range(4):
                            cc = q4 * 4 + c
                            tr(PS[0:gs, b, c * 128:(c + 1) * 128], yn[s2][:, cc, 0:gs], idf, [b_yn[s2], b_const], [pbuf[b]], inc=(c == 3))
                        act(ob[s2][0:gs, q4 * 512:(q4 + 1) * 512], PS[0:gs, b, :], AF.Copy, [pbuf[b]], [obb[s2]])
                    k.dma("sp", tinfo[ti]["ydst"][i * gs:(i + 1) * gs, :], ob[s2][0:gs], reads=[obb[s2]])
            k.barrier()
            A.release(m_tile)
            A0.release(mk0)
            accn["banks"] = [0, 1]

        def attn_prompt(n_, qc_, qcb_, oT_, oTb_):
            mk_ = A.mark()
            ET = A.alloc([2, 512], BF16)
            rden = A.alloc([512], F32)
            attn_core(n_, range(4),
                      lambda hd, dc, nh: mkT[:, hd * 4 + dc, nh * 128:(nh + 1) * 128], lambda hd, dc: b_mkT[hd * 4 + dc],
                      lambda hd, dc, nh: mvt[:, nh, (hd * 4 + dc) * 128:(hd * 4 + dc + 1) * 128], lambda hd, dc, nh: b_mvt[nh][hd],
                      qc_, qcb_, oT_, oTb_, ET, Buf("ET"), rden, Buf("rden"))
            k.barrier()
            A.release(mk_)
        def attn_sample(n_, qc_, qcb_, oT_, oTb_):
            mk_ = A.mark()
            Ks = [A.alloc([2, 512], BF16) for _ in range(2)]
            mkTs = [A.alloc([4, 256], BF16) for _ in range(2)]
            mvts = [A.alloc([2, 512], BF16) for _ in range(2)]
            ET = [A.alloc([2, 16], BF16) for _ in range(2)]
            rden = [A.alloc([16], F32) for _ in range(2)]
            b_Ks = [Buf("Ks0"), Buf("Ks1")]
            b_mk1 = [Buf("mk0"), Buf("mk1")]
            b_mv1 = [Buf("mv0"), Buf("mv1")]
            b_ET = [Buf("ET0"), Buf("ET1")]
            b_rden = [Buf("rd0"), Buf("rd1")]
            its = [(tok, hd) for tok in range(NS) for hd in range(4)]

            def dma_k(it):
                tok, hd = its[it]
                s2 = it % 2
                k.dma("pool", Ks[s2], ck_d[tok][:, hd * 512:(hd + 1) * 512].rearrange("(nh p) d -> p nh d", p=128), writes=[b_Ks[s2]])

            def dma_v(it):
                tok, hd = its[it]
                s2 = it % 2
                k.dma("pool", mvts[s2], cv_d[tok][:, hd * 512:(hd + 1) * 512].rearrange("(nh p) d -> p nh d", p=128), writes=[b_mv1[s2]])

            def st_a(it):
                tok, hd = its[it]
                s2 = it % 2
                b = tr_bank()
                pv = psb(b)
                for dc in range(4):
                    for nh in range(2):
                        tr(pv[:, (dc * 2 + nh) * 128:(dc * 2 + nh + 1) * 128], Ks[s2][:, nh, dc * 128:(dc + 1) * 128], idb, [b_Ks[s2], b_const], [pbuf[b]],
                           inc=(dc == 3 and nh == 1))
                if it % 2 == 0:
                    act(mkTs[s2], pv.rearrange("p (c n) -> p c n", c=4), AF.Copy, [pbuf[b]], [b_mk1[s2]])
                else:
                    dv(lambda e, pv=pv, s2=s2: e.tensor_copy(mkTs[s2], pv.rearrange("p (c n) -> p c n", c=4)), [pbuf[b]], [b_mk1[s2]])

            def st_b(it):
                tok, hd = its[it]
                s2 = it % 2
                attn_core(1, [hd],
                          lambda hd_, dc, nh: mkTs[s2][:, dc, nh * 128:(nh + 1) * 128], lambda hd_, dc: b_mk1[s2],
                          lambda hd_, dc, nh: mvts[s2][:, nh, dc * 128:(dc + 1) * 128], lambda hd_, dc, nh: b_mv1[s2],
                          qc_[:, :, tok:tok + 1], qcb_, oT_[:, :, tok:tok + 1], oTb_, ET[s2], b_ET[s2], rden[s2], b_rden[s2])
            dma_k(0)
            dma_k(1)
            dma_v(0)
            st_a(0)
            for it in range(len(its)):
                if it + 2 < len(its):
                    dma_k(it + 2)
                if it + 1 < len(its):
                    dma_v(it + 1)
                    st_a(it + 1)
                st_b(it)
            k.barrier()
            A.release(mk_)
        tinfo = [dict(xsrc=xm_d[t0:t0 + n], yT=yT[:, :, t0:t0 + n], yT_rb=[yTb[cc][ti] for cc in range(8)], yT_mb=[yTb[cc][ti] for cc in range(8, 16)],
                      ssum=ssum[:, t0:t0 + n], b_ssum=b_ssum, attn=attn_prompt, ydst=y_d[t0:t0 + n], gs=128) for ti, (t0, n) in enumerate(TILES)]
        tinfo.append(dict(xsrc=xs_d, yT=yTs, yT_rb=[b_yTs_r], yT_mb=[b_yTs_m], ssum=ssum_s, b_ssum=b_ssum_s, attn=attn_sample, ydst=ys_d, gs=NS))
        post_tile(TILES + [(1024, NS)], tinfo)
        assert k.dead or wstate["used"] == len(wsched), (wstate, len(wsched))
        k.finish()
        k.emit()
        print("instructions:", k.nins, "arena peak", A.peak, "of", NW, "A0 peak", A0.peak, "of", R0_HI)
    return nc


_CACHE = {}


def _consts():
    ident = np.eye(128, dtype=np.float32)
    s = np.arange(128)[:, None]
    t = np.arange(128)[None, :]
    maskneg = np.where(s <= t, 0.0, -30000.0).astype(np.float32)
    sel = np.zeros((4, 4, 128), np.float32)
    for h in range(4):
        sel[h, h, :] = 1.0
    return ident, maskneg, sel.reshape(4, 512)


def kernel(**inp):
    f = lambda a: np.ascontiguousarray(np.asarray(a, dtype=np.float32))
    if "nc" not in _CACHE:
        _CACHE["nc"] = build_program()
    nc = _CACHE["nc"]
    ident, maskneg, sel = _consts()
    prm = np.zeros((128, NPRM), np.float32)

    def colmajor(v, nch):
        return np.asarray(v, np.float32).reshape(nch, 128).T
    prm[:, P_GMIX:P_GMIX + 16] = colmajor(inp["g_mix"][0], 16)
    prm[:, P_GXA:P_GXA + 16] = colmajor(inp["g_xattn"][0], 16)
    prm[:, P_GMEM:P_GMEM + 16] = colmajor(inp["g_mem"][0], 16)
    prm[:, P_GFFN:P_GFFN + 16] = colmajor(inp["g_ffn"][0], 16)
    prm[:, P_GFIN:P_GFIN + 16] = colmajor(inp["g_final"], 16)
    for tap in range(4):
        prm[:, P_CRW + tap * 8:P_CRW + tap * 8 + 8] = colmajor(inp["conv_rnn_w"][0, tap], 8)
        prm[:, P_CMW + tap * 8:P_CMW + tap * 8 + 8] = colmajor(inp["conv_ml_w"][0, tap], 8)
    prm[:, P_CRB:P_CRB + 8] = colmajor(inp["conv_rnn_b"][0], 8)
    prm[:, P_CMB:P_CMB + 8] = colmajor(inp["conv_ml_b"][0], 8)
    prm[:, P_LBA:P_LBA + 8] = np.asarray(inp["lru_ba"][0], np.float32).T
    prm[:, P_LBX:P_LBX + 8] = np.asarray(inp["lru_bx"][0], np.float32).T
    prm[:, P_LAM:P_LAM + 8] = colmajor(inp["lru_lambda"][0], 8)
    prm[:, P_GRN:P_GRN + 8] = colmajor(inp["g_rnn_out"][0], 8)
    prm[:, P_GML2:P_GML2 + 2] = colmajor(inp["g_ml_out"][0], 2)
    prm[0:4, P_BI] = np.asarray(inp["ml_bi"][0], np.float32)
    prm[0:4, P_BF] = np.asarray(inp["ml_bf"][0], np.float32)
    gmlrep = np.ascontiguousarray(np.broadcast_to(np.asarray(inp["g_ml_out"][0], np.float32)[None, :], (128, 256)))
    shared = dict(
        prm=prm, gmlrep=gmlrep, ident=ident, maskneg=maskneg, sel=sel,
        w_in=f(inp["w_in"][0]), lru_wa=f(inp["lru_wa"][0]), lru_wx=f(inp["lru_wx"][0]),
        ml_wq=f(inp["ml_wq"][0]), ml_wk=f(inp["ml_wk"][0]), w_out=f(inp["w_out"][0]),
        w_cq=f(inp["w_cq"][0]), w_mk=f(inp["w_mk"][0]), w_mv=f(inp["w_mv"][0]), w_co=f(inp["w_co"][0]),
        w_up=f(inp["w_up"][0]), w_down=f(inp["w_down"][0]),
    )
    shared["cmw_rep"] = np.ascontiguousarray(np.broadcast_to(np.asarray(inp["conv_ml_w"][0], np.float32)[None], (16, 4, 1024)))
    st_ = np.zeros((16, 16, 128), np.float32)
    for t_ in range(16):
        st_[t_, t_, :] = 1.0
    shared["seltok"] = st_.reshape(16, 2048)
    shared["cmb_rep"] = np.ascontiguousarray(np.broadcast_to(np.asarray(inp["conv_ml_b"][0], np.float32)[None], (16, 1024)))
    shared["gb_rep"] = np.ascontiguousarray(np.broadcast_to(
        np.concatenate([np.asarray(inp["ml_bi"][0], np.float32), np.asarray(inp["ml_bf"][0], np.float32)])[None], (16, 8)))
    xsm = np.asarray(inp["x_sample"], np.float32)
    xpr = np.asarray(inp["x_prompt"], np.float32)
    memp = np.asarray(inp["mem_prompt"], np.float32)
    in_maps = []
    for c in range(8):
        b, hf = c // 2, c % 2
        d = dict(shared)
        d["xm"] = np.ascontiguousarray(xpr[b, hf * 1024:(hf + 1) * 1024])
        d["xp"] = np.ascontiguousarray(xpr[b, 0:1024])
        d["mem"] = np.ascontiguousarray(memp[b])
        d["mask"] = np.full((128, 1), float(hf), np.float32)
        sl = slice(c * 16, (c + 1) * 16)
        d["xs"] = np.ascontiguousarray(xsm[sl, 0])
        d["s_h"] = f(inp["state_rglru_h"][0, sl])
        d["s_rc"] = f(inp["state_rglru_conv"][0, sl])
        d["s_C"] = f(inp["state_mlstm_C"][0, sl])
        d["s_n"] = f(inp["state_mlstm_n"][0, sl])
        d["s_m"] = f(inp["state_mlstm_m"][0, sl])
        d["s_mc"] = f(inp["state_mlstm_conv"][0, sl])
        d["ck"] = f(inp["cache_mem_k"][0, sl]).reshape(16, 256, 2048)
        d["cv"] = f(inp["cache_mem_v"][0, sl]).reshape(16, 256, 2048)
        in_maps.append(d)
    res = run_bass_kernel_spmd(nc, in_maps, core_ids=list(range(8)))
    R = res.results
    B = 4
    y_prompt = np.zeros((B, 2048, 2048), np.float32)
    p_h = np.zeros((1, B, 1024), np.float32)
    p_rc = np.zeros((1, B, 3, 1024), np.float32)
    p_C = np.zeros((1, B, 4, 256, 256), np.float32)
    p_n = np.zeros((1, B, 4, 256), np.float32)
    p_m = np.zeros((1, B, 4), np.float32)
    p_mc = np.zeros((1, B, 3, 1024), np.float32)
    p_mk = np.zeros((1, B, 256, 4, 512), np.float32)
    p_mv = np.zeros((1, B, 256, 4, 512), np.float32)
    for c in range(8):
        b, hf = c // 2, c % 2
        r = R[c]
        y_prompt[b, hf * 1024:(hf + 1) * 1024] = r["o_y"]
        if hf == 1:
            p_h[0, b] = r["o_ph"].T.reshape(1024)
            p_rc[0, b] = r["o_prc"].transpose(2, 1, 0).reshape(3, 1024)
            p_mc[0, b] = r["o_pmc"].transpose(2, 1, 0).reshape(3, 1024)
            oc = r["o_pC"]
            p_C[0, b] = oc[:, :, :, 0:256].transpose(1, 3, 2, 0).reshape(4, 256, 256)
            p_n[0, b] = oc[:, :, :, 256].transpose(1, 2, 0).reshape(4, 256)
            p_m[0, b] = r["o_pm"].reshape(4)
            p_mk[0, b] = r["o_mkT"].transpose(2, 1, 0).reshape(256, 4, 512)
            p_mv[0, b] = r["o_mv"].reshape(256, 4, 512)
    y_s = np.zeros((128, 1, 2048), np.float32)
    s_h = np.zeros((1, 128, 1024), np.float32)
    s_rc = np.zeros((1, 128, 3, 1024), np.float32)
    s_C = np.zeros((1, 128, 4, 256, 256), np.float32)
    s_n = np.zeros((1, 128, 4, 256), np.float32)
    s_m = np.zeros((1, 128, 4), np.float32)
    s_mc = np.zeros((1, 128, 3, 1024), np.float32)
    for c in range(8):
        r = R[c]
        sl = slice(c * 16, (c + 1) * 16)
        y_s[sl, 0] = r["o_ys"]
        s_h[0, sl] = r["o_sh"].transpose(2, 1, 0).reshape(16, 1024)
        s_rc[0, sl] = r["o_src"].transpose(3, 2, 1, 0).reshape(16, 3, 1024)
        s_C[0, sl] = r["o_sC"]
        s_n[0, sl] = r["o_sn"]
        s_m[0, sl] = r["o_sm"]
        s_mc[0, sl] = r["o_smc"]
    return (y_prompt, y_s, p_h, p_rc, p_C, p_n, p_m, p_mc, p_mk, p_mv, s_h, s_rc, s_C, s_n, s_m, s_mc)
```

```python
import numpy as np
from contextlib import ExitStack
import concourse.bass as bass
import concourse.mybir as mybir
from concourse.bass_utils import run_bass_kernel_spmd

F32 = mybir.dt.float32
BF16 = mybir.dt.bfloat16
AF = mybir.ActivationFunctionType
ALU = mybir.AluOpType
AX = mybir.AxisListType

SAME_ENGINE_WAIT = True
EPS = 1e-6
NSLOT = 2

P_GMIX, P_GXA, P_GMEM, P_GFFN, P_GFIN = 0, 16, 32, 48, 64
P_CRW, P_CRB, P_LBA, P_LBX, P_LAM, P_GRN = 80, 112, 120, 128, 136, 144
P_CMW, P_CMB, P_GML2, P_BI, P_BF = 152, 184, 192, 194, 195
NPRM = 196


import os
STOP = int(os.environ.get("KSTOP", "0"))
DEBUG_SITES = bool(int(os.environ.get("KSITES", "0")))
DBG2 = int(os.environ.get("DBG2", "0"))
DBG3 = int(os.environ.get("DBG3", "0"))


class _Stop(Exception):
    pass


class Buf:
    __slots__ = ("name", "w", "r", "dsem", "dcount")

    def __init__(self, name="b"):
        self.name = name
        self.w = None
        self.r = {}
        self.dsem = None
        self.dcount = 0


class K:
    ENG = ("pe", "act", "dve", "pool", "sp")

    def __init__(self, nc, es):
        self.nc = nc
        self.es = es
        self.ops = {e: [] for e in self.ENG}
        self.sem = {e: es.enter_context(nc.semaphore("s_" + e)) for e in self.ENG}
        self.cnt = {e: 0 for e in self.ENG}
        self.known = {e: {} for e in self.ENG}
        self.semobj = {e: self.sem[e] for e in self.ENG}
        self.nd = 0
        self.dbufs = []
        self.nins = 0
        self.dead = False

    def _need(self, eng, reads, writes):
        need = {}

        def add(k, v):
            if need.get(k, 0) < v:
                need[k] = v
        for b in reads:
            if b.w:
                add(*b.w)
        for b in writes:
            if b.w:
                add(*b.w)
            for k, v in b.r.items():
                add(k, v)
        waits = []
        for k, v in need.items():
            if k == eng and (eng == "pe" or not SAME_ENGINE_WAIT):
                continue
            if self.known[eng].get(k, 0) >= v:
                continue
            self.known[eng][k] = v
            waits.append((self.semobj[k], v))
        return waits

    def op(self, eng, fn, reads=(), writes=(), inc=True):
        if self.dead:
            return
        waits = self._need(eng, reads, writes)
        val = self.cnt[eng] + 1
        if inc:
            self.cnt[eng] = val
        for b in reads:
            if b.r.get(eng, 0) < val:
                b.r[eng] = val
        for b in writes:
            b.w = (eng, val)
            b.r = {}
        sem = self.sem[eng]
        self.nins += 1
        if getattr(self, "trace", False):
            print("TRACE", eng, "val", val, "inc", inc, "waits", [(str(s_), v_) for s_, v_ in waits], "reads", [(b.name, b.w) for b in reads], "writes", [b.name for b in writes])
        import sys as _sys
        fr = _sys._getframe(1)
        site = []
        while fr is not None and len(site) < 3:
            site.append(fr.f_lineno)
            fr = fr.f_back
        site = "SITE" + "_".join(map(str, site)) + "_" + getattr(self, "tag", "")

        def run(e, waits=waits, fn=fn, inc=inc, sem=sem, site=site):
            for s, v in waits:
                e.wait_ge(s, v)
            ins = fn(e)
            if DEBUG_SITES:
                ins.annotate(site)
            if inc:
                ins.then_inc(sem, 1)
        self.ops[eng].append(run)

    def _dsem(self, b):
        if b.dsem is None:
            key = "d%d" % self.nd
            self.nd += 1
            b.dsem = key
            self.semobj[key] = self.es.enter_context(self.nc.semaphore(key))
            self.dbufs.append(b)
        return b.dsem

    def dma(self, q, out, in_, reads=(), writes=(), **kw):
        if self.dead:
            return
        waits = self._need(q, reads, writes)
        bl = list(reads) + list(writes)
        assert len(bl) == 1
        b = bl[0]
        kk = self._dsem(b)
        b.dcount += 16
        v = b.dcount
        if reads:
            b.r[kk] = v
        else:
            b.w = (kk, v)
            b.r = {}
        s = self.semobj[kk]
        self.nins += 1

        def run(e, waits=waits, s=s, out=out, in_=in_, kw=kw):
            for ws, wv in waits:
                e.wait_ge(ws, wv)
            e.dma_start(out=out, in_=in_, **kw).then_inc(s, 16)
        self.ops[q].append(run)

    def barrier(self):
        if self.dead:
            return
        tgt = [(e, self.cnt[e]) for e in self.ENG if self.cnt[e] > 0]
        tgt += [(b.dsem, b.dcount) for b in self.dbufs]
        for eng in self.ENG:
            waits = []
            for kk, v in tgt:
                if kk == eng:
                    continue
                if self.known[eng].get(kk, 0) >= v:
                    continue
                self.known[eng][kk] = v
                waits.append((self.semobj[kk], v))

            def run(e, waits=waits):
                for s, v in waits:
                    e.wait_ge(s, v)
            if waits:
                self.ops[eng].append(run)

    def finish(self):
        self.barrier()

    def emit(self):
        nc = self.nc
        with nc.Block() as block:
            @block.tensor
            def _(e):
                for f in self.ops["pe"]:
                    f(e)

            @block.scalar
            def _(e):
                for f in self.ops["act"]:
                    f(e)

            @block.vector
            def _(e):
                for f in self.ops["dve"]:
                    f(e)

            @block.gpsimd
            def _(e):
                for f in self.ops["pool"]:
                    f(e)

            @block.sync
            def _(e):
                for f in self.ops["sp"]:
                    f(e)


class Arena:
    def __init__(self, ap, lo, hi):
        self.ap = ap
        self.n = hi
        self.top = lo

    def alloc(self, shape, dt, parts=128):
        n = 1
        for s in shape:
            n *= s
        esz = 4 if dt == F32 else 2
        words = (n * esz + 3) // 4
        words = (words + 15) // 16 * 16
        off = self.top
        self.top += words
        assert self.top <= self.n, "arena overflow %d > %d" % (self.top, self.n)
        self.peak = max(getattr(self, "peak", 0), self.top)
        v = self.ap[:, off:off + words]
        if dt != F32:
            v = v.bitcast(dt)
        v = v[:, 0:n]
        if len(shape) == 2:
            v = v.rearrange("p (a b) -> p a b", a=shape[0])
        elif len(shape) == 3:
            v = v.rearrange("p (a b c) -> p a b c", a=shape[0], b=shape[1])
        elif len(shape) == 4:
            v = v.rearrange("p (a b c d) -> p a b c d", a=shape[0], b=shape[1], c=shape[2])
        if parts != 128:
            v = v[0:parts]
        return v

    def mark(self):
        return self.top

    def release(self, m):
        self.top = m


def build_program():
    nc = bass.Bass("TRN2", target_bir_lowering=False)

    def DI(name, shape):
        return nc.dram_tensor(name, list(shape), F32, kind="ExternalInput").ap()

    def DO(name, shape):
        return nc.dram_tensor(name, list(shape), F32, kind="ExternalOutput").ap()

    xm_d = DI("xm", [1024, 2048])
    xp_d = DI("xp", [1024, 2048])
    mem_d = DI("mem", [256, 2048])
    mask_d = DI("mask", [128, 1])
    prm_d = DI("prm", [128, NPRM])
    gml_d = DI("gmlrep", [128, 256])
    id_d = DI("ident", [128, 128])
    mneg_d = DI("maskneg", [128, 128])
    sel_d = DI("sel", [4, 4 * 128])
    w_in_d = DI("w_in", [2048, 5128])
    lwa_d = DI("lru_wa", [8, 128, 128])
    lwx_d = DI("lru_wx", [8, 128, 128])
    wq_d = DI("ml_wq", [4, 256, 256])
    wk_d = DI("ml_wk", [4, 256, 256])
    w_out_d = DI("w_out", [2048, 2048])
    w_cq_d = DI("w_cq", [2048, 2048])
    w_mk_d = DI("w_mk", [2048, 2048])
    w_mv_d = DI("w_mv", [2048, 2048])
    w_co_d = DI("w_co", [2048, 2048])
    w_up_d = DI("w_up", [2048, 8192])
    w_dn_d = DI("w_down", [8192, 2048])

    xs_d = DI("xs", [16, 2048])
    sh_d = DI("s_h", [16, 1024])
    src_d = DI("s_rc", [16, 3, 1024])
    sC_d = DI("s_C", [16, 4, 256, 256])
    sn_d = DI("s_n", [16, 4, 256])
    sm_d = DI("s_m", [16, 4])
    smc_d = DI("s_mc", [16, 3, 1024])
    ck_d = DI("ck", [16, 256, 2048])
    cv_d = DI("cv", [16, 256, 2048])
    seltok_d = DI("seltok", [16, 16 * 128])
    cmw_d = DI("cmw_rep", [16, 4, 1024])
    cmb_d = DI("cmb_rep", [16, 1024])
    gb_d = DI("gb_rep", [16, 8])
    ys_d = DO("o_ys", [16, 2048])
    osh_d = DO("o_sh", [128, 8, 16])
    osrc_d = DO("o_src", [128, 8, 3, 16])
    osC_d = DO("o_sC", [16, 4, 256, 256])
    osn_d = DO("o_sn", [16, 4, 256])
    osm_d = DO("o_sm", [16, 4])
    osmc_d = DO("o_smc", [16, 3, 1024])
    y_d = DO("o_y", [1024, 2048])
    oph_d = DO("o_ph", [128, 8])
    oprc_d = DO("o_prc", [128, 8, 3])
    opmc_d = DO("o_pmc", [128, 8, 3])
    opC_d = DO("o_pC", [128, 4, 2, 257])
    opm_d = DO("o_pm", [4, 1])
    omk_d = DO("o_mkT", [128, 16, 256])
    omv_d = DO("o_mv", [256, 2048])

    with ExitStack() as es:
        k = K(nc, es)
        NW = 52992
        ar_t = es.enter_context(nc.sbuf_tensor("arena", [128, NW], F32))
        A = Arena(ar_t, 0, NW)
        PS = es.enter_context(nc.psum_tensor("ps", [128, 8, 512], F32))

        def psb(b):
            return PS[:, b, :].bitcast(BF16)
        pbuf = [Buf("ps%d" % i) for i in range(8)]

        wslot = [A.alloc([16, 512], BF16) for _ in range(NSLOT)]
        wsb = [Buf("ws%d" % i) for i in range(NSLOT)]
        idf = A.alloc([128], F32)[:, :]
        idb = A.alloc([128], BF16)
        onesb = A.alloc([128], BF16)
        onesf = A.alloc([128], F32)
        mneg = A.alloc([128], F32)
        m01 = A.alloc([128], BF16)
        prm = A.alloc([NPRM], F32)
        gml = A.alloc([256], F32)
        maskc = A.alloc([1], F32)
        sel = A.alloc([4 * 128], F32, parts=4)
        off_wqb = A.top
        wqb = A.alloc([4, 2, 256], BF16)
        wkb = A.alloc([4, 2, 256], BF16)
        lwab = A.alloc([8, 128], BF16)
        lwxb = A.alloc([8, 128], BF16)
        C32 = A.alloc([4, 2, 257], F32)
        Cb = A.alloc([2, 257], BF16)
        hcar = A.alloc([8], F32)
        rtail = A.alloc([8, 3], F32)
        mtail = A.alloc([8, 3], F32)
        ccol = A.alloc([8], F32)
        ccol2 = A.alloc([8], F32)
        negbf = A.alloc([1], F32, parts=4)
        st0 = A.alloc([1], F32)
        gcar = A.alloc([4], F32, parts=4)
        b_const = Buf("const")
        yTs = A.alloc([16, 16], BF16)
        ssum_s = A.alloc([16], F32)
        PBASE = A.top
        R0_LO, R0_HI = PBASE, PBASE + 9216
        A0 = Arena(ar_t, R0_LO, R0_HI)
        A = Arena(ar_t, R0_HI, NW)
        print("persistent words", PBASE, "R12 words", NW - R0_HI)
        b_C32, b_Cb, b_hcar, b_rtail, b_mtail, b_gcar = [Buf(n) for n in "C32 Cb hcar rtail mtail gcar".split()]

        def act(out, in_, func, reads, writes, **kw):
            k.op("act", lambda e: e.activation(out, in_, func, **kw), reads=reads, writes=writes)

        def dve(fn, reads, writes):
            k.op("dve", fn, reads=reads, writes=writes)

        def mm(out, lhsT, rhs, start, stop, reads, writes, inc):
            k.op("pe", lambda e: e.matmul(out, lhsT, rhs, start=start, stop=stop), reads=reads, writes=writes, inc=inc)

        def tr(out, in_, ident, reads, writes, inc):
            k.op("pe", lambda e: e.transpose(out, in_, ident), reads=reads, writes=writes, inc=inc)

        def chk(n):
            if STOP == n:
                k.finish()
                k.dead = True
        for dst, src in ((idf, id_d), (mneg, mneg_d), (prm, prm_d), (gml, gml_d), (maskc, mask_d), (sel, sel_d)):
            k.dma("sp", dst, src, writes=[b_const])
        b_cw = Buf("constw")
        k.dma("pool", wqb, wq_d.rearrange("h (c p) n -> p h c n", p=128), writes=[b_cw])
        k.dma("pool", wkb, wk_d.rearrange("h (c p) n -> p h c n", p=128), writes=[b_cw])
        k.dma("pool", lwab, lwa_d.rearrange("h p n -> p h n"), writes=[b_cw])
        k.dma("pool", lwxb, lwx_d.rearrange("h p n -> p h n"), writes=[b_cw])
        k.op("dve", lambda e: e.memset(st0, 0.0), reads=[b_cw, b_const], writes=[b_const])
        dve(lambda e: e.tensor_copy(idb, idf), [b_const], [b_const])
        dve(lambda e: e.memset(onesb, 1.0), [], [b_const])
        dve(lambda e: e.tensor_scalar(m01, mneg, 0.0, None, op0=ALU.is_equal), [b_const], [b_const])
        dve(lambda e: e.memset(onesf, 1.0), [], [b_const])
        dve(lambda e: e.memset(C32, 0.0), [], [b_C32])
        dve(lambda e: e.memset(hcar, 0.0), [], [b_hcar])
        dve(lambda e: e.memset(gcar, 0.0), [], [b_gcar])
        dve(lambda e: e.memset(rtail, 0.0), [], [b_rtail])
        dve(lambda e: e.memset(mtail, 0.0), [], [b_mtail])
        act(ccol, prm[:, P_LAM:P_LAM + 8], AF.Exp, [b_const], [b_const], scale=-1.0)
        act(ccol, ccol, AF.Ln, [b_const], [b_const], bias=1.0)
        dve(lambda e: e.tensor_scalar(ccol2, ccol, -16.0, None, op0=ALU.mult), [b_const], [b_const])
        dve(lambda e: e.tensor_scalar(ccol, ccol, -8.0, None, op0=ALU.mult), [b_const], [b_const])
        dve(lambda e: e.tensor_scalar(negbf, prm[0:4, P_BF:P_BF + 1], -1.0, None, op0=ALU.mult), [b_const], [b_const])

        chk(1)
        wsched = []
        wstate = {"issued": 0, "used": 0, "cnt": [0, 0]}
        wassign = {}
        wflat = [w_.rearrange("p a b -> p (a b)") for w_ in wslot]
        hslot = [wflat[kk // 2][:, (kk % 2) * 4096:(kk % 2 + 1) * 4096].rearrange("p (a b) -> p a b", a=16) for kk in range(4)]
        hsb = [Buf("hs%d" % i) for i in range(4)]

        def wplan(ap, half=False):
            wsched.append((ap, half))

        def wplan256(wd, r0, c0):
            for hh_ in range(2):
                wplan(wd[r0:r0 + 2048, c0 + hh_ * 256:c0 + (hh_ + 1) * 256], True)

        def wissue():
            i = wstate["issued"]
            ap, half = wsched[i]
            nco = ap.shape[1]
            md = 1 if half else 0
            cidx = wstate["cnt"][md]
            wstate["cnt"][md] += 1
            if half:
                sl_, bf_ = hslot[cidx % 4], hsb[cidx % 4]
            else:
                sl_, bf_ = wslot[cidx % 2], wsb[cidx % 2]
            k.dma("pool", sl_[:, :, 0:nco], ap.rearrange("(c p) n -> p c n", p=128), writes=[bf_])
            wassign[i] = (sl_, bf_)
            wstate["issued"] = i + 1

        def wnext():
            i = wstate["used"]
            half = wsched[i][1]
            if wstate["issued"] <= i:
                if i > 0 and wsched[i - 1][1] != half:
                    k.barrier()
                wissue()
            depth = 4 if half else NSLOT
            while wstate["issued"] < min(len(wsched), i + depth) and wsched[wstate["issued"]][1] == half:
                wissue()
            wstate["used"] = i + 1
            return wassign.pop(i)

        def cols(wd, r0, c0, n):
            return wd[r0:r0 + 2048, c0:c0 + n]

        wplan(cols(w_in_d, 0, 5120, 8))
        for c0 in (3072, 3584, 4096, 4608, 2048, 2560, 1024, 1536, 0, 512):
            wplan(cols(w_in_d, 0, c0, 512))
        for ps_ in range(2):
            wplan(cols(w_in_d, 0, 5120, 8))
            for pr in range(2):
                wplan(cols(w_in_d, 0, 3072 + pr * 512, 512))
                if ps_ == 1:
                    wplan(cols(w_in_d, 0, 4096 + pr * 512, 512))
                wplan(cols(w_in_d, 0, 2048 + pr * 512, 512))
            for pr in range(2):
                if ps_ == 1:
                    wplan(cols(w_in_d, 0, 1024 + pr * 512, 512))
                wplan(cols(w_in_d, 0, 0 + pr * 512, 512))
        for j in range(4):
            wplan(cols(w_mk_d, 0, j * 512, 512))
        for j in range(4):
            wplan(cols(w_mv_d, 0, j * 512, 512))
        def plan_post():
            for j in range(4):
                wplan256(w_out_d, 0, j * 512)
            for j in range(4):
                wplan256(w_cq_d, 0, j * 512)
            for j in range(4):
                wplan256(w_co_d, 0, j * 512)
            for g in range(4):
                for j in range(4):
                    wplan256(w_up_d, 0, g * 2048 + j * 512)
                for j in range(4):
                    wplan256(w_dn_d, g * 2048, j * 512)
        plan_post()

        accn = {"i": 0, "banks": [0, 1]}

        def acc_bank():
            bl = accn["banks"]
            b = bl[accn["i"] % len(bl)]
            accn["i"] += 1
            return b

        trn = {"i": 0}

        def tr_bank():
            b = 2 + trn["i"] % 2
            trn["i"] += 1
            return b

        def load_norm(src, T, gcol0, xn, xnb, scratch):
            stg, stgb, xb2, xbb2, junk2, junkb2, st2, stb2 = scratch
            ng = T // 128

            def stage_a(i):
                s2 = i % 2
                junk, junkb, st, stb = junk2[s2], junkb2[s2], st2[s2], stb2[s2]
                k.dma("sp", stg[s2], src[i * 128:(i + 1) * 128, :], writes=[stgb[s2]])
                act(junk, stg[s2], AF.Square, [stgb[s2]], [junkb, stb], accum_out=st[:, 0:1])
                dve(lambda e, st=st: e.tensor_scalar(st[:, 1:2], st[:, 0:1], 1.0 / 2048, EPS, op0=ALU.mult, op1=ALU.add), [stb], [stb])
                act(st[:, 2:3], st[:, 1:2], AF.Sqrt, [stb], [stb])
                dve(lambda e, st=st: e.reciprocal(st[:, 3:4], st[:, 2:3]), [stb], [stb])

            def stage_b(i):
                s2 = i % 2
                xb, xbb, st, stb = xb2[s2], xbb2[s2], st2[s2], stb2[s2]
                dve(lambda e, s2=s2, xb=xb, st=st: e.tensor_scalar(xb, stg[s2], st[:, 3:4], None, op0=ALU.mult), [stb, stgb[s2]], [xbb])
                for hh in range(2):
                    b = tr_bank()
                    pv = psb(b).rearrange("p (a b) -> p a b", a=8)
                    for c in range(8):
                        cc = hh * 8 + c
                        tr(pv[:, c, :], xb[:, cc * 128:(cc + 1) * 128], idb, [xbb, b_const], [pbuf[b]], inc=(c == 7))
                    g = prm[:, gcol0 + hh * 8:gcol0 + hh * 8 + 8].unsqueeze(2).to_broadcast([128, 8, 128])
                    dve(lambda e, pv=pv, g=g, hh=hh, i=i: e.tensor_tensor(xn[:, hh * 8:hh * 8 + 8, i * 128:(i + 1) * 128], pv, g, ALU.mult),
                        [pbuf[b], b_const], [xnb[i]])
            stage_a(0)
            for i in range(ng):
                if i + 1 < ng:
                    stage_a(i + 1)
                stage_b(i)

        def fm_block(slot, sb_, nchunks, xin, xin_bufs, tiles, epi, kc=16):
            for j in range(nchunks):
                for ti, (t0, n) in enumerate(tiles):
                    b = acc_bank()
                    for c in range(kc):
                        mm(PS[:, b, 0:n], slot[:, c, j * 128:(j + 1) * 128], xin[:, c, t0:t0 + n], c == 0, c == kc - 1,
                           [sb_] + xin_bufs(t0, n), [pbuf[b]], inc=(c == kc - 1))
                    epi(j, ti, t0, n, PS[:, b, 0:n], pbuf[b])

        def tm_block(slot, sb_, ncols, xin, xin_bufs, nchunk_tok, epi, kc=16):
            for i in range(nchunk_tok):
                b = acc_bank()
                for c in range(kc):
                    mm(PS[:, b, 0:ncols], xin[:, c, i * 128:(i + 1) * 128], slot[:, c, 0:ncols], c == 0, c == kc - 1,
                       [sb_] + xin_bufs(i * 128, 128), [pbuf[b]], inc=(c == kc - 1))
                epi(i, PS[:, b, 0:ncols], pbuf[b])

        chk(12)
        NS = 16
        mS = A.mark()
        b_yTs_r, b_yTs_m = Buf("yTs_r"), Buf("yTs_m")
        b_ssum_s = Buf("ssum_s")
        xnS = A.alloc([16, NS], BF16)
        b_xnS = Buf("xnS")
        bc = lambda ap, shape: ap.to_broadcast(shape)

        def dv(fn, reads, writes):
            k.op("dve", fn, reads=reads, writes=writes)
        mS1 = A.mark()
        stgS = A.alloc([2048], F32)
        xbS = A.alloc([2048], BF16)
        junkS = A.alloc([2048], BF16)
        stS = A.alloc([4], F32)
        b_stgS, b_l = Buf("stgS"), Buf("l")
        k.dma("sp", stgS[0:16], xs_d, writes=[b_stgS])
        act(junkS[0:16], stgS[0:16], AF.Square, [b_stgS], [b_l], accum_out=stS[0:16, 0:1])
        dv(lambda e: e.tensor_scalar(stS[0:16, 1:2], stS[0:16, 0:1], 1.0 / 2048, EPS, op0=ALU.mult, op1=ALU.add), [b_l], [b_l])
        act(stS[0:16, 2:3], stS[0:16, 1:2], AF.Sqrt, [b_l], [b_l])
        dv(lambda e: e.reciprocal(stS[0:16, 3:4], stS[0:16, 2:3]), [b_l], [b_l])
        dv(lambda e: e.tensor_scalar(xbS[0:16], stgS[0:16], stS[0:16, 3:4], None, op0=ALU.mult), [b_l, b_stgS], [b_l])
        for hh in range(2):
            b = tr_bank()
            pv = psb(b)[:, 0:8 * 16].rearrange("p (a b) -> p a b", a=8)
            for c in range(8):
                cc = hh * 8 + c
                tr(pv[:, c, :], xbS[0:16, cc * 128:(cc + 1) * 128], idb[0:16, 0:16], [b_l, b_const], [pbuf[b]], inc=(c == 7))
            g = prm[:, P_GMIX + hh * 8:P_GMIX + hh * 8 + 8].unsqueeze(2).to_broadcast([128, 8, 16])
            dv(lambda e, pv=pv, g=g, hh=hh: e.tensor_tensor(xnS[:, hh * 8:hh * 8 + 8, :], pv, g, ALU.mult), [pbuf[b], b_const], [b_xnS])
        k.barrier()
        A.release(mS1)

        gz = A.alloc([8], F32)
        v_s = A.alloc([1024], F32)
        og_s = A.alloc([1024], F32)
        u_s = A.alloc([1024], F32)
        gel_s = A.alloc([8, NS], F32)
        xr_s = A.alloc([8, NS], F32)
        b_z = {n_: Buf(n_) for n_ in "gz v og u gel xr".split()}

        def tm_s(ncols, epi):
            slot, sb_ = wnext()
            b = acc_bank()
            for c in range(16):
                mm(PS[0:16, b, 0:ncols], xnS[:, c, :], slot[:, c, 0:ncols], c == 0, c == 15, [sb_, b_xnS], [pbuf[b]], inc=(c == 15))
            epi(PS[0:16, b, 0:ncols], pbuf[b])

        def fm_s(epi):
            slot, sb_ = wnext()
            b = acc_bank()
            for j in range(4):
                for c in range(16):
                    mm(PS[:, b, j * 16:(j + 1) * 16], slot[:, c, j * 128:(j + 1) * 128], xnS[:, c, :], c == 0, c == 15, [sb_, b_xnS], [pbuf[b]],
                       inc=(c == 15 and j == 3))
            epi(PS[:, b, 0:64].rearrange("p (j t) -> p j t", j=4), pbuf[b])
        tm_s(8, lambda acc, ab: act(gz[0:16], acc, AF.Copy, [ab], [b_z["gz"]]))
        for pr in range(2):
            tm_s(512, lambda acc, ab, pr=pr: act(v_s[0:16, pr * 512:(pr + 1) * 512], acc, AF.Copy, [ab], [b_z["v"]]))
        for pr in range(2):
            tm_s(512, lambda acc, ab, pr=pr: act(og_s[0:16, pr * 512:(pr + 1) * 512], acc, AF.Sigmoid, [ab], [b_z["og"]]))
        for pr in range(2):
            tm_s(512, lambda acc, ab, pr=pr: act(u_s[0:16, pr * 512:(pr + 1) * 512], acc, AF.Copy, [ab], [b_z["u"]]))
        for pr in range(2):
            fm_s(lambda acc, ab, pr=pr: act(gel_s[:, pr * 4:pr * 4 + 4, :], acc, AF.Gelu, [ab], [b_z["gel"]]))
        for pr in range(2):
            fm_s(lambda acc, ab, pr=pr: act(xr_s[:, pr * 4:pr * 4 + 4, :], acc, AF.Copy, [ab], [b_z["xr"]]))

        mS2 = A.mark()
        sh_tok = A.alloc([1024], F32)
        src_tok = A.alloc([3, 1024], F32)
        b_sh, b_src = Buf("sh"), Buf("src")
        k.dma("sp", sh_tok[0:16], sh_d, writes=[b_sh])
        k.dma("sp", src_tok[0:16], src_d, writes=[b_src])
        h0T = A.alloc([8, NS], F32)
        bufT = A.alloc([8, 3, NS], F32)
        b_h0T, b_bufT = Buf("h0T"), Buf("bufT")
        b = tr_bank()
        for c in range(8):
            tr(PS[:, b, c * 16:(c + 1) * 16], sh_tok[0:16, c * 128:(c + 1) * 128], idf[0:16, 0:16], [b_sh, b_const], [pbuf[b]], inc=(c == 7))
        dv(lambda e, b=b: e.tensor_copy(h0T, PS[:, b, 0:128].rearrange("p (c t) -> p c t", c=8)), [pbuf[b]], [b_h0T])
        b = tr_bank()
        for c in range(8):
            for j in range(3):
                tr(PS[:, b, (c * 3 + j) * 16:(c * 3 + j + 1) * 16], src_tok[0:16, j, c * 128:(c + 1) * 128], idf[0:16, 0:16], [b_src, b_const], [pbuf[b]],
                   inc=(c == 7 and j == 2))
        dv(lambda e, b=b: e.tensor_copy(bufT, PS[:, b, 0:384].rearrange("p (c j t) -> p c j t", c=8, j=3)), [pbuf[b]], [b_bufT])
        xcS = A.alloc([8, NS], F32)
        tS = A.alloc([8, NS], F32)
        xcbS = A.alloc([8, NS], BF16)
        rS = A.alloc([8, NS], F32)
        iS = A.alloc([8, NS], F32)
        aS = A.alloc([8, NS], F32)
        muS = A.alloc([8, NS], F32)
        hS = A.alloc([8, NS], F32)
        srcN = A.alloc([8, 3, NS], F32)
        b_r = [Buf("r%d" % i) for i in range(10)]
        Wt = lambda tap: prm[:, P_CRW + tap * 8:P_CRW + tap * 8 + 8].unsqueeze(2).to_broadcast([128, 8, NS])
        pbS = lambda col: prm[:, col:col + 8].unsqueeze(2).to_broadcast([128, 8, NS])
        dv(lambda e: e.tensor_tensor(xcS, bufT[:, :, 0, :], Wt(0), ALU.mult), [b_bufT, b_const], [b_r[0]])
        for j in (1, 2):
            dv(lambda e, j=j: e.tensor_tensor(tS, bufT[:, :, j, :], Wt(j), ALU.mult), [b_bufT, b_const], [b_r[1]])
            dv(lambda e: e.tensor_tensor(xcS, xcS, tS, ALU.add), [b_r[0], b_r[1]], [b_r[0]])
        dv(lambda e: e.tensor_tensor(tS, xr_s, Wt(3), ALU.mult), [b_z["xr"], b_const], [b_r[1]])
        dv(lambda e: e.tensor_tensor(xcS, xcS, tS, ALU.add), [b_r[0], b_r[1]], [b_r[0]])
        dv(lambda e: e.tensor_tensor(xcS, xcS, pbS(P_CRB), ALU.add), [b_r[0], b_const], [b_r[0]])
        dv(lambda e: e.tensor_copy(xcbS, xcS), [b_r[0]], [b_r[2]])
        for (W, dst, db, pcol) in ((lwab, rS, b_r[3], P_LBA), (lwxb, iS, b_r[4], P_LBX)):
            b = acc_bank()
            for c in range(8):
                mm(PS[:, b, c * 16:(c + 1) * 16], W[:, c, :], xcbS[:, c, :], True, True, [b_cw, b_r[2]], [pbuf[b]], inc=(c == 7))
            dv(lambda e, b=b, dst=dst, pcol=pcol: e.tensor_tensor(dst, PS[:, b, 0:128].rearrange("p (c t) -> p c t", c=8), pbS(pcol), ALU.add),
               [pbuf[b], b_const], [db])
            act(dst, dst, AF.Sigmoid, [db], [db])
        dv(lambda e: e.tensor_tensor(tS, rS, ccol[:, 0:8].unsqueeze(2).to_broadcast([128, 8, NS]), ALU.mult), [b_r[3], b_const], [b_r[1]])
        act(aS, tS, AF.Exp, [b_r[1]], [b_r[5]])
        act(muS, tS, AF.Exp, [b_r[1]], [b_r[6]], scale=2.0)
        act(muS, muS, AF.Sqrt, [b_r[6]], [b_r[6]], scale=-1.0, bias=1.0)
        dv(lambda e: e.tensor_tensor(iS, iS, xcS, ALU.mult), [b_r[4], b_r[0]], [b_r[4]])
        dv(lambda e: e.tensor_tensor(muS, muS, iS, ALU.mult), [b_r[6], b_r[4]], [b_r[6]])
        dv(lambda e: e.tensor_tensor(hS, aS, h0T, ALU.mult), [b_r[5], b_h0T], [b_r[7]])
        dv(lambda e: e.tensor_tensor(hS, hS, muS, ALU.add), [b_r[7], b_r[6]], [b_r[7]])
        k.dma("sp", osh_d, hS, reads=[b_r[7]])
        dv(lambda e: e.tensor_copy(srcN[:, :, 0:2, :], bufT[:, :, 1:3, :]), [b_bufT], [b_r[8]])
        dv(lambda e: e.tensor_copy(srcN[:, :, 2, :], xr_s), [b_z["xr"], b_r[8]], [b_r[8]])
        k.dma("sp", osrc_d, srcN, reads=[b_r[8]])
        dv(lambda e: e.tensor_tensor(tS, hS, gel_s, ALU.mult), [b_r[7], b_z["gel"]], [b_r[1]])
        dv(lambda e: e.tensor_tensor(yTs[:, 0:8, :], tS, pbS(P_GRN), ALU.mult), [b_r[1], b_const], [b_yTs_r])
        dv(lambda e: e.tensor_tensor(rS, tS, tS, ALU.mult), [b_r[1], b_r[3]], [b_r[3]])
        dv(lambda e: e.tensor_reduce(ssum_s, rS.rearrange("p c t -> p t c"), AX.X, ALU.add), [b_r[3]], [b_ssum_s])
        k.barrier()
        A.release(mS2)

        mS3 = A.mark()
        smc = A0.alloc([3, 1024], F32)
        snt = A.alloc([4, 256], F32)
        smt = A.alloc([4], F32)
        cmw = A0.alloc([4, 1024], F32)
        cmb = A0.alloc([1024], F32)
        gb = A.alloc([8], F32)
        b_in = Buf("sin")
        for dst, srcd in ((smc, smc_d), (snt, sn_d), (smt, sm_d), (cmw, cmw_d), (cmb, cmb_d), (gb, gb_d)):
            k.dma("sp", dst[0:16], srcd, writes=[b_in])
        P16 = slice(0, 16)
        ucp = A.alloc([1024], F32)
        t1 = A.alloc([1024], F32)
        ucbS = A.alloc([1024], BF16)
        ucT = A.alloc([8, NS], BF16)
        q_s = A.alloc([1024], F32)
        k_s = A.alloc([1024], F32)
        G = A.alloc([48], F32)
        b_m = [Buf("m%d" % i) for i in range(16)]
        dv(lambda e: e.tensor_tensor(ucp[P16], smc[P16, 0, :], cmw[P16, 0, :], ALU.mult), [b_in], [b_m[0]])
        for j in (1, 2):
            dv(lambda e, j=j: e.tensor_tensor(t1[P16], smc[P16, j, :], cmw[P16, j, :], ALU.mult), [b_in], [b_m[1]])
            dv(lambda e: e.tensor_tensor(ucp[P16], ucp[P16], t1[P16], ALU.add), [b_m[0], b_m[1]], [b_m[0]])
        dv(lambda e: e.tensor_tensor(t1[P16], u_s[P16], cmw[P16, 3, :], ALU.mult), [b_in, b_z["u"]], [b_m[1]])
        dv(lambda e: e.tensor_tensor(ucp[P16], ucp[P16], t1[P16], ALU.add), [b_m[0], b_m[1]], [b_m[0]])
        dv(lambda e: e.tensor_tensor(ucp[P16], ucp[P16], cmb[P16], ALU.add), [b_m[0], b_in], [b_m[0]])
        act(ucbS[P16], ucp[P16], AF.Silu, [b_m[0]], [b_m[2]])
        k.dma("sp", osmc_d[:, 0:2, :], smc[P16, 1:3, :], reads=[b_in])
        k.dma("sp", osmc_d[:, 2, :], u_s[P16], reads=[b_z["u"]])
        b = tr_bank()
        pv = psb(b)[:, 0:128].rearrange("p (a b) -> p a b", a=8)
        for c in range(8):
            tr(pv[:, c, :], ucbS[P16, c * 128:(c + 1) * 128], idb[0:16, 0:16], [b_m[2], b_const], [pbuf[b]], inc=(c == 7))
        dv(lambda e, pv=pv: e.tensor_copy(ucT, pv), [pbuf[b]], [b_m[4]])
        for h in range(4):
            for (W, dst, db, scl) in ((wqb, q_s, b_m[5], 1.0), (wkb, k_s, b_m[6], 1.0 / 16)):
                b = acc_bank()
                for ic in range(2):
                    mm(PS[0:16, b, 0:256], ucT[:, h * 2 + ic, :], W[:, h, ic, :], ic == 0, ic == 1, [b_cw, b_m[4]], [pbuf[b]], inc=(ic == 1))
                act(dst[P16, h * 256:(h + 1) * 256], PS[0:16, b, 0:256], AF.Copy, [pbuf[b]], [db], scale=scl)
        gG = lambda i: G[P16, i * 4:(i + 1) * 4]
        b_G = Buf("G")
        dv(lambda e: e.tensor_tensor(G[P16, 0:8], gz[P16], gb[P16], ALU.add), [b_z["gz"], b_in], [b_G])
        act(gG(1), gG(1), AF.Exp, [b_G], [b_G], scale=-1.0)
        act(gG(1), gG(1), AF.Ln, [b_G], [b_G], bias=1.0)
        dv(lambda e: e.tensor_tensor(gG(2), smt[P16], gG(1), ALU.subtract), [b_G, b_in], [b_G])
        dv(lambda e: e.tensor_tensor(gG(3), gG(2), gG(0), ALU.max), [b_G], [b_G])
        k.dma("sp", osm_d, gG(3), reads=[b_G])
        dv(lambda e: e.tensor_tensor(gG(4), gG(2), gG(3), ALU.subtract), [b_G], [b_G])
        act(gG(4), gG(4), AF.Exp, [b_G], [b_G])
        dv(lambda e: e.tensor_tensor(gG(5), gG(0), gG(3), ALU.subtract), [b_G], [b_G])
        act(gG(5), gG(5), AF.Exp, [b_G], [b_G])
        act(gG(6), gG(3), AF.Exp, [b_G], [b_G], scale=-1.0)
        v4 = lambda ap: ap.rearrange("p (h d) -> p h d", h=4)
        g4 = lambda i: gG(i).unsqueeze(2).to_broadcast([16, 4, 256])
        dv(lambda e: e.tensor_tensor(t1[P16], q_s[P16], k_s[P16], ALU.mult), [b_m[5], b_m[6]], [b_m[1]])
        dv(lambda e: e.tensor_reduce(gG(7), v4(t1[P16]), AX.X, ALU.add), [b_m[1], b_G], [b_G])
        dv(lambda e: e.tensor_tensor(v4(t1[P16]), v4(q_s[P16]), snt[P16], ALU.mult), [b_m[5], b_in, b_G], [b_m[1]])
        dv(lambda e: e.tensor_reduce(gG(8), v4(t1[P16]), AX.X, ALU.add), [b_m[1], b_G], [b_G])
        dv(lambda e: e.tensor_tensor(gG(9), gG(7), gG(5), ALU.mult), [b_G], [b_G])
        dv(lambda e: e.tensor_tensor(gG(10), gG(4), gG(8), ALU.mult), [b_G], [b_G])
        dv(lambda e: e.tensor_tensor(gG(10), gG(10), gG(9), ALU.add), [b_G], [b_G])
        act(gG(10), gG(10), AF.Abs, [b_G], [b_G])
        dv(lambda e: e.tensor_tensor(gG(10), gG(10), gG(6), ALU.max), [b_G], [b_G])
        dv(lambda e: e.reciprocal(gG(10), gG(10)), [b_G], [b_G])
        nN = A.alloc([4, 256], F32)
        gvS = A.alloc([4, 256], F32)
        dv(lambda e: e.tensor_tensor(nN[P16], snt[P16], g4(4), ALU.mult), [b_in, b_G], [b_m[7]])
        dv(lambda e: e.tensor_tensor(v4(t1[P16]), v4(k_s[P16]), g4(5), ALU.mult), [b_m[6], b_G, b_m[1]], [b_m[1]])
        dv(lambda e: e.tensor_tensor(nN[P16], nN[P16], v4(t1[P16]), ALU.add), [b_m[7], b_m[1]], [b_m[7]])
        k.dma("sp", osn_d, nN[P16], reads=[b_m[7]])
        dv(lambda e: e.tensor_tensor(gvS[P16], v4(v_s[P16]), g4(5), ALU.mult), [b_z["v"], b_G], [b_m[8]])
        Cq = A.alloc([4, 256], F32)
        b_Cq = Buf("Cq")
        selT = A.alloc([16 * 128], F32)
        b_selT = Buf("selT")
        k.dma("sp", selT[P16], seltok_d, writes=[b_selT])
        vT = A.alloc([8, NS], F32)
        CqT = A.alloc([8, NS], F32)
        wgR = A.alloc([NS, 8], F32)
        qR = [A.alloc([1024], F32) for _ in range(2)]
        kR = [A.alloc([1024], F32) for _ in range(2)]
        Ct = [A.alloc([8, 256], F32) for _ in range(2)]
        jk = A.alloc([256], F32)
        tT = [A.alloc([256], F32) for _ in range(2)]
        b_vT, b_CqT, b_wgR, b_jk = [Buf(x) for x in "vT CqT wgR jk".split()]
        b_tT = [Buf("tT0"), Buf("tT1")]
        b_qR, b_kR, b_Ct = [Buf("qR0"), Buf("qR1")], [Buf("kR0"), Buf("kR1")], [Buf("Ct0"), Buf("Ct1")]
        b = tr_bank()
        for c in range(8):
            tr(PS[:, b, c * 16:(c + 1) * 16], v_s[P16, c * 128:(c + 1) * 128], idf[0:16, 0:16], [b_z["v"], b_const], [pbuf[b]], inc=(c == 7))
        dv(lambda e, b=b: e.tensor_copy(vT, PS[:, b, 0:128].rearrange("p (c t) -> p c t", c=8)), [pbuf[b]], [b_vT])
        b = acc_bank()
        for tok in range(NS):
            mm(PS[:, b, tok * 8:(tok + 1) * 8], selT[P16, tok * 128:(tok + 1) * 128], G[P16, 16:24], True, True, [b_selT, b_G], [pbuf[b]], inc=(tok == NS - 1))
        dv(lambda e, b=b: e.tensor_copy(wgR, PS[:, b, 0:128].rearrange("p (t g) -> p t g", t=NS)), [pbuf[b]], [b_wgR])
        for tok in range(NS):
            s2 = tok % 2
            k.dma("sp", Ct[s2], sC_d[tok].rearrange("h (vh p) k -> p (h vh) k", p=128), writes=[b_Ct[s2]])
            for (src, dstR, dbR, sb1) in ((q_s, qR, b_qR, b_m[5]), (k_s, kR, b_kR, b_m[6])):
                for hh in range(2):
                    bb = 4 + (hh if src is q_s else 2 + hh)
                    mm(PS[:, bb, :], selT[P16, tok * 128:(tok + 1) * 128], src[P16, hh * 512:(hh + 1) * 512], True, True, [b_selT, sb1], [pbuf[bb]], inc=True)
                    act(dstR[s2][:, hh * 512:(hh + 1) * 512], PS[:, bb, :], AF.Copy, [pbuf[bb]], [dbR[s2]])
            for hv in range(8):
                h = hv // 2
                dv(lambda e, s2=s2, hv=hv, h=h, tok=tok: e.scalar_tensor_tensor(jk, Ct[s2][:, hv, :], 1.0, qR[s2][:, h * 256:(h + 1) * 256], op0=ALU.mult, op1=ALU.mult,
                                                                               accum_out=CqT[:, hv, tok:tok + 1]),
                   [b_Ct[s2], b_qR[s2], b_CqT], [b_jk, b_CqT])
                k.op("pool", lambda e, s2=s2, hv=hv, h=h, tok=tok: e.tensor_scalar(tT[hv % 2], kR[s2][:, h * 256:(h + 1) * 256], vT[:, hv, tok:tok + 1], wgR[:, tok, 4 + h:5 + h],
                                                                                  op0=ALU.mult, op1=ALU.mult),
                     reads=[b_kR[s2], b_vT, b_wgR], writes=[b_tT[hv % 2]])
                dv(lambda e, s2=s2, hv=hv, h=h, tok=tok: e.scalar_tensor_tensor(Ct[s2][:, hv, :], Ct[s2][:, hv, :], wgR[:, tok, h:h + 1], tT[hv % 2], op0=ALU.mult, op1=ALU.add),
                   [b_Ct[s2], b_wgR, b_tT[hv % 2]], [b_Ct[s2]])
            k.dma("sp", osC_d[tok].rearrange("h (vh p) k -> p (h vh) k", p=128), Ct[s2], reads=[b_Ct[s2]])
        for q4 in range(2):
            b = tr_bank()
            for c in range(4):
                cc = q4 * 4 + c
                tr(PS[0:16, b, c * 128:(c + 1) * 128], CqT[:, cc, :], idf, [b_CqT, b_const], [pbuf[b]], inc=(c == 3))
            dv(lambda e, b=b, q4=q4: e.tensor_copy(Cq[P16, q4 * 2:q4 * 2 + 2, :], PS[0:16, b, :].rearrange("p (h d) -> p h d", h=2)), [pbuf[b]], [b_Cq])
        hN = A.alloc([4, 256], F32)
        dv(lambda e: e.tensor_tensor(hN[P16], Cq[P16], g4(4), ALU.mult), [b_Cq, b_G], [b_m[9]])
        dv(lambda e: e.tensor_tensor(v4(t1[P16]), v4(v_s[P16]), g4(9), ALU.mult), [b_z["v"], b_G, b_m[1]], [b_m[1]])
        dv(lambda e: e.tensor_tensor(hN[P16], hN[P16], v4(t1[P16]), ALU.add), [b_m[9], b_m[1]], [b_m[9]])
        dv(lambda e: e.tensor_tensor(hN[P16], hN[P16], g4(10), ALU.mult), [b_m[9], b_G], [b_m[9]])
        dv(lambda e: e.tensor_tensor(v4(t1[P16]), hN[P16], hN[P16], ALU.mult), [b_m[9], b_m[1]], [b_m[1]])
        dv(lambda e: e.tensor_reduce(gG(11), v4(t1[P16]), AX.X, ALU.add), [b_m[1], b_G], [b_G])
        dv(lambda e: e.tensor_scalar(gG(11), gG(11), 1.0 / 256, EPS, op0=ALU.mult, op1=ALU.add), [b_G], [b_G])
        act(gG(11), gG(11), AF.Sqrt, [b_G], [b_G])
        dv(lambda e: e.reciprocal(gG(11), gG(11)), [b_G], [b_G])
        dv(lambda e: e.tensor_tensor(hN[P16], hN[P16], g4(11), ALU.mult), [b_m[9], b_G], [b_m[9]])
        dv(lambda e: e.tensor_tensor(hN[P16], hN[P16], gml[P16].unsqueeze(1).to_broadcast([16, 4, 256]), ALU.mult), [b_m[9], b_const], [b_m[9]])
        dv(lambda e: e.tensor_tensor(v4(ucbS[P16]), hN[P16], v4(og_s[P16]), ALU.mult), [b_m[9], b_z["og"], b_m[2], b_m[4]], [b_m[2]])
        b = tr_bank()
        pv = psb(b)[:, 0:128].rearrange("p (a b) -> p a b", a=8)
        for c in range(8):
            tr(pv[:, c, :], ucbS[P16, c * 128:(c + 1) * 128], idb[0:16, 0:16], [b_m[2], b_const], [pbuf[b]], inc=(c == 7))
        dv(lambda e, pv=pv: e.tensor_copy(yTs[:, 8:16, :], pv), [pbuf[b]], [b_yTs_m])
        k.barrier()
        A.release(mS3)
        k.barrier()
        A.release(mS)
        A0.release(R0_LO)

        m_mix = A.mark()
        yT = A0.alloc([16, 1024], BF16)
        ssum = A0.alloc([1024], F32)
        xn = A.alloc([16, 1024], BF16)
        xnb = [Buf("xn%d" % i) for i in range(8)]
        yTb = [[Buf("yT%d_%d" % (c, t)) for t in range(2)] for c in range(16)]
        b_ssum = Buf("ssum")
        TILES = [(0, 512), (512, 512)]

        def xn_bufs(t0, n):
            return xnb[t0 // 128:(t0 + n + 127) // 128]

        for ps_ in range(2):
            main = ps_ == 1
            src = xm_d if main else xp_d
            m0 = A.mark()
            stg = [A.alloc([2048], F32) for _ in range(2)]
            scratch = (stg, [Buf("stg0"), Buf("stg1")], [A.alloc([2048], BF16) for _ in range(2)], [Buf("xb0"), Buf("xb1")],
                       [A.alloc([2048], BF16) for _ in range(2)], [Buf("jk0"), Buf("jk1")], [A.alloc([4], F32) for _ in range(2)], [Buf("st0"), Buf("st1")])
            load_norm(src, 1024, P_GMIX, xn, xnb, scratch)
            k.barrier()
            chk(2)
            A.release(m0)

            if main:
                dve(lambda e: e.tensor_scalar(C32, C32, maskc[:, 0:1], None, op0=ALU.mult), [b_C32, b_const], [b_C32])
                dve(lambda e: e.tensor_scalar(hcar, hcar, maskc[:, 0:1], None, op0=ALU.mult), [b_hcar, b_const], [b_hcar])
                dve(lambda e: e.tensor_scalar(gcar, gcar, maskc[0:4, 0:1], None, op0=ALU.mult), [b_gcar, b_const], [b_gcar])
                dve(lambda e: e.tensor_scalar(rtail, rtail, maskc[:, 0:1], None, op0=ALU.mult), [b_rtail, b_const], [b_rtail])
                dve(lambda e: e.tensor_scalar(mtail, mtail, maskc[:, 0:1], None, op0=ALU.mult), [b_mtail, b_const], [b_mtail])
                dve(lambda e: e.memset(ssum, 0.0), [], [b_ssum])
                chk(20)

            m1 = A.mark()
            R_B = A.alloc([1024], F32, parts=4)
            R_A = A.alloc([1024], F32, parts=4)
            R_ig = R_A
            R_M = A.alloc([1024], F32, parts=4)
            R_w = R_B
            R_g = A.alloc([1024], F32, parts=4)
            R_e = A.alloc([1024], F32, parts=4)
            R_s = A.alloc([16], F32, parts=4)
            gcols = A.alloc([8, 4, 4], F32)
            gsrep = A.alloc([4, 8], F32)
            b_rows = Buf("rows")
            b_gcols = Buf("gcols")
            b_gsrep = Buf("gsrep")

            slot, sb_ = wnext()
            for gi in range(2):
                for ti, (t0, n) in enumerate(TILES):
                    b = acc_bank()
                    for c in range(16):
                        mm(PS[0:4, b, 0:n], slot[:, c, gi * 4:gi * 4 + 4], xn[:, c, t0:t0 + n], c == 0, c == 15,
                           [sb_] + xn_bufs(t0, n), [pbuf[b]], inc=(c == 15))
                    if gi == 0:
                        act(R_ig[:, t0:t0 + n], PS[0:4, b, 0:n], AF.Identity, [pbuf[b], b_const], [b_rows], bias=prm[0:4, P_BI:P_BI + 1])
                    else:
                        act(R_e[:, t0:t0 + n], PS[0:4, b, 0:n], AF.Exp, [pbuf[b], b_const], [b_rows], scale=-1.0, bias=negbf[:, 0:1])
            act(R_e, R_e, AF.Ln, [b_rows], [b_rows], bias=1.0)
            dve(lambda e: e.tensor_tensor_scan(R_B, onesf[0:4, 0:1].to_broadcast([4, 1024]), R_e, gcar[:, 0:1], ALU.mult, ALU.subtract),
                [b_rows, b_gcar, b_const], [b_rows])
            dve(lambda e: e.tensor_tensor(R_A, R_ig, R_B, ALU.subtract), [b_rows], [b_rows])
            dve(lambda e: e.tensor_tensor_scan(R_M, onesf[0:4, 0:1].to_broadcast([4, 1024]), R_A, gcar[:, 1:2], ALU.mult, ALU.max),
                [b_rows, b_gcar], [b_rows])
            dve(lambda e: e.tensor_copy(R_s[:, 0:1], gcar[:, 1:2]), [b_gcar, b_rows], [b_rows])
            dve(lambda e: e.tensor_copy(R_s[:, 1:8], R_M[:, 127:896:128]), [b_rows], [b_rows])
            dve(lambda e: e.tensor_copy(R_s[:, 8:16], R_M[:, 127:1024:128]), [b_rows], [b_rows])
            v3 = lambda r: r.rearrange("p (c t) -> p c t", c=8)
            dve(lambda e: e.tensor_tensor(v3(R_g), v3(R_A), R_s[:, 8:16].unsqueeze(2).to_broadcast([4, 8, 128]), ALU.subtract), [b_rows], [b_rows])
            act(R_g, R_g, AF.Exp, [b_rows], [b_rows])
            dve(lambda e: e.tensor_tensor(R_e, R_B, R_M, ALU.add), [b_rows], [b_rows])
            dve(lambda e: e.tensor_copy(gcar[:, 2:3], R_e[:, 1023:1024]), [b_rows, b_gcar], [b_gcar])
            act(R_e, R_e, AF.Exp, [b_rows], [b_rows], scale=-1.0)
            dve(lambda e: e.tensor_copy(gcar[:, 0:1], R_B[:, 1023:1024]), [b_rows, b_gcar], [b_gcar])
            dve(lambda e: e.tensor_copy(gcar[:, 1:2], R_M[:, 1023:1024]), [b_rows, b_gcar], [b_gcar])
            dve(lambda e: e.tensor_tensor(v3(R_w), R_s[:, 0:8].unsqueeze(2).to_broadcast([4, 8, 128]), v3(R_M), ALU.subtract), [b_rows], [b_rows])
            act(R_w, R_w, AF.Exp, [b_rows], [b_rows])
            b = 4
            pgc = PS[:, b, 0:128].rearrange("p (c q h) -> p c q h", c=8, q=4)
            for c in range(8):
                for q, R in enumerate((R_A, R_w, R_e, R_g)):
                    tr(pgc[:, c, q, :], R[:, c * 128:(c + 1) * 128], idf[0:4, 0:4], [b_rows, b_const], [pbuf[b]], inc=(c == 7 and q == 3))
            dve(lambda e: e.tensor_copy(gcols, pgc), [pbuf[b]], [b_gcols])
            b = 5
            for h in range(4):
                mm(PS[:, b, h * 8:h * 8 + 8], sel[:, h * 128:(h + 1) * 128], R_w[:, 127:1024:128], True, True,
                   [b_rows, b_const], [pbuf[b]], inc=(h == 3))
            dve(lambda e: e.tensor_copy(gsrep, PS[:, 5, 0:32].rearrange("p (h c) -> p h c", h=4)), [pbuf[5]], [b_gsrep])
            chk(3)

            m2 = A.mark()
            vtok = A.alloc([8, 2, 257], BF16)
            b_vtok = [Buf("vtok%d" % i) for i in range(8)]
            ogt = A.alloc([8, 512], BF16)
            b_ogt = [Buf("og%d" % i) for i in range(8)]
            off_ub = A.top
            ub = A.alloc([4, 1028], BF16)
            ndv = ar_t[:, off_ub:off_ub + 2056].rearrange("p (a b) -> p a b", a=8)
            b_ub = [Buf("ub%d" % j) for j in range(4)]
            uc = A.alloc([4, 1024], BF16)
            b_uc = [[Buf("uc%d_%d" % (j, t)) for t in range(2)] for j in range(4)]
            diag = A.alloc([4, 128], BF16)
            b_diag = Buf("diag")
            qT = A.alloc([2, 1024], BF16)
            kT = A.alloc([2, 1024], BF16)
            ktok = A.alloc([8, 256], BF16)
            b_qT, b_kT, b_ktok = Buf("qT"), Buf("kT"), Buf("ktok")
            wk1 = A.alloc([257], F32)
            Eh = A.alloc([512], BF16)
            Pb = A.alloc([128], BF16)
            gv = A.alloc([257], BF16)
            ytk4 = A.alloc([4, 256], BF16)
            sm8 = A.alloc([32], F32)
            b_wk = [Buf("wk%d" % i) for i in range(8)]
            for pr in range(2):
                dve(lambda e: e.memset(vtok[:, :, :, 256:257], 1.0), [], b_vtok)
                slot, sb_ = wnext()

                def epi_v(i, acc, ab):
                    act(vtok[:, i, :, 0:256], acc.rearrange("p (h d) -> p h d", h=2), AF.Copy, [ab], [b_vtok[i]])
                tm_block(slot, sb_, 512, xn, xn_bufs, 8, epi_v)
                if main:
                    slot, sb_ = wnext()

                    def epi_og(i, acc, ab):
                        act(ogt[:, i, :], acc, AF.Sigmoid, [ab], [b_ogt[i]])
                    tm_block(slot, sb_, 512, xn, xn_bufs, 8, epi_og)
                slot, sb_ = wnext()
                dve(lambda e: e.memset(ub[:, :, 0:4], 0.0), [], b_ub)
                dve(lambda e, pr=pr: e.tensor_copy(ub[:, :, 1:4], mtail[:, pr * 4:pr * 4 + 4, :]), [b_mtail], b_ub)

                def epi_u(j, ti, t0, n, acc, ab, pr=pr):
                    act(ub[:, j, 4 + t0:4 + t0 + n], acc, AF.Copy, [ab], [b_ub[j]])
                    if ti == 1:
                        dve(lambda e: e.tensor_copy(mtail[:, pr * 4 + j, :], acc[:, n - 3:n]), [ab, b_ub[j]], [b_mtail])
                fm_block(slot, sb_, 4, xn, xn_bufs, TILES, epi_u)
                for j in range(4):
                    cg = pr * 4 + j
                    for tap in range(4):
                        dve(lambda e, tap=tap, cg=cg: e.tensor_scalar(diag[:, tap, :], idf, prm[:, P_CMW + tap * 8 + cg:P_CMW + tap * 8 + cg + 1], None, op0=ALU.mult),
                            [b_const], [b_diag])
                    for ti, (t0, n) in enumerate(TILES):
                        b = acc_bank()
                        for tap in range(4):
                            k.tag = "ps%dpr%dj%dti%dtap%d" % (ps_, pr, j, ti, tap)
                            if j > 0:
                                k.trace = False
                            mm(PS[:, b, 0:n], diag[:, tap, :], ub[:, j, t0 + tap + 1:t0 + tap + 1 + n], tap == 0, tap == 3,
                               [b_diag, b_ub[j]], [pbuf[b]], inc=(tap == 3))
                        act(uc[:, j, t0:t0 + n], PS[:, b, 0:n], AF.Silu, [pbuf[b], b_const], [b_uc[j][ti]], bias=prm[:, P_CMB + cg:P_CMB + cg + 1])
                if main and pr == 0:
                    chk(21)
                for hl in range(2):
                    h = pr * 2 + hl
                    ucb = lambda t0, n, hl=hl: [b_uc[hl * 2 + ic][t0 // 512] for ic in range(2)]
                    if main:
                        for (W, dstT, dbf, scl) in ((wqb, qT, b_qT, 1.0), (wkb, kT, b_kT, 1.0 / 16)):
                            for oc in range(2):
                                for ti, (t0, n) in enumerate(TILES):
                                    b = acc_bank()
                                    for ic in range(2):
                                        mm(PS[:, b, 0:n], W[:, h, ic, oc * 128:(oc + 1) * 128], uc[:, hl * 2 + ic, t0:t0 + n], ic == 0, ic == 1,
                                           [b_const] + ucb(t0, n), [pbuf[b]], inc=(ic == 1))
                                    act(dstT[:, oc, t0:t0 + n], PS[:, b, 0:n], AF.Copy, [pbuf[b]], [dbf], scale=scl)
                    for i in range(8):
                        b = acc_bank()
                        for ic in range(2):
                            mm(PS[:, b, 0:256], uc[:, hl * 2 + ic, i * 128:(i + 1) * 128], wkb[:, h, ic, :], ic == 0, ic == 1,
                               [b_const] + ucb(i * 128, 128), [pbuf[b]], inc=(ic == 1))
                        act(ktok[:, i, :], PS[:, b, 0:256], AF.Copy, [pbuf[b]], [b_ktok], scale=1.0 / 16)
                    if main and h == 0:
                        chk(22)
                    act(Cb, C32[:, h, :, :], AF.Copy, [b_C32], [b_Cb])
                    for i in range(8):
                        cs = slice(i * 128, (i + 1) * 128)
                        gc = lambda q, i=i, h=h: gcols[:, i, q, h:h + 1]
                        if main and i % 4 == 0:
                            mm(PS[:, 4, :], sel[:, h * 128:(h + 1) * 128], R_M[:, i * 128:i * 128 + 512], True, True, [b_rows, b_const], [pbuf[4]], inc=True)
                            for i4 in range(4):
                                act(Eh[:, i4 * 128:(i4 + 1) * 128], PS[:, 4, i4 * 128:(i4 + 1) * 128], AF.Exp, [pbuf[4], b_gcols], [b_wk[1]],
                                    scale=-1.0, bias=gcols[:, i + i4, 0, h:h + 1])
                            dve(lambda e: e.tensor_tensor(Eh.rearrange("p (c t) -> p c t", c=4), Eh.rearrange("p (c t) -> p c t", c=4),
                                                          m01.unsqueeze(1).to_broadcast([128, 4, 128]), ALU.mult), [b_wk[1], b_const], [b_wk[1]])
                        dve(lambda e, i=i, hl=hl, gc=gc: e.tensor_scalar(gv, vtok[:, i, hl, :], gc(3), None, op0=ALU.mult), [b_vtok[i], b_gcols], [b_wk[0]])
                        if main:
                            for dc in range(2):
                                mm(PS[:, 1, 0:128], kT[:, dc, cs], qT[:, dc, cs], dc == 0, dc == 1, [b_kT, b_qT], [pbuf[1]], inc=(dc == 1))
                        mm(PS[:, 7, 0:257], ktok[:, i, 0:128], gv, True, True, [b_ktok, b_wk[0]], [pbuf[7]], inc=True)
                        mm(PS[:, 0, 0:257], ktok[:, i, 128:256], gv, True, True, [b_ktok, b_wk[0]], [pbuf[0]], inc=True)
                        if main:
                            dve(lambda e, i=i: e.tensor_tensor(Pb, PS[:, 1, 0:128], Eh[:, (i % 4) * 128:(i % 4 + 1) * 128], ALU.mult), [pbuf[1], b_wk[1]], [b_wk[3]])
                            mm(PS[:, 5, 0:257], Pb, vtok[:, i, hl, :], True, True, [b_wk[3], b_vtok[i]], [pbuf[5]], inc=True)
                            for kc in range(2):
                                mm(PS[:, 6, 0:257], qT[:, kc, cs], Cb[:, kc, :], kc == 0, kc == 1, [b_qT, b_Cb], [pbuf[6]], inc=(kc == 1))
                            act(wk1, PS[:, 6, 0:257], AF.Identity, [pbuf[6], b_gcols], [b_wk[4]], scale=gc(1))
                            dve(lambda e, i=i: e.tensor_tensor(ndv[:, i, :], wk1, PS[:, 5, 0:257], ALU.add), [b_wk[4], pbuf[5]] + b_ub, b_ub)
                        if main and h == 0 and i == 0:
                            chk(23)
                        dve(lambda e, h=h, i=i: e.scalar_tensor_tensor(C32[:, h, 0, :], C32[:, h, 0, :], gsrep[:, h, i:i + 1], PS[:, 7, 0:257], op0=ALU.mult, op1=ALU.add),
                            [b_C32, b_gsrep, pbuf[7]], [b_C32])
                        dve(lambda e, h=h, i=i: e.scalar_tensor_tensor(C32[:, h, 1, :], C32[:, h, 1, :], gsrep[:, h, i:i + 1], PS[:, 0, 0:257], op0=ALU.mult, op1=ALU.add),
                            [b_C32, b_gsrep, pbuf[0]], [b_C32])
                        if main and i < 7:
                            act(Cb, C32[:, h, :, :], AF.Copy, [b_C32], [b_Cb])
                    if main:
                        ndh = ndv[:, :, 0:256]
                        act(sm8[:, 0:8], ndv[:, :, 256], AF.Abs, b_ub, [b_wk[6]])
                        dve(lambda e, h=h: e.tensor_tensor(sm8[:, 0:8], sm8[:, 0:8], gcols[:, :, 2, h], ALU.max), [b_wk[6], b_gcols], [b_wk[6]])
                        dve(lambda e: e.reciprocal(sm8[:, 8:16], sm8[:, 0:8]), [b_wk[6]], [b_wk[6]])
                        dve(lambda e: e.tensor_tensor(ndh, ndh, sm8[:, 8:16].unsqueeze(2).to_broadcast([128, 8, 256]), ALU.mult), [b_wk[6]] + b_ub, b_ub)
                        for i in range(8):
                            act(wk1[:, 0:256], ndv[:, i, 0:256], AF.Square, b_ub + [b_wk[6]], [b_wk[4], b_wk[6]], accum_out=sm8[:, 16 + i:17 + i])
                        dve(lambda e: e.tensor_scalar(sm8[:, 24:32], sm8[:, 16:24], 1.0 / 256, EPS, op0=ALU.mult, op1=ALU.add), [b_wk[6]], [b_wk[6]])
                        act(sm8[:, 24:32], sm8[:, 24:32], AF.Sqrt, [b_wk[6]], [b_wk[6]])
                        dve(lambda e: e.reciprocal(sm8[:, 24:32], sm8[:, 24:32]), [b_wk[6]], [b_wk[6]])
                        dve(lambda e: e.tensor_tensor(ndh, ndh, sm8[:, 24:32].unsqueeze(2).to_broadcast([128, 8, 256]), ALU.mult), [b_wk[6]] + b_ub, b_ub)
                        dve(lambda e: e.tensor_tensor(ndh, ndh, gml.unsqueeze(1).to_broadcast([128, 8, 256]), ALU.mult), [b_const] + b_ub, b_ub)
                        for half in range(2):
                            dve(lambda e, half=half, hl=hl: e.tensor_tensor(ytk4, ndv[:, half * 4:half * 4 + 4, 0:256], ogt[:, half * 4:half * 4 + 4, hl * 256:(hl + 1) * 256], ALU.mult),
                                b_ub + b_ogt[half * 4:half * 4 + 4], [b_wk[7]])
                            bT = tr_bank()
                            pv = psb(bT)
                            for i4 in range(4):
                                for hf in range(2):
                                    tr(pv[:, (hf * 4 + i4) * 128:(hf * 4 + i4 + 1) * 128], ytk4[:, i4, hf * 128:(hf + 1) * 128], idb, [b_wk[7], b_const], [pbuf[bT]],
                                       inc=(i4 == 3 and hf == 1))
                            for hf in range(2):
                                cgl = 8 + h * 2 + hf
                                act(yT[:, cgl, half * 512:(half + 1) * 512], pv[:, hf * 512:(hf + 1) * 512], AF.Copy, [pbuf[bT]], [yTb[cgl][half]])
                if main and pr == 0:
                    chk(24)
            k.barrier()
            A.release(m2)
            if main:
                chk(25)
                k.dma("sp", opC_d, C32, reads=[b_C32])
                k.dma("sp", opm_d, gcar[:, 2:3], reads=[b_gcar])
                k.dma("sp", opmc_d, mtail, reads=[b_mtail])
            k.barrier()
            A.release(m1)

            chk(4 if not main else 6)
            m2 = A.mark()
            xrb = A.alloc([4, 1028], BF16)
            b_xrb = [Buf("xrb%d" % j) for j in range(4)]
            gel = A.alloc([4, 1024], BF16)
            b_gel = [[Buf("gel") for t in range(2)] for j in range(4)]
            dgr = A.alloc([4, 128], BF16)
            b_dgr = Buf("dgr")
            RW = [dict(xc=A.alloc([1024], F32), xcb=A.alloc([1024], BF16), rr=A.alloc([1024], F32), ii=A.alloc([1024], F32),
                       aa=A.alloc([1024], F32), mu=A.alloc([1024], F32), hh_=A.alloc([1024], F32), bw=[Buf("rw%d" % i) for i in range(8)]) for _ in range(2)]
            for pr in range(2):
                if main:
                    slot, sb_ = wnext()

                    def epi_gr(j, ti, t0, n, acc, ab):
                        act(gel[:, j, t0:t0 + n], acc, AF.Gelu, [ab], [b_gel[j][ti]])
                    fm_block(slot, sb_, 4, xn, xn_bufs, TILES, epi_gr)
                slot, sb_ = wnext()
                dve(lambda e: e.memset(xrb[:, :, 0:4], 0.0), [], b_xrb)
                dve(lambda e, pr=pr: e.tensor_copy(xrb[:, :, 1:4], rtail[:, pr * 4:pr * 4 + 4, :]), [b_rtail], b_xrb)

                def epi_xr(j, ti, t0, n, acc, ab, pr=pr):
                    act(xrb[:, j, 4 + t0:4 + t0 + n], acc, AF.Copy, [ab], [b_xrb[j]])
                    if ti == 1:
                        dve(lambda e: e.tensor_copy(rtail[:, pr * 4 + j, :], acc[:, n - 3:n]), [ab, b_xrb[j]], [b_rtail])
                fm_block(slot, sb_, 4, xn, xn_bufs, TILES, epi_xr)
                def rg_front(j, pr=pr):
                    cg = pr * 4 + j
                    rw_ = RW[j % 2]
                    xc, xcb, rr, ii, aa, mu, hh_, bw = rw_['xc'], rw_['xcb'], rw_['rr'], rw_['ii'], rw_['aa'], rw_['mu'], rw_['hh_'], rw_['bw']
                    for tap in range(4):
                        dve(lambda e, tap=tap, cg=cg, xc=xc, xcb=xcb, rr=rr, ii=ii, aa=aa, mu=mu, hh_=hh_: e.tensor_scalar(dgr[:, tap, :], idf, prm[:, P_CRW + tap * 8 + cg:P_CRW + tap * 8 + cg + 1], None, op0=ALU.mult),
                            [b_const], [b_dgr])
                    for ti, (t0, n) in enumerate(TILES):
                        b = acc_bank()
                        for tap in range(4):
                            mm(PS[:, b, 0:n], dgr[:, tap, :], xrb[:, j, t0 + tap + 1:t0 + tap + 1 + n], tap == 0, tap == 3,
                               [b_dgr, b_xrb[j]], [pbuf[b]], inc=(tap == 3))
                        act(xc[:, t0:t0 + n], PS[:, b, 0:n], AF.Identity, [pbuf[b], b_const], [bw[0]], bias=prm[:, P_CRB + cg:P_CRB + cg + 1])
                    dve(lambda e, xc=xc, xcb=xcb, rr=rr, ii=ii, aa=aa, mu=mu, hh_=hh_: e.tensor_copy(xcb, xc), [bw[0]], [bw[1]])
                    for (W, dst, db, pb) in ((lwab, rr, bw[2], P_LBA), (lwxb, ii, bw[3], P_LBX)):
                        for ti, (t0, n) in enumerate(TILES):
                            b = acc_bank()
                            mm(PS[:, b, 0:n], W[:, cg, :], xcb[:, t0:t0 + n], True, True, [b_const, bw[1]], [pbuf[b]], inc=True)
                            act(dst[:, t0:t0 + n], PS[:, b, 0:n], AF.Sigmoid, [pbuf[b], b_const], [db], bias=prm[:, pb + cg:pb + cg + 1])
                def rg_back(j, pr=pr):
                    cg = pr * 4 + j
                    rw_ = RW[j % 2]
                    xc, xcb, rr, ii, aa, mu, hh_, bw = rw_['xc'], rw_['xcb'], rw_['rr'], rw_['ii'], rw_['aa'], rw_['mu'], rw_['hh_'], rw_['bw']
                    act(aa, rr, AF.Exp, [bw[2], b_const], [bw[4]], scale=ccol[:, cg:cg + 1])
                    act(mu, rr, AF.Exp, [bw[2], b_const], [bw[5]], scale=ccol2[:, cg:cg + 1])
                    act(mu, mu, AF.Sqrt, [bw[5]], [bw[5]], scale=-1.0, bias=1.0)
                    dve(lambda e, xc=xc, xcb=xcb, rr=rr, ii=ii, aa=aa, mu=mu, hh_=hh_: e.tensor_tensor(ii, ii, xc, ALU.mult), [bw[3], bw[0]], [bw[3]])
                    dve(lambda e, xc=xc, xcb=xcb, rr=rr, ii=ii, aa=aa, mu=mu, hh_=hh_: e.tensor_tensor(mu, mu, ii, ALU.mult), [bw[5], bw[3]], [bw[5]])
                    dve(lambda e, cg=cg, xc=xc, xcb=xcb, rr=rr, ii=ii, aa=aa, mu=mu, hh_=hh_: e.tensor_tensor_scan(hh_, aa, mu, hcar[:, cg:cg + 1], ALU.mult, ALU.add), [bw[4], bw[5], b_hcar], [bw[6]])
                    dve(lambda e, cg=cg, xc=xc, xcb=xcb, rr=rr, ii=ii, aa=aa, mu=mu, hh_=hh_: e.tensor_copy(hcar[:, cg:cg + 1], hh_[:, 1023:1024]), [bw[6], b_hcar], [b_hcar])
                    if main:
                        dve(lambda e, j=j, xc=xc, xcb=xcb, rr=rr, ii=ii, aa=aa, mu=mu, hh_=hh_: e.tensor_tensor(hh_, hh_, gel[:, j, :], ALU.mult), [bw[6]] + b_gel[j], [bw[6]])
                        dve(lambda e, cg=cg, xc=xc, xcb=xcb, rr=rr, ii=ii, aa=aa, mu=mu, hh_=hh_: e.tensor_scalar(yT[:, cg, :], hh_, prm[:, P_GRN + cg:P_GRN + cg + 1], None, op0=ALU.mult),
                            [bw[6], b_const], yTb[cg])
                        dve(lambda e, xc=xc, xcb=xcb, rr=rr, ii=ii, aa=aa, mu=mu, hh_=hh_: e.tensor_tensor(rr, hh_, hh_, ALU.mult), [bw[6], bw[2]], [bw[2]])
                        dve(lambda e, xc=xc, xcb=xcb, rr=rr, ii=ii, aa=aa, mu=mu, hh_=hh_: e.tensor_tensor(ssum, ssum, rr, ALU.add), [bw[2], b_ssum], [b_ssum])
                rg_front(0)
                for j in range(4):
                    if j + 1 < 4:
                        rg_front(j + 1)
                    rg_back(j)
            k.barrier()
            A.release(m2)
            chk(5 if not main else 7)
            if main:
                k.dma("sp", oph_d, hcar, reads=[b_hcar])
                k.dma("sp", oprc_d, rtail, reads=[b_rtail])

        k.barrier()
        A.release(m_mix)
        AM = Arena(ar_t, off_wqb, off_wqb + 4096)
        mkT = AM.alloc([16, 256], BF16)
        b_mkT = [Buf("mkT%d" % c) for c in range(16)]
        mvt = AM.alloc([2, 2048], BF16)
        b_mvt = [[Buf("mvt") for jb in range(4)] for nh in range(2)]
        m4 = A.mark()
        mn = A.alloc([16, 256], BF16)
        mnb = [Buf("mn0"), Buf("mn1")]
        m5 = A.mark()
        stg = [A.alloc([2048], F32) for _ in range(2)]
        scratch = (stg, [Buf("stg0"), Buf("stg1")], [A.alloc([2048], BF16) for _ in range(2)], [Buf("xb0"), Buf("xb1")],
                   [A.alloc([2048], BF16) for _ in range(2)], [Buf("jk0"), Buf("jk1")], [A.alloc([4], F32) for _ in range(2)], [Buf("st0"), Buf("st1")])
        chk(30)
        load_norm(mem_d, 256, P_GMEM, mn, mnb, scratch)
        k.barrier()
        chk(31)
        A.release(m5)
        ost = [A.alloc([4, 256], F32) for _ in range(2)]
        ostb = [Buf("ost0"), Buf("ost1")]
        mn_bufs = lambda t0, n: mnb[t0 // 128:(t0 + n + 127) // 128]
        for jb in range(4):
            slot, sb_ = wnext()
            s2 = jb % 2

            def epi_mk(j, ti, t0, n, acc, ab, jb=jb, s2=s2):
                act(ost[s2][:, j, :], acc, AF.Copy, [ab], [ostb[s2]])
                dve(lambda e: e.tensor_copy(mkT[:, jb * 4 + j, :], ost[s2][:, j, :]), [ostb[s2]], [b_mkT[jb * 4 + j]])
            fm_block(slot, sb_, 4, mn, mn_bufs, [(0, 256)], epi_mk)
            k.dma("sp", omk_d[:, jb * 4:jb * 4 + 4, :], ost[s2], reads=[ostb[s2]])
        chk(32)
        ost2 = [ost[0].rearrange("p a b -> p (a b)")[:, 0:512], ost[1].rearrange("p a b -> p (a b)")[:, 0:512]]
        cnt2 = 0
        for jb in range(4):
            slot, sb_ = wnext()

            def epi_mv(i, acc, ab, jb=jb):
                nonlocal cnt2
                s2 = cnt2 % 2
                cnt2 += 1
                act(ost2[s2], acc, AF.Copy, [ab], [ostb[s2]])
                dve(lambda e: e.tensor_copy(mvt[:, i, jb * 512:(jb + 1) * 512], ost2[s2]), [ostb[s2]], [b_mvt[i][jb]])
                k.dma("sp", omv_d[i * 128:(i + 1) * 128, jb * 512:(jb + 1) * 512], ost2[s2], reads=[ostb[s2]])
            tm_block(slot, sb_, 512, mn, mn_bufs, 2, epi_mv)
        k.barrier()
        A.release(m4)

        chk(8)

        def attn_core(n, heads, kfn, kbuf, vfn, vbuf, qc_, qcb_, oT_, oTb_, ET, b_ET, rden, b_rden):
            for hd in heads:
                for nh in range(2):
                    b = acc_bank()
                    for dc in range(4):
                        c = hd * 4 + dc
                        mm(PS[:, b, 0:n], kfn(hd, dc, nh), qc_[:, c, :], dc == 0, dc == 3,
                           [kbuf(hd, dc), qcb_[c]], [pbuf[b]], inc=(dc == 3))
                    act(ET[:, nh, 0:n], PS[:, b, 0:n], AF.Exp, [pbuf[b]], [b_ET], scale=float(512 ** -0.5))
                b = acc_bank()
                for nh in range(2):
                    mm(PS[:, b, 0:n], onesb, ET[:, nh, 0:n], nh == 0, nh == 1, [b_const, b_ET], [pbuf[b]], inc=(nh == 1))
                dve(lambda e, b=b: e.reciprocal(rden[:, 0:n], PS[:, b, 0:n]), [pbuf[b]], [b_rden])
                for dc in range(4):
                    c = hd * 4 + dc
                    b = acc_bank()
                    for nh in range(2):
                        mm(PS[:, b, 0:n], vfn(hd, dc, nh), ET[:, nh, 0:n], nh == 0, nh == 1,
                           [vbuf(hd, dc, nh), b_ET], [pbuf[b]], inc=(nh == 1))
                    dve(lambda e, b=b, c=c: e.tensor_tensor(oT_[:, c, :], PS[:, b, 0:n], rden[:, 0:n], ALU.mult), [pbuf[b], b_rden], [oTb_[c]])

        def post_tile(tiles, tinfo):
            NT = sum(n_ for _, n_ in tiles)
            nmax = max(n_ for _, n_ in tiles)
            accn["banks"] = [0, 1, 4, 5, 6, 7]
            m_tile = A.mark()
            X = A.alloc([16, NT], F32)
            off_hq = A.top
            hq = A.alloc([16, NT], BF16)
            Xb = [[Buf("X%d_%d" % (c, ti)) for ti in range(len(tiles))] for c in range(16)]
            m3 = A.mark()
            if NT >= 512:
                AH = Arena(ar_t, off_hq, off_hq + 16 * NT // 2)
                stg = [AH.alloc([2048], F32) for _ in range(2)]
            else:
                stg = [A.alloc([2048], F32) for _ in range(2)]
            stgb = [Buf("stg0"), Buf("stg1")]
            rstd = A.alloc([NT], F32)
            b_rstd = Buf("rstd")
            tmp = A.alloc([nmax], F32)
            b_tmp = Buf("tmp")
            gi = 0
            for ti, (t0, n) in enumerate(tiles):
                gs = tinfo[ti]["gs"]
                for i in range(n // gs):
                    s2 = gi % 2
                    gi += 1
                    r0 = t0 + i * gs
                    k.dma("sp", stg[s2][0:gs], tinfo[ti]["xsrc"][i * gs:(i + 1) * gs, :], writes=[stgb[s2]])
                    for q4 in range(4):
                        b = tr_bank()
                        for c in range(4):
                            cc = q4 * 4 + c
                            tr(PS[:, b, c * gs:(c + 1) * gs], stg[s2][0:gs, cc * 128:(cc + 1) * 128], idf[0:gs, 0:gs], [stgb[s2], b_const], [pbuf[b]], inc=(c == 3))
                        act(X[:, q4 * 4:q4 * 4 + 4, r0:r0 + gs], PS[:, b, 0:4 * gs].rearrange("p (c t) -> p c t", c=4), AF.Copy,
                            [pbuf[b]], [Xb[q4 * 4 + c][ti] for c in range(4)])
                b = acc_bank()
                mm(PS[:, b, 0:n], onesf, tinfo[ti]["ssum"], True, True, [b_const, tinfo[ti]["b_ssum"]], [pbuf[b]], inc=True)
                act(rstd[:, t0:t0 + n], PS[:, b, 0:n], AF.Sqrt, [pbuf[b]], [b_rstd], scale=1.0 / 1024, bias=EPS)
            dve(lambda e: e.reciprocal(rstd, rstd), [b_rstd], [b_rstd])
            for jb in range(8):
                slot, sb_ = wnext()
                for j in range(2):
                    m = jb * 2 + j
                    for ti, (t0, n) in enumerate(tiles):
                        b1 = acc_bank()
                        for c in range(8):
                            mm(PS[:, b1, 0:n], slot[:, c, j * 128:(j + 1) * 128], tinfo[ti]["yT"][:, c, :], c == 0, c == 7,
                               [sb_] + tinfo[ti]["yT_rb"], [pbuf[b1]], inc=(c == 7))
                        b2 = acc_bank()
                        for c in range(8, 16):
                            mm(PS[:, b2, 0:n], slot[:, c, j * 128:(j + 1) * 128], tinfo[ti]["yT"][:, c, :], c == 8, c == 15,
                               [sb_] + tinfo[ti]["yT_mb"], [pbuf[b2]], inc=(c == 15))
                        dve(lambda e, b1=b1, t0=t0, n=n: e.tensor_tensor(tmp[:, 0:n], PS[:, b1, 0:n], rstd[:, t0:t0 + n], ALU.mult), [pbuf[b1], b_rstd], [b_tmp])
                        dve(lambda e, m=m, b2=b2, t0=t0, n=n: e.tensor_tensor(X[:, m, t0:t0 + n], X[:, m, t0:t0 + n], PS[:, b2, 0:n], ALU.add), [pbuf[b2], Xb[m][ti]], [Xb[m][ti]])
                        dve(lambda e, m=m, t0=t0, n=n: e.tensor_tensor(X[:, m, t0:t0 + n], X[:, m, t0:t0 + n], tmp[:, 0:n], ALU.add), [b_tmp, Xb[m][ti]], [Xb[m][ti]])
            k.barrier()
            A.release(m3)
            A0.release(R0_LO)

            def rmsnorm_fm(gcol0, out, outb):
                mk_ = A.mark()
                mk0 = A0.mark()
                sq = A0.alloc([16, nmax], BF16)
                rs = A.alloc([nmax], F32)
                b_sq, b_rs = Buf("sq"), Buf("rs")
                for ti, (t0, n) in enumerate(tiles):
                    for c in range(16):
                        act(sq[:, c, 0:n], X[:, c, t0:t0 + n], AF.Square, [Xb[c][ti]], [b_sq])
                    b = acc_bank()
                    for c in range(16):
                        mm(PS[:, b, 0:n], onesb, sq[:, c, 0:n], c == 0, c == 15, [b_const, b_sq], [pbuf[b]], inc=(c == 15))
                    act(rs[:, 0:n], PS[:, b, 0:n], AF.Sqrt, [pbuf[b]], [b_rs], scale=1.0 / 2048, bias=EPS)
                    dve(lambda e, n=n: e.reciprocal(rs[:, 0:n], rs[:, 0:n]), [b_rs], [b_rs])
                    for c in range(16):
                        dve(lambda e, c=c, t0=t0, n=n: e.scalar_tensor_tensor(out[:, c, t0:t0 + n], X[:, c, t0:t0 + n], prm[:, gcol0 + c:gcol0 + c + 1], rs[:, 0:n],
                                                                               op0=ALU.mult, op1=ALU.mult),
                            [Xb[c][ti], b_const, b_rs], [outb[c][ti]])
                k.barrier()
                A.release(mk_)
                A0.release(mk0)

            def tb(bl):
                return lambda t0_, n_: [bl[c][[t for t, _ in tiles].index(t0_)] for c in range(16)]
            chk(9)
            hqb = [[Buf("hq") for _ in tiles] for c in range(16)]
            rmsnorm_fm(P_GXA, hq, hqb)
            m6 = A.mark()
            mk0 = A0.mark()
            qc = A0.alloc([16, NT], BF16)
            qcb = [[Buf("qc") for _ in tiles] for c in range(16)]
            for jb in range(8):
                slot, sb_ = wnext()

                def epi_q(j, ti_, t0_, n_, acc, ab, jb=jb):
                    act(qc[:, jb * 2 + j, t0_:t0_ + n_], acc, AF.Copy, [ab], [qcb[jb * 2 + j][ti_]])
                fm_block(slot, sb_, 2, hq, tb(hqb), tiles, epi_q)
            k.barrier()
            oT = hq
            oTb = [[Buf("oT") for _ in tiles] for c in range(16)]
            for ti, (t0, n) in enumerate(tiles):
                tinfo[ti]["attn"](n, qc[:, :, t0:t0 + n], [qcb[c][ti] for c in range(16)], oT[:, :, t0:t0 + n], [oTb[c][ti] for c in range(16)])
            for jb in range(8):
                slot, sb_ = wnext()

                def epi_co(j, ti_, t0_, n_, acc, ab, jb=jb):
                    m = jb * 2 + j
                    dve(lambda e: e.tensor_tensor(X[:, m, t0_:t0_ + n_], X[:, m, t0_:t0_ + n_], acc, ALU.add), [ab, Xb[m][ti_]], [Xb[m][ti_]])
                fm_block(slot, sb_, 2, oT, tb(oTb), tiles, epi_co)
            k.barrier()
            A.release(m6)
            A0.release(mk0)

            chk(10)
            hn = hq
            hnb = [[Buf("hn") for _ in tiles] for c in range(16)]
            rmsnorm_fm(P_GFFN, hn, hnb)
            m6 = A.mark()
            mk0 = A0.mark()
            hG = A0.alloc([16, NT], BF16)
            hGb = [[Buf("hG") for _ in tiles] for c in range(16)]
            rl = [A.alloc([nmax], F32) for _ in range(2)]
            rlb = [Buf("rl0"), Buf("rl1")]
            cnt3 = [0]
            for g in range(4):
                for jb in range(8):
                    slot, sb_ = wnext()

                    def epi_up(j, ti_, t0_, n_, acc, ab, jb=jb):
                        s2 = cnt3[0] % 2
                        cnt3[0] += 1
                        act(rl[s2][:, 0:n_], acc, AF.Relu, [ab], [rlb[s2]])
                        dve(lambda e: e.tensor_tensor(hG[:, jb * 2 + j, t0_:t0_ + n_], rl[s2][:, 0:n_], rl[s2][:, 0:n_], ALU.mult), [rlb[s2]], [hGb[jb * 2 + j][ti_]])
                    fm_block(slot, sb_, 2, hn, tb(hnb), tiles, epi_up)
                for jb in range(8):
                    slot, sb_ = wnext()

                    def epi_dn(j, ti_, t0_, n_, acc, ab, jb=jb):
                        m = jb * 2 + j
                        dve(lambda e: e.tensor_tensor(X[:, m, t0_:t0_ + n_], X[:, m, t0_:t0_ + n_], acc, ALU.add), [ab, Xb[m][ti_]], [Xb[m][ti_]])
                    fm_block(slot, sb_, 2, hG, tb(hGb), tiles, epi_dn)
            k.barrier()
            A.release(m6)
            A0.release(mk0)

            chk(11)
            m7 = A.mark()
            mk0 = A0.mark()
            sq = hq
            rs = A.alloc([nmax], F32)
            yn = [A0.alloc([16, 128], F32) for _ in range(2)]
            ob = [A0.alloc([2048], F32) for _ in range(2)]
            b_sq, b_rs = Buf("sq"), Buf("rs")
            b_yn = [Buf("yn0"), Buf("yn1")]
            obb = [Buf("ob0"), Buf("ob1")]
            gi = 0
            for ti, (t0, n) in enumerate(tiles):
                for c in range(16):
                    act(sq[:, c, t0:t0 + n], X[:, c, t0:t0 + n], AF.Square, [Xb[c][ti]], [b_sq])
                b = acc_bank()
                for c in range(16):
                    mm(PS[:, b, 0:n], onesb, sq[:, c, t0:t0 + n], c == 0, c == 15, [b_const, b_sq], [pbuf[b]], inc=(c == 15))
                act(rs[:, 0:n], PS[:, b, 0:n], AF.Sqrt, [pbuf[b]], [b_rs], scale=1.0 / 2048, bias=EPS)
                dve(lambda e, n=n: e.reciprocal(rs[:, 0:n], rs[:, 0:n]), [b_rs], [b_rs])
                gs = tinfo[ti]["gs"]
                for i in range(n // gs):
                    s2 = gi % 2
                    gi += 1
                    r0 = t0 + i * gs
                    for c in range(16):
                        dve(lambda e, c=c, i=i, s2=s2, r0=r0, gs=gs: e.scalar_tensor_tensor(yn[s2][:, c, 0:gs], X[:, c, r0:r0 + gs], prm[:, P_GFIN + c:P_GFIN + c + 1],
                                                                                             rs[:, i * gs:(i + 1) * gs], op0=ALU.mult, op1=ALU.mult),
                            [Xb[c][ti], b_const, b_rs], [b_yn[s2]])
                    for q4 in range(4):
                        b = tr_bank()
                        for c in range(4):
                            cc = q4 * 4 + c
                            tr(PS[0:gs, b, c * 128:(c + 1) * 128], yn[s2][:, cc, 0:gs], idf, [b_yn[s2], b_const], [pbuf[b]], inc=(c == 3))
                        act(ob[s2][0:gs, q4 * 512:(q4 + 1) * 512], PS[0:gs, b, :], AF.Copy, [pbuf[b]], [obb[s2]])
                    k.dma("sp", tinfo[ti]["ydst"][i * gs:(i + 1) * gs, :], ob[s2][0:gs], reads=[obb[s2]])
            k.barrier()
            A.release(m_tile)
            A0.release(mk0)
            accn["banks"] = [0, 1]

        def attn_prompt(n_, qc_, qcb_, oT_, oTb_):
            mk_ = A.mark()
            ET = A.alloc([2, 512], BF16)
            rden = A.alloc([512], F32)
            attn_core(n_, range(4),
                      lambda hd, dc, nh: mkT[:, hd * 4 + dc, nh * 128:(nh + 1) * 128], lambda hd, dc: b_mkT[hd * 4 + dc],
                      lambda hd, dc, nh: mvt[:, nh, (hd * 4 + dc) * 128:(hd * 4 + dc + 1) * 128], lambda hd, dc, nh: b_mvt[nh][hd],
                      qc_, qcb_, oT_, oTb_, ET, Buf("ET"), rden, Buf("rden"))
            k.barrier()
            A.release(mk_)
        def attn_sample(n_, qc_, qcb_, oT_, oTb_):
            mk_ = A.mark()
            Ks = [A.alloc([2, 512], BF16) for _ in range(2)]
            mkTs = [A.alloc([4, 256], BF16) for _ in range(2)]
            mvts = [A.alloc([2, 512], BF16) for _ in range(2)]
            ET = [A.alloc([2, 16], BF16) for _ in range(2)]
            rden = [A.alloc([16], F32) for _ in range(2)]
            b_Ks = [Buf("Ks0"), Buf("Ks1")]
            b_mk1 = [Buf("mk0"), Buf("mk1")]
            b_mv1 = [Buf("mv0"), Buf("mv1")]
            b_ET = [Buf("ET0"), Buf("ET1")]
            b_rden = [Buf("rd0"), Buf("rd1")]
            its = [(tok, hd) for tok in range(NS) for hd in range(4)]

            def dma_k(it):
                tok, hd = its[it]
                s2 = it % 2
                k.dma("pool", Ks[s2], ck_d[tok][:, hd * 512:(hd + 1) * 512].rearrange("(nh p) d -> p nh d", p=128), writes=[b_Ks[s2]])

            def dma_v(it):
                tok, hd = its[it]
                s2 = it % 2
                k.dma("pool", mvts[s2], cv_d[tok][:, hd * 512:(hd + 1) * 512].rearrange("(nh p) d -> p nh d", p=128), writes=[b_mv1[s2]])

            def st_a(it):
                tok, hd = its[it]
                s2 = it % 2
                b = tr_bank()
                pv = psb(b)
                for dc in range(4):
                    for nh in range(2):
                        tr(pv[:, (dc * 2 + nh) * 128:(dc * 2 + nh + 1) * 128], Ks[s2][:, nh, dc * 128:(dc + 1) * 128], idb, [b_Ks[s2], b_const], [pbuf[b]],
                           inc=(dc == 3 and nh == 1))
                if it % 2 == 0:
                    act(mkTs[s2], pv.rearrange("p (c n) -> p c n", c=4), AF.Copy, [pbuf[b]], [b_mk1[s2]])
                else:
                    dv(lambda e, pv=pv, s2=s2: e.tensor_copy(mkTs[s2], pv.rearrange("p (c n) -> p c n", c=4)), [pbuf[b]], [b_mk1[s2]])

            def st_b(it):
                tok, hd = its[it]
                s2 = it % 2
                attn_core(1, [hd],
                          lambda hd_, dc, nh: mkTs[s2][:, dc, nh * 128:(nh + 1) * 128], lambda hd_, dc: b_mk1[s2],
                          lambda hd_, dc, nh: mvts[s2][:, nh, dc * 128:(dc + 1) * 128], lambda hd_, dc, nh: b_mv1[s2],
                          qc_[:, :, tok:tok + 1], qcb_, oT_[:, :, tok:tok + 1], oTb_, ET[s2], b_ET[s2], rden[s2], b_rden[s2])
            dma_k(0)
            dma_k(1)
            dma_v(0)
            st_a(0)
            for it in range(len(its)):
                if it + 2 < len(its):
                    dma_k(it + 2)
                if it + 1 < len(its):
                    dma_v(it + 1)
                    st_a(it + 1)
                st_b(it)
            k.barrier()
            A.release(mk_)
        tinfo = [dict(xsrc=xm_d[t0:t0 + n], yT=yT[:, :, t0:t0 + n], yT_rb=[yTb[cc][ti] for cc in range(8)], yT_mb=[yTb[cc][ti] for cc in range(8, 16)],
                      ssum=ssum[:, t0:t0 + n], b_ssum=b_ssum, attn=attn_prompt, ydst=y_d[t0:t0 + n], gs=128) for ti, (t0, n) in enumerate(TILES)]
        tinfo.append(dict(xsrc=xs_d, yT=yTs, yT_rb=[b_yTs_r], yT_mb=[b_yTs_m], ssum=ssum_s, b_ssum=b_ssum_s, attn=attn_sample, ydst=ys_d, gs=NS))
        post_tile(TILES + [(1024, NS)], tinfo)
        assert k.dead or wstate["used"] == len(wsched), (wstate, len(wsched))
        k.finish()
        k.emit()
        print("instructions:", k.nins, "arena peak", A.peak, "of", NW, "A0 peak", A0.peak, "of", R0_HI)
    return nc


_CACHE = {}


def _consts():
    ident = np.eye(128, dtype=np.float32)
    s = np.arange(128)[:, None]
    t = np.arange(128)[None, :]
    maskneg = np.where(s <= t, 0.0, -30000.0).astype(np.float32)
    sel = np.zeros((4, 4, 128), np.float32)
    for h in range(4):
        sel[h, h, :] = 1.0
    return ident, maskneg, sel.reshape(4, 512)


def kernel(**inp):
    f = lambda a: np.ascontiguousarray(np.asarray(a, dtype=np.float32))
    if "nc" not in _CACHE:
        _CACHE["nc"] = build_program()
    nc = _CACHE["nc"]
    ident, maskneg, sel = _consts()
    prm = np.zeros((128, NPRM), np.float32)

    def colmajor(v, nch):
        return np.asarray(v, np.float32).reshape(nch, 128).T
    prm[:, P_GMIX:P_GMIX + 16] = colmajor(inp["g_mix"][0], 16)
    prm[:, P_GXA:P_GXA + 16] = colmajor(inp["g_xattn"][0], 16)
    prm[:, P_GMEM:P_GMEM + 16] = colmajor(inp["g_mem"][0], 16)
    prm[:, P_GFFN:P_GFFN + 16] = colmajor(inp["g_ffn"][0], 16)
    prm[:, P_GFIN:P_GFIN + 16] = colmajor(inp["g_final"], 16)
    for tap in range(4):
        prm[:, P_CRW + tap * 8:P_CRW + tap * 8 + 8] = colmajor(inp["conv_rnn_w"][0, tap], 8)
        prm[:, P_CMW + tap * 8:P_CMW + tap * 8 + 8] = colmajor(inp["conv_ml_w"][0, tap], 8)
    prm[:, P_CRB:P_CRB + 8] = colmajor(inp["conv_rnn_b"][0], 8)
    prm[:, P_CMB:P_CMB + 8] = colmajor(inp["conv_ml_b"][0], 8)
    prm[:, P_LBA:P_LBA + 8] = np.asarray(inp["lru_ba"][0], np.float32).T
    prm[:, P_LBX:P_LBX + 8] = np.asarray(inp["lru_bx"][0], np.float32).T
    prm[:, P_LAM:P_LAM + 8] = colmajor(inp["lru_lambda"][0], 8)
    prm[:, P_GRN:P_GRN + 8] = colmajor(inp["g_rnn_out"][0], 8)
    prm[:, P_GML2:P_GML2 + 2] = colmajor(inp["g_ml_out"][0], 2)
    prm[0:4, P_BI] = np.asarray(inp["ml_bi"][0], np.float32)
    prm[0:4, P_BF] = np.asarray(inp["ml_bf"][0], np.float32)
    gmlrep = np.ascontiguousarray(np.broadcast_to(np.asarray(inp["g_ml_out"][0], np.float32)[None, :], (128, 256)))
    shared = dict(
        prm=prm, gmlrep=gmlrep, ident=ident, maskneg=maskneg, sel=sel,
        w_in=f(inp["w_in"][0]), lru_wa=f(inp["lru_wa"][0]), lru_wx=f(inp["lru_wx"][0]),
        ml_wq=f(inp["ml_wq"][0]), ml_wk=f(inp["ml_wk"][0]), w_out=f(inp["w_out"][0]),
        w_cq=f(inp["w_cq"][0]), w_mk=f(inp["w_mk"][0]), w_mv=f(inp["w_mv"][0]), w_co=f(inp["w_co"][0]),
        w_up=f(inp["w_up"][0]), w_down=f(inp["w_down"][0]),
    )
    shared["cmw_rep"] = np.ascontiguousarray(np.broadcast_to(np.asarray(inp["conv_ml_w"][0], np.float32)[None], (16, 4, 1024)))
    st_ = np.zeros((16, 16, 128), np.float32)
    for t_ in range(16):
        st_[t_, t_, :] = 1.0
    shared["seltok"] = st_.reshape(16, 2048)
    shared["cmb_rep"] = np.ascontiguousarray(np.broadcast_to(np.asarray(inp["conv_ml_b"][0], np.float32)[None], (16, 1024)))
    shared["gb_rep"] = np.ascontiguousarray(np.broadcast_to(
        np.concatenate([np.asarray(inp["ml_bi"][0], np.float32), np.asarray(inp["ml_bf"][0], np.float32)])[None], (16, 8)))
    xsm = np.asarray(inp["x_sample"], np.float32)
    xpr = np.asarray(inp["x_prompt"], np.float32)
    memp = np.asarray(inp["mem_prompt"], np.float32)
    in_maps = []
    for c in range(8):
        b, hf = c // 2, c % 2
        d = dict(shared)
        d["xm"] = np.ascontiguousarray(xpr[b, hf * 1024:(hf + 1) * 1024])
        d["xp"] = np.ascontiguousarray(xpr[b, 0:1024])
        d["mem"] = np.ascontiguousarray(memp[b])
        d["mask"] = np.full((128, 1), float(hf), np.float32)
        sl = slice(c * 16, (c + 1) * 16)
        d["xs"] = np.ascontiguousarray(xsm[sl, 0])
        d["s_h"] = f(inp["state_rglru_h"][0, sl])
        d["s_rc"] = f(inp["state_rglru_conv"][0, sl])
        d["s_C"] = f(inp["state_mlstm_C"][0, sl])
        d["s_n"] = f(inp["state_mlstm_n"][0, sl])
        d["s_m"] = f(inp["state_mlstm_m"][0, sl])
        d["s_mc"] = f(inp["state_mlstm_conv"][0, sl])
        d["ck"] = f(inp["cache_mem_k"][0, sl]).reshape(16, 256, 2048)
        d["cv"] = f(inp["cache_mem_v"][0, sl]).reshape(16, 256, 2048)
        in_maps.append(d)
    res = run_bass_kernel_spmd(nc, in_maps, core_ids=list(range(8)))
    R = res.results
    B = 4
    y_prompt = np.zeros((B, 2048, 2048), np.float32)
    p_h = np.zeros((1, B, 1024), np.float32)
    p_rc = np.zeros((1, B, 3, 1024), np.float32)
    p_C = np.zeros((1, B, 4, 256, 256), np.float32)
    p_n = np.zeros((1, B, 4, 256), np.float32)
    p_m = np.zeros((1, B, 4), np.float32)
    p_mc = np.zeros((1, B, 3, 1024), np.float32)
    p_mk = np.zeros((1, B, 256, 4, 512), np.float32)
    p_mv = np.zeros((1, B, 256, 4, 512), np.float32)
    for c in range(8):
        b, hf = c // 2, c % 2
        r = R[c]
        y_prompt[b, hf * 1024:(hf + 1) * 1024] = r["o_y"]
        if hf == 1:
            p_h[0, b] = r["o_ph"].T.reshape(1024)
            p_rc[0, b] = r["o_prc"].transpose(2, 1, 0).reshape(3, 1024)
            p_mc[0, b] = r["o_pmc"].transpose(2, 1, 0).reshape(3, 1024)
            oc = r["o_pC"]
            p_C[0, b] = oc[:, :, :, 0:256].transpose(1, 3, 2, 0).reshape(4, 256, 256)
            p_n[0, b] = oc[:, :, :, 256].transpose(1, 2, 0).reshape(4, 256)
            p_m[0, b] = r["o_pm"].reshape(4)
            p_mk[0, b] = r["o_mkT"].transpose(2, 1, 0).reshape(256, 4, 512)
            p_mv[0, b] = r["o_mv"].reshape(256, 4, 512)
    y_s = np.zeros((128, 1, 2048), np.float32)
    s_h = np.zeros((1, 128, 1024), np.float32)
    s_rc = np.zeros((1, 128, 3, 1024), np.float32)
    s_C = np.zeros((1, 128, 4, 256, 256), np.float32)
    s_n = np.zeros((1, 128, 4, 256), np.float32)
    s_m = np.zeros((1, 128, 4), np.float32)
    s_mc = np.zeros((1, 128, 3, 1024), np.float32)
    for c in range(8):
        r = R[c]
        sl = slice(c * 16, (c + 1) * 16)
        y_s[sl, 0] = r["o_ys"]
        s_h[0, sl] = r["o_sh"].transpose(2, 1, 0).reshape(16, 1024)
        s_rc[0, sl] = r["o_src"].transpose(3, 2, 1, 0).reshape(16, 3, 1024)
        s_C[0, sl] = r["o_sC"]
        s_n[0, sl] = r["o_sn"]
        s_m[0, sl] = r["o_sm"]
        s_mc[0, sl] = r["o_smc"]
    return (y_prompt, y_s, p_h, p_rc, p_C, p_n, p_m, p_mc, p_mk, p_mv, s_h, s_rc, s_C, s_n, s_m, s_mc)
```

```python
import numpy as np
from contextlib import ExitStack
import concourse.bass as bass
import concourse.mybir as mybir
from concourse.bass_utils import run_bass_kernel_spmd

F32 = mybir.dt.float32
BF16 = mybir.dt.bfloat16
AF = mybir.ActivationFunctionType
ALU = mybir.AluOpType
AX = mybir.AxisListType

SAME_ENGINE_WAIT = True
EPS = 1e-6
NSLOT = 2

P_GMIX, P_GXA, P_GMEM, P_GFFN, P_GFIN = 0, 16, 32, 48, 64
P_CRW, P_CRB, P_LBA, P_LBX, P_LAM, P_GRN = 80, 112, 120, 128, 136, 144
P_CMW, P_CMB, P_GML2, P_BI, P_BF = 152, 184, 192, 194, 195
NPRM = 196


import os
STOP = int(os.environ.get("KSTOP", "0"))
DEBUG_SITES = bool(int(os.environ.get("KSITES", "0")))
DBG2 = int(os.environ.get("DBG2", "0"))
DBG3 = int(os.environ.get("DBG3", "0"))


class _Stop(Exception):
    pass


class Buf:
    __slots__ = ("name", "w", "r", "dsem", "dcount")

    def __init__(self, name="b"):
        self.name = name
        self.w = None
        self.r = {}
        self.dsem = None
        self.dcount = 0


class K:
    ENG = ("pe", "act", "dve", "pool", "sp")

    def __init__(self, nc, es):
        self.nc = nc
        self.es = es
        self.ops = {e: [] for e in self.ENG}
        self.sem = {e: es.enter_context(nc.semaphore("s_" + e)) for e in self.ENG}
        self.cnt = {e: 0 for e in self.ENG}
        self.known = {e: {} for e in self.ENG}
        self.semobj = {e: self.sem[e] for e in self.ENG}
        self.nd = 0
        self.dbufs = []
        self.nins = 0
        self.dead = False

    def _need(self, eng, reads, writes):
        need = {}

        def add(k, v):
            if need.get(k, 0) < v:
                need[k] = v
        for b in reads:
            if b.w:
                add(*b.w)
        for b in writes:
            if b.w:
                add(*b.w)
            for k, v in b.r.items():
                add(k, v)
        waits = []
        for k, v in need.items():
            if k == eng and (eng == "pe" or not SAME_ENGINE_WAIT):
                continue
            if self.known[eng].get(k, 0) >= v:
                continue
            self.known[eng][k] = v
            waits.append((self.semobj[k], v))
        return waits

    def op(self, eng, fn, reads=(), writes=(), inc=True):
        if self.dead:
            return
        waits = self._need(eng, reads, writes)
        val = self.cnt[eng] + 1
        if inc:
            self.cnt[eng] = val
        for b in reads:
            if b.r.get(eng, 0) < val:
                b.r[eng] = val
        for b in writes:
            b.w = (eng, val)
            b.r = {}
        sem = self.sem[eng]
        self.nins += 1
        if getattr(self, "trace", False):
            print("TRACE", eng, "val", val, "inc", inc, "waits", [(str(s_), v_) for s_, v_ in waits], "reads", [(b.name, b.w) for b in reads], "writes", [b.name for b in writes])
        import sys as _sys
        fr = _sys._getframe(1)
        site = []
        while fr is not None and len(site) < 3:
            site.append(fr.f_lineno)
            fr = fr.f_back
        site = "SITE" + "_".join(map(str, site)) + "_" + getattr(self, "tag", "")

        def run(e, waits=waits, fn=fn, inc=inc, sem=sem, site=site):
            for s, v in waits:
                e.wait_ge(s, v)
            ins = fn(e)
            if DEBUG_SITES:
                ins.annotate(site)
            if inc:
                ins.then_inc(sem, 1)
        self.ops[eng].append(run)

    def _dsem(self, b):
        if b.dsem is None:
            key = "d%d" % self.nd
            self.nd += 1
            b.dsem = key
            self.semobj[key] = self.es.enter_context(self.nc.semaphore(key))
            self.dbufs.append(b)
        return b.dsem

    def dma(self, q, out, in_, reads=(), writes=(), **kw):
        if self.dead:
            return
        waits = self._need(q, reads, writes)
        bl = list(reads) + list(writes)
        assert len(bl) == 1
        b = bl[0]
        kk = self._dsem(b)
        b.dcount += 16
        v = b.dcount
        if reads:
            b.r[kk] = v
        else:
            b.w = (kk, v)
            b.r = {}
        s = self.semobj[kk]
        self.nins += 1

        def run(e, waits=waits, s=s, out=out, in_=in_, kw=kw):
            for ws, wv in waits:
                e.wait_ge(ws, wv)
            e.dma_start(out=out, in_=in_, **kw).then_inc(s, 16)
        self.ops[q].append(run)

    def barrier(self):
        if self.dead:
            return
        tgt = [(e, self.cnt[e]) for e in self.ENG if self.cnt[e] > 0]
        tgt += [(b.dsem, b.dcount) for b in self.dbufs]
        for eng in self.ENG:
            waits = []
            for kk, v in tgt:
                if kk == eng:
                    continue
                if self.known[eng].get(kk, 0) >= v:
                    continue
                self.known[eng][kk] = v
                waits.append((self.semobj[kk], v))

            def run(e, waits=waits):
                for s, v in waits:
                    e.wait_ge(s, v)
            if waits:
                self.ops[eng].append(run)

    def finish(self):
        self.barrier()

    def emit(self):
        nc = self.nc
        with nc.Block() as block:
            @block.tensor
            def _(e):
                for f in self.ops["pe"]:
                    f(e)

            @block.scalar
            def _(e):
                for f in self.ops["act"]:
                    f(e)

            @block.vector
            def _(e):
                for f in self.ops["dve"]:
                    f(e)

            @block.gpsimd
            def _(e):
                for f in self.ops["pool"]:
                    f(e)

            @block.sync
            def _(e):
                for f in self.ops["sp"]:
                    f(e)


class Arena:
    def __init__(self, ap, lo, hi):
        self.ap = ap
        self.n = hi
        self.top = lo

    def alloc(self, shape, dt, parts=128):
        n = 1
        for s in shape:
            n *= s
        esz = 4 if dt == F32 else 2
        words = (n * esz + 3) // 4
        words = (words + 15) // 16 * 16
        off = self.top
        self.top += words
        assert self.top <= self.n, "arena overflow %d > %d" % (self.top, self.n)
        self.peak = max(getattr(self, "peak", 0), self.top)
        v = self.ap[:, off:off + words]
        if dt != F32:
            v = v.bitcast(dt)
        v = v[:, 0:n]
        if len(shape) == 2:
            v = v.rearrange("p (a b) -> p a b", a=shape[0])
        elif len(shape) == 3:
            v = v.rearrange("p (a b c) -> p a b c", a=shape[0], b=shape[1])
        elif len(shape) == 4:
            v = v.rearrange("p (a b c d) -> p a b c d", a=shape[0], b=shape[1], c=shape[2])
        if parts != 128:
            v = v[0:parts]
        return v

    def mark(self):
        return self.top

    def release(self, m):
        self.top = m


def build_program():
    nc = bass.Bass("TRN2", target_bir_lowering=False)

    def DI(name, shape):
        return nc.dram_tensor(name, list(shape), F32, kind="ExternalInput").ap()

    def DO(name, shape):
        return nc.dram_tensor(name, list(shape), F32, kind="ExternalOutput").ap()

    xm_d = DI("xm", [1024, 2048])
    xp_d = DI("xp", [1024, 2048])
    mem_d = DI("mem", [256, 2048])
    mask_d = DI("mask", [128, 1])
    prm_d = DI("prm", [128, NPRM])
    gml_d = DI("gmlrep", [128, 256])
    id_d = DI("ident", [128, 128])
    mneg_d = DI("maskneg", [128, 128])
    sel_d = DI("sel", [4, 4 * 128])
    w_in_d = DI("w_in", [2048, 5128])
    lwa_d = DI("lru_wa", [8, 128, 128])
    lwx_d = DI("lru_wx", [8, 128, 128])
    wq_d = DI("ml_wq", [4, 256, 256])
    wk_d = DI("ml_wk", [4, 256, 256])
    w_out_d = DI("w_out", [2048, 2048])
    w_cq_d = DI("w_cq", [2048, 2048])
    w_mk_d = DI("w_mk", [2048, 2048])
    w_mv_d = DI("w_mv", [2048, 2048])
    w_co_d = DI("w_co", [2048, 2048])
    w_up_d = DI("w_up", [2048, 8192])
    w_dn_d = DI("w_down", [8192, 2048])

    xs_d = DI("xs", [16, 2048])
    sh_d = DI("s_h", [16, 1024])
    src_d = DI("s_rc", [16, 3, 1024])
    sC_d = DI("s_C", [16, 4, 256, 256])
    sn_d = DI("s_n", [16, 4, 256])
    sm_d = DI("s_m", [16, 4])
    smc_d = DI("s_mc", [16, 3, 1024])
    ck_d = DI("ck", [16, 256, 2048])
    cv_d = DI("cv", [16, 256, 2048])
    seltok_d = DI("seltok", [16, 16 * 128])
    cmw_d = DI("cmw_rep", [16, 4, 1024])
    cmb_d = DI("cmb_rep", [16, 1024])
    gb_d = DI("gb_rep", [16, 8])
    ys_d = DO("o_ys", [16, 2048])
    osh_d = DO("o_sh", [128, 8, 16])
    osrc_d = DO("o_src", [128, 8, 3, 16])
    osC_d = DO("o_sC", [16, 4, 256, 256])
    osn_d = DO("o_sn", [16, 4, 256])
    osm_d = DO("o_sm", [16, 4])
    osmc_d = DO("o_smc", [16, 3, 1024])
    y_d = DO("o_y", [1024, 2048])
    oph_d = DO("o_ph", [128, 8])
    oprc_d = DO("o_prc", [128, 8, 3])
    opmc_d = DO("o_pmc", [128, 8, 3])
    opC_d = DO("o_pC", [128, 4, 2, 257])
    opm_d = DO("o_pm", [4, 1])
    omk_d = DO("o_mkT", [128, 16, 256])
    omv_d = DO("o_mv", [256, 2048])

    with ExitStack() as es:
        k = K(nc, es)
        NW = 52992
        ar_t = es.enter_context(nc.sbuf_tensor("arena", [128, NW], F32))
        A = Arena(ar_t, 0, NW)
        PS = es.enter_context(nc.psum_tensor("ps", [128, 8, 512], F32))

        def psb(b):
            return PS[:, b, :].bitcast(BF16)
        pbuf = [Buf("ps%d" % i) for i in range(8)]

        wslot = [A.alloc([16, 512], BF16) for _ in range(NSLOT)]
        wsb = [Buf("ws%d" % i) for i in range(NSLOT)]
        idf = A.alloc([128], F32)[:, :]
        idb = A.alloc([128], BF16)
        onesb = A.alloc([128], BF16)
        onesf = A.alloc([128], F32)
        mneg = A.alloc([128], F32)
        m01 = A.alloc([128], BF16)
        prm = A.alloc([NPRM], F32)
        gml = A.alloc([256], F32)
        maskc = A.alloc([1], F32)
        sel = A.alloc([4 * 128], F32, parts=4)
        off_wqb = A.top
        wqb = A.alloc([4, 2, 256], BF16)
        wkb = A.alloc([4, 2, 256], BF16)
        lwab = A.alloc([8, 128], BF16)
        lwxb = A.alloc([8, 128], BF16)
        C32 = A.alloc([4, 2, 257], F32)
        Cb = A.alloc([2, 257], BF16)
        hcar = A.alloc([8], F32)
        rtail = A.alloc([8, 3], F32)
        mtail = A.alloc([8, 3], F32)
        ccol = A.alloc([8], F32)
        ccol2 = A.alloc([8], F32)
        negbf = A.alloc([1], F32, parts=4)
        st0 = A.alloc([1], F32)
        gcar = A.alloc([4], F32, parts=4)
        b_const = Buf("const")
        yTs = A.alloc([16, 16], BF16)
        ssum_s = A.alloc([16], F32)
        PBASE = A.top
        R0_LO, R0_HI = PBASE, PBASE + 9216
        A0 = Arena(ar_t, R0_LO, R0_HI)
        A = Arena(ar_t, R0_HI, NW)
        print("persistent words", PBASE, "R12 words", NW - R0_HI)
        b_C32, b_Cb, b_hcar, b_rtail, b_mtail, b_gcar = [Buf(n) for n in "C32 Cb hcar rtail mtail gcar".split()]

        def act(out, in_, func, reads, writes, **kw):
            k.op("act", lambda e: e.activation(out, in_, func, **kw), reads=reads, writes=writes)

        def dve(fn, reads, writes):
            k.op("dve", fn, reads=reads, writes=writes)

        def mm(out, lhsT, rhs, start, stop, reads, writes, inc):
            k.op("pe", lambda e: e.matmul(out, lhsT, rhs, start=start, stop=stop), reads=reads, writes=writes, inc=inc)

        def tr(out, in_, ident, reads, writes, inc):
            k.op("pe", lambda e: e.transpose(out, in_, ident), reads=reads, writes=writes, inc=inc)

        def chk(n):
            if STOP == n:
                k.finish()
                k.dead = True
        for dst, src in ((idf, id_d), (mneg, mneg_d), (prm, prm_d), (gml, gml_d), (maskc, mask_d), (sel, sel_d)):
            k.dma("sp", dst, src, writes=[b_const])
        b_cw = Buf("constw")
        k.dma("pool", wqb, wq_d.rearrange("h (c p) n -> p h c n", p=128), writes=[b_cw])
        k.dma("pool", wkb, wk_d.rearrange("h (c p) n -> p h c n", p=128), writes=[b_cw])
        k.dma("pool", lwab, lwa_d.rearrange("h p n -> p h n"), writes=[b_cw])
        k.dma("pool", lwxb, lwx_d.rearrange("h p n -> p h n"), writes=[b_cw])
        k.op("dve", lambda e: e.memset(st0, 0.0), reads=[b_cw, b_const], writes=[b_const])
        dve(lambda e: e.tensor_copy(idb, idf), [b_const], [b_const])
        dve(lambda e: e.memset(onesb, 1.0), [], [b_const])
        dve(lambda e: e.tensor_scalar(m01, mneg, 0.0, None, op0=ALU.is_equal), [b_const], [b_const])
        dve(lambda e: e.memset(onesf, 1.0), [], [b_const])
        dve(lambda e: e.memset(C32, 0.0), [], [b_C32])
        dve(lambda e: e.memset(hcar, 0.0), [], [b_hcar])
        dve(lambda e: e.memset(gcar, 0.0), [], [b_gcar])
        dve(lambda e: e.memset(rtail, 0.0), [], [b_rtail])
        dve(lambda e: e.memset(mtail, 0.0), [], [b_mtail])
        act(ccol, prm[:, P_LAM:P_LAM + 8], AF.Exp, [b_const], [b_const], scale=-1.0)
        act(ccol, ccol, AF.Ln, [b_const], [b_const], bias=1.0)
        dve(lambda e: e.tensor_scalar(ccol2, ccol, -16.0, None, op0=ALU.mult), [b_const], [b_const])
        dve(lambda e: e.tensor_scalar(ccol, ccol, -8.0, None, op0=ALU.mult), [b_const], [b_const])
        dve(lambda e: e.tensor_scalar(negbf, prm[0:4, P_BF:P_BF + 1], -1.0, None, op0=ALU.mult), [b_const], [b_const])

        chk(1)
        wsched = []
        wstate = {"issued": 0, "used": 0, "cnt": [0, 0]}
        wassign = {}
        wflat = [w_.rearrange("p a b -> p (a b)") for w_ in wslot]
        hslot = [wflat[kk // 2][:, (kk % 2) * 4096:(kk % 2 + 1) * 4096].rearrange("p (a b) -> p a b", a=16) for kk in range(4)]
        hsb = [Buf("hs%d" % i) for i in range(4)]

        def wplan(ap, half=False):
            wsched.append((ap, half))

        def wplan256(wd, r0, c0):
            for hh_ in range(2):
                wplan(wd[r0:r0 + 2048, c0 + hh_ * 256:c0 + (hh_ + 1) * 256], True)

        def wissue():
            i = wstate["issued"]
            ap, half = wsched[i]
            nco = ap.shape[1]
            md = 1 if half else 0
            cidx = wstate["cnt"][md]
            wstate["cnt"][md] += 1
            if half:
                sl_, bf_ = hslot[cidx % 4], hsb[cidx % 4]
            else:
                sl_, bf_ = wslot[cidx % 2], wsb[cidx % 2]
            k.dma("pool", sl_[:, :, 0:nco], ap.rearrange("(c p) n -> p c n", p=128), writes=[bf_])
            wassign[i] = (sl_, bf_)
            wstate["issued"] = i + 1

        def wnext():
            i = wstate["used"]
            half = wsched[i][1]
            if wstate["issued"] <= i:
                if i > 0 and wsched[i - 1][1] != half:
                    k.barrier()
                wissue()
            depth = 4 if half else NSLOT
            while wstate["issued"] < min(len(wsched), i + depth) and wsched[wstate["issued"]][1] == half:
                wissue()
            wstate["used"] = i + 1
            return wassign.pop(i)

        def cols(wd, r0, c0, n):
            return wd[r0:r0 + 2048, c0:c0 + n]

        wplan(cols(w_in_d, 0, 5120, 8))
        for c0 in (3072, 3584, 4096, 4608, 2048, 2560, 1024, 1536, 0, 512):
            wplan(cols(w_in_d, 0, c0, 512))
        for ps_ in range(2):
            wplan(cols(w_in_d, 0, 5120, 8))
            for pr in range(2):
                wplan(cols(w_in_d, 0, 3072 + pr * 512, 512))
                if ps_ == 1:
                    wplan(cols(w_in_d, 0, 4096 + pr * 512, 512))
                wplan(cols(w_in_d, 0, 2048 + pr * 512, 512))
            for pr in range(2):
                if ps_ == 1:
                    wplan(cols(w_in_d, 0, 1024 + pr * 512, 512))
                wplan(cols(w_in_d, 0, 0 + pr * 512, 512))
        for j in range(4):
            wplan(cols(w_mk_d, 0, j * 512, 512))
        for j in range(4):
            wplan(cols(w_mv_d, 0, j * 512, 512))
        def plan_post():
            for j in range(4):
                wplan256(w_out_d, 0, j * 512)
            for j in range(4):
                wplan256(w_cq_d, 0, j * 512)
            for j in range(4):
                wplan256(w_co_d, 0, j * 512)
            for g in range(4):
                for j in range(4):
                    wplan256(w_up_d, 0, g * 2048 + j * 512)
                for j in range(4):
                    wplan256(w_dn_d, g * 2048, j * 512)
        plan_post()

        accn = {"i": 0, "banks": [0, 1]}

        def acc_bank():
            bl = accn["banks"]
            b = bl[accn["i"] % len(bl)]
            accn["i"] += 1
            return b

        trn = {"i": 0}

        def tr_bank():
            b = 2 + trn["i"] % 2
            trn["i"] += 1
            return b

        def load_norm(src, T, gcol0, xn, xnb, scratch):
            stg, stgb, xb2, xbb2, junk2, junkb2, st2, stb2 = scratch
            ng = T // 128

            def stage_a(i):
                s2 = i % 2
                junk, junkb, st, stb = junk2[s2], junkb2[s2], st2[s2], stb2[s2]
                k.dma("sp", stg[s2], src[i * 128:(i + 1) * 128, :], writes=[stgb[s2]])
                act(junk, stg[s2], AF.Square, [stgb[s2]], [junkb, stb], accum_out=st[:, 0:1])
                dve(lambda e, st=st: e.tensor_scalar(st[:, 1:2], st[:, 0:1], 1.0 / 2048, EPS, op0=ALU.mult, op1=ALU.add), [stb], [stb])
                act(st[:, 2:3], st[:, 1:2], AF.Sqrt, [stb], [stb])
                dve(lambda e, st=st: e.reciprocal(st[:, 3:4], st[:, 2:3]), [stb], [stb])

            def stage_b(i):
                s2 = i % 2
                xb, xbb, st, stb = xb2[s2], xbb2[s2], st2[s2], stb2[s2]
                dve(lambda e, s2=s2, xb=xb, st=st: e.tensor_scalar(xb, stg[s2], st[:, 3:4], None, op0=ALU.mult), [stb, stgb[s2]], [xbb])
                for hh in range(2):
                    b = tr_bank()
                    pv = psb(b).rearrange("p (a b) -> p a b", a=8)
                    for c in range(8):
                        cc = hh * 8 + c
                        tr(pv[:, c, :], xb[:, cc * 128:(cc + 1) * 128], idb, [xbb, b_const], [pbuf[b]], inc=(c == 7))
                    g = prm[:, gcol0 + hh * 8:gcol0 + hh * 8 + 8].unsqueeze(2).to_broadcast([128, 8, 128])
                    dve(lambda e, pv=pv, g=g, hh=hh, i=i: e.tensor_tensor(xn[:, hh * 8:hh * 8 + 8, i * 128:(i + 1) * 128], pv, g, ALU.mult),
                        [pbuf[b], b_const], [xnb[i]])
            stage_a(0)
            for i in range(ng):
                if i + 1 < ng:
                    stage_a(i + 1)
                stage_b(i)

        def fm_block(slot, sb_, nchunks, xin, xin_bufs, tiles, epi, kc=16):
            for j in range(nchunks):
                for ti, (t0, n) in enumerate(tiles):
                    b = acc_bank()
                    for c in range(kc):
                        mm(PS[:, b, 0:n], slot[:, c, j * 128:(j + 1) * 128], xin[:, c, t0:t0 + n], c == 0, c == kc - 1,
                           [sb_] + xin_bufs(t0, n), [pbuf[b]], inc=(c == kc - 1))
                    epi(j, ti, t0, n, PS[:, b, 0:n], pbuf[b])

        def tm_block(slot, sb_, ncols, xin, xin_bufs, nchunk_tok, epi, kc=16):
            for i in range(nchunk_tok):
                b = acc_bank()
                for c in range(kc):
                    mm(PS[:, b, 0:ncols], xin[:, c, i * 128:(i + 1) * 128], slot[:, c, 0:ncols], c == 0, c == kc - 1,
                       [sb_] + xin_bufs(i * 128, 128), [pbuf[b]], inc=(c == kc - 1))
                epi(i, PS[:, b, 0:ncols], pbuf[b])

        chk(12)
        NS = 16
        mS = A.mark()
        b_yTs_r, b_yTs_m = Buf("yTs_r"), Buf("yTs_m")
        b_ssum_s = Buf("ssum_s")
        xnS = A.alloc([16, NS], BF16)
        b_xnS = Buf("xnS")
        bc = lambda ap, shape: ap.to_broadcast(shape)

        def dv(fn, reads, writes):
            k.op("dve", fn, reads=reads, writes=writes)
        mS1 = A.mark()
        stgS = A.alloc([2048], F32)
        xbS = A.alloc([2048], BF16)
        junkS = A.alloc([2048], BF16)
        stS = A.alloc([4], F32)
        b_stgS, b_l = Buf("stgS"), Buf("l")
        k.dma("sp", stgS[0:16], xs_d, writes=[b_stgS])
        act(junkS[0:16], stgS[0:16], AF.Square, [b_stgS], [b_l], accum_out=stS[0:16, 0:1])
        dv(lambda e: e.tensor_scalar(stS[0:16, 1:2], stS[0:16, 0:1], 1.0 / 2048, EPS, op0=ALU.mult, op1=ALU.add), [b_l], [b_l])
        act(stS[0:16, 2:3], stS[0:16, 1:2], AF.Sqrt, [b_l], [b_l])
        dv(lambda e: e.reciprocal(stS[0:16, 3:4], stS[0:16, 2:3]), [b_l], [b_l])
        dv(lambda e: e.tensor_scalar(xbS[0:16], stgS[0:16], stS[0:16, 3:4], None, op0=ALU.mult), [b_l, b_stgS], [b_l])
        for hh in range(2):
            b = tr_bank()
            pv = psb(b)[:, 0:8 * 16].rearrange("p (a b) -> p a b", a=8)
            for c in range(8):
                cc = hh * 8 + c
                tr(pv[:, c, :], xbS[0:16, cc * 128:(cc + 1) * 128], idb[0:16, 0:16], [b_l, b_const], [pbuf[b]], inc=(c == 7))
            g = prm[:, P_GMIX + hh * 8:P_GMIX + hh * 8 + 8].unsqueeze(2).to_broadcast([128, 8, 16])
            dv(lambda e, pv=pv, g=g, hh=hh: e.tensor_tensor(xnS[:, hh * 8:hh * 8 + 8, :], pv, g, ALU.mult), [pbuf[b], b_const], [b_xnS])
        k.barrier()
        A.release(mS1)

        gz = A.alloc([8], F32)
        v_s = A.alloc([1024], F32)
        og_s = A.alloc([1024], F32)
        u_s = A.alloc([1024], F32)
        gel_s = A.alloc([8, NS], F32)
        xr_s = A.alloc([8, NS], F32)
        b_z = {n_: Buf(n_) for n_ in "gz v og u gel xr".split()}

        def tm_s(ncols, epi):
            slot, sb_ = wnext()
            b = acc_bank()
            for c in range(16):
                mm(PS[0:16, b, 0:ncols], xnS[:, c, :], slot[:, c, 0:ncols], c == 0, c == 15, [sb_, b_xnS], [pbuf[b]], inc=(c == 15))
            epi(PS[0:16, b, 0:ncols], pbuf[b])

        def fm_s(epi):
            slot, sb_ = wnext()
            b = acc_bank()
            for j in range(4):
                for c in range(16):
                    mm(PS[:, b, j * 16:(j + 1) * 16], slot[:, c, j * 128:(j + 1) * 128], xnS[:, c, :], c == 0, c == 15, [sb_, b_xnS], [pbuf[b]],
                       inc=(c == 15 and j == 3))
            epi(PS[:, b, 0:64].rearrange("p (j t) -> p j t", j=4), pbuf[b])
        tm_s(8, lambda acc, ab: act(gz[0:16], acc, AF.Copy, [ab], [b_z["gz"]]))
        for pr in range(2):
            tm_s(512, lambda acc, ab, pr=pr: act(v_s[0:16, pr * 512:(pr + 1) * 512], acc, AF.Copy, [ab], [b_z["v"]]))
        for pr in range(2):
            tm_s(512, lambda acc, ab, pr=pr: act(og_s[0:16, pr * 512:(pr + 1) * 512], acc, AF.Sigmoid, [ab], [b_z["og"]]))
        for pr in range(2):
            tm_s(512, lambda acc, ab, pr=pr: act(u_s[0:16, pr * 512:(pr + 1) * 512], acc, AF.Copy, [ab], [b_z["u"]]))
        for pr in range(2):
            fm_s(lambda acc, ab, pr=pr: act(gel_s[:, pr * 4:pr * 4 + 4, :], acc, AF.Gelu, [ab], [b_z["gel"]]))
        for pr in range(2):
            fm_s(lambda acc, ab, pr=pr: act(xr_s[:, pr * 4:pr * 4 + 4, :], acc, AF.Copy, [ab], [b_z["xr"]]))

        mS2 = A.mark()
        sh_tok = A.alloc([1024], F32)
        src_tok = A.alloc([3, 1024], F32)
        b_sh, b_src = Buf("sh"), Buf("src")
        k.dma("sp", sh_tok[0:16], sh_d, writes=[b_sh])
        k.dma("sp", src_tok[0:16], src_d, writes=[b_src])
        h0T = A.alloc([8, NS], F32)
        bufT = A.alloc([8, 3, NS], F32)
        b_h0T, b_bufT = Buf("h0T"), Buf("bufT")
        b = tr_bank()
        for c in range(8):
            tr(PS[:, b, c * 16:(c + 1) * 16], sh_tok[0:16, c * 128:(c + 1) * 128], idf[0:16, 0:16], [b_sh, b_const], [pbuf[b]], inc=(c == 7))
        dv(lambda e, b=b: e.tensor_copy(h0T, PS[:, b, 0:128].rearrange("p (c t) -> p c t", c=8)), [pbuf[b]], [b_h0T])
        b = tr_bank()
        for c in range(8):
            for j in range(3):
                tr(PS[:, b, (c * 3 + j) * 16:(c * 3 + j + 1) * 16], src_tok[0:16, j, c * 128:(c + 1) * 128], idf[0:16, 0:16], [b_src, b_const], [pbuf[b]],
                   inc=(c == 7 and j == 2))
        dv(lambda e, b=b: e.tensor_copy(bufT, PS[:, b, 0:384].rearrange("p (c j t) -> p c j t", c=8, j=3)), [pbuf[b]], [b_bufT])
        xcS = A.alloc([8, NS], F32)
        tS = A.alloc([8, NS], F32)
        xcbS = A.alloc([8, NS], BF16)
        rS = A.alloc([8, NS], F32)
        iS = A.alloc([8, NS], F32)
        aS = A.alloc([8, NS], F32)
        muS = A.alloc([8, NS], F32)
        hS = A.alloc([8, NS], F32)
        srcN = A.alloc([8, 3, NS], F32)
        b_r = [Buf("r%d" % i) for i in range(10)]
        Wt = lambda tap: prm[:, P_CRW + tap * 8:P_CRW + tap * 8 + 8].unsqueeze(2).to_broadcast([128, 8, NS])
        pbS = lambda col: prm[:, col:col + 8].unsqueeze(2).to_broadcast([128, 8, NS])
        dv(lambda e: e.tensor_tensor(xcS, bufT[:, :, 0, :], Wt(0), ALU.mult), [b_bufT, b_const], [b_r[0]])
        for j in (1, 2):
            dv(lambda e, j=j: e.tensor_tensor(tS, bufT[:, :, j, :], Wt(j), ALU.mult), [b_bufT, b_const], [b_r[1]])
            dv(lambda e: e.tensor_tensor(xcS, xcS, tS, ALU.add), [b_r[0], b_r[1]], [b_r[0]])
        dv(lambda e: e.tensor_tensor(tS, xr_s, Wt(3), ALU.mult), [b_z["xr"], b_const], [b_r[1]])
        dv(lambda e: e.tensor_tensor(xcS, xcS, tS, ALU.add), [b_r[0], b_r[1]], [b_r[0]])
        dv(lambda e: e.tensor_tensor(xcS, xcS, pbS(P_CRB), ALU.add), [b_r[0], b_const], [b_r[0]])
        dv(lambda e: e.tensor_copy(xcbS, xcS), [b_r[0]], [b_r[2]])
        for (W, dst, db, pcol) in ((lwab, rS, b_r[3], P_LBA), (lwxb, iS, b_r[4], P_LBX)):
            b = acc_bank()
            for c in range(8):
                mm(PS[:, b, c * 16:(c + 1) * 16], W[:, c, :], xcbS[:, c, :], True, True, [b_cw, b_r[2]], [pbuf[b]], inc=(c == 7))
            dv(lambda e, b=b, dst=dst, pcol=pcol: e.tensor_tensor(dst, PS[:, b, 0:128].rearrange("p (c t) -> p c t", c=8), pbS(pcol), ALU.add),
               [pbuf[b], b_const], [db])
            act(dst, dst, AF.Sigmoid, [db], [db])
        dv(lambda e: e.tensor_tensor(tS, rS, ccol[:, 0:8].unsqueeze(2).to_broadcast([128, 8, NS]), ALU.mult), [b_r[3], b_const], [b_r[1]])
        act(aS, tS, AF.Exp, [b_r[1]], [b_r[5]])
        act(muS, tS, AF.Exp, [b_r[1]], [b_r[6]], scale=2.0)
        act(muS, muS, AF.Sqrt, [b_r[6]], [b_r[6]], scale=-1.0, bias=1.0)
        dv(lambda e: e.tensor_tensor(iS, iS, xcS, ALU.mult), [b_r[4], b_r[0]], [b_r[4]])
        dv(lambda e: e.tensor_tensor(muS, muS, iS, ALU.mult), [b_r[6], b_r[4]], [b_r[6]])
        dv(lambda e: e.tensor_tensor(hS, aS, h0T, ALU.mult), [b_r[5], b_h0T], [b_r[7]])
        dv(lambda e: e.tensor_tensor(hS, hS, muS, ALU.add), [b_r[7], b_r[6]], [b_r[7]])
        k.dma("sp", osh_d, hS, reads=[b_r[7]])
        dv(lambda e: e.tensor_copy(srcN[:, :, 0:2, :], bufT[:, :, 1:3, :]), [b_bufT], [b_r[8]])
        dv(lambda e: e.tensor_copy(srcN[:, :, 2, :], xr_s), [b_z["xr"], b_r[8]], [b_r[8]])
        k.dma("sp", osrc_d, srcN, reads=[b_r[8]])
        dv(lambda e: e.tensor_tensor(tS, hS, gel_s, ALU.mult), [b_r[7], b_z["gel"]], [b_r[1]])
        dv(lambda e: e.tensor_tensor(yTs[:, 0:8, :], tS, pbS(P_GRN), ALU.mult), [b_r[1], b_const], [b_yTs_r])
        dv(lambda e: e.tensor_tensor(rS, tS, tS, ALU.mult), [b_r[1], b_r[3]], [b_r[3]])
        dv(lambda e: e.tensor_reduce(ssum_s, rS.rearrange("p c t -> p t c"), AX.X, ALU.add), [b_r[3]], [b_ssum_s])
        k.barrier()
        A.release(mS2)

        mS3 = A.mark()
        smc = A0.alloc([3, 1024], F32)
        snt = A.alloc([4, 256], F32)
        smt = A.alloc([4], F32)
        cmw = A0.alloc([4, 1024], F32)
        cmb = A0.alloc([1024], F32)
        gb = A.alloc([8], F32)
        b_in = Buf("sin")
        for dst, srcd in ((smc, smc_d), (snt, sn_d), (smt, sm_d), (cmw, cmw_d), (cmb, cmb_d), (gb, gb_d)):
            k.dma("sp", dst[0:16], srcd, writes=[b_in])
        P16 = slice(0, 16)
        ucp = A.alloc([1024], F32)
        t1 = A.alloc([1024], F32)
        ucbS = A.alloc([1024], BF16)
        ucT = A.alloc([8, NS], BF16)
        q_s = A.alloc([1024], F32)
        k_s = A.alloc([1024], F32)
        G = A.alloc([48], F32)
        b_m = [Buf("m%d" % i) for i in range(16)]
        dv(lambda e: e.tensor_tensor(ucp[P16], smc[P16, 0, :], cmw[P16, 0, :], ALU.mult), [b_in], [b_m[0]])
        for j in (1, 2):
            dv(lambda e, j=j: e.tensor_tensor(t1[P16], smc[P16, j, :], cmw[P16, j, :], ALU.mult), [b_in], [b_m[1]])
            dv(lambda e: e.tensor_tensor(ucp[P16], ucp[P16], t1[P16], ALU.add), [b_m[0], b_m[1]], [b_m[0]])
        dv(lambda e: e.tensor_tensor(t1[P16], u_s[P16], cmw[P16, 3, :], ALU.mult), [b_in, b_z["u"]], [b_m[1]])
        dv(lambda e: e.tensor_tensor(ucp[P16], ucp[P16], t1[P16], ALU.add), [b_m[0], b_m[1]], [b_m[0]])
        dv(lambda e: e.tensor_tensor(ucp[P16], ucp[P16], cmb[P16], ALU.add), [b_m[0], b_in], [b_m[0]])
        act(ucbS[P16], ucp[P16], AF.Silu, [b_m[0]], [b_m[2]])
        k.dma("sp", osmc_d[:, 0:2, :], smc[P16, 1:3, :], reads=[b_in])
        k.dma("sp", osmc_d[:, 2, :], u_s[P16], reads=[b_z["u"]])
        b = tr_bank()
        pv = psb(b)[:, 0:128].rearrange("p (a b) -> p a b", a=8)
        for c in range(8):
            tr(pv[:, c, :], ucbS[P16, c * 128:(c + 1) * 128], idb[0:16, 0:16], [b_m[2], b_const], [pbuf[b]], inc=(c == 7))
        dv(lambda e, pv=pv: e.tensor_copy(ucT, pv), [pbuf[b]], [b_m[4]])
        for h in range(4):
            for (W, dst, db, scl) in ((wqb, q_s, b_m[5], 1.0), (wkb, k_s, b_m[6], 1.0 / 16)):
                b = acc_bank()
                for ic in range(2):
                    mm(PS[0:16, b, 0:256], ucT[:, h * 2 + ic, :], W[:, h, ic, :], ic == 0, ic == 1, [b_cw, b_m[4]], [pbuf[b]], inc=(ic == 1))
                act(dst[P16, h * 256:(h + 1) * 256], PS[0:16, b, 0:256], AF.Copy, [pbuf[b]], [db], scale=scl)
        gG = lambda i: G[P16, i * 4:(i + 1) * 4]
        b_G = Buf("G")
        dv(lambda e: e.tensor_tensor(G[P16, 0:8], gz[P16], gb[P16], ALU.add), [b_z["gz"], b_in], [b_G])
        act(gG(1), gG(1), AF.Exp, [b_G], [b_G], scale=-1.0)
        act(gG(1), gG(1), AF.Ln, [b_G], [b_G], bias=1.0)
        dv(lambda e: e.tensor_tensor(gG(2), smt[P16], gG(1), ALU.subtract), [b_G, b_in], [b_G])
        dv(lambda e: e.tensor_tensor(gG(3), gG(2), gG(0), ALU.max), [b_G], [b_G])
        k.dma("sp", osm_d, gG(3), reads=[b_G])
        dv(lambda e: e.tensor_tensor(gG(4), gG(2), gG(3), ALU.subtract), [b_G], [b_G])
        act(gG(4), gG(4), AF.Exp, [b_G], [b_G])
        dv(lambda e: e.tensor_tensor(gG(5), gG(0), gG(3), ALU.subtract), [b_G], [b_G])
        act(gG(5), gG(5), AF.Exp, [b_G], [b_G])
        act(gG(6), gG(3), AF.Exp, [b_G], [b_G], scale=-1.0)
        v4 = lambda ap: ap.rearrange("p (h d) -> p h d", h=4)
        g4 = lambda i: gG(i).unsqueeze(2).to_broadcast([16, 4, 256])
        dv(lambda e: e.tensor_tensor(t1[P16], q_s[P16], k_s[P16], ALU.mult), [b_m[5], b_m[6]], [b_m[1]])
        dv(lambda e: e.tensor_reduce(gG(7), v4(t1[P16]), AX.X, ALU.add), [b_m[1], b_G], [b_G])
        dv(lambda e: e.tensor_tensor(v4(t1[P16]), v4(q_s[P16]), snt[P16], ALU.mult), [b_m[5], b_in, b_G], [b_m[1]])
        dv(lambda e: e.tensor_reduce(gG(8), v4(t1[P16]), AX.X, ALU.add), [b_m[1], b_G], [b_G])
        dv(lambda e: e.tensor_tensor(gG(9), gG(7), gG(5), ALU.mult), [b_G], [b_G])
        dv(lambda e: e.tensor_tensor(gG(10), gG(4), gG(8), ALU.mult), [b_G], [b_G])
        dv(lambda e: e.tensor_tensor(gG(10), gG(10), gG(9), ALU.add), [b_G], [b_G])
        act(gG(10), gG(10), AF.Abs, [b_G], [b_G])
        dv(lambda e: e.tensor_tensor(gG(10), gG(10), gG(6), ALU.max), [b_G], [b_G])
        dv(lambda e: e.reciprocal(gG(10), gG(10)), [b_G], [b_G])
        nN = A.alloc([4, 256], F32)
        gvS = A.alloc([4, 256], F32)
        dv(lambda e: e.tensor_tensor(nN[P16], snt[P16], g4(4), ALU.mult), [b_in, b_G], [b_m[7]])
        dv(lambda e: e.tensor_tensor(v4(t1[P16]), v4(k_s[P16]), g4(5), ALU.mult), [b_m[6], b_G, b_m[1]], [b_m[1]])
        dv(lambda e: e.tensor_tensor(nN[P16], nN[P16], v4(t1[P16]), ALU.add), [b_m[7], b_m[1]], [b_m[7]])
        k.dma("sp", osn_d, nN[P16], reads=[b_m[7]])
        dv(lambda e: e.tensor_tensor(gvS[P16], v4(v_s[P16]), g4(5), ALU.mult), [b_z["v"], b_G], [b_m[8]])
        Cq = A.alloc([4, 256], F32)
        b_Cq = Buf("Cq")
        selT = A.alloc([16 * 128], F32)
        b_selT = Buf("selT")
        k.dma("sp", selT[P16], seltok_d, writes=[b_selT])
        vT = A.alloc([8, NS], F32)
        CqT = A.alloc([8, NS], F32)
        wgR = A.alloc([NS, 8], F32)
        qR = [A.alloc([1024], F32) for _ in range(2)]
        kR = [A.alloc([1024], F32) for _ in range(2)]
        Ct = [A.alloc([8, 256], F32) for _ in range(2)]
        jk = A.alloc([256], F32)
        tT = [A.alloc([256], F32) for _ in range(2)]
        b_vT, b_CqT, b_wgR, b_jk = [Buf(x) for x in "vT CqT wgR jk".split()]
        b_tT = [Buf("tT0"), Buf("tT1")]
        b_qR, b_kR, b_Ct = [Buf("qR0"), Buf("qR1")], [Buf("kR0"), Buf("kR1")], [Buf("Ct0"), Buf("Ct1")]
        b = tr_bank()
        for c in range(8):
            tr(PS[:, b, c * 16:(c + 1) * 16], v_s[P16, c * 128:(c + 1) * 128], idf[0:16, 0:16], [b_z["v"], b_const], [pbuf[b]], inc=(c == 7))
        dv(lambda e, b=b: e.tensor_copy(vT, PS[:, b, 0:128].rearrange("p (c t) -> p c t", c=8)), [pbuf[b]], [b_vT])
        b = acc_bank()
        for tok in range(NS):
            mm(PS[:, b, tok * 8:(tok + 1) * 8], selT[P16, tok * 128:(tok + 1) * 128], G[P16, 16:24], True, True, [b_selT, b_G], [pbuf[b]], inc=(tok == NS - 1))
        dv(lambda e, b=b: e.tensor_copy(wgR, PS[:, b, 0:128].rearrange("p (t g) -> p t g", t=NS)), [pbuf[b]], [b_wgR])
        def c_prefetch(tok):
            s2 = tok % 2
            k.dma("sp", Ct[s2], sC_d[tok].rearrange("h (vh p) k -> p (h vh) k", p=128), writes=[b_Ct[s2]])
            for (src, dstR, dbR, sb1) in ((q_s, qR, b_qR, b_m[5]), (k_s, kR, b_kR, b_m[6])):
                for hh in range(2):
                    bb = 4 + (hh if src is q_s else 2 + hh)
                    mm(PS[:, bb, :], selT[P16, tok * 128:(tok + 1) * 128], src[P16, hh * 512:(hh + 1) * 512], True, True, [b_selT, sb1], [pbuf[bb]], inc=True)
                    act(dstR[s2][:, hh * 512:(hh + 1) * 512], PS[:, bb, :], AF.Copy, [pbuf[bb]], [dbR[s2]])
        c_prefetch(0)
        for tok in range(NS):
            s2 = tok % 2
            if tok + 1 < NS:
                c_prefetch(tok + 1)
            for hv in range(8):
                h = hv // 2
                dv(lambda e, s2=s2, hv=hv, h=h, tok=tok: e.scalar_tensor_tensor(jk, Ct[s2][:, hv, :], 1.0, qR[s2][:, h * 256:(h + 1) * 256], op0=ALU.mult, op1=ALU.mult,
                                                                               accum_out=CqT[:, hv, tok:tok + 1]),
                   [b_Ct[s2], b_qR[s2], b_CqT], [b_jk, b_CqT])
                k.op("pool", lambda e, s2=s2, hv=hv, h=h, tok=tok: e.tensor_scalar(tT[hv % 2], kR[s2][:, h * 256:(h + 1) * 256], vT[:, hv, tok:tok + 1], wgR[:, tok, 4 + h:5 + h],
                                                                                  op0=ALU.mult, op1=ALU.mult),
                     reads=[b_kR[s2], b_vT, b_wgR], writes=[b_tT[hv % 2]])
                dv(lambda e, s2=s2, hv=hv, h=h, tok=tok: e.scalar_tensor_tensor(Ct[s2][:, hv, :], Ct[s2][:, hv, :], wgR[:, tok, h:h + 1], tT[hv % 2], op0=ALU.mult, op1=ALU.add),
                   [b_Ct[s2], b_wgR, b_tT[hv % 2]], [b_Ct[s2]])
            k.dma("sp", osC_d[tok].rearrange("h (vh p) k -> p (h vh) k", p=128), Ct[s2], reads=[b_Ct[s2]])
        for q4 in range(2):
            b = tr_bank()
            for c in range(4):
                cc = q4 * 4 + c
                tr(PS[0:16, b, c * 128:(c + 1) * 128], CqT[:, cc, :], idf, [b_CqT, b_const], [pbuf[b]], inc=(c == 3))
            dv(lambda e, b=b, q4=q4: e.tensor_copy(Cq[P16, q4 * 2:q4 * 2 + 2, :], PS[0:16, b, :].rearrange("p (h d) -> p h d", h=2)), [pbuf[b]], [b_Cq])
        hN = A.alloc([4, 256], F32)
        dv(lambda e: e.tensor_tensor(hN[P16], Cq[P16], g4(4), ALU.mult), [b_Cq, b_G], [b_m[9]])
        dv(lambda e: e.tensor_tensor(v4(t1[P16]), v4(v_s[P16]), g4(9), ALU.mult), [b_z["v"], b_G, b_m[1]], [b_m[1]])
        dv(lambda e: e.tensor_tensor(hN[P16], hN[P16], v4(t1[P16]), ALU.add), [b_m[9], b_m[1]], [b_m[9]])
        dv(lambda e: e.tensor_tensor(hN[P16], hN[P16], g4(10), ALU.mult), [b_m[9], b_G], [b_m[9]])
        dv(lambda e: e.tensor_tensor(v4(t1[P16]), hN[P16], hN[P16], ALU.mult), [b_m[9], b_m[1]], [b_m[1]])
        dv(lambda e: e.tensor_reduce(gG(11), v4(t1[P16]), AX.X, ALU.add), [b_m[1], b_G], [b_G])
        dv(lambda e: e.tensor_scalar(gG(11), gG(11), 1.0 / 256, EPS, op0=ALU.mult, op1=ALU.add), [b_G], [b_G])
        act(gG(11), gG(11), AF.Sqrt, [b_G], [b_G])
        dv(lambda e: e.reciprocal(gG(11), gG(11)), [b_G], [b_G])
        dv(lambda e: e.tensor_tensor(hN[P16], hN[P16], g4(11), ALU.mult), [b_m[9], b_G], [b_m[9]])
        dv(lambda e: e.tensor_tensor(hN[P16], hN[P16], gml[P16].unsqueeze(1).to_broadcast([16, 4, 256]), ALU.mult), [b_m[9], b_const], [b_m[9]])
        dv(lambda e: e.tensor_tensor(v4(ucbS[P16]), hN[P16], v4(og_s[P16]), ALU.mult), [b_m[9], b_z["og"], b_m[2], b_m[4]], [b_m[2]])
        b = tr_bank()
        pv = psb(b)[:, 0:128].rearrange("p (a b) -> p a b", a=8)
        for c in range(8):
            tr(pv[:, c, :], ucbS[P16, c * 128:(c + 1) * 128], idb[0:16, 0:16], [b_m[2], b_const], [pbuf[b]], inc=(c == 7))
        dv(lambda e, pv=pv: e.tensor_copy(yTs[:, 8:16, :], pv), [pbuf[b]], [b_yTs_m])
        k.barrier()
        A.release(mS3)
        k.barrier()
        A.release(mS)
        A0.release(R0_LO)

        m_mix = A.mark()
        yT = A0.alloc([16, 1024], BF16)
        ssum = A0.alloc([1024], F32)
        xn = A.alloc([16, 1024], BF16)
        xnb = [Buf("xn%d" % i) for i in range(8)]
        yTb = [[Buf("yT%d_%d" % (c, t)) for t in range(2)] for c in range(16)]
        b_ssum = Buf("ssum")
        TILES = [(0, 512), (512, 512)]

        def xn_bufs(t0, n):
            return xnb[t0 // 128:(t0 + n + 127) // 128]

        for ps_ in range(2):
            main = ps_ == 1
            src = xm_d if main else xp_d
            m0 = A.mark()
            stg = [A.alloc([2048], F32) for _ in range(2)]
            scratch = (stg, [Buf("stg0"), Buf("stg1")], [A.alloc([2048], BF16) for _ in range(2)], [Buf("xb0"), Buf("xb1")],
                       [A.alloc([2048], BF16) for _ in range(2)], [Buf("jk0"), Buf("jk1")], [A.alloc([4], F32) for _ in range(2)], [Buf("st0"), Buf("st1")])
            load_norm(src, 1024, P_GMIX, xn, xnb, scratch)
            k.barrier()
            chk(2)
            A.release(m0)

            if main:
                dve(lambda e: e.tensor_scalar(C32, C32, maskc[:, 0:1], None, op0=ALU.mult), [b_C32, b_const], [b_C32])
                dve(lambda e: e.tensor_scalar(hcar, hcar, maskc[:, 0:1], None, op0=ALU.mult), [b_hcar, b_const], [b_hcar])
                dve(lambda e: e.tensor_scalar(gcar, gcar, maskc[0:4, 0:1], None, op0=ALU.mult), [b_gcar, b_const], [b_gcar])
                dve(lambda e: e.tensor_scalar(rtail, rtail, maskc[:, 0:1], None, op0=ALU.mult), [b_rtail, b_const], [b_rtail])
                dve(lambda e: e.tensor_scalar(mtail, mtail, maskc[:, 0:1], None, op0=ALU.mult), [b_mtail, b_const], [b_mtail])
                dve(lambda e: e.memset(ssum, 0.0), [], [b_ssum])
                chk(20)

            m1 = A.mark()
            R_B = A.alloc([1024], F32, parts=4)
            R_A = A.alloc([1024], F32, parts=4)
            R_ig = R_A
            R_M = A.alloc([1024], F32, parts=4)
            R_w = R_B
            R_g = A.alloc([1024], F32, parts=4)
            R_e = A.alloc([1024], F32, parts=4)
            R_s = A.alloc([16], F32, parts=4)
            gcols = A.alloc([8, 4, 4], F32)
            gsrep = A.alloc([4, 8], F32)
            b_rows = Buf("rows")
            b_gcols = Buf("gcols")
            b_gsrep = Buf("gsrep")

            slot, sb_ = wnext()
            for gi in range(2):
                for ti, (t0, n) in enumerate(TILES):
                    b = acc_bank()
                    for c in range(16):
                        mm(PS[0:4, b, 0:n], slot[:, c, gi * 4:gi * 4 + 4], xn[:, c, t0:t0 + n], c == 0, c == 15,
                           [sb_] + xn_bufs(t0, n), [pbuf[b]], inc=(c == 15))
                    if gi == 0:
                        act(R_ig[:, t0:t0 + n], PS[0:4, b, 0:n], AF.Identity, [pbuf[b], b_const], [b_rows], bias=prm[0:4, P_BI:P_BI + 1])
                    else:
                        act(R_e[:, t0:t0 + n], PS[0:4, b, 0:n], AF.Exp, [pbuf[b], b_const], [b_rows], scale=-1.0, bias=negbf[:, 0:1])
            act(R_e, R_e, AF.Ln, [b_rows], [b_rows], bias=1.0)
            dve(lambda e: e.tensor_tensor_scan(R_B, onesf[0:4, 0:1].to_broadcast([4, 1024]), R_e, gcar[:, 0:1], ALU.mult, ALU.subtract),
                [b_rows, b_gcar, b_const], [b_rows])
            dve(lambda e: e.tensor_tensor(R_A, R_ig, R_B, ALU.subtract), [b_rows], [b_rows])
            dve(lambda e: e.tensor_tensor_scan(R_M, onesf[0:4, 0:1].to_broadcast([4, 1024]), R_A, gcar[:, 1:2], ALU.mult, ALU.max),
                [b_rows, b_gcar], [b_rows])
            dve(lambda e: e.tensor_copy(R_s[:, 0:1], gcar[:, 1:2]), [b_gcar, b_rows], [b_rows])
            dve(lambda e: e.tensor_copy(R_s[:, 1:8], R_M[:, 127:896:128]), [b_rows], [b_rows])
            dve(lambda e: e.tensor_copy(R_s[:, 8:16], R_M[:, 127:1024:128]), [b_rows], [b_rows])
            v3 = lambda r: r.rearrange("p (c t) -> p c t", c=8)
            dve(lambda e: e.tensor_tensor(v3(R_g), v3(R_A), R_s[:, 8:16].unsqueeze(2).to_broadcast([4, 8, 128]), ALU.subtract), [b_rows], [b_rows])
            act(R_g, R_g, AF.Exp, [b_rows], [b_rows])
            dve(lambda e: e.tensor_tensor(R_e, R_B, R_M, ALU.add), [b_rows], [b_rows])
            dve(lambda e: e.tensor_copy(gcar[:, 2:3], R_e[:, 1023:1024]), [b_rows, b_gcar], [b_gcar])
            act(R_e, R_e, AF.Exp, [b_rows], [b_rows], scale=-1.0)
            dve(lambda e: e.tensor_copy(gcar[:, 0:1], R_B[:, 1023:1024]), [b_rows, b_gcar], [b_gcar])
            dve(lambda e: e.tensor_copy(gcar[:, 1:2], R_M[:, 1023:1024]), [b_rows, b_gcar], [b_gcar])
            dve(lambda e: e.tensor_tensor(v3(R_w), R_s[:, 0:8].unsqueeze(2).to_broadcast([4, 8, 128]), v3(R_M), ALU.subtract), [b_rows], [b_rows])
            act(R_w, R_w, AF.Exp, [b_rows], [b_rows])
            b = 4
            pgc = PS[:, b, 0:128].rearrange("p (c q h) -> p c q h", c=8, q=4)
            for c in range(8):
                for q, R in enumerate((R_A, R_w, R_e, R_g)):
                    tr(pgc[:, c, q, :], R[:, c * 128:(c + 1) * 128], idf[0:4, 0:4], [b_rows, b_const], [pbuf[b]], inc=(c == 7 and q == 3))
            dve(lambda e: e.tensor_copy(gcols, pgc), [pbuf[b]], [b_gcols])
            b = 5
            for h in range(4):
                mm(PS[:, b, h * 8:h * 8 + 8], sel[:, h * 128:(h + 1) * 128], R_w[:, 127:1024:128], True, True,
                   [b_rows, b_const], [pbuf[b]], inc=(h == 3))
            dve(lambda e: e.tensor_copy(gsrep, PS[:, 5, 0:32].rearrange("p (h c) -> p h c", h=4)), [pbuf[5]], [b_gsrep])
            chk(3)

            m2 = A.mark()
            vtok = A.alloc([8, 2, 257], BF16)
            b_vtok = [Buf("vtok%d" % i) for i in range(8)]
            ogt = A.alloc([8, 512], BF16)
            b_ogt = [Buf("og%d" % i) for i in range(8)]
            off_ub = A.top
            ub = A.alloc([4, 1028], BF16)
            ndv = ar_t[:, off_ub:off_ub + 2056].rearrange("p (a b) -> p a b", a=8)
            b_ub = [Buf("ub%d" % j) for j in range(4)]
            uc = A.alloc([4, 1024], BF16)
            b_uc = [[Buf("uc%d_%d" % (j, t)) for t in range(2)] for j in range(4)]
            diag = A.alloc([4, 128], BF16)
            b_diag = Buf("diag")
            qT = A.alloc([2, 1024], BF16)
            kT = A.alloc([2, 1024], BF16)
            ktok = A.alloc([8, 256], BF16)
            b_qT, b_kT, b_ktok = Buf("qT"), Buf("kT"), Buf("ktok")
            wk1 = A.alloc([257], F32)
            Eh = A.alloc([512], BF16)
            Pb = A.alloc([128], BF16)
            gv = A.alloc([257], BF16)
            ytk4 = A.alloc([4, 256], BF16)
            sm8 = A.alloc([32], F32)
            b_wk = [Buf("wk%d" % i) for i in range(8)]
            for pr in range(2):
                dve(lambda e: e.memset(vtok[:, :, :, 256:257], 1.0), [], b_vtok)
                slot, sb_ = wnext()

                def epi_v(i, acc, ab):
                    act(vtok[:, i, :, 0:256], acc.rearrange("p (h d) -> p h d", h=2), AF.Copy, [ab], [b_vtok[i]])
                tm_block(slot, sb_, 512, xn, xn_bufs, 8, epi_v)
                if main:
                    slot, sb_ = wnext()

                    def epi_og(i, acc, ab):
                        act(ogt[:, i, :], acc, AF.Sigmoid, [ab], [b_ogt[i]])
                    tm_block(slot, sb_, 512, xn, xn_bufs, 8, epi_og)
                slot, sb_ = wnext()
                dve(lambda e: e.memset(ub[:, :, 0:4], 0.0), [], b_ub)
                dve(lambda e, pr=pr: e.tensor_copy(ub[:, :, 1:4], mtail[:, pr * 4:pr * 4 + 4, :]), [b_mtail], b_ub)

                def epi_u(j, ti, t0, n, acc, ab, pr=pr):
                    act(ub[:, j, 4 + t0:4 + t0 + n], acc, AF.Copy, [ab], [b_ub[j]])
                    if ti == 1:
                        dve(lambda e: e.tensor_copy(mtail[:, pr * 4 + j, :], acc[:, n - 3:n]), [ab, b_ub[j]], [b_mtail])
                fm_block(slot, sb_, 4, xn, xn_bufs, TILES, epi_u)
                for j in range(4):
                    cg = pr * 4 + j
                    for tap in range(4):
                        dve(lambda e, tap=tap, cg=cg: e.tensor_scalar(diag[:, tap, :], idf, prm[:, P_CMW + tap * 8 + cg:P_CMW + tap * 8 + cg + 1], None, op0=ALU.mult),
                            [b_const], [b_diag])
                    for ti, (t0, n) in enumerate(TILES):
                        b = acc_bank()
                        for tap in range(4):
                            k.tag = "ps%dpr%dj%dti%dtap%d" % (ps_, pr, j, ti, tap)
                            if j > 0:
                                k.trace = False
                            mm(PS[:, b, 0:n], diag[:, tap, :], ub[:, j, t0 + tap + 1:t0 + tap + 1 + n], tap == 0, tap == 3,
                               [b_diag, b_ub[j]], [pbuf[b]], inc=(tap == 3))
                        act(uc[:, j, t0:t0 + n], PS[:, b, 0:n], AF.Silu, [pbuf[b], b_const], [b_uc[j][ti]], bias=prm[:, P_CMB + cg:P_CMB + cg + 1])
                if main and pr == 0:
                    chk(21)
                for hl in range(2):
                    h = pr * 2 + hl
                    ucb = lambda t0, n, hl=hl: [b_uc[hl * 2 + ic][t0 // 512] for ic in range(2)]
                    if main:
                        for (W, dstT, dbf, scl) in ((wqb, qT, b_qT, 1.0), (wkb, kT, b_kT, 1.0 / 16)):
                            for oc in range(2):
                                for ti, (t0, n) in enumerate(TILES):
                                    b = acc_bank()
                                    for ic in range(2):
                                        mm(PS[:, b, 0:n], W[:, h, ic, oc * 128:(oc + 1) * 128], uc[:, hl * 2 + ic, t0:t0 + n], ic == 0, ic == 1,
                                           [b_const] + ucb(t0, n), [pbuf[b]], inc=(ic == 1))
                                    act(dstT[:, oc, t0:t0 + n], PS[:, b, 0:n], AF.Copy, [pbuf[b]], [dbf], scale=scl)
                    for i in range(8):
                        b = acc_bank()
                        for ic in range(2):
                            mm(PS[:, b, 0:256], uc[:, hl * 2 + ic, i * 128:(i + 1) * 128], wkb[:, h, ic, :], ic == 0, ic == 1,
                               [b_const] + ucb(i * 128, 128), [pbuf[b]], inc=(ic == 1))
                        act(ktok[:, i, :], PS[:, b, 0:256], AF.Copy, [pbuf[b]], [b_ktok], scale=1.0 / 16)
                    if main and h == 0:
                        chk(22)
                    act(Cb, C32[:, h, :, :], AF.Copy, [b_C32], [b_Cb])
                    for i in range(8):
                        cs = slice(i * 128, (i + 1) * 128)
                        gc = lambda q, i=i, h=h: gcols[:, i, q, h:h + 1]
                        if main and i % 4 == 0:
                            mm(PS[:, 4, :], sel[:, h * 128:(h + 1) * 128], R_M[:, i * 128:i * 128 + 512], True, True, [b_rows, b_const], [pbuf[4]], inc=True)
                            for i4 in range(4):
                                act(Eh[:, i4 * 128:(i4 + 1) * 128], PS[:, 4, i4 * 128:(i4 + 1) * 128], AF.Exp, [pbuf[4], b_gcols], [b_wk[1]],
                                    scale=-1.0, bias=gcols[:, i + i4, 0, h:h + 1])
                            dve(lambda e: e.tensor_tensor(Eh.rearrange("p (c t) -> p c t", c=4), Eh.rearrange("p (c t) -> p c t", c=4),
                                                          m01.unsqueeze(1).to_broadcast([128, 4, 128]), ALU.mult), [b_wk[1], b_const], [b_wk[1]])
                        dve(lambda e, i=i, hl=hl, gc=gc: e.tensor_scalar(gv, vtok[:, i, hl, :], gc(3), None, op0=ALU.mult), [b_vtok[i], b_gcols], [b_wk[0]])
                        if main:
                            for dc in range(2):
                                mm(PS[:, 1, 0:128], kT[:, dc, cs], qT[:, dc, cs], dc == 0, dc == 1, [b_kT, b_qT], [pbuf[1]], inc=(dc == 1))
                        mm(PS[:, 7, 0:257], ktok[:, i, 0:128], gv, True, True, [b_ktok, b_wk[0]], [pbuf[7]], inc=True)
                        mm(PS[:, 0, 0:257], ktok[:, i, 128:256], gv, True, True, [b_ktok, b_wk[0]], [pbuf[0]], inc=True)
                        if main:
                            dve(lambda e, i=i: e.tensor_tensor(Pb, PS[:, 1, 0:128], Eh[:, (i % 4) * 128:(i % 4 + 1) * 128], ALU.mult), [pbuf[1], b_wk[1]], [b_wk[3]])
                            mm(PS[:, 5, 0:257], Pb, vtok[:, i, hl, :], True, True, [b_wk[3], b_vtok[i]], [pbuf[5]], inc=True)
                            for kc in range(2):
                                mm(PS[:, 6, 0:257], qT[:, kc, cs], Cb[:, kc, :], kc == 0, kc == 1, [b_qT, b_Cb], [pbuf[6]], inc=(kc == 1))
                            act(wk1, PS[:, 6, 0:257], AF.Identity, [pbuf[6], b_gcols], [b_wk[4]], scale=gc(1))
                            dve(lambda e, i=i: e.tensor_tensor(ndv[:, i, :], wk1, PS[:, 5, 0:257], ALU.add), [b_wk[4], pbuf[5]] + b_ub, b_ub)
                        if main and h == 0 and i == 0:
                            chk(23)
                        dve(lambda e, h=h, i=i: e.scalar_tensor_tensor(C32[:, h, 0, :], C32[:, h, 0, :], gsrep[:, h, i:i + 1], PS[:, 7, 0:257], op0=ALU.mult, op1=ALU.add),
                            [b_C32, b_gsrep, pbuf[7]], [b_C32])
                        dve(lambda e, h=h, i=i: e.scalar_tensor_tensor(C32[:, h, 1, :], C32[:, h, 1, :], gsrep[:, h, i:i + 1], PS[:, 0, 0:257], op0=ALU.mult, op1=ALU.add),
                            [b_C32, b_gsrep, pbuf[0]], [b_C32])
                        if main and i < 7:
                            act(Cb, C32[:, h, :, :], AF.Copy, [b_C32], [b_Cb])
                    if main:
                        ndh = ndv[:, :, 0:256]
                        act(sm8[:, 0:8], ndv[:, :, 256], AF.Abs, b_ub, [b_wk[6]])
                        dve(lambda e, h=h: e.tensor_tensor(sm8[:, 0:8], sm8[:, 0:8], gcols[:, :, 2, h], ALU.max), [b_wk[6], b_gcols], [b_wk[6]])
                        dve(lambda e: e.reciprocal(sm8[:, 8:16], sm8[:, 0:8]), [b_wk[6]], [b_wk[6]])
                        dve(lambda e: e.tensor_tensor(ndh, ndh, sm8[:, 8:16].unsqueeze(2).to_broadcast([128, 8, 256]), ALU.mult), [b_wk[6]] + b_ub, b_ub)
                        for i in range(8):
                            act(wk1[:, 0:256], ndv[:, i, 0:256], AF.Square, b_ub + [b_wk[6]], [b_wk[4], b_wk[6]], accum_out=sm8[:, 16 + i:17 + i])
                        dve(lambda e: e.tensor_scalar(sm8[:, 24:32], sm8[:, 16:24], 1.0 / 256, EPS, op0=ALU.mult, op1=ALU.add), [b_wk[6]], [b_wk[6]])
                        act(sm8[:, 24:32], sm8[:, 24:32], AF.Sqrt, [b_wk[6]], [b_wk[6]])
                        dve(lambda e: e.reciprocal(sm8[:, 24:32], sm8[:, 24:32]), [b_wk[6]], [b_wk[6]])
                        dve(lambda e: e.tensor_tensor(ndh, ndh, sm8[:, 24:32].unsqueeze(2).to_broadcast([128, 8, 256]), ALU.mult), [b_wk[6]] + b_ub, b_ub)
                        dve(lambda e: e.tensor_tensor(ndh, ndh, gml.unsqueeze(1).to_broadcast([128, 8, 256]), ALU.mult), [b_const] + b_ub, b_ub)
                        for half in range(2):
                            dve(lambda e, half=half, hl=hl: e.tensor_tensor(ytk4, ndv[:, half * 4:half * 4 + 4, 0:256], ogt[:, half * 4:half * 4 + 4, hl * 256:(hl + 1) * 256], ALU.mult),
                                b_ub + b_ogt[half * 4:half * 4 + 4], [b_wk[7]])
                            bT = tr_bank()
                            pv = psb(bT)
                            for i4 in range(4):
                                for hf in range(2):
                                    tr(pv[:, (hf * 4 + i4) * 128:(hf * 4 + i4 + 1) * 128], ytk4[:, i4, hf * 128:(hf + 1) * 128], idb, [b_wk[7], b_const], [pbuf[bT]],
                                       inc=(i4 == 3 and hf == 1))
                            for hf in range(2):
                                cgl = 8 + h * 2 + hf
                                act(yT[:, cgl, half * 512:(half + 1) * 512], pv[:, hf * 512:(hf + 1) * 512], AF.Copy, [pbuf[bT]], [yTb[cgl][half]])
                if main and pr == 0:
                    chk(24)
            k.barrier()
            A.release(m2)
            if main:
                chk(25)
                k.dma("sp", opC_d, C32, reads=[b_C32])
                k.dma("sp", opm_d, gcar[:, 2:3], reads=[b_gcar])
                k.dma("sp", opmc_d, mtail, reads=[b_mtail])
            k.barrier()
            A.release(m1)

            chk(4 if not main else 6)
            m2 = A.mark()
            xrb = A.alloc([4, 1028], BF16)
            b_xrb = [Buf("xrb%d" % j) for j in range(4)]
            gel = A.alloc([4, 1024], BF16)
            b_gel = [[Buf("gel") for t in range(2)] for j in range(4)]
            dgr = A.alloc([4, 128], BF16)
            b_dgr = Buf("dgr")
            RW = [dict(xc=A.alloc([1024], F32), xcb=A.alloc([1024], BF16), rr=A.alloc([1024], F32), ii=A.alloc([1024], F32),
                       aa=A.alloc([1024], F32), mu=A.alloc([1024], F32), hh_=A.alloc([1024], F32), bw=[Buf("rw%d" % i) for i in range(8)]) for _ in range(2)]
            for pr in range(2):
                if main:
                    slot, sb_ = wnext()

                    def epi_gr(j, ti, t0, n, acc, ab):
                        act(gel[:, j, t0:t0 + n], acc, AF.Gelu, [ab], [b_gel[j][ti]])
                    fm_block(slot, sb_, 4, xn, xn_bufs, TILES, epi_gr)
                slot, sb_ = wnext()
                dve(lambda e: e.memset(xrb[:, :, 0:4], 0.0), [], b_xrb)
                dve(lambda e, pr=pr: e.tensor_copy(xrb[:, :, 1:4], rtail[:, pr * 4:pr * 4 + 4, :]), [b_rtail], b_xrb)

                def epi_xr(j, ti, t0, n, acc, ab, pr=pr):
                    act(xrb[:, j, 4 + t0:4 + t0 + n], acc, AF.Copy, [ab], [b_xrb[j]])
                    if ti == 1:
                        dve(lambda e: e.tensor_copy(rtail[:, pr * 4 + j, :], acc[:, n - 3:n]), [ab, b_xrb[j]], [b_rtail])
                fm_block(slot, sb_, 4, xn, xn_bufs, TILES, epi_xr)
                def rg_front(j, pr=pr):
                    cg = pr * 4 + j
                    rw_ = RW[j % 2]
                    xc, xcb, rr, ii, aa, mu, hh_, bw = rw_['xc'], rw_['xcb'], rw_['rr'], rw_['ii'], rw_['aa'], rw_['mu'], rw_['hh_'], rw_['bw']
                    for tap in range(4):
                        dve(lambda e, tap=tap, cg=cg, xc=xc, xcb=xcb, rr=rr, ii=ii, aa=aa, mu=mu, hh_=hh_: e.tensor_scalar(dgr[:, tap, :], idf, prm[:, P_CRW + tap * 8 + cg:P_CRW + tap * 8 + cg + 1], None, op0=ALU.mult),
                            [b_const], [b_dgr])
                    for ti, (t0, n) in enumerate(TILES):
                        b = acc_bank()
                        for tap in range(4):
                            mm(PS[:, b, 0:n], dgr[:, tap, :], xrb[:, j, t0 + tap + 1:t0 + tap + 1 + n], tap == 0, tap == 3,
                               [b_dgr, b_xrb[j]], [pbuf[b]], inc=(tap == 3))
                        act(xc[:, t0:t0 + n], PS[:, b, 0:n], AF.Identity, [pbuf[b], b_const], [bw[0]], bias=prm[:, P_CRB + cg:P_CRB + cg + 1])
                    dve(lambda e, xc=xc, xcb=xcb, rr=rr, ii=ii, aa=aa, mu=mu, hh_=hh_: e.tensor_copy(xcb, xc), [bw[0]], [bw[1]])
                    for (W, dst, db, pb) in ((lwab, rr, bw[2], P_LBA), (lwxb, ii, bw[3], P_LBX)):
                        for ti, (t0, n) in enumerate(TILES):
                            b = acc_bank()
                            mm(PS[:, b, 0:n], W[:, cg, :], xcb[:, t0:t0 + n], True, True, [b_const, bw[1]], [pbuf[b]], inc=True)
                            act(dst[:, t0:t0 + n], PS[:, b, 0:n], AF.Sigmoid, [pbuf[b], b_const], [db], bias=prm[:, pb + cg:pb + cg + 1])
                def rg_back(j, pr=pr):
                    cg = pr * 4 + j
                    rw_ = RW[j % 2]
                    xc, xcb, rr, ii, aa, mu, hh_, bw = rw_['xc'], rw_['xcb'], rw_['rr'], rw_['ii'], rw_['aa'], rw_['mu'], rw_['hh_'], rw_['bw']
                    act(aa, rr, AF.Exp, [bw[2], b_const], [bw[4]], scale=ccol[:, cg:cg + 1])
                    act(mu, rr, AF.Exp, [bw[2], b_const], [bw[5]], scale=ccol2[:, cg:cg + 1])
                    act(mu, mu, AF.Sqrt, [bw[5]], [bw[5]], scale=-1.0, bias=1.0)
                    dve(lambda e, xc=xc, xcb=xcb, rr=rr, ii=ii, aa=aa, mu=mu, hh_=hh_: e.tensor_tensor(ii, ii, xc, ALU.mult), [bw[3], bw[0]], [bw[3]])
                    dve(lambda e, xc=xc, xcb=xcb, rr=rr, ii=ii, aa=aa, mu=mu, hh_=hh_: e.tensor_tensor(mu, mu, ii, ALU.mult), [bw[5], bw[3]], [bw[5]])
                    dve(lambda e, cg=cg, xc=xc, xcb=xcb, rr=rr, ii=ii, aa=aa, mu=mu, hh_=hh_: e.tensor_tensor_scan(hh_, aa, mu, hcar[:, cg:cg + 1], ALU.mult, ALU.add), [bw[4], bw[5], b_hcar], [bw[6]])
                    dve(lambda e, cg=cg, xc=xc, xcb=xcb, rr=rr, ii=ii, aa=aa, mu=mu, hh_=hh_: e.tensor_copy(hcar[:, cg:cg + 1], hh_[:, 1023:1024]), [bw[6], b_hcar], [b_hcar])
                    if main:
                        dve(lambda e, j=j, xc=xc, xcb=xcb, rr=rr, ii=ii, aa=aa, mu=mu, hh_=hh_: e.tensor_tensor(hh_, hh_, gel[:, j, :], ALU.mult), [bw[6]] + b_gel[j], [bw[6]])
                        dve(lambda e, cg=cg, xc=xc, xcb=xcb, rr=rr, ii=ii, aa=aa, mu=mu, hh_=hh_: e.tensor_scalar(yT[:, cg, :], hh_, prm[:, P_GRN + cg:P_GRN + cg + 1], None, op0=ALU.mult),
                            [bw[6], b_const], yTb[cg])
                        dve(lambda e, xc=xc, xcb=xcb, rr=rr, ii=ii, aa=aa, mu=mu, hh_=hh_: e.tensor_tensor(rr, hh_, hh_, ALU.mult), [bw[6], bw[2]], [bw[2]])
                        dve(lambda e, xc=xc, xcb=xcb, rr=rr, ii=ii, aa=aa, mu=mu, hh_=hh_: e.tensor_tensor(ssum, ssum, rr, ALU.add), [bw[2], b_ssum], [b_ssum])
                rg_front(0)
                for j in range(4):
                    if j + 1 < 4:
                        rg_front(j + 1)
                    rg_back(j)
            k.barrier()
            A.release(m2)
            chk(5 if not main else 7)
            if main:
                k.dma("sp", oph_d, hcar, reads=[b_hcar])
                k.dma("sp", oprc_d, rtail, reads=[b_rtail])

        k.barrier()
        A.release(m_mix)
        AM = Arena(ar_t, off_wqb, off_wqb + 4096)
        mkT = AM.alloc([16, 256], BF16)
        b_mkT = [Buf("mkT%d" % c) for c in range(16)]
        mvt = AM.alloc([2, 2048], BF16)
        b_mvt = [[Buf("mvt") for jb in range(4)] for nh in range(2)]
        m4 = A.mark()
        mn = A.alloc([16, 256], BF16)
        mnb = [Buf("mn0"), Buf("mn1")]
        m5 = A.mark()
        stg = [A.alloc([2048], F32) for _ in range(2)]
        scratch = (stg, [Buf("stg0"), Buf("stg1")], [A.alloc([2048], BF16) for _ in range(2)], [Buf("xb0"), Buf("xb1")],
                   [A.alloc([2048], BF16) for _ in range(2)], [Buf("jk0"), Buf("jk1")], [A.alloc([4], F32) for _ in range(2)], [Buf("st0"), Buf("st1")])
        chk(30)
        load_norm(mem_d, 256, P_GMEM, mn, mnb, scratch)
        k.barrier()
        chk(31)
        A.release(m5)
        ost = [A.alloc([4, 256], F32) for _ in range(2)]
        ostb = [Buf("ost0"), Buf("ost1")]
        mn_bufs = lambda t0, n: mnb[t0 // 128:(t0 + n + 127) // 128]
        for jb in range(4):
            slot, sb_ = wnext()
            s2 = jb % 2

            def epi_mk(j, ti, t0, n, acc, ab, jb=jb, s2=s2):
                act(ost[s2][:, j, :], acc, AF.Copy, [ab], [ostb[s2]])
                dve(lambda e: e.tensor_copy(mkT[:, jb * 4 + j, :], ost[s2][:, j, :]), [ostb[s2]], [b_mkT[jb * 4 + j]])
            fm_block(slot, sb_, 4, mn, mn_bufs, [(0, 256)], epi_mk)
            k.dma("sp", omk_d[:, jb * 4:jb * 4 + 4, :], ost[s2], reads=[ostb[s2]])
        chk(32)
        ost2 = [ost[0].rearrange("p a b -> p (a b)")[:, 0:512], ost[1].rearrange("p a b -> p (a b)")[:, 0:512]]
        cnt2 = 0
        for jb in range(4):
            slot, sb_ = wnext()

            def epi_mv(i, acc, ab, jb=jb):
                nonlocal cnt2
                s2 = cnt2 % 2
                cnt2 += 1
                act(ost2[s2], acc, AF.Copy, [ab], [ostb[s2]])
                dve(lambda e: e.tensor_copy(mvt[:, i, jb * 512:(jb + 1) * 512], ost2[s2]), [ostb[s2]], [b_mvt[i][jb]])
                k.dma("sp", omv_d[i * 128:(i + 1) * 128, jb * 512:(jb + 1) * 512], ost2[s2], reads=[ostb[s2]])
            tm_block(slot, sb_, 512, mn, mn_bufs, 2, epi_mv)
        k.barrier()
        A.release(m4)

        chk(8)

        def attn_core(n, heads, kfn, kbuf, vfn, vbuf, qc_, qcb_, oT_, oTb_, ET, b_ET, rden, b_rden):
            for hd in heads:
                for nh in range(2):
                    b = acc_bank()
                    for dc in range(4):
                        c = hd * 4 + dc
                        mm(PS[:, b, 0:n], kfn(hd, dc, nh), qc_[:, c, :], dc == 0, dc == 3,
                           [kbuf(hd, dc), qcb_[c]], [pbuf[b]], inc=(dc == 3))
                    act(ET[:, nh, 0:n], PS[:, b, 0:n], AF.Exp, [pbuf[b]], [b_ET], scale=float(512 ** -0.5))
                b = acc_bank()
                for nh in range(2):
                    mm(PS[:, b, 0:n], onesb, ET[:, nh, 0:n], nh == 0, nh == 1, [b_const, b_ET], [pbuf[b]], inc=(nh == 1))
                dve(lambda e, b=b: e.reciprocal(rden[:, 0:n], PS[:, b, 0:n]), [pbuf[b]], [b_rden])
                for dc in range(4):
                    c = hd * 4 + dc
                    b = acc_bank()
                    for nh in range(2):
                        mm(PS[:, b, 0:n], vfn(hd, dc, nh), ET[:, nh, 0:n], nh == 0, nh == 1,
                           [vbuf(hd, dc, nh), b_ET], [pbuf[b]], inc=(nh == 1))
                    dve(lambda e, b=b, c=c: e.tensor_tensor(oT_[:, c, :], PS[:, b, 0:n], rden[:, 0:n], ALU.mult), [pbuf[b], b_rden], [oTb_[c]])

        def post_tile(tiles, tinfo):
            NT = sum(n_ for _, n_ in tiles)
            nmax = max(n_ for _, n_ in tiles)
            accn["banks"] = [0, 1, 4, 5, 6, 7]
            m_tile = A.mark()
            X = A.alloc([16, NT], F32)
            off_hq = A.top
            hq = A.alloc([16, NT], BF16)
            Xb = [[Buf("X%d_%d" % (c, ti)) for ti in range(len(tiles))] for c in range(16)]
            m3 = A.mark()
            if NT >= 512:
                AH = Arena(ar_t, off_hq, off_hq + 16 * NT // 2)
                stg = [AH.alloc([2048], F32) for _ in range(2)]
            else:
                stg = [A.alloc([2048], F32) for _ in range(2)]
            stgb = [Buf("stg0"), Buf("stg1")]
            rstd = A.alloc([NT], F32)
            b_rstd = Buf("rstd")
            tmp = A.alloc([nmax], F32)
            b_tmp = Buf("tmp")
            gi = 0
            for ti, (t0, n) in enumerate(tiles):
                gs = tinfo[ti]["gs"]
                for i in range(n // gs):
                    s2 = gi % 2
                    gi += 1
                    r0 = t0 + i * gs
                    k.dma("sp", stg[s2][0:gs], tinfo[ti]["xsrc"][i * gs:(i + 1) * gs, :], writes=[stgb[s2]])
                    for q4 in range(4):
                        b = tr_bank()
                        for c in range(4):
                            cc = q4 * 4 + c
                            tr(PS[:, b, c * gs:(c + 1) * gs], stg[s2][0:gs, cc * 128:(cc + 1) * 128], idf[0:gs, 0:gs], [stgb[s2], b_const], [pbuf[b]], inc=(c == 3))
                        act(X[:, q4 * 4:q4 * 4 + 4, r0:r0 + gs], PS[:, b, 0:4 * gs].rearrange("p (c t) -> p c t", c=4), AF.Copy,
                            [pbuf[b]], [Xb[q4 * 4 + c][ti] for c in range(4)])
                b = acc_bank()
                mm(PS[:, b, 0:n], onesf, tinfo[ti]["ssum"], True, True, [b_const, tinfo[ti]["b_ssum"]], [pbuf[b]], inc=True)
                act(rstd[:, t0:t0 + n], PS[:, b, 0:n], AF.Sqrt, [pbuf[b]], [b_rstd], scale=1.0 / 1024, bias=EPS)
            dve(lambda e: e.reciprocal(rstd, rstd), [b_rstd], [b_rstd])
            for jb in range(8):
                slot, sb_ = wnext()
                for j in range(2):
                    m = jb * 2 + j
                    for ti, (t0, n) in enumerate(tiles):
                        b1 = acc_bank()
                        for c in range(8):
                            mm(PS[:, b1, 0:n], slot[:, c, j * 128:(j + 1) * 128], tinfo[ti]["yT"][:, c, :], c == 0, c == 7,
                               [sb_] + tinfo[ti]["yT_rb"], [pbuf[b1]], inc=(c == 7))
                        b2 = acc_bank()
                        for c in range(8, 16):
                            mm(PS[:, b2, 0:n], slot[:, c, j * 128:(j + 1) * 128], tinfo[ti]["yT"][:, c, :], c == 8, c == 15,
                               [sb_] + tinfo[ti]["yT_mb"], [pbuf[b2]], inc=(c == 15))
                        dve(lambda e, b1=b1, t0=t0, n=n: e.tensor_tensor(tmp[:, 0:n], PS[:, b1, 0:n], rstd[:, t0:t0 + n], ALU.mult), [pbuf[b1], b_rstd], [b_tmp])
                        dve(lambda e, m=m, b2=b2, t0=t0, n=n: e.tensor_tensor(X[:, m, t0:t0 + n], X[:, m, t0:t0 + n], PS[:, b2, 0:n], ALU.add), [pbuf[b2], Xb[m][ti]], [Xb[m][ti]])
                        dve(lambda e, m=m, t0=t0, n=n: e.tensor_tensor(X[:, m, t0:t0 + n], X[:, m, t0:t0 + n], tmp[:, 0:n], ALU.add), [b_tmp, Xb[m][ti]], [Xb[m][ti]])
            k.barrier()
            A.release(m3)
            A0.release(R0_LO)

            def rmsnorm_fm(gcol0, out, outb):
                mk_ = A.mark()
                mk0 = A0.mark()
                sq = A0.alloc([16, nmax], BF16)
                rs = A.alloc([nmax], F32)
                b_sq, b_rs = Buf("sq"), Buf("rs")
                for ti, (t0, n) in enumerate(tiles):
                    for c in range(16):
                        act(sq[:, c, 0:n], X[:, c, t0:t0 + n], AF.Square, [Xb[c][ti]], [b_sq])
                    b = acc_bank()
                    for c in range(16):
                        mm(PS[:, b, 0:n], onesb, sq[:, c, 0:n], c == 0, c == 15, [b_const, b_sq], [pbuf[b]], inc=(c == 15))
                    act(rs[:, 0:n], PS[:, b, 0:n], AF.Sqrt, [pbuf[b]], [b_rs], scale=1.0 / 2048, bias=EPS)
                    dve(lambda e, n=n: e.reciprocal(rs[:, 0:n], rs[:, 0:n]), [b_rs], [b_rs])
                    for c in range(16):
                        dve(lambda e, c=c, t0=t0, n=n: e.scalar_tensor_tensor(out[:, c, t0:t0 + n], X[:, c, t0:t0 + n], prm[:, gcol0 + c:gcol0 + c + 1], rs[:, 0:n],
                                                                               op0=ALU.mult, op1=ALU.mult),
                            [Xb[c][ti], b_const, b_rs], [outb[c][ti]])
                k.barrier()
                A.release(mk_)
                A0.release(mk0)

            def tb(bl):
                return lambda t0_, n_: [bl[c][[t for t, _ in tiles].index(t0_)] for c in range(16)]
            chk(9)
            hqb = [[Buf("hq") for _ in tiles] for c in range(16)]
            rmsnorm_fm(P_GXA, hq, hqb)
            m6 = A.mark()
            mk0 = A0.mark()
            qc = A0.alloc([16, NT], BF16)
            qcb = [[Buf("qc") for _ in tiles] for c in range(16)]
            for jb in range(8):
                slot, sb_ = wnext()

                def epi_q(j, ti_, t0_, n_, acc, ab, jb=jb):
                    act(qc[:, jb * 2 + j, t0_:t0_ + n_], acc, AF.Copy, [ab], [qcb[jb * 2 + j][ti_]])
                fm_block(slot, sb_, 2, hq, tb(hqb), tiles, epi_q)
            k.barrier()
            oT = hq
            oTb = [[Buf("oT") for _ in tiles] for c in range(16)]
            for ti, (t0, n) in enumerate(tiles):
                tinfo[ti]["attn"](n, qc[:, :, t0:t0 + n], [qcb[c][ti] for c in range(16)], oT[:, :, t0:t0 + n], [oTb[c][ti] for c in range(16)])
            for jb in range(8):
                slot, sb_ = wnext()

                def epi_co(j, ti_, t0_, n_, acc, ab, jb=jb):
                    m = jb * 2 + j
                    dve(lambda e: e.tensor_tensor(X[:, m, t0_:t0_ + n_], X[:, m, t0_:t0_ + n_], acc, ALU.add), [ab, Xb[m][ti_]], [Xb[m][ti_]])
                fm_block(slot, sb_, 2, oT, tb(oTb), tiles, epi_co)
            k.barrier()
            A.release(m6)
            A0.release(mk0)

            chk(10)
            hn = hq
            hnb = [[Buf("hn") for _ in tiles] for c in range(16)]
            rmsnorm_fm(P_GFFN, hn, hnb)
            m6 = A.mark()
            mk0 = A0.mark()
            hG = A0.alloc([16, NT], BF16)
            hGb = [[Buf("hG") for _ in tiles] for c in range(16)]
            rl = [A.alloc([nmax], F32) for _ in range(2)]
            rlb = [Buf("rl0"), Buf("rl1")]
            cnt3 = [0]
            for g in range(4):
                for jb in range(8):
                    slot, sb_ = wnext()

                    def epi_up(j, ti_, t0_, n_, acc, ab, jb=jb):
                        s2 = cnt3[0] % 2
                        cnt3[0] += 1
                        act(rl[s2][:, 0:n_], acc, AF.Relu, [ab], [rlb[s2]])
                        dve(lambda e: e.tensor_tensor(hG[:, jb * 2 + j, t0_:t0_ + n_], rl[s2][:, 0:n_], rl[s2][:, 0:n_], ALU.mult), [rlb[s2]], [hGb[jb * 2 + j][ti_]])
                    fm_block(slot, sb_, 2, hn, tb(hnb), tiles, epi_up)
                for jb in range(8):
                    slot, sb_ = wnext()

                    def epi_dn(j, ti_, t0_, n_, acc, ab, jb=jb):
                        m = jb * 2 + j
                        dve(lambda e: e.tensor_tensor(X[:, m, t0_:t0_ + n_], X[:, m, t0_:t0_ + n_], acc, ALU.add), [ab, Xb[m][ti_]], [Xb[m][ti_]])
                    fm_block(slot, sb_, 2, hG, tb(hGb), tiles, epi_dn)
            k.barrier()
            A.release(m6)
            A0.release(mk0)

            chk(11)
            m7 = A.mark()
            mk0 = A0.mark()
            sq = hq
            rs = A.alloc([nmax], F32)
            yn = [A0.alloc([16, 128], F32) for _ in range(2)]
            ob = [A0.alloc([2048], F32) for _ in range(2)]
            b_sq, b_rs = Buf("sq"), Buf("rs")
            b_yn = [Buf("yn0"), Buf("yn1")]
            obb = [Buf("ob0"), Buf("ob1")]
            gi = 0
            for ti, (t0, n) in enumerate(tiles):
                for c in range(16):
                    act(sq[:, c, t0:t0 + n], X[:, c, t0:t0 + n], AF.Square, [Xb[c][ti]], [b_sq])
                b = acc_bank()
                for c in range(16):
                    mm(PS[:, b, 0:n], onesb, sq[:, c, t0:t0 + n], c == 0, c == 15, [b_const, b_sq], [pbuf[b]], inc=(c == 15))
                act(rs[:, 0:n], PS[:, b, 0:n], AF.Sqrt, [pbuf[b]], [b_rs], scale=1.0 / 2048, bias=EPS)
                dve(lambda e, n=n: e.reciprocal(rs[:, 0:n], rs[:, 0:n]), [b_rs], [b_rs])
                gs = tinfo[ti]["gs"]
                for i in range(n // gs):
                    s2 = gi % 2
                    gi += 1
                    r0 = t0 + i * gs
                    for c in range(16):
                        dve(lambda e, c=c, i=i, s2=s2, r0=r0, gs=gs: e.scalar_tensor_tensor(yn[s2][:, c, 0:gs], X[:, c, r0:r0 + gs], prm[:, P_GFIN + c:P_GFIN + c + 1],
                                                                                             rs[:, i * gs:(i + 1) * gs], op0=ALU.mult, op1=ALU.mult),
                            [Xb[c][ti], b_const, b_rs], [b_yn[s2]])
                    for q4 in range(4):
                        b = tr_bank()
                        for c in range(4):
                            cc = q4 * 4 + c
                            tr(PS[0:gs, b, c * 128:(c + 1) * 128], yn[s2][:, cc, 0:gs], idf, [b_yn[s2], b_const], [pbuf[b]], inc=(c == 3))
                        act(ob[s2][0:gs, q4 * 512:(q4 + 1) * 512], PS[0:gs, b, :], AF.Copy, [pbuf[b]], [obb[s2]])
                    k.dma("sp", tinfo[ti]["ydst"][i * gs:(i + 1) * gs, :], ob[s2][0:gs], reads=[obb[s2]])
            k.barrier()
            A.release(m_tile)
            A0.release(mk0)
            accn["banks"] = [0, 1]

        def attn_prompt(n_, qc_, qcb_, oT_, oTb_):
            mk_ = A.mark()
            ET = A.alloc([2, 512], BF16)
            rden = A.alloc([512], F32)
            attn_core(n_, range(4),
                      lambda hd, dc, nh: mkT[:, hd * 4 + dc, nh * 128:(nh + 1) * 128], lambda hd, dc: b_mkT[hd * 4 + dc],
                      lambda hd, dc, nh: mvt[:, nh, (hd * 4 + dc) * 128:(hd * 4 + dc + 1) * 128], lambda hd, dc, nh: b_mvt[nh][hd],
                      qc_, qcb_, oT_, oTb_, ET, Buf("ET"), rden, Buf("rden"))
            k.barrier()
            A.release(mk_)
        def attn_sample(n_, qc_, qcb_, oT_, oTb_):
            mk_ = A.mark()
            Ks = [A.alloc([2, 512], BF16) for _ in range(2)]
            mkTs = [A.alloc([4, 256], BF16) for _ in range(2)]
            mvts = [A.alloc([2, 512], BF16) for _ in range(2)]
            ET = [A.alloc([2, 16], BF16) for _ in range(2)]
            rden = [A.alloc([16], F32) for _ in range(2)]
            b_Ks = [Buf("Ks0"), Buf("Ks1")]
            b_mk1 = [Buf("mk0"), Buf("mk1")]
            b_mv1 = [Buf("mv0"), Buf("mv1")]
            b_ET = [Buf("ET0"), Buf("ET1")]
            b_rden = [Buf("rd0"), Buf("rd1")]
            its = [(tok, hd) for tok in range(NS) for hd in range(4)]

            def dma_k(it):
                tok, hd = its[it]
                s2 = it % 2
                k.dma("pool", Ks[s2], ck_d[tok][:, hd * 512:(hd + 1) * 512].rearrange("(nh p) d -> p nh d", p=128), writes=[b_Ks[s2]])

            def dma_v(it):
                tok, hd = its[it]
                s2 = it % 2
                k.dma("pool", mvts[s2], cv_d[tok][:, hd * 512:(hd + 1) * 512].rearrange("(nh p) d -> p nh d", p=128), writes=[b_mv1[s2]])

            def st_a(it):
                tok, hd = its[it]
                s2 = it % 2
                b = tr_bank()
                pv = psb(b)
                for dc in range(4):
                    for nh in range(2):
                        tr(pv[:, (dc * 2 + nh) * 128:(dc * 2 + nh + 1) * 128], Ks[s2][:, nh, dc * 128:(dc + 1) * 128], idb, [b_Ks[s2], b_const], [pbuf[b]],
                           inc=(dc == 3 and nh == 1))
                if it % 2 == 0:
                    act(mkTs[s2], pv.rearrange("p (c n) -> p c n", c=4), AF.Copy, [pbuf[b]], [b_mk1[s2]])
                else:
                    dv(lambda e, pv=pv, s2=s2: e.tensor_copy(mkTs[s2], pv.rearrange("p (c n) -> p c n", c=4)), [pbuf[b]], [b_mk1[s2]])

            def st_b(it):
                tok, hd = its[it]
                s2 = it % 2
                attn_core(1, [hd],
                          lambda hd_, dc, nh: mkTs[s2][:, dc, nh * 128:(nh + 1) * 128], lambda hd_, dc: b_mk1[s2],
                          lambda hd_, dc, nh: mvts[s2][:, nh, dc * 128:(dc + 1) * 128], lambda hd_, dc, nh: b_mv1[s2],
                          qc_[:, :, tok:tok + 1], qcb_, oT_[:, :, tok:tok + 1], oTb_, ET[s2], b_ET[s2], rden[s2], b_rden[s2])
            dma_k(0)
            dma_k(1)
            dma_v(0)
            st_a(0)
            for it in range(len(its)):
                if it + 2 < len(its):
                    dma_k(it + 2)
                if it + 1 < len(its):
                    dma_v(it + 1)
                    st_a(it + 1)
                st_b(it)
            k.barrier()
            A.release(mk_)
        tinfo = [dict(xsrc=xm_d[t0:t0 + n], yT=yT[:, :, t0:t0 + n], yT_rb=[yTb[cc][ti] for cc in range(8)], yT_mb=[yTb[cc][ti] for cc in range(8, 16)],
                      ssum=ssum[:, t0:t0 + n], b_ssum=b_ssum, attn=attn_prompt, ydst=y_d[t0:t0 + n], gs=128) for ti, (t0, n) in enumerate(TILES)]
        tinfo.append(dict(xsrc=xs_d, yT=yTs, yT_rb=[b_yTs_r], yT_mb=[b_yTs_m], ssum=ssum_s, b_ssum=b_ssum_s, attn=attn_sample, ydst=ys_d, gs=NS))
        post_tile(TILES + [(1024, NS)], tinfo)
        assert k.dead or wstate["used"] == len(wsched), (wstate, len(wsched))
        k.finish()
        k.emit()
        print("instructions:", k.nins, "arena peak", A.peak, "of", NW, "A0 peak", A0.peak, "of", R0_HI)
    return nc


_CACHE = {}


def _consts():
    ident = np.eye(128, dtype=np.float32)
    s = np.arange(128)[:, None]
    t = np.arange(128)[None, :]
    maskneg = np.where(s <= t, 0.0, -30000.0).astype(np.float32)
    sel = np.zeros((4, 4, 128), np.float32)
    for h in range(4):
        sel[h, h, :] = 1.0
    return ident, maskneg, sel.reshape(4, 512)


def kernel(**inp):
    f = lambda a: np.ascontiguousarray(np.asarray(a, dtype=np.float32))
    if "nc" not in _CACHE:
        _CACHE["nc"] = build_program()
    nc = _CACHE["nc"]
    ident, maskneg, sel = _consts()
    prm = np.zeros((128, NPRM), np.float32)

    def colmajor(v, nch):
        return np.asarray(v, np.float32).reshape(nch, 128).T
    prm[:, P_GMIX:P_GMIX + 16] = colmajor(inp["g_mix"][0], 16)
    prm[:, P_GXA:P_GXA + 16] = colmajor(inp["g_xattn"][0], 16)
    prm[:, P_GMEM:P_GMEM + 16] = colmajor(inp["g_mem"][0], 16)
    prm[:, P_GFFN:P_GFFN + 16] = colmajor(inp["g_ffn"][0], 16)
    prm[:, P_GFIN:P_GFIN + 16] = colmajor(inp["g_final"], 16)
    for tap in range(4):
        prm[:, P_CRW + tap * 8:P_CRW + tap * 8 + 8] = colmajor(inp["conv_rnn_w"][0, tap], 8)
        prm[:, P_CMW + tap * 8:P_CMW + tap * 8 + 8] = colmajor(inp["conv_ml_w"][0, tap], 8)
    prm[:, P_CRB:P_CRB + 8] = colmajor(inp["conv_rnn_b"][0], 8)
    prm[:, P_CMB:P_CMB + 8] = colmajor(inp["conv_ml_b"][0], 8)
    prm[:, P_LBA:P_LBA + 8] = np.asarray(inp["lru_ba"][0], np.float32).T
    prm[:, P_LBX:P_LBX + 8] = np.asarray(inp["lru_bx"][0], np.float32).T
    prm[:, P_LAM:P_LAM + 8] = colmajor(inp["lru_lambda"][0], 8)
    prm[:, P_GRN:P_GRN + 8] = colmajor(inp["g_rnn_out"][0], 8)
    prm[:, P_GML2:P_GML2 + 2] = colmajor(inp["g_ml_out"][0], 2)
    prm[0:4, P_BI] = np.asarray(inp["ml_bi"][0], np.float32)
    prm[0:4, P_BF] = np.asarray(inp["ml_bf"][0], np.float32)
    gmlrep = np.ascontiguousarray(np.broadcast_to(np.asarray(inp["g_ml_out"][0], np.float32)[None, :], (128, 256)))
    shared = dict(
        prm=prm, gmlrep=gmlrep, ident=ident, maskneg=maskneg, sel=sel,
        w_in=f(inp["w_in"][0]), lru_wa=f(inp["lru_wa"][0]), lru_wx=f(inp["lru_wx"][0]),
        ml_wq=f(inp["ml_wq"][0]), ml_wk=f(inp["ml_wk"][0]), w_out=f(inp["w_out"][0]),
        w_cq=f(inp["w_cq"][0]), w_mk=f(inp["w_mk"][0]), w_mv=f(inp["w_mv"][0]), w_co=f(inp["w_co"][0]),
        w_up=f(inp["w_up"][0]), w_down=f(inp["w_down"][0]),
    )
    shared["cmw_rep"] = np.ascontiguousarray(np.broadcast_to(np.asarray(inp["conv_ml_w"][0], np.float32)[None], (16, 4, 1024)))
    st_ = np.zeros((16, 16, 128), np.float32)
    for t_ in range(16):
        st_[t_, t_, :] = 1.0
    shared["seltok"] = st_.reshape(16, 2048)
    shared["cmb_rep"] = np.ascontiguousarray(np.broadcast_to(np.asarray(inp["conv_ml_b"][0], np.float32)[None], (16, 1024)))
    shared["gb_rep"] = np.ascontiguousarray(np.broadcast_to(
        np.concatenate([np.asarray(inp["ml_bi"][0], np.float32), np.asarray(inp["ml_bf"][0], np.float32)])[None], (16, 8)))
    xsm = np.asarray(inp["x_sample"], np.float32)
    xpr = np.asarray(inp["x_prompt"], np.float32)
    memp = np.asarray(inp["mem_prompt"], np.float32)
    in_maps = []
    for c in range(8):
        b, hf = c // 2, c % 2
        d = dict(shared)
        d["xm"] = np.ascontiguousarray(xpr[b, hf * 1024:(hf + 1) * 1024])
        d["xp"] = np.ascontiguousarray(xpr[b, 0:1024])
        d["mem"] = np.ascontiguousarray(memp[b])
        d["mask"] = np.full((128, 1), float(hf), np.float32)
        sl = slice(c * 16, (c + 1) * 16)
        d["xs"] = np.ascontiguousarray(xsm[sl, 0])
        d["s_h"] = f(inp["state_rglru_h"][0, sl])
        d["s_rc"] = f(inp["state_rglru_conv"][0, sl])
        d["s_C"] = f(inp["state_mlstm_C"][0, sl])
        d["s_n"] = f(inp["state_mlstm_n"][0, sl])
        d["s_m"] = f(inp["state_mlstm_m"][0, sl])
        d["s_mc"] = f(inp["state_mlstm_conv"][0, sl])
        d["ck"] = f(inp["cache_mem_k"][0, sl]).reshape(16, 256, 2048)
        d["cv"] = f(inp["cache_mem_v"][0, sl]).reshape(16, 256, 2048)
        in_maps.append(d)
    res = run_bass_kernel_spmd(nc, in_maps, core_ids=list(range(8)))
    R = res.results
    B = 4
    y_prompt = np.zeros((B, 2048, 2048), np.float32)
    p_h = np.zeros((1, B, 1024), np.float32)
    p_rc = np.zeros((1, B, 3, 1024), np.float32)
    p_C = np.zeros((1, B, 4, 256, 256), np.float32)
    p_n = np.zeros((1, B, 4, 256), np.float32)
    p_m = np.zeros((1, B, 4), np.float32)
    p_mc = np.zeros((1, B, 3, 1024), np.float32)
    p_mk = np.zeros((1, B, 256, 4, 512), np.float32)
    p_mv = np.zeros((1, B, 256, 4, 512), np.float32)
    for c in range(8):
        b, hf = c // 2, c % 2
        r = R[c]
        y_prompt[b, hf * 1024:(hf + 1) * 1024] = r["o_y"]
        if hf == 1:
            p_h[0, b] = r["o_ph"].T.reshape(1024)
            p_rc[0, b] = r["o_prc"].transpose(2, 1, 0).reshape(3, 1024)
            p_mc[0, b] = r["o_pmc"].transpose(2, 1, 0).reshape(3, 1024)
            oc = r["o_pC"]
            p_C[0, b] = oc[:, :, :, 0:256].transpose(1, 3, 2, 0).reshape(4, 256, 256)
            p_n[0, b] = oc[:, :, :, 256].transpose(1, 2, 0).reshape(4, 256)
            p_m[0, b] = r["o_pm"].reshape(4)
            p_mk[0, b] = r["o_mkT"].transpose(2, 1, 0).reshape(256, 4, 512)
            p_mv[0, b] = r["o_mv"].reshape(256, 4, 512)
    y_s = np.zeros((128, 1, 2048), np.float32)
    s_h = np.zeros((1, 128, 1024), np.float32)
    s_rc = np.zeros((1, 128, 3, 1024), np.float32)
    s_C = np.zeros((1, 128, 4, 256, 256), np.float32)
    s_n = np.zeros((1, 128, 4, 256), np.float32)
    s_m = np.zeros((1, 128, 4), np.float32)
    s_mc = np.zeros((1, 128, 3, 1024), np.float32)
    for c in range(8):
        r = R[c]
        sl = slice(c * 16, (c + 1) * 16)
        y_s[sl, 0] = r["o_ys"]
        s_h[0, sl] = r["o_sh"].transpose(2, 1, 0).reshape(16, 1024)
        s_rc[0, sl] = r["o_src"].transpose(3, 2, 1, 0).reshape(16, 3, 1024)
        s_C[0, sl] = r["o_sC"]
        s_n[0, sl] = r["o_sn"]
        s_m[0, sl] = r["o_sm"]
        s_mc[0, sl] = r["o_smc"]
    return (y_prompt, y_s, p_h, p_rc, p_C, p_n, p_m, p_mc, p_mk, p_mv, s_h, s_rc, s_C, s_n, s_m, s_mc)
```

```python
import numpy as np
from contextlib import ExitStack
import concourse.bass as bass
import concourse.mybir as mybir
from concourse.bass_utils import run_bass_kernel_spmd

F32 = mybir.dt.float32
BF16 = mybir.dt.bfloat16
AF = mybir.ActivationFunctionType
ALU = mybir.AluOpType
AX = mybir.AxisListType

SAME_ENGINE_WAIT = True
EPS = 1e-6
NSLOT = 2

P_GMIX, P_GXA, P_GMEM, P_GFFN, P_GFIN = 0, 16, 32, 48, 64
P_CRW, P_CRB, P_LBA, P_LBX, P_LAM, P_GRN = 80, 112, 120, 128, 136, 144
P_CMW, P_CMB, P_GML2, P_BI, P_BF = 152, 184, 192, 194, 195
NPRM = 196


import os
STOP = int(os.environ.get("KSTOP", "0"))
DEBUG_SITES = bool(int(os.environ.get("KSITES", "0")))
DBG2 = int(os.environ.get("DBG2", "0"))
DBG3 = int(os.environ.get("DBG3", "0"))


class _Stop(Exception):
    pass


class Buf:
    __slots__ = ("name", "w", "r", "dsem", "dcount")

    def __init__(self, name="b"):
        self.name = name
        self.w = None
        self.r = {}
        self.dsem = None
        self.dcount = 0


class K:
    ENG = ("pe", "act", "dve", "pool", "sp")

    def __init__(self, nc, es):
        self.nc = nc
        self.es = es
        self.ops = {e: [] for e in self.ENG}
        self.sem = {e: es.enter_context(nc.semaphore("s_" + e)) for e in self.ENG}
        self.cnt = {e: 0 for e in self.ENG}
        self.known = {e: {} for e in self.ENG}
        self.semobj = {e: self.sem[e] for e in self.ENG}
        self.nd = 0
        self.dbufs = []
        self.nins = 0
        self.dead = False

    def _need(self, eng, reads, writes):
        need = {}

        def add(k, v):
            if need.get(k, 0) < v:
                need[k] = v
        for b in reads:
            if b.w:
                add(*b.w)
        for b in writes:
            if b.w:
                add(*b.w)
            for k, v in b.r.items():
                add(k, v)
        waits = []
        for k, v in need.items():
            if k == eng and (eng == "pe" or not SAME_ENGINE_WAIT):
                continue
            if self.known[eng].get(k, 0) >= v:
                continue
            self.known[eng][k] = v
            waits.append((self.semobj[k], v))
        return waits

    def op(self, eng, fn, reads=(), writes=(), inc=True):
        if self.dead:
            return
        waits = self._need(eng, reads, writes)
        val = self.cnt[eng] + 1
        if inc:
            self.cnt[eng] = val
        for b in reads:
            if b.r.get(eng, 0) < val:
                b.r[eng] = val
        for b in writes:
            b.w = (eng, val)
            b.r = {}
        sem = self.sem[eng]
        self.nins += 1
        if getattr(self, "trace", False):
            print("TRACE", eng, "val", val, "inc", inc, "waits", [(str(s_), v_) for s_, v_ in waits], "reads", [(b.name, b.w) for b in reads], "writes", [b.name for b in writes])
        import sys as _sys
        fr = _sys._getframe(1)
        site = []
        while fr is not None and len(site) < 3:
            site.append(fr.f_lineno)
            fr = fr.f_back
        site = "SITE" + "_".join(map(str, site)) + "_" + getattr(self, "tag", "")

        def run(e, waits=waits, fn=fn, inc=inc, sem=sem, site=site):
            for s, v in waits:
                e.wait_ge(s, v)
            ins = fn(e)
            if DEBUG_SITES:
                ins.annotate(site)
            if inc:
                ins.then_inc(sem, 1)
        self.ops[eng].append(run)

    def _dsem(self, b):
        if b.dsem is None:
            key = "d%d" % self.nd
            self.nd += 1
            b.dsem = key
            self.semobj[key] = self.es.enter_context(self.nc.semaphore(key))
            self.dbufs.append(b)
        return b.dsem

    def dma(self, q, out, in_, reads=(), writes=(), **kw):
        if self.dead:
            return
        waits = self._need(q, reads, writes)
        bl = list(reads) + list(writes)
        assert len(bl) == 1
        b = bl[0]
        kk = self._dsem(b)
        b.dcount += 16
        v = b.dcount
        if reads:
            b.r[kk] = v
        else:
            b.w = (kk, v)
            b.r = {}
        s = self.semobj[kk]
        self.nins += 1

        def run(e, waits=waits, s=s, out=out, in_=in_, kw=kw):
            for ws, wv in waits:
                e.wait_ge(ws, wv)
            e.dma_start(out=out, in_=in_, **kw).then_inc(s, 16)
        self.ops[q].append(run)

    def barrier(self):
        if self.dead:
            return
        tgt = [(e, self.cnt[e]) for e in self.ENG if self.cnt[e] > 0]
        tgt += [(b.dsem, b.dcount) for b in self.dbufs]
        for eng in self.ENG:
            waits = []
            for kk, v in tgt:
                if kk == eng:
                    continue
                if self.known[eng].get(kk, 0) >= v:
                    continue
                self.known[eng][kk] = v
                waits.append((self.semobj[kk], v))

            def run(e, waits=waits):
                for s, v in waits:
                    e.wait_ge(s, v)
            if waits:
                self.ops[eng].append(run)

    def finish(self):
        self.barrier()

    def emit(self):
        nc = self.nc
        with nc.Block() as block:
            @block.tensor
            def _(e):
                for f in self.ops["pe"]:
                    f(e)

            @block.scalar
            def _(e):
                for f in self.ops["act"]:
                    f(e)

            @block.vector
            def _(e):
                for f in self.ops["dve"]:
                    f(e)

            @block.gpsimd
            def _(e):
                for f in self.ops["pool"]:
                    f(e)

            @block.sync
            def _(e):
                for f in self.ops["sp"]:
                    f(e)


class Arena:
    def __init__(self, ap, lo, hi):
        self.ap = ap
        self.n = hi
        self.top = lo

    def alloc(self, shape, dt, parts=128):
        n = 1
        for s in shape:
            n *= s
        esz = 4 if dt == F32 else 2
        words = (n * esz + 3) // 4
        words = (words + 15) // 16 * 16
        off = self.top
        self.top += words
        assert self.top <= self.n, "arena overflow %d > %d" % (self.top, self.n)
        self.peak = max(getattr(self, "peak", 0), self.top)
        v = self.ap[:, off:off + words]
        if dt != F32:
            v = v.bitcast(dt)
        v = v[:, 0:n]
        if len(shape) == 2:
            v = v.rearrange("p (a b) -> p a b", a=shape[0])
        elif len(shape) == 3:
            v = v.rearrange("p (a b c) -> p a b c", a=shape[0], b=shape[1])
        elif len(shape) == 4:
            v = v.rearrange("p (a b c d) -> p a b c d", a=shape[0], b=shape[1], c=shape[2])
        if parts != 128:
            v = v[0:parts]
        return v

    def mark(self):
        return self.top

    def release(self, m):
        self.top = m


def build_program():
    nc = bass.Bass("TRN2", target_bir_lowering=False)

    def DI(name, shape):
        return nc.dram_tensor(name, list(shape), F32, kind="ExternalInput").ap()

    def DO(name, shape):
        return nc.dram_tensor(name, list(shape), F32, kind="ExternalOutput").ap()

    xm_d = DI("xm", [1024, 2048])
    xp_d = DI("xp", [1024, 2048])
    mem_d = DI("mem", [256, 2048])
    mask_d = DI("mask", [128, 1])
    prm_d = DI("prm", [128, NPRM])
    gml_d = DI("gmlrep", [128, 256])
    id_d = DI("ident", [128, 128])
    mneg_d = DI("maskneg", [128, 128])
    sel_d = DI("sel", [4, 4 * 128])
    w_in_d = DI("w_in", [2048, 5128])
    lwa_d = DI("lru_wa", [8, 128, 128])
    lwx_d = DI("lru_wx", [8, 128, 128])
    wq_d = DI("ml_wq", [4, 256, 256])
    wk_d = DI("ml_wk", [4, 256, 256])
    w_out_d = DI("w_out", [2048, 2048])
    w_cq_d = DI("w_cq", [2048, 2048])
    w_mk_d = DI("w_mk", [2048, 2048])
    w_mv_d = DI("w_mv", [2048, 2048])
    w_co_d = DI("w_co", [2048, 2048])
    w_up_d = DI("w_up", [2048, 8192])
    w_dn_d = DI("w_down", [8192, 2048])

    xs_d = DI("xs", [16, 2048])
    sh_d = DI("s_h", [16, 1024])
    src_d = DI("s_rc", [16, 3, 1024])
    sC_d = DI("s_C", [16, 4, 256, 256])
    sn_d = DI("s_n", [16, 4, 256])
    sm_d = DI("s_m", [16, 4])
    smc_d = DI("s_mc", [16, 3, 1024])
    ck_d = DI("ck", [16, 256, 2048])
    cv_d = DI("cv", [16, 256, 2048])
    seltok_d = DI("seltok", [16, 16 * 128])
    cmw_d = DI("cmw_rep", [16, 4, 1024])
    cmb_d = DI("cmb_rep", [16, 1024])
    gb_d = DI("gb_rep", [16, 8])
    ys_d = DO("o_ys", [16, 2048])
    osh_d = DO("o_sh", [128, 8, 16])
    osrc_d = DO("o_src", [128, 8, 3, 16])
    osC_d = DO("o_sC", [16, 4, 256, 256])
    osn_d = DO("o_sn", [16, 4, 256])
    osm_d = DO("o_sm", [16, 4])
    osmc_d = DO("o_smc", [16, 3, 1024])
    y_d = DO("o_y", [1024, 2048])
    oph_d = DO("o_ph", [128, 8])
    oprc_d = DO("o_prc", [128, 8, 3])
    opmc_d = DO("o_pmc", [128, 8, 3])
    opC_d = DO("o_pC", [128, 4, 2, 257])
    opm_d = DO("o_pm", [4, 1])
    omk_d = DO("o_mkT", [128, 16, 256])
    omv_d = DO("o_mv", [256, 2048])

    with ExitStack() as es:
        k = K(nc, es)
        NW = 52992
        ar_t = es.enter_context(nc.sbuf_tensor("arena", [128, NW], F32))
        A = Arena(ar_t, 0, NW)
        PS = es.enter_context(nc.psum_tensor("ps", [128, 8, 512], F32))

        def psb(b):
            return PS[:, b, :].bitcast(BF16)
        pbuf = [Buf("ps%d" % i) for i in range(8)]

        wslot = [A.alloc([16, 512], BF16) for _ in range(NSLOT)]
        wsb = [Buf("ws%d" % i) for i in range(NSLOT)]
        idf = A.alloc([128], F32)[:, :]
        idb = A.alloc([128], BF16)
        onesb = A.alloc([128], BF16)
        onesf = A.alloc([128], F32)
        mneg = A.alloc([128], F32)
        m01 = A.alloc([128], BF16)
        prm = A.alloc([NPRM], F32)
        gml = A.alloc([256], F32)
        maskc = A.alloc([1], F32)
        sel = A.alloc([4 * 128], F32, parts=4)
        off_wqb = A.top
        wqb = A.alloc([4, 2, 256], BF16)
        wkb = A.alloc([4, 2, 256], BF16)
        lwab = A.alloc([8, 128], BF16)
        lwxb = A.alloc([8, 128], BF16)
        C32 = A.alloc([4, 2, 257], F32)
        Cb = A.alloc([2, 257], BF16)
        hcar = A.alloc([8], F32)
        rtail = A.alloc([8, 3], F32)
        mtail = A.alloc([8, 3], F32)
        ccol = A.alloc([8], F32)
        ccol2 = A.alloc([8], F32)
        negbf = A.alloc([1], F32, parts=4)
        st0 = A.alloc([1], F32)
        gcar = A.alloc([4], F32, parts=4)
        b_const = Buf("const")
        yTs = A.alloc([16, 16], BF16)
        ssum_s = A.alloc([16], F32)
        PBASE = A.top
        R0_LO, R0_HI = PBASE, PBASE + 9216
        A0 = Arena(ar_t, R0_LO, R0_HI)
        A = Arena(ar_t, R0_HI, NW)
        print("persistent words", PBASE, "R12 words", NW - R0_HI)
        b_C32, b_Cb, b_hcar, b_rtail, b_mtail, b_gcar = [Buf(n) for n in "C32 Cb hcar rtail mtail gcar".split()]

        def act(out, in_, func, reads, writes, **kw):
            k.op("act", lambda e: e.activation(out, in_, func, **kw), reads=reads, writes=writes)

        def dve(fn, reads, writes):
            k.op("dve", fn, reads=reads, writes=writes)

        def mm(out, lhsT, rhs, start, stop, reads, writes, inc):
            k.op("pe", lambda e: e.matmul(out, lhsT, rhs, start=start, stop=stop), reads=reads, writes=writes, inc=inc)

        def tr(out, in_, ident, reads, writes, inc):
            k.op("pe", lambda e: e.transpose(out, in_, ident), reads=reads, writes=writes, inc=inc)

        def chk(n):
            if STOP == n:
                k.finish()
                k.dead = True
        for dst, src in ((idf, id_d), (mneg, mneg_d), (prm, prm_d), (gml, gml_d), (maskc, mask_d), (sel, sel_d)):
            k.dma("sp", dst, src, writes=[b_const])
        b_cw = Buf("constw")
        k.dma("pool", wqb, wq_d.rearrange("h (c p) n -> p h c n", p=128), writes=[b_cw])
        k.dma("pool", wkb, wk_d.rearrange("h (c p) n -> p h c n", p=128), writes=[b_cw])
        k.dma("pool", lwab, lwa_d.rearrange("h p n -> p h n"), writes=[b_cw])
        k.dma("pool", lwxb, lwx_d.rearrange("h p n -> p h n"), writes=[b_cw])
        k.op("dve", lambda e: e.memset(st0, 0.0), reads=[b_cw, b_const], writes=[b_const])
        dve(lambda e: e.tensor_copy(idb, idf), [b_const], [b_const])
        dve(lambda e: e.memset(onesb, 1.0), [], [b_const])
        dve(lambda e: e.tensor_scalar(m01, mneg, 0.0, None, op0=ALU.is_equal), [b_const], [b_const])
        dve(lambda e: e.memset(onesf, 1.0), [], [b_const])
        dve(lambda e: e.memset(C32, 0.0), [], [b_C32])
        dve(lambda e: e.memset(hcar, 0.0), [], [b_hcar])
        dve(lambda e: e.memset(gcar, 0.0), [], [b_gcar])
        dve(lambda e: e.memset(rtail, 0.0), [], [b_rtail])
        dve(lambda e: e.memset(mtail, 0.0), [], [b_mtail])
        act(ccol, prm[:, P_LAM:P_LAM + 8], AF.Exp, [b_const], [b_const], scale=-1.0)
        act(ccol, ccol, AF.Ln, [b_const], [b_const], bias=1.0)
        dve(lambda e: e.tensor_scalar(ccol2, ccol, -16.0, None, op0=ALU.mult), [b_const], [b_const])
        dve(lambda e: e.tensor_scalar(ccol, ccol, -8.0, None, op0=ALU.mult), [b_const], [b_const])
        dve(lambda e: e.tensor_scalar(negbf, prm[0:4, P_BF:P_BF + 1], -1.0, None, op0=ALU.mult), [b_const], [b_const])

        chk(1)
        wsched = []
        wstate = {"issued": 0, "used": 0, "cnt": [0, 0]}
        wassign = {}
        wflat = [w_.rearrange("p a b -> p (a b)") for w_ in wslot]
        hslot = [wflat[kk // 2][:, (kk % 2) * 4096:(kk % 2 + 1) * 4096].rearrange("p (a b) -> p a b", a=16) for kk in range(4)]
        hsb = [Buf("hs%d" % i) for i in range(4)]

        def wplan(ap, half=False):
            wsched.append((ap, half))

        def wplan256(wd, r0, c0):
            for hh_ in range(2):
                wplan(wd[r0:r0 + 2048, c0 + hh_ * 256:c0 + (hh_ + 1) * 256], True)

        def wissue():
            i = wstate["issued"]
            ap, half = wsched[i]
            nco = ap.shape[1]
            md = 1 if half else 0
            cidx = wstate["cnt"][md]
            wstate["cnt"][md] += 1
            if half:
                sl_, bf_ = hslot[cidx % 4], hsb[cidx % 4]
            else:
                sl_, bf_ = wslot[cidx % 2], wsb[cidx % 2]
            k.dma("pool", sl_[:, :, 0:nco], ap.rearrange("(c p) n -> p c n", p=128), writes=[bf_])
            wassign[i] = (sl_, bf_)
            wstate["issued"] = i + 1

        def wnext():
            i = wstate["used"]
            half = wsched[i][1]
            if wstate["issued"] <= i:
                if i > 0 and wsched[i - 1][1] != half:
                    k.barrier()
                wissue()
            depth = 4 if half else NSLOT
            while wstate["issued"] < min(len(wsched), i + depth) and wsched[wstate["issued"]][1] == half:
                wissue()
            wstate["used"] = i + 1
            return wassign.pop(i)

        def cols(wd, r0, c0, n):
            return wd[r0:r0 + 2048, c0:c0 + n]

        wplan(cols(w_in_d, 0, 5120, 8))
        for c0 in (3072, 3584, 4096, 4608, 2048, 2560, 1024, 1536, 0, 512):
            wplan(cols(w_in_d, 0, c0, 512))
        for ps_ in range(2):
            wplan(cols(w_in_d, 0, 5120, 8))
            for pr in range(2):
                wplan(cols(w_in_d, 0, 3072 + pr * 512, 512))
                if ps_ == 1:
                    wplan(cols(w_in_d, 0, 4096 + pr * 512, 512))
                wplan(cols(w_in_d, 0, 2048 + pr * 512, 512))
            for pr in range(2):
                if ps_ == 1:
                    wplan(cols(w_in_d, 0, 1024 + pr * 512, 512))
                wplan(cols(w_in_d, 0, 0 + pr * 512, 512))
        for j in range(4):
            wplan(cols(w_mk_d, 0, j * 512, 512))
        for j in range(4):
            wplan(cols(w_mv_d, 0, j * 512, 512))
        def plan_post():
            for j in range(4):
                wplan256(w_out_d, 0, j * 512)
            for j in range(4):
                wplan256(w_cq_d, 0, j * 512)
            for j in range(4):
                wplan256(w_co_d, 0, j * 512)
            for g in range(4):
                for j in range(4):
                    wplan256(w_up_d, 0, g * 2048 + j * 512)
                for j in range(4):
                    wplan256(w_dn_d, g * 2048, j * 512)
        plan_post()

        accn = {"i": 0, "banks": [0, 1]}

        def acc_bank():
            bl = accn["banks"]
            b = bl[accn["i"] % len(bl)]
            accn["i"] += 1
            return b

        trn = {"i": 0}

        def tr_bank():
            b = 2 + trn["i"] % 2
            trn["i"] += 1
            return b

        def load_norm(src, T, gcol0, xn, xnb, scratch):
            stg, stgb, xb2, xbb2, junk2, junkb2, st2, stb2 = scratch
            ng = T // 128

            def stage_a(i):
                s2 = i % 2
                junk, junkb, st, stb = junk2[s2], junkb2[s2], st2[s2], stb2[s2]
                k.dma("sp", stg[s2], src[i * 128:(i + 1) * 128, :], writes=[stgb[s2]])
                act(junk, stg[s2], AF.Square, [stgb[s2]], [junkb, stb], accum_out=st[:, 0:1])
                dve(lambda e, st=st: e.tensor_scalar(st[:, 1:2], st[:, 0:1], 1.0 / 2048, EPS, op0=ALU.mult, op1=ALU.add), [stb], [stb])
                act(st[:, 2:3], st[:, 1:2], AF.Sqrt, [stb], [stb])
                dve(lambda e, st=st: e.reciprocal(st[:, 3:4], st[:, 2:3]), [stb], [stb])

            def stage_b(i):
                s2 = i % 2
                xb, xbb, st, stb = xb2[s2], xbb2[s2], st2[s2], stb2[s2]
                dve(lambda e, s2=s2, xb=xb, st=st: e.tensor_scalar(xb, stg[s2], st[:, 3:4], None, op0=ALU.mult), [stb, stgb[s2]], [xbb])
                for hh in range(2):
                    b = tr_bank()
                    pv = psb(b).rearrange("p (a b) -> p a b", a=8)
                    for c in range(8):
                        cc = hh * 8 + c
                        tr(pv[:, c, :], xb[:, cc * 128:(cc + 1) * 128], idb, [xbb, b_const], [pbuf[b]], inc=(c == 7))
                    g = prm[:, gcol0 + hh * 8:gcol0 + hh * 8 + 8].unsqueeze(2).to_broadcast([128, 8, 128])
                    dve(lambda e, pv=pv, g=g, hh=hh, i=i: e.tensor_tensor(xn[:, hh * 8:hh * 8 + 8, i * 128:(i + 1) * 128], pv, g, ALU.mult),
                        [pbuf[b], b_const], [xnb[i]])
            stage_a(0)
            for i in range(ng):
                if i + 1 < ng:
                    stage_a(i + 1)
                stage_b(i)

        def fm_block(slot, sb_, nchunks, xin, xin_bufs, tiles, epi, kc=16):
            for j in range(nchunks):
                for ti, (t0, n) in enumerate(tiles):
                    b = acc_bank()
                    for c in range(kc):
                        mm(PS[:, b, 0:n], slot[:, c, j * 128:(j + 1) * 128], xin[:, c, t0:t0 + n], c == 0, c == kc - 1,
                           [sb_] + xin_bufs(t0, n), [pbuf[b]], inc=(c == kc - 1))
                    epi(j, ti, t0, n, PS[:, b, 0:n], pbuf[b])

        def tm_block(slot, sb_, ncols, xin, xin_bufs, nchunk_tok, epi, kc=16):
            for i in range(nchunk_tok):
                b = acc_bank()
                for c in range(kc):
                    mm(PS[:, b, 0:ncols], xin[:, c, i * 128:(i + 1) * 128], slot[:, c, 0:ncols], c == 0, c == kc - 1,
                       [sb_] + xin_bufs(i * 128, 128), [pbuf[b]], inc=(c == kc - 1))
                epi(i, PS[:, b, 0:ncols], pbuf[b])

        chk(12)
        NS = 16
        mS = A.mark()
        b_yTs_r, b_yTs_m = Buf("yTs_r"), Buf("yTs_m")
        b_ssum_s = Buf("ssum_s")
        xnS = A.alloc([16, NS], BF16)
        b_xnS = Buf("xnS")
        bc = lambda ap, shape: ap.to_broadcast(shape)

        def dv(fn, reads, writes):
            k.op("dve", fn, reads=reads, writes=writes)
        mS1 = A.mark()
        stgS = A.alloc([2048], F32)
        xbS = A.alloc([2048], BF16)
        junkS = A.alloc([2048], BF16)
        stS = A.alloc([4], F32)
        b_stgS, b_l = Buf("stgS"), Buf("l")
        k.dma("sp", stgS[0:16], xs_d, writes=[b_stgS])
        act(junkS[0:16], stgS[0:16], AF.Square, [b_stgS], [b_l], accum_out=stS[0:16, 0:1])
        dv(lambda e: e.tensor_scalar(stS[0:16, 1:2], stS[0:16, 0:1], 1.0 / 2048, EPS, op0=ALU.mult, op1=ALU.add), [b_l], [b_l])
        act(stS[0:16, 2:3], stS[0:16, 1:2], AF.Sqrt, [b_l], [b_l])
        dv(lambda e: e.reciprocal(stS[0:16, 3:4], stS[0:16, 2:3]), [b_l], [b_l])
        dv(lambda e: e.tensor_scalar(xbS[0:16], stgS[0:16], stS[0:16, 3:4], None, op0=ALU.mult), [b_l, b_stgS], [b_l])
        for hh in range(2):
            b = tr_bank()
            pv = psb(b)[:, 0:8 * 16].rearrange("p (a b) -> p a b", a=8)
            for c in range(8):
                cc = hh * 8 + c
                tr(pv[:, c, :], xbS[0:16, cc * 128:(cc + 1) * 128], idb[0:16, 0:16], [b_l, b_const], [pbuf[b]], inc=(c == 7))
            g = prm[:, P_GMIX + hh * 8:P_GMIX + hh * 8 + 8].unsqueeze(2).to_broadcast([128, 8, 16])
            dv(lambda e, pv=pv, g=g, hh=hh: e.tensor_tensor(xnS[:, hh * 8:hh * 8 + 8, :], pv, g, ALU.mult), [pbuf[b], b_const], [b_xnS])
        k.barrier()
        A.release(mS1)

        gz = A.alloc([8], F32)
        v_s = A.alloc([1024], F32)
        og_s = A.alloc([1024], F32)
        u_s = A.alloc([1024], F32)
        gel_s = A.alloc([8, NS], F32)
        xr_s = A.alloc([8, NS], F32)
        b_z = {n_: Buf(n_) for n_ in "gz v og u gel xr".split()}

        def tm_s(ncols, epi):
            slot, sb_ = wnext()
            b = acc_bank()
            for c in range(16):
                mm(PS[0:16, b, 0:ncols], xnS[:, c, :], slot[:, c, 0:ncols], c == 0, c == 15, [sb_, b_xnS], [pbuf[b]], inc=(c == 15))
            epi(PS[0:16, b, 0:ncols], pbuf[b])

        def fm_s(epi):
            slot, sb_ = wnext()
            b = acc_bank()
            for j in range(4):
                for c in range(16):
                    mm(PS[:, b, j * 16:(j + 1) * 16], slot[:, c, j * 128:(j + 1) * 128], xnS[:, c, :], c == 0, c == 15, [sb_, b_xnS], [pbuf[b]],
                       inc=(c == 15 and j == 3))
            epi(PS[:, b, 0:64].rearrange("p (j t) -> p j t", j=4), pbuf[b])
        tm_s(8, lambda acc, ab: act(gz[0:16], acc, AF.Copy, [ab], [b_z["gz"]]))
        for pr in range(2):
            tm_s(512, lambda acc, ab, pr=pr: act(v_s[0:16, pr * 512:(pr + 1) * 512], acc, AF.Copy, [ab], [b_z["v"]]))
        for pr in range(2):
            tm_s(512, lambda acc, ab, pr=pr: act(og_s[0:16, pr * 512:(pr + 1) * 512], acc, AF.Sigmoid, [ab], [b_z["og"]]))
        for pr in range(2):
            tm_s(512, lambda acc, ab, pr=pr: act(u_s[0:16, pr * 512:(pr + 1) * 512], acc, AF.Copy, [ab], [b_z["u"]]))
        for pr in range(2):
            fm_s(lambda acc, ab, pr=pr: act(gel_s[:, pr * 4:pr * 4 + 4, :], acc, AF.Gelu, [ab], [b_z["gel"]]))
        for pr in range(2):
            fm_s(lambda acc, ab, pr=pr: act(xr_s[:, pr * 4:pr * 4 + 4, :], acc, AF.Copy, [ab], [b_z["xr"]]))

        mS2 = A.mark()
        sh_tok = A.alloc([1024], F32)
        src_tok = A.alloc([3, 1024], F32)
        b_sh, b_src = Buf("sh"), Buf("src")
        k.dma("sp", sh_tok[0:16], sh_d, writes=[b_sh])
        k.dma("sp", src_tok[0:16], src_d, writes=[b_src])
        h0T = A.alloc([8, NS], F32)
        bufT = A.alloc([8, 3, NS], F32)
        b_h0T, b_bufT = Buf("h0T"), Buf("bufT")
        b = tr_bank()
        for c in range(8):
            tr(PS[:, b, c * 16:(c + 1) * 16], sh_tok[0:16, c * 128:(c + 1) * 128], idf[0:16, 0:16], [b_sh, b_const], [pbuf[b]], inc=(c == 7))
        dv(lambda e, b=b: e.tensor_copy(h0T, PS[:, b, 0:128].rearrange("p (c t) -> p c t", c=8)), [pbuf[b]], [b_h0T])
        b = tr_bank()
        for c in range(8):
            for j in range(3):
                tr(PS[:, b, (c * 3 + j) * 16:(c * 3 + j + 1) * 16], src_tok[0:16, j, c * 128:(c + 1) * 128], idf[0:16, 0:16], [b_src, b_const], [pbuf[b]],
                   inc=(c == 7 and j == 2))
        dv(lambda e, b=b: e.tensor_copy(bufT, PS[:, b, 0:384].rearrange("p (c j t) -> p c j t", c=8, j=3)), [pbuf[b]], [b_bufT])
        xcS = A.alloc([8, NS], F32)
        tS = A.alloc([8, NS], F32)
        xcbS = A.alloc([8, NS], BF16)
        rS = A.alloc([8, NS], F32)
        iS = A.alloc([8, NS], F32)
        aS = A.alloc([8, NS], F32)
        muS = A.alloc([8, NS], F32)
        hS = A.alloc([8, NS], F32)
        srcN = A.alloc([8, 3, NS], F32)
        b_r = [Buf("r%d" % i) for i in range(10)]
        Wt = lambda tap: prm[:, P_CRW + tap * 8:P_CRW + tap * 8 + 8].unsqueeze(2).to_broadcast([128, 8, NS])
        pbS = lambda col: prm[:, col:col + 8].unsqueeze(2).to_broadcast([128, 8, NS])
        dv(lambda e: e.tensor_tensor(xcS, bufT[:, :, 0, :], Wt(0), ALU.mult), [b_bufT, b_const], [b_r[0]])
        for j in (1, 2):
            dv(lambda e, j=j: e.tensor_tensor(tS, bufT[:, :, j, :], Wt(j), ALU.mult), [b_bufT, b_const], [b_r[1]])
            dv(lambda e: e.tensor_tensor(xcS, xcS, tS, ALU.add), [b_r[0], b_r[1]], [b_r[0]])
        dv(lambda e: e.tensor_tensor(tS, xr_s, Wt(3), ALU.mult), [b_z["xr"], b_const], [b_r[1]])
        dv(lambda e: e.tensor_tensor(xcS, xcS, tS, ALU.add), [b_r[0], b_r[1]], [b_r[0]])
        dv(lambda e: e.tensor_tensor(xcS, xcS, pbS(P_CRB), ALU.add), [b_r[0], b_const], [b_r[0]])
        dv(lambda e: e.tensor_copy(xcbS, xcS), [b_r[0]], [b_r[2]])
        for (W, dst, db, pcol) in ((lwab, rS, b_r[3], P_LBA), (lwxb, iS, b_r[4], P_LBX)):
            b = acc_bank()
            for c in range(8):
                mm(PS[:, b, c * 16:(c + 1) * 16], W[:, c, :], xcbS[:, c, :], True, True, [b_cw, b_r[2]], [pbuf[b]], inc=(c == 7))
            dv(lambda e, b=b, dst=dst, pcol=pcol: e.tensor_tensor(dst, PS[:, b, 0:128].rearrange("p (c t) -> p c t", c=8), pbS(pcol), ALU.add),
               [pbuf[b], b_const], [db])
            act(dst, dst, AF.Sigmoid, [db], [db])
        dv(lambda e: e.tensor_tensor(tS, rS, ccol[:, 0:8].unsqueeze(2).to_broadcast([128, 8, NS]), ALU.mult), [b_r[3], b_const], [b_r[1]])
        act(aS, tS, AF.Exp, [b_r[1]], [b_r[5]])
        act(muS, tS, AF.Exp, [b_r[1]], [b_r[6]], scale=2.0)
        act(muS, muS, AF.Sqrt, [b_r[6]], [b_r[6]], scale=-1.0, bias=1.0)
        dv(lambda e: e.tensor_tensor(iS, iS, xcS, ALU.mult), [b_r[4], b_r[0]], [b_r[4]])
        dv(lambda e: e.tensor_tensor(muS, muS, iS, ALU.mult), [b_r[6], b_r[4]], [b_r[6]])
        dv(lambda e: e.tensor_tensor(hS, aS, h0T, ALU.mult), [b_r[5], b_h0T], [b_r[7]])
        dv(lambda e: e.tensor_tensor(hS, hS, muS, ALU.add), [b_r[7], b_r[6]], [b_r[7]])
        k.dma("sp", osh_d, hS, reads=[b_r[7]])
        dv(lambda e: e.tensor_copy(srcN[:, :, 0:2, :], bufT[:, :, 1:3, :]), [b_bufT], [b_r[8]])
        dv(lambda e: e.tensor_copy(srcN[:, :, 2, :], xr_s), [b_z["xr"], b_r[8]], [b_r[8]])
        k.dma("sp", osrc_d, srcN, reads=[b_r[8]])
        dv(lambda e: e.tensor_tensor(tS, hS, gel_s, ALU.mult), [b_r[7], b_z["gel"]], [b_r[1]])
        dv(lambda e: e.tensor_tensor(yTs[:, 0:8, :], tS, pbS(P_GRN), ALU.mult), [b_r[1], b_const], [b_yTs_r])
        dv(lambda e: e.tensor_tensor(rS, tS, tS, ALU.mult), [b_r[1], b_r[3]], [b_r[3]])
        dv(lambda e: e.tensor_reduce(ssum_s, rS.rearrange("p c t -> p t c"), AX.X, ALU.add), [b_r[3]], [b_ssum_s])
        k.barrier()
        A.release(mS2)

        mS3 = A.mark()
        smc = A0.alloc([3, 1024], F32)
        snt = A.alloc([4, 256], F32)
        smt = A.alloc([4], F32)
        cmw = A0.alloc([4, 1024], F32)
        cmb = A0.alloc([1024], F32)
        gb = A.alloc([8], F32)
        b_in = Buf("sin")
        for dst, srcd in ((smc, smc_d), (snt, sn_d), (smt, sm_d), (cmw, cmw_d), (cmb, cmb_d), (gb, gb_d)):
            k.dma("sp", dst[0:16], srcd, writes=[b_in])
        P16 = slice(0, 16)
        ucp = A.alloc([1024], F32)
        t1 = A.alloc([1024], F32)
        ucbS = A.alloc([1024], BF16)
        ucT = A.alloc([8, NS], BF16)
        q_s = A.alloc([1024], F32)
        k_s = A.alloc([1024], F32)
        G = A.alloc([48], F32)
        b_m = [Buf("m%d" % i) for i in range(16)]
        dv(lambda e: e.tensor_tensor(ucp[P16], smc[P16, 0, :], cmw[P16, 0, :], ALU.mult), [b_in], [b_m[0]])
        for j in (1, 2):
            dv(lambda e, j=j: e.tensor_tensor(t1[P16], smc[P16, j, :], cmw[P16, j, :], ALU.mult), [b_in], [b_m[1]])
            dv(lambda e: e.tensor_tensor(ucp[P16], ucp[P16], t1[P16], ALU.add), [b_m[0], b_m[1]], [b_m[0]])
        dv(lambda e: e.tensor_tensor(t1[P16], u_s[P16], cmw[P16, 3, :], ALU.mult), [b_in, b_z["u"]], [b_m[1]])
        dv(lambda e: e.tensor_tensor(ucp[P16], ucp[P16], t1[P16], ALU.add), [b_m[0], b_m[1]], [b_m[0]])
        dv(lambda e: e.tensor_tensor(ucp[P16], ucp[P16], cmb[P16], ALU.add), [b_m[0], b_in], [b_m[0]])
        act(ucbS[P16], ucp[P16], AF.Silu, [b_m[0]], [b_m[2]])
        k.dma("sp", osmc_d[:, 0:2, :], smc[P16, 1:3, :], reads=[b_in])
        k.dma("sp", osmc_d[:, 2, :], u_s[P16], reads=[b_z["u"]])
        b = tr_bank()
        pv = psb(b)[:, 0:128].rearrange("p (a b) -> p a b", a=8)
        for c in range(8):
            tr(pv[:, c, :], ucbS[P16, c * 128:(c + 1) * 128], idb[0:16, 0:16], [b_m[2], b_const], [pbuf[b]], inc=(c == 7))
        dv(lambda e, pv=pv: e.tensor_copy(ucT, pv), [pbuf[b]], [b_m[4]])
        for h in range(4):
            for (W, dst, db, scl) in ((wqb, q_s, b_m[5], 1.0), (wkb, k_s, b_m[6], 1.0 / 16)):
                b = acc_bank()
                for ic in range(2):
                    mm(PS[0:16, b, 0:256], ucT[:, h * 2 + ic, :], W[:, h, ic, :], ic == 0, ic == 1, [b_cw, b_m[4]], [pbuf[b]], inc=(ic == 1))
                act(dst[P16, h * 256:(h + 1) * 256], PS[0:16, b, 0:256], AF.Copy, [pbuf[b]], [db], scale=scl)
        gG = lambda i: G[P16, i * 4:(i + 1) * 4]
        b_G = Buf("G")
        dv(lambda e: e.tensor_tensor(G[P16, 0:8], gz[P16], gb[P16], ALU.add), [b_z["gz"], b_in], [b_G])
        act(gG(1), gG(1), AF.Exp, [b_G], [b_G], scale=-1.0)
        act(gG(1), gG(1), AF.Ln, [b_G], [b_G], bias=1.0)
        dv(lambda e: e.tensor_tensor(gG(2), smt[P16], gG(1), ALU.subtract), [b_G, b_in], [b_G])
        dv(lambda e: e.tensor_tensor(gG(3), gG(2), gG(0), ALU.max), [b_G], [b_G])
        k.dma("sp", osm_d, gG(3), reads=[b_G])
        dv(lambda e: e.tensor_tensor(gG(4), gG(2), gG(3), ALU.subtract), [b_G], [b_G])
        act(gG(4), gG(4), AF.Exp, [b_G], [b_G])
        dv(lambda e: e.tensor_tensor(gG(5), gG(0), gG(3), ALU.subtract), [b_G], [b_G])
        act(gG(5), gG(5), AF.Exp, [b_G], [b_G])
        act(gG(6), gG(3), AF.Exp, [b_G], [b_G], scale=-1.0)
        v4 = lambda ap: ap.rearrange("p (h d) -> p h d", h=4)
        g4 = lambda i: gG(i).unsqueeze(2).to_broadcast([16, 4, 256])
        dv(lambda e: e.tensor_tensor(t1[P16], q_s[P16], k_s[P16], ALU.mult), [b_m[5], b_m[6]], [b_m[1]])
        dv(lambda e: e.tensor_reduce(gG(7), v4(t1[P16]), AX.X, ALU.add), [b_m[1], b_G], [b_G])
        dv(lambda e: e.tensor_tensor(v4(t1[P16]), v4(q_s[P16]), snt[P16], ALU.mult), [b_m[5], b_in, b_G], [b_m[1]])
        dv(lambda e: e.tensor_reduce(gG(8), v4(t1[P16]), AX.X, ALU.add), [b_m[1], b_G], [b_G])
        dv(lambda e: e.tensor_tensor(gG(9), gG(7), gG(5), ALU.mult), [b_G], [b_G])
        dv(lambda e: e.tensor_tensor(gG(10), gG(4), gG(8), ALU.mult), [b_G], [b_G])
        dv(lambda e: e.tensor_tensor(gG(10), gG(10), gG(9), ALU.add), [b_G], [b_G])
        act(gG(10), gG(10), AF.Abs, [b_G], [b_G])
        dv(lambda e: e.tensor_tensor(gG(10), gG(10), gG(6), ALU.max), [b_G], [b_G])
        dv(lambda e: e.reciprocal(gG(10), gG(10)), [b_G], [b_G])
        nN = A.alloc([4, 256], F32)
        gvS = A.alloc([4, 256], F32)
        dv(lambda e: e.tensor_tensor(nN[P16], snt[P16], g4(4), ALU.mult), [b_in, b_G], [b_m[7]])
        dv(lambda e: e.tensor_tensor(v4(t1[P16]), v4(k_s[P16]), g4(5), ALU.mult), [b_m[6], b_G, b_m[1]], [b_m[1]])
        dv(lambda e: e.tensor_tensor(nN[P16], nN[P16], v4(t1[P16]), ALU.add), [b_m[7], b_m[1]], [b_m[7]])
        k.dma("sp", osn_d, nN[P16], reads=[b_m[7]])
        dv(lambda e: e.tensor_tensor(gvS[P16], v4(v_s[P16]), g4(5), ALU.mult), [b_z["v"], b_G], [b_m[8]])
        Cq = A.alloc([4, 256], F32)
        b_Cq = Buf("Cq")
        selT = A.alloc([16 * 128], F32)
        b_selT = Buf("selT")
        k.dma("sp", selT[P16], seltok_d, writes=[b_selT])
        vT = A.alloc([8, NS], F32)
        CqT = A.alloc([8, NS], F32)
        wgR = A.alloc([NS, 8], F32)
        qR = [A.alloc([1024], F32) for _ in range(2)]
        kR = [A.alloc([1024], F32) for _ in range(2)]
        Ct = [A.alloc([8, 256], F32) for _ in range(2)]
        jk = A.alloc([256], F32)
        tT = [A.alloc([256], F32) for _ in range(2)]
        b_vT, b_CqT, b_wgR, b_jk = [Buf(x) for x in "vT CqT wgR jk".split()]
        b_tT = [Buf("tT0"), Buf("tT1")]
        b_qR, b_kR, b_Ct = [Buf("qR0"), Buf("qR1")], [Buf("kR0"), Buf("kR1")], [Buf("Ct0"), Buf("Ct1")]
        b = tr_bank()
        for c in range(8):
            tr(PS[:, b, c * 16:(c + 1) * 16], v_s[P16, c * 128:(c + 1) * 128], idf[0:16, 0:16], [b_z["v"], b_const], [pbuf[b]], inc=(c == 7))
        dv(lambda e, b=b: e.tensor_copy(vT, PS[:, b, 0:128].rearrange("p (c t) -> p c t", c=8)), [pbuf[b]], [b_vT])
        b = acc_bank()
        for tok in range(NS):
            mm(PS[:, b, tok * 8:(tok + 1) * 8], selT[P16, tok * 128:(tok + 1) * 128], G[P16, 16:24], True, True, [b_selT, b_G], [pbuf[b]], inc=(tok == NS - 1))
        dv(lambda e, b=b: e.tensor_copy(wgR, PS[:, b, 0:128].rearrange("p (t g) -> p t g", t=NS)), [pbuf[b]], [b_wgR])
        def c_prefetch(tok):
            s2 = tok % 2
            k.dma("sp", Ct[s2], sC_d[tok].rearrange("h (vh p) k -> p (h vh) k", p=128), writes=[b_Ct[s2]])
            for (src, dstR, dbR, sb1) in ((q_s, qR, b_qR, b_m[5]), (k_s, kR, b_kR, b_m[6])):
                for hh in range(2):
                    bb = 4 + (hh if src is q_s else 2 + hh)
                    mm(PS[:, bb, :], selT[P16, tok * 128:(tok + 1) * 128], src[P16, hh * 512:(hh + 1) * 512], True, True, [b_selT, sb1], [pbuf[bb]], inc=True)
                    act(dstR[s2][:, hh * 512:(hh + 1) * 512], PS[:, bb, :], AF.Copy, [pbuf[bb]], [dbR[s2]])
        c_prefetch(0)
        for tok in range(NS):
            s2 = tok % 2
            if tok + 1 < NS:
                c_prefetch(tok + 1)
            for hv in range(8):
                h = hv // 2
                dv(lambda e, s2=s2, hv=hv, h=h, tok=tok: e.scalar_tensor_tensor(jk, Ct[s2][:, hv, :], 1.0, qR[s2][:, h * 256:(h + 1) * 256], op0=ALU.mult, op1=ALU.mult,
                                                                               accum_out=CqT[:, hv, tok:tok + 1]),
                   [b_Ct[s2], b_qR[s2], b_CqT], [b_jk, b_CqT])
                k.op("pool", lambda e, s2=s2, hv=hv, h=h, tok=tok: e.tensor_scalar(tT[hv % 2], kR[s2][:, h * 256:(h + 1) * 256], vT[:, hv, tok:tok + 1], wgR[:, tok, 4 + h:5 + h],
                                                                                  op0=ALU.mult, op1=ALU.mult),
                     reads=[b_kR[s2], b_vT, b_wgR], writes=[b_tT[hv % 2]])
                dv(lambda e, s2=s2, hv=hv, h=h, tok=tok: e.scalar_tensor_tensor(Ct[s2][:, hv, :], Ct[s2][:, hv, :], wgR[:, tok, h:h + 1], tT[hv % 2], op0=ALU.mult, op1=ALU.add),
                   [b_Ct[s2], b_wgR, b_tT[hv % 2]], [b_Ct[s2]])
            k.dma("sp", osC_d[tok].rearrange("h (vh p) k -> p (h vh) k", p=128), Ct[s2], reads=[b_Ct[s2]])
        for q4 in range(2):
            b = tr_bank()
            for c in range(4):
                cc = q4 * 4 + c
                tr(PS[0:16, b, c * 128:(c + 1) * 128], CqT[:, cc, :], idf, [b_CqT, b_const], [pbuf[b]], inc=(c == 3))
            dv(lambda e, b=b, q4=q4: e.tensor_copy(Cq[P16, q4 * 2:q4 * 2 + 2, :], PS[0:16, b, :].rearrange("p (h d) -> p h d", h=2)), [pbuf[b]], [b_Cq])
        hN = A.alloc([4, 256], F32)
        dv(lambda e: e.tensor_tensor(hN[P16], Cq[P16], g4(4), ALU.mult), [b_Cq, b_G], [b_m[9]])
        dv(lambda e: e.tensor_tensor(v4(t1[P16]), v4(v_s[P16]), g4(9), ALU.mult), [b_z["v"], b_G, b_m[1]], [b_m[1]])
        dv(lambda e: e.tensor_tensor(hN[P16], hN[P16], v4(t1[P16]), ALU.add), [b_m[9], b_m[1]], [b_m[9]])
        dv(lambda e: e.tensor_tensor(hN[P16], hN[P16], g4(10), ALU.mult), [b_m[9], b_G], [b_m[9]])
        dv(lambda e: e.tensor_tensor(v4(t1[P16]), hN[P16], hN[P16], ALU.mult), [b_m[9], b_m[1]], [b_m[1]])
        dv(lambda e: e.tensor_reduce(gG(11), v4(t1[P16]), AX.X, ALU.add), [b_m[1], b_G], [b_G])
        dv(lambda e: e.tensor_scalar(gG(11), gG(11), 1.0 / 256, EPS, op0=ALU.mult, op1=ALU.add), [b_G], [b_G])
        act(gG(11), gG(11), AF.Sqrt, [b_G], [b_G])
        dv(lambda e: e.reciprocal(gG(11), gG(11)), [b_G], [b_G])
        dv(lambda e: e.tensor_tensor(hN[P16], hN[P16], g4(11), ALU.mult), [b_m[9], b_G], [b_m[9]])
        dv(lambda e: e.tensor_tensor(hN[P16], hN[P16], gml[P16].unsqueeze(1).to_broadcast([16, 4, 256]), ALU.mult), [b_m[9], b_const], [b_m[9]])
        dv(lambda e: e.tensor_tensor(v4(ucbS[P16]), hN[P16], v4(og_s[P16]), ALU.mult), [b_m[9], b_z["og"], b_m[2], b_m[4]], [b_m[2]])
        b = tr_bank()
        pv = psb(b)[:, 0:128].rearrange("p (a b) -> p a b", a=8)
        for c in range(8):
            tr(pv[:, c, :], ucbS[P16, c * 128:(c + 1) * 128], idb[0:16, 0:16], [b_m[2], b_const], [pbuf[b]], inc=(c == 7))
        dv(lambda e, pv=pv: e.tensor_copy(yTs[:, 8:16, :], pv), [pbuf[b]], [b_yTs_m])
        k.barrier()
        A.release(mS3)
        k.barrier()
        A.release(mS)
        A0.release(R0_LO)

        m_mix = A.mark()
        yT = A0.alloc([16, 1024], BF16)
        ssum = A0.alloc([1024], F32)
        xn = A.alloc([16, 1024], BF16)
        xnb = [Buf("xn%d" % i) for i in range(8)]
        yTb = [[Buf("yT%d_%d" % (c, t)) for t in range(2)] for c in range(16)]
        b_ssum = Buf("ssum")
        TILES = [(0, 512), (512, 512)]

        def xn_bufs(t0, n):
            return xnb[t0 // 128:(t0 + n + 127) // 128]

        for ps_ in range(2):
            main = ps_ == 1
            src = xm_d if main else xp_d
            m0 = A.mark()
            stg = [A.alloc([2048], F32) for _ in range(2)]
            scratch = (stg, [Buf("stg0"), Buf("stg1")], [A.alloc([2048], BF16) for _ in range(2)], [Buf("xb0"), Buf("xb1")],
                       [A.alloc([2048], BF16) for _ in range(2)], [Buf("jk0"), Buf("jk1")], [A.alloc([4], F32) for _ in range(2)], [Buf("st0"), Buf("st1")])
            load_norm(src, 1024, P_GMIX, xn, xnb, scratch)
            k.barrier()
            chk(2)
            A.release(m0)

            if main:
                dve(lambda e: e.tensor_scalar(C32, C32, maskc[:, 0:1], None, op0=ALU.mult), [b_C32, b_const], [b_C32])
                dve(lambda e: e.tensor_scalar(hcar, hcar, maskc[:, 0:1], None, op0=ALU.mult), [b_hcar, b_const], [b_hcar])
                dve(lambda e: e.tensor_scalar(gcar, gcar, maskc[0:4, 0:1], None, op0=ALU.mult), [b_gcar, b_const], [b_gcar])
                dve(lambda e: e.tensor_scalar(rtail, rtail, maskc[:, 0:1], None, op0=ALU.mult), [b_rtail, b_const], [b_rtail])
                dve(lambda e: e.tensor_scalar(mtail, mtail, maskc[:, 0:1], None, op0=ALU.mult), [b_mtail, b_const], [b_mtail])
                dve(lambda e: e.memset(ssum, 0.0), [], [b_ssum])
                chk(20)

            m1 = A.mark()
            R_B = A.alloc([1024], F32, parts=4)
            R_A = A.alloc([1024], F32, parts=4)
            R_ig = R_A
            R_M = A.alloc([1024], F32, parts=4)
            R_w = R_B
            R_g = A.alloc([1024], F32, parts=4)
            R_e = A.alloc([1024], F32, parts=4)
            R_s = A.alloc([16], F32, parts=4)
            gcols = A.alloc([8, 4, 4], F32)
            gsrep = A.alloc([4, 8], F32)
            b_rows = Buf("rows")
            b_gcols = Buf("gcols")
            b_gsrep = Buf("gsrep")

            slot, sb_ = wnext()
            for gi in range(2):
                for ti, (t0, n) in enumerate(TILES):
                    b = acc_bank()
                    for c in range(16):
                        mm(PS[0:4, b, 0:n], slot[:, c, gi * 4:gi * 4 + 4], xn[:, c, t0:t0 + n], c == 0, c == 15,
                           [sb_] + xn_bufs(t0, n), [pbuf[b]], inc=(c == 15))
                    if gi == 0:
                        act(R_ig[:, t0:t0 + n], PS[0:4, b, 0:n], AF.Identity, [pbuf[b], b_const], [b_rows], bias=prm[0:4, P_BI:P_BI + 1])
                    else:
                        act(R_e[:, t0:t0 + n], PS[0:4, b, 0:n], AF.Exp, [pbuf[b], b_const], [b_rows], scale=-1.0, bias=negbf[:, 0:1])
            act(R_e, R_e, AF.Ln, [b_rows], [b_rows], bias=1.0)
            dve(lambda e: e.tensor_tensor_scan(R_B, onesf[0:4, 0:1].to_broadcast([4, 1024]), R_e, gcar[:, 0:1], ALU.mult, ALU.subtract),
                [b_rows, b_gcar, b_const], [b_rows])
            dve(lambda e: e.tensor_tensor(R_A, R_ig, R_B, ALU.subtract), [b_rows], [b_rows])
            dve(lambda e: e.tensor_tensor_scan(R_M, onesf[0:4, 0:1].to_broadcast([4, 1024]), R_A, gcar[:, 1:2], ALU.mult, ALU.max),
                [b_rows, b_gcar], [b_rows])
            dve(lambda e: e.tensor_copy(R_s[:, 0:1], gcar[:, 1:2]), [b_gcar, b_rows], [b_rows])
            dve(lambda e: e.tensor_copy(R_s[:, 1:8], R_M[:, 127:896:128]), [b_rows], [b_rows])
            dve(lambda e: e.tensor_copy(R_s[:, 8:16], R_M[:, 127:1024:128]), [b_rows], [b_rows])
            v3 = lambda r: r.rearrange("p (c t) -> p c t", c=8)
            dve(lambda e: e.tensor_tensor(v3(R_g), v3(R_A), R_s[:, 8:16].unsqueeze(2).to_broadcast([4, 8, 128]), ALU.subtract), [b_rows], [b_rows])
            act(R_g, R_g, AF.Exp, [b_rows], [b_rows])
            dve(lambda e: e.tensor_tensor(R_e, R_B, R_M, ALU.add), [b_rows], [b_rows])
            dve(lambda e: e.tensor_copy(gcar[:, 2:3], R_e[:, 1023:1024]), [b_rows, b_gcar], [b_gcar])
            act(R_e, R_e, AF.Exp, [b_rows], [b_rows], scale=-1.0)
            dve(lambda e: e.tensor_copy(gcar[:, 0:1], R_B[:, 1023:1024]), [b_rows, b_gcar], [b_gcar])
            dve(lambda e: e.tensor_copy(gcar[:, 1:2], R_M[:, 1023:1024]), [b_rows, b_gcar], [b_gcar])
            dve(lambda e: e.tensor_tensor(v3(R_w), R_s[:, 0:8].unsqueeze(2).to_broadcast([4, 8, 128]), v3(R_M), ALU.subtract), [b_rows], [b_rows])
            act(R_w, R_w, AF.Exp, [b_rows], [b_rows])
            b = 4
            pgc = PS[:, b, 0:128].rearrange("p (c q h) -> p c q h", c=8, q=4)
            for c in range(8):
                for q, R in enumerate((R_A, R_w, R_e, R_g)):
                    tr(pgc[:, c, q, :], R[:, c * 128:(c + 1) * 128], idf[0:4, 0:4], [b_rows, b_const], [pbuf[b]], inc=(c == 7 and q == 3))
            dve(lambda e: e.tensor_copy(gcols, pgc), [pbuf[b]], [b_gcols])
            b = 5
            for h in range(4):
                mm(PS[:, b, h * 8:h * 8 + 8], sel[:, h * 128:(h + 1) * 128], R_w[:, 127:1024:128], True, True,
                   [b_rows, b_const], [pbuf[b]], inc=(h == 3))
            dve(lambda e: e.tensor_copy(gsrep, PS[:, 5, 0:32].rearrange("p (h c) -> p h c", h=4)), [pbuf[5]], [b_gsrep])
            chk(3)

            m2 = A.mark()
            vtok = A.alloc([8, 2, 257], BF16)
            b_vtok = [Buf("vtok%d" % i) for i in range(8)]
            ogt = A.alloc([8, 512], BF16)
            b_ogt = [Buf("og%d" % i) for i in range(8)]
            off_ub = A.top
            ub = A.alloc([4, 1028], BF16)
            ndv = ar_t[:, off_ub:off_ub + 2056].rearrange("p (a b) -> p a b", a=8)
            b_ub = [Buf("ub%d" % j) for j in range(4)]
            uc = A.alloc([4, 1024], BF16)
            b_uc = [[Buf("uc%d_%d" % (j, t)) for t in range(2)] for j in range(4)]
            diag = A.alloc([4, 128], BF16)
            b_diag = Buf("diag")
            qT = A.alloc([2, 1024], BF16)
            kT = A.alloc([2, 1024], BF16)
            ktok = A.alloc([8, 256], BF16)
            b_qT, b_kT, b_ktok = Buf("qT"), Buf("kT"), Buf("ktok")
            wk1 = A.alloc([257], F32)
            Eh = A.alloc([512], BF16)
            Pb = A.alloc([128], BF16)
            gv = A.alloc([257], BF16)
            ytk4 = A.alloc([4, 256], BF16)
            sm8 = A.alloc([32], F32)
            b_wk = [Buf("wk%d" % i) for i in range(8)]
            for pr in range(2):
                dve(lambda e: e.memset(vtok[:, :, :, 256:257], 1.0), [], b_vtok)
                slot, sb_ = wnext()

                def epi_v(i, acc, ab):
                    act(vtok[:, i, :, 0:256], acc.rearrange("p (h d) -> p h d", h=2), AF.Copy, [ab], [b_vtok[i]])
                tm_block(slot, sb_, 512, xn, xn_bufs, 8, epi_v)
                if main:
                    slot, sb_ = wnext()

                    def epi_og(i, acc, ab):
                        act(ogt[:, i, :], acc, AF.Sigmoid, [ab], [b_ogt[i]])
                    tm_block(slot, sb_, 512, xn, xn_bufs, 8, epi_og)
                slot, sb_ = wnext()
                dve(lambda e: e.memset(ub[:, :, 0:4], 0.0), [], b_ub)
                dve(lambda e, pr=pr: e.tensor_copy(ub[:, :, 1:4], mtail[:, pr * 4:pr * 4 + 4, :]), [b_mtail], b_ub)

                def epi_u(j, ti, t0, n, acc, ab, pr=pr):
                    act(ub[:, j, 4 + t0:4 + t0 + n], acc, AF.Copy, [ab], [b_ub[j]])
                    if ti == 1:
                        dve(lambda e: e.tensor_copy(mtail[:, pr * 4 + j, :], acc[:, n - 3:n]), [ab, b_ub[j]], [b_mtail])
                fm_block(slot, sb_, 4, xn, xn_bufs, TILES, epi_u)
                for j in range(4):
                    cg = pr * 4 + j
                    for tap in range(4):
                        dve(lambda e, tap=tap, cg=cg: e.tensor_scalar(diag[:, tap, :], idf, prm[:, P_CMW + tap * 8 + cg:P_CMW + tap * 8 + cg + 1], None, op0=ALU.mult),
                            [b_const], [b_diag])
                    for ti, (t0, n) in enumerate(TILES):
                        b = acc_bank()
                        for tap in range(4):
                            k.tag = "ps%dpr%dj%dti%dtap%d" % (ps_, pr, j, ti, tap)
                            if j > 0:
                                k.trace = False
                            mm(PS[:, b, 0:n], diag[:, tap, :], ub[:, j, t0 + tap + 1:t0 + tap + 1 + n], tap == 0, tap == 3,
                               [b_diag, b_ub[j]], [pbuf[b]], inc=(tap == 3))
                        act(uc[:, j, t0:t0 + n], PS[:, b, 0:n], AF.Silu, [pbuf[b], b_const], [b_uc[j][ti]], bias=prm[:, P_CMB + cg:P_CMB + cg + 1])
                if main and pr == 0:
                    chk(21)
                for hl in range(2):
                    h = pr * 2 + hl
                    ucb = lambda t0, n, hl=hl: [b_uc[hl * 2 + ic][t0 // 512] for ic in range(2)]
                    if main:
                        for (W, dstT, dbf, scl) in ((wqb, qT, b_qT, 1.0), (wkb, kT, b_kT, 1.0 / 16)):
                            for oc in range(2):
                                for ti, (t0, n) in enumerate(TILES):
                                    b = acc_bank()
                                    for ic in range(2):
                                        mm(PS[:, b, 0:n], W[:, h, ic, oc * 128:(oc + 1) * 128], uc[:, hl * 2 + ic, t0:t0 + n], ic == 0, ic == 1,
                                           [b_const] + ucb(t0, n), [pbuf[b]], inc=(ic == 1))
                                    act(dstT[:, oc, t0:t0 + n], PS[:, b, 0:n], AF.Copy, [pbuf[b]], [dbf], scale=scl)
                    for i in range(8):
                        b = acc_bank()
                        for ic in range(2):
                            mm(PS[:, b, 0:256], uc[:, hl * 2 + ic, i * 128:(i + 1) * 128], wkb[:, h, ic, :], ic == 0, ic == 1,
                               [b_const] + ucb(i * 128, 128), [pbuf[b]], inc=(ic == 1))
                        act(ktok[:, i, :], PS[:, b, 0:256], AF.Copy, [pbuf[b]], [b_ktok], scale=1.0 / 16)
                    if main and h == 0:
                        chk(22)
                    act(Cb, C32[:, h, :, :], AF.Copy, [b_C32], [b_Cb])
                    for i in range(8):
                        cs = slice(i * 128, (i + 1) * 128)
                        gc = lambda q, i=i, h=h: gcols[:, i, q, h:h + 1]
                        if main and i % 4 == 0:
                            mm(PS[:, 4, :], sel[:, h * 128:(h + 1) * 128], R_M[:, i * 128:i * 128 + 512], True, True, [b_rows, b_const], [pbuf[4]], inc=True)
                            for i4 in range(4):
                                act(Eh[:, i4 * 128:(i4 + 1) * 128], PS[:, 4, i4 * 128:(i4 + 1) * 128], AF.Exp, [pbuf[4], b_gcols], [b_wk[1]],
                                    scale=-1.0, bias=gcols[:, i + i4, 0, h:h + 1])
                            dve(lambda e: e.tensor_tensor(Eh.rearrange("p (c t) -> p c t", c=4), Eh.rearrange("p (c t) -> p c t", c=4),
                                                          m01.unsqueeze(1).to_broadcast([128, 4, 128]), ALU.mult), [b_wk[1], b_const], [b_wk[1]])
                        dve(lambda e, i=i, hl=hl, gc=gc: e.tensor_scalar(gv, vtok[:, i, hl, :], gc(3), None, op0=ALU.mult), [b_vtok[i], b_gcols], [b_wk[0]])
                        if main:
                            for dc in range(2):
                                mm(PS[:, 1, 0:128], kT[:, dc, cs], qT[:, dc, cs], dc == 0, dc == 1, [b_kT, b_qT], [pbuf[1]], inc=(dc == 1))
                        mm(PS[:, 7, 0:257], ktok[:, i, 0:128], gv, True, True, [b_ktok, b_wk[0]], [pbuf[7]], inc=True)
                        mm(PS[:, 0, 0:257], ktok[:, i, 128:256], gv, True, True, [b_ktok, b_wk[0]], [pbuf[0]], inc=True)
                        if main:
                            dve(lambda e, i=i: e.tensor_tensor(Pb, PS[:, 1, 0:128], Eh[:, (i % 4) * 128:(i % 4 + 1) * 128], ALU.mult), [pbuf[1], b_wk[1]], [b_wk[3]])
                            mm(PS[:, 5, 0:257], Pb, vtok[:, i, hl, :], True, True, [b_wk[3], b_vtok[i]], [pbuf[5]], inc=True)
                            for kc in range(2):
                                mm(PS[:, 6, 0:257], qT[:, kc, cs], Cb[:, kc, :], kc == 0, kc == 1, [b_qT, b_Cb], [pbuf[6]], inc=(kc == 1))
                            act(wk1, PS[:, 6, 0:257], AF.Identity, [pbuf[6], b_gcols], [b_wk[4]], scale=gc(1))
                            dve(lambda e, i=i: e.tensor_tensor(ndv[:, i, :], wk1, PS[:, 5, 0:257], ALU.add), [b_wk[4], pbuf[5]] + b_ub, b_ub)
                        if main and h == 0 and i == 0:
                            chk(23)
                        dve(lambda e, h=h, i=i: e.scalar_tensor_tensor(C32[:, h, 0, :], C32[:, h, 0, :], gsrep[:, h, i:i + 1], PS[:, 7, 0:257], op0=ALU.mult, op1=ALU.add),
                            [b_C32, b_gsrep, pbuf[7]], [b_C32])
                        dve(lambda e, h=h, i=i: e.scalar_tensor_tensor(C32[:, h, 1, :], C32[:, h, 1, :], gsrep[:, h, i:i + 1], PS[:, 0, 0:257], op0=ALU.mult, op1=ALU.add),
                            [b_C32, b_gsrep, pbuf[0]], [b_C32])
                        if main and i < 7:
                            act(Cb, C32[:, h, :, :], AF.Copy, [b_C32], [b_Cb])
                    if main:
                        ndh = ndv[:, :, 0:256]
                        act(sm8[:, 0:8], ndv[:, :, 256], AF.Abs, b_ub, [b_wk[6]])
                        dve(lambda e, h=h: e.tensor_tensor(sm8[:, 0:8], sm8[:, 0:8], gcols[:, :, 2, h], ALU.max), [b_wk[6], b_gcols], [b_wk[6]])
                        dve(lambda e: e.reciprocal(sm8[:, 8:16], sm8[:, 0:8]), [b_wk[6]], [b_wk[6]])
                        dve(lambda e: e.tensor_tensor(ndh, ndh, sm8[:, 8:16].unsqueeze(2).to_broadcast([128, 8, 256]), ALU.mult), [b_wk[6]] + b_ub, b_ub)
                        for i in range(8):
                            act(wk1[:, 0:256], ndv[:, i, 0:256], AF.Square, b_ub + [b_wk[6]], [b_wk[4], b_wk[6]], accum_out=sm8[:, 16 + i:17 + i])
                        dve(lambda e: e.tensor_scalar(sm8[:, 24:32], sm8[:, 16:24], 1.0 / 256, EPS, op0=ALU.mult, op1=ALU.add), [b_wk[6]], [b_wk[6]])
                        act(sm8[:, 24:32], sm8[:, 24:32], AF.Sqrt, [b_wk[6]], [b_wk[6]])
                        dve(lambda e: e.reciprocal(sm8[:, 24:32], sm8[:, 24:32]), [b_wk[6]], [b_wk[6]])
                        dve(lambda e: e.tensor_tensor(ndh, ndh, sm8[:, 24:32].unsqueeze(2).to_broadcast([128, 8, 256]), ALU.mult), [b_wk[6]] + b_ub, b_ub)
                        dve(lambda e: e.tensor_tensor(ndh, ndh, gml.unsqueeze(1).to_broadcast([128, 8, 256]), ALU.mult), [b_const] + b_ub, b_ub)
                        for half in range(2):
                            dve(lambda e, half=half, hl=hl: e.tensor_tensor(ytk4, ndv[:, half * 4:half * 4 + 4, 0:256], ogt[:, half * 4:half * 4 + 4, hl * 256:(hl + 1) * 256], ALU.mult),
                                b_ub + b_ogt[half * 4:half * 4 + 4], [b_wk[7]])
                            bT = tr_bank()
                            pv = psb(bT)
                            for i4 in range(4):
                                for hf in range(2):
                                    tr(pv[:, (hf * 4 + i4) * 128:(hf * 4 + i4 + 1) * 128], ytk4[:, i4, hf * 128:(hf + 1) * 128], idb, [b_wk[7], b_const], [pbuf[bT]],
                                       inc=(i4 == 3 and hf == 1))
                            for hf in range(2):
                                cgl = 8 + h * 2 + hf
                                act(yT[:, cgl, half * 512:(half + 1) * 512], pv[:, hf * 512:(hf + 1) * 512], AF.Copy, [pbuf[bT]], [yTb[cgl][half]])
                if main and pr == 0:
                    chk(24)
            k.barrier()
            A.release(m2)
            if main:
                chk(25)
                k.dma("sp", opC_d, C32, reads=[b_C32])
                k.dma("sp", opm_d, gcar[:, 2:3], reads=[b_gcar])
                k.dma("sp", opmc_d, mtail, reads=[b_mtail])
            k.barrier()
            A.release(m1)

            chk(4 if not main else 6)
            m2 = A.mark()
            xrb = A.alloc([4, 1028], BF16)
            b_xrb = [Buf("xrb%d" % j) for j in range(4)]
            gel = A.alloc([4, 1024], BF16)
            b_gel = [[Buf("gel") for t in range(2)] for j in range(4)]
            dgr = A.alloc([4, 128], BF16)
            b_dgr = Buf("dgr")
            FSET = [dict(xc=A.alloc([1024], F32), xcb=A.alloc([1024], BF16), rr=A.alloc([1024], F32), ii=A.alloc([1024], F32),
                         bw=[Buf("rwf%d" % i) for i in range(4)]) for _ in range(3)]
            BSET = dict(aa=A.alloc([1024], F32), mu=A.alloc([1024], F32), hh_=A.alloc([1024], F32), bw=[Buf("rwb%d" % i) for i in range(4)])
            for pr in range(2):
                if main:
                    slot, sb_ = wnext()

                    def epi_gr(j, ti, t0, n, acc, ab):
                        act(gel[:, j, t0:t0 + n], acc, AF.Gelu, [ab], [b_gel[j][ti]])
                    fm_block(slot, sb_, 4, xn, xn_bufs, TILES, epi_gr)
                slot, sb_ = wnext()
                dve(lambda e: e.memset(xrb[:, :, 0:4], 0.0), [], b_xrb)
                dve(lambda e, pr=pr: e.tensor_copy(xrb[:, :, 1:4], rtail[:, pr * 4:pr * 4 + 4, :]), [b_rtail], b_xrb)

                def epi_xr(j, ti, t0, n, acc, ab, pr=pr):
                    act(xrb[:, j, 4 + t0:4 + t0 + n], acc, AF.Copy, [ab], [b_xrb[j]])
                    if ti == 1:
                        dve(lambda e: e.tensor_copy(rtail[:, pr * 4 + j, :], acc[:, n - 3:n]), [ab, b_xrb[j]], [b_rtail])
                fm_block(slot, sb_, 4, xn, xn_bufs, TILES, epi_xr)
                def rg_front(j, pr=pr):
                    cg = pr * 4 + j
                    fs_ = FSET[j % 3]
                    xc, xcb, rr, ii = fs_['xc'], fs_['xcb'], fs_['rr'], fs_['ii']
                    aa, mu, hh_ = BSET['aa'], BSET['mu'], BSET['hh_']
                    bw = fs_['bw'] + BSET['bw']
                    for tap in range(4):
                        dve(lambda e, tap=tap, cg=cg, xc=xc, xcb=xcb, rr=rr, ii=ii, aa=aa, mu=mu, hh_=hh_: e.tensor_scalar(dgr[:, tap, :], idf, prm[:, P_CRW + tap * 8 + cg:P_CRW + tap * 8 + cg + 1], None, op0=ALU.mult),
                            [b_const], [b_dgr])
                    for ti, (t0, n) in enumerate(TILES):
                        b = acc_bank()
                        for tap in range(4):
                            mm(PS[:, b, 0:n], dgr[:, tap, :], xrb[:, j, t0 + tap + 1:t0 + tap + 1 + n], tap == 0, tap == 3,
                               [b_dgr, b_xrb[j]], [pbuf[b]], inc=(tap == 3))
                        act(xc[:, t0:t0 + n], PS[:, b, 0:n], AF.Identity, [pbuf[b], b_const], [bw[0]], bias=prm[:, P_CRB + cg:P_CRB + cg + 1])
                    dve(lambda e, xc=xc, xcb=xcb, rr=rr, ii=ii, aa=aa, mu=mu, hh_=hh_: e.tensor_copy(xcb, xc), [bw[0]], [bw[1]])
                    for (W, dst, db, pb) in ((lwab, rr, bw[2], P_LBA), (lwxb, ii, bw[3], P_LBX)):
                        for ti, (t0, n) in enumerate(TILES):
                            b = acc_bank()
                            mm(PS[:, b, 0:n], W[:, cg, :], xcb[:, t0:t0 + n], True, True, [b_const, bw[1]], [pbuf[b]], inc=True)
                            act(dst[:, t0:t0 + n], PS[:, b, 0:n], AF.Sigmoid, [pbuf[b], b_const], [db], bias=prm[:, pb + cg:pb + cg + 1])
                def rg_back(j, pr=pr):
                    cg = pr * 4 + j
                    fs_ = FSET[j % 3]
                    xc, xcb, rr, ii = fs_['xc'], fs_['xcb'], fs_['rr'], fs_['ii']
                    aa, mu, hh_ = BSET['aa'], BSET['mu'], BSET['hh_']
                    bw = fs_['bw'] + BSET['bw']
                    act(aa, rr, AF.Exp, [bw[2], b_const], [bw[4]], scale=ccol[:, cg:cg + 1])
                    act(mu, rr, AF.Exp, [bw[2], b_const], [bw[5]], scale=ccol2[:, cg:cg + 1])
                    act(mu, mu, AF.Sqrt, [bw[5]], [bw[5]], scale=-1.0, bias=1.0)
                    dve(lambda e, xc=xc, xcb=xcb, rr=rr, ii=ii, aa=aa, mu=mu, hh_=hh_: e.tensor_tensor(ii, ii, xc, ALU.mult), [bw[3], bw[0]], [bw[3]])
                    dve(lambda e, xc=xc, xcb=xcb, rr=rr, ii=ii, aa=aa, mu=mu, hh_=hh_: e.tensor_tensor(mu, mu, ii, ALU.mult), [bw[5], bw[3]], [bw[5]])
                    dve(lambda e, cg=cg, xc=xc, xcb=xcb, rr=rr, ii=ii, aa=aa, mu=mu, hh_=hh_: e.tensor_tensor_scan(hh_, aa, mu, hcar[:, cg:cg + 1], ALU.mult, ALU.add), [bw[4], bw[5], b_hcar], [bw[6]])
                    dve(lambda e, cg=cg, xc=xc, xcb=xcb, rr=rr, ii=ii, aa=aa, mu=mu, hh_=hh_: e.tensor_copy(hcar[:, cg:cg + 1], hh_[:, 1023:1024]), [bw[6], b_hcar], [b_hcar])
                    if main:
                        dve(lambda e, j=j, xc=xc, xcb=xcb, rr=rr, ii=ii, aa=aa, mu=mu, hh_=hh_: e.tensor_tensor(hh_, hh_, gel[:, j, :], ALU.mult), [bw[6]] + b_gel[j], [bw[6]])
                        dve(lambda e, cg=cg, xc=xc, xcb=xcb, rr=rr, ii=ii, aa=aa, mu=mu, hh_=hh_: e.tensor_scalar(yT[:, cg, :], hh_, prm[:, P_GRN + cg:P_GRN + cg + 1], None, op0=ALU.mult),
                            [bw[6], b_const], yTb[cg])
                        dve(lambda e, xc=xc, xcb=xcb, rr=rr, ii=ii, aa=aa, mu=mu, hh_=hh_: e.tensor_tensor(rr, hh_, hh_, ALU.mult), [bw[6], bw[2]], [bw[2]])
                        dve(lambda e, xc=xc, xcb=xcb, rr=rr, ii=ii, aa=aa, mu=mu, hh_=hh_: e.tensor_tensor(ssum, ssum, rr, ALU.add), [bw[2], b_ssum], [b_ssum])
                rg_front(0)
                rg_front(1)
                for j in range(4):
                    if j + 2 < 4:
                        rg_front(j + 2)
                    rg_back(j)
            k.barrier()
            A.release(m2)
            chk(5 if not main else 7)
            if main:
                k.dma("sp", oph_d, hcar, reads=[b_hcar])
                k.dma("sp", oprc_d, rtail, reads=[b_rtail])

        k.barrier()
        A.release(m_mix)
        AM = Arena(ar_t, off_wqb, off_wqb + 4096)
        mkT = AM.alloc([16, 256], BF16)
        b_mkT = [Buf("mkT%d" % c) for c in range(16)]
        mvt = AM.alloc([2, 2048], BF16)
        b_mvt = [[Buf("mvt") for jb in range(4)] for nh in range(2)]
        m4 = A.mark()
        mn = A.alloc([16, 256], BF16)
        mnb = [Buf("mn0"), Buf("mn1")]
        m5 = A.mark()
        stg = [A.alloc([2048], F32) for _ in range(2)]
        scratch = (stg, [Buf("stg0"), Buf("stg1")], [A.alloc([2048], BF16) for _ in range(2)], [Buf("xb0"), Buf("xb1")],
                   [A.alloc([2048], BF16) for _ in range(2)], [Buf("jk0"), Buf("jk1")], [A.alloc([4], F32) for _ in range(2)], [Buf("st0"), Buf("st1")])
        chk(30)
        load_norm(mem_d, 256, P_GMEM, mn, mnb, scratch)
        k.barrier()
        chk(31)
        A.release(m5)
        ost = [A.alloc([4, 256], F32) for _ in range(2)]
        ostb = [Buf("ost0"), Buf("ost1")]
        mn_bufs = lambda t0, n: mnb[t0 // 128:(t0 + n + 127) // 128]
        for jb in range(4):
            slot, sb_ = wnext()
            s2 = jb % 2

            def epi_mk(j, ti, t0, n, acc, ab, jb=jb, s2=s2):
                act(ost[s2][:, j, :], acc, AF.Copy, [ab], [ostb[s2]])
                dve(lambda e: e.tensor_copy(mkT[:, jb * 4 + j, :], ost[s2][:, j, :]), [ostb[s2]], [b_mkT[jb * 4 + j]])
            fm_block(slot, sb_, 4, mn, mn_bufs, [(0, 256)], epi_mk)
            k.dma("sp", omk_d[:, jb * 4:jb * 4 + 4, :], ost[s2], reads=[ostb[s2]])
        chk(32)
        ost2 = [ost[0].rearrange("p a b -> p (a b)")[:, 0:512], ost[1].rearrange("p a b -> p (a b)")[:, 0:512]]
        cnt2 = 0
        for jb in range(4):
            slot, sb_ = wnext()

            def epi_mv(i, acc, ab, jb=jb):
                nonlocal cnt2
                s2 = cnt2 % 2
                cnt2 += 1
                act(ost2[s2], acc, AF.Copy, [ab], [ostb[s2]])
                dve(lambda e: e.tensor_copy(mvt[:, i, jb * 512:(jb + 1) * 512], ost2[s2]), [ostb[s2]], [b_mvt[i][jb]])
                k.dma("sp", omv_d[i * 128:(i + 1) * 128, jb * 512:(jb + 1) * 512], ost2[s2], reads=[ostb[s2]])
            tm_block(slot, sb_, 512, mn, mn_bufs, 2, epi_mv)
        k.barrier()
        A.release(m4)

        chk(8)

        def attn_core(n, heads, kfn, kbuf, vfn, vbuf, qc_, qcb_, oT_, oTb_, ET, b_ET, rden, b_rden):
            for hd in heads:
                for nh in range(2):
                    b = acc_bank()
                    for dc in range(4):
                        c = hd * 4 + dc
                        mm(PS[:, b, 0:n], kfn(hd, dc, nh), qc_[:, c, :], dc == 0, dc == 3,
                           [kbuf(hd, dc), qcb_[c]], [pbuf[b]], inc=(dc == 3))
                    act(ET[:, nh, 0:n], PS[:, b, 0:n], AF.Exp, [pbuf[b]], [b_ET], scale=float(512 ** -0.5))
                b = acc_bank()
                for nh in range(2):
                    mm(PS[:, b, 0:n], onesb, ET[:, nh, 0:n], nh == 0, nh == 1, [b_const, b_ET], [pbuf[b]], inc=(nh == 1))
                dve(lambda e, b=b: e.reciprocal(rden[:, 0:n], PS[:, b, 0:n]), [pbuf[b]], [b_rden])
                for dc in range(4):
                    c = hd * 4 + dc
                    b = acc_bank()
                    for nh in range(2):
                        mm(PS[:, b, 0:n], vfn(hd, dc, nh), ET[:, nh, 0:n], nh == 0, nh == 1,
                           [vbuf(hd, dc, nh), b_ET], [pbuf[b]], inc=(nh == 1))
                    dve(lambda e, b=b, c=c: e.tensor_tensor(oT_[:, c, :], PS[:, b, 0:n], rden[:, 0:n], ALU.mult), [pbuf[b], b_rden], [oTb_[c]])

        def post_tile(tiles, tinfo):
            NT = sum(n_ for _, n_ in tiles)
            nmax = max(n_ for _, n_ in tiles)
            accn["banks"] = [0, 1, 4, 5, 6, 7]
            m_tile = A.mark()
            X = A.alloc([16, NT], F32)
            off_hq = A.top
            hq = A.alloc([16, NT], BF16)
            Xb = [[Buf("X%d_%d" % (c, ti)) for ti in range(len(tiles))] for c in range(16)]
            m3 = A.mark()
            if NT >= 512:
                AH = Arena(ar_t, off_hq, off_hq + 16 * NT // 2)
                stg = [AH.alloc([2048], F32) for _ in range(2)]
            else:
                stg = [A.alloc([2048], F32) for _ in range(2)]
            stgb = [Buf("stg0"), Buf("stg1")]
            rstd = A.alloc([NT], F32)
            b_rstd = Buf("rstd")
            tmp = A.alloc([nmax], F32)
            b_tmp = Buf("tmp")
            gi = 0
            for ti, (t0, n) in enumerate(tiles):
                gs = tinfo[ti]["gs"]
                for i in range(n // gs):
                    s2 = gi % 2
                    gi += 1
                    r0 = t0 + i * gs
                    k.dma("sp", stg[s2][0:gs], tinfo[ti]["xsrc"][i * gs:(i + 1) * gs, :], writes=[stgb[s2]])
                    for q4 in range(4):
                        b = tr_bank()
                        for c in range(4):
                            cc = q4 * 4 + c
                            tr(PS[:, b, c * gs:(c + 1) * gs], stg[s2][0:gs, cc * 128:(cc + 1) * 128], idf[0:gs, 0:gs], [stgb[s2], b_const], [pbuf[b]], inc=(c == 3))
                        act(X[:, q4 * 4:q4 * 4 + 4, r0:r0 + gs], PS[:, b, 0:4 * gs].rearrange("p (c t) -> p c t", c=4), AF.Copy,
                            [pbuf[b]], [Xb[q4 * 4 + c][ti] for c in range(4)])
                b = acc_bank()
                mm(PS[:, b, 0:n], onesf, tinfo[ti]["ssum"], True, True, [b_const, tinfo[ti]["b_ssum"]], [pbuf[b]], inc=True)
                act(rstd[:, t0:t0 + n], PS[:, b, 0:n], AF.Sqrt, [pbuf[b]], [b_rstd], scale=1.0 / 1024, bias=EPS)
            dve(lambda e: e.reciprocal(rstd, rstd), [b_rstd], [b_rstd])
            for jb in range(8):
                slot, sb_ = wnext()
                for j in range(2):
                    m = jb * 2 + j
                    for ti, (t0, n) in enumerate(tiles):
                        b1 = acc_bank()
                        for c in range(8):
                            mm(PS[:, b1, 0:n], slot[:, c, j * 128:(j + 1) * 128], tinfo[ti]["yT"][:, c, :], c == 0, c == 7,
                               [sb_] + tinfo[ti]["yT_rb"], [pbuf[b1]], inc=(c == 7))
                        b2 = acc_bank()
                        for c in range(8, 16):
                            mm(PS[:, b2, 0:n], slot[:, c, j * 128:(j + 1) * 128], tinfo[ti]["yT"][:, c, :], c == 8, c == 15,
                               [sb_] + tinfo[ti]["yT_mb"], [pbuf[b2]], inc=(c == 15))
                        dve(lambda e, b1=b1, t0=t0, n=n: e.tensor_tensor(tmp[:, 0:n], PS[:, b1, 0:n], rstd[:, t0:t0 + n], ALU.mult), [pbuf[b1], b_rstd], [b_tmp])
                        dve(lambda e, m=m, b2=b2, t0=t0, n=n: e.tensor_tensor(X[:, m, t0:t0 + n], X[:, m, t0:t0 + n], PS[:, b2, 0:n], ALU.add), [pbuf[b2], Xb[m][ti]], [Xb[m][ti]])
                        dve(lambda e, m=m, t0=t0, n=n: e.tensor_tensor(X[:, m, t0:t0 + n], X[:, m, t0:t0 + n], tmp[:, 0:n], ALU.add), [b_tmp, Xb[m][ti]], [Xb[m][ti]])
            k.barrier()
            A.release(m3)
            A0.release(R0_LO)

            def rmsnorm_fm(gcol0, out, outb):
                mk_ = A.mark()
                mk0 = A0.mark()
                sq = A0.alloc([16, nmax], BF16)
                rs = A.alloc([nmax], F32)
                b_sq, b_rs = Buf("sq"), Buf("rs")
                for ti, (t0, n) in enumerate(tiles):
                    for c in range(16):
                        act(sq[:, c, 0:n], X[:, c, t0:t0 + n], AF.Square, [Xb[c][ti]], [b_sq])
                    b = acc_bank()
                    for c in range(16):
                        mm(PS[:, b, 0:n], onesb, sq[:, c, 0:n], c == 0, c == 15, [b_const, b_sq], [pbuf[b]], inc=(c == 15))
                    act(rs[:, 0:n], PS[:, b, 0:n], AF.Sqrt, [pbuf[b]], [b_rs], scale=1.0 / 2048, bias=EPS)
                    dve(lambda e, n=n: e.reciprocal(rs[:, 0:n], rs[:, 0:n]), [b_rs], [b_rs])
                    for c in range(16):
                        dve(lambda e, c=c, t0=t0, n=n: e.scalar_tensor_tensor(out[:, c, t0:t0 + n], X[:, c, t0:t0 + n], prm[:, gcol0 + c:gcol0 + c + 1], rs[:, 0:n],
                                                                               op0=ALU.mult, op1=ALU.mult),
                            [Xb[c][ti], b_const, b_rs], [outb[c][ti]])
                k.barrier()
                A.release(mk_)
                A0.release(mk0)

            def tb(bl):
                return lambda t0_, n_: [bl[c][[t for t, _ in tiles].index(t0_)] for c in range(16)]
            chk(9)
            hqb = [[Buf("hq") for _ in tiles] for c in range(16)]
            rmsnorm_fm(P_GXA, hq, hqb)
            m6 = A.mark()
            mk0 = A0.mark()
            qc = A0.alloc([16, NT], BF16)
            qcb = [[Buf("qc") for _ in tiles] for c in range(16)]
            for jb in range(8):
                slot, sb_ = wnext()

                def epi_q(j, ti_, t0_, n_, acc, ab, jb=jb):
                    act(qc[:, jb * 2 + j, t0_:t0_ + n_], acc, AF.Copy, [ab], [qcb[jb * 2 + j][ti_]])
                fm_block(slot, sb_, 2, hq, tb(hqb), tiles, epi_q)
            k.barrier()
            oT = hq
            oTb = [[Buf("oT") for _ in tiles] for c in range(16)]
            for ti, (t0, n) in enumerate(tiles):
                tinfo[ti]["attn"](n, qc[:, :, t0:t0 + n], [qcb[c][ti] for c in range(16)], oT[:, :, t0:t0 + n], [oTb[c][ti] for c in range(16)])
            for jb in range(8):
                slot, sb_ = wnext()

                def epi_co(j, ti_, t0_, n_, acc, ab, jb=jb):
                    m = jb * 2 + j
                    dve(lambda e: e.tensor_tensor(X[:, m, t0_:t0_ + n_], X[:, m, t0_:t0_ + n_], acc, ALU.add), [ab, Xb[m][ti_]], [Xb[m][ti_]])
                fm_block(slot, sb_, 2, oT, tb(oTb), tiles, epi_co)
            k.barrier()
            A.release(m6)
            A0.release(mk0)

            chk(10)
            hn = hq
            hnb = [[Buf("hn") for _ in tiles] for c in range(16)]
            rmsnorm_fm(P_GFFN, hn, hnb)
            m6 = A.mark()
            mk0 = A0.mark()
            hG = A0.alloc([16, NT], BF16)
            hGb = [[Buf("hG") for _ in tiles] for c in range(16)]
            rl = [A.alloc([nmax], F32) for _ in range(2)]
            rlb = [Buf("rl0"), Buf("rl1")]
            cnt3 = [0]
            for g in range(4):
                for jb in range(8):
                    slot, sb_ = wnext()

                    def epi_up(j, ti_, t0_, n_, acc, ab, jb=jb):
                        s2 = cnt3[0] % 2
                        cnt3[0] += 1
                        act(rl[s2][:, 0:n_], acc, AF.Relu, [ab], [rlb[s2]])
                        dve(lambda e: e.tensor_tensor(hG[:, jb * 2 + j, t0_:t0_ + n_], rl[s2][:, 0:n_], rl[s2][:, 0:n_], ALU.mult), [rlb[s2]], [hGb[jb * 2 + j][ti_]])
                    fm_block(slot, sb_, 2, hn, tb(hnb), tiles, epi_up)
                for jb in range(8):
                    slot, sb_ = wnext()

                    def epi_dn(j, ti_, t0_, n_, acc, ab, jb=jb):
                        m = jb * 2 + j
                        dve(lambda e: e.tensor_tensor(X[:, m, t0_:t0_ + n_], X[:, m, t0_:t0_ + n_], acc, ALU.add), [ab, Xb[m][ti_]], [Xb[m][ti_]])
                    fm_block(slot, sb_, 2, hG, tb(hGb), tiles, epi_dn)
            k.barrier()
            A.release(m6)
            A0.release(mk0)

            chk(11)
            m7 = A.mark()
            mk0 = A0.mark()
            sq = hq
            rs = A.alloc([nmax], F32)
            yn = [A0.alloc([16, 128], F32) for _ in range(2)]
            ob = [A0.alloc([2048], F32) for _ in range(2)]
            b_sq, b_rs = Buf("sq"), Buf("rs")
            b_yn = [Buf("yn0"), Buf("yn1")]
            obb = [Buf("ob0"), Buf("ob1")]
            gi = 0
            for ti, (t0, n) in enumerate(tiles):
                for c in range(16):
                    act(sq[:, c, t0:t0 + n], X[:, c, t0:t0 + n], AF.Square, [Xb[c][ti]], [b_sq])
                b = acc_bank()
                for c in range(16):
                    mm(PS[:, b, 0:n], onesb, sq[:, c, t0:t0 + n], c == 0, c == 15, [b_const, b_sq], [pbuf[b]], inc=(c == 15))
                act(rs[:, 0:n], PS[:, b, 0:n], AF.Sqrt, [pbuf[b]], [b_rs], scale=1.0 / 2048, bias=EPS)
                dve(lambda e, n=n: e.reciprocal(rs[:, 0:n], rs[:, 0:n]), [b_rs], [b_rs])
                gs = tinfo[ti]["gs"]
                for i in range(n // gs):
                    s2 = gi % 2
                    gi += 1
                    r0 = t0 + i * gs
                    for c in range(16):
                        dve(lambda e, c=c, i=i, s2=s2, r0=r0, gs=gs: e.scalar_tensor_tensor(yn[s2][:, c, 0:gs], X[:, c, r0:r0 + gs], prm[:, P_GFIN + c:P_GFIN + c + 1],
                                                                                             rs[:, i * gs:(i + 1) * gs], op0=ALU.mult, op1=ALU.mult),
                            [Xb[c][ti], b_const, b_rs], [b_yn[s2]])
                    for q4 in range(4):
                        b = tr_bank()
                        for c in range(4):
                            cc = q4 * 4 + c
                            tr(PS[0:gs, b, c * 128:(c + 1) * 128], yn[s2][:, cc, 0:gs], idf, [b_yn[s2], b_const], [pbuf[b]], inc=(c == 3))
                        act(ob[s2][0:gs, q4 * 512:(q4 + 1) * 512], PS[0:gs, b, :], AF.Copy, [pbuf[b]], [obb[s2]])
                    k.dma("sp", tinfo[ti]["ydst"][i * gs:(i + 1) * gs, :], ob[s2][0:gs], reads=[obb[s2]])
            k.barrier()
            A.release(m_tile)
            A0.release(mk0)
            accn["banks"] = [0, 1]

        def attn_prompt(n_, qc_, qcb_, oT_, oTb_):
            mk_ = A.mark()
            ET = A.alloc([2, 512], BF16)
            rden = A.alloc([512], F32)
            attn_core(n_, range(4),
                      lambda hd, dc, nh: mkT[:, hd * 4 + dc, nh * 128:(nh + 1) * 128], lambda hd, dc: b_mkT[hd * 4 + dc],
                      lambda hd, dc, nh: mvt[:, nh, (hd * 4 + dc) * 128:(hd * 4 + dc + 1) * 128], lambda hd, dc, nh: b_mvt[nh][hd],
                      qc_, qcb_, oT_, oTb_, ET, Buf("ET"), rden, Buf("rden"))
            k.barrier()
            A.release(mk_)
        def attn_sample(n_, qc_, qcb_, oT_, oTb_):
            mk_ = A.mark()
            Ks = [A.alloc([2, 512], BF16) for _ in range(2)]
            mkTs = [A.alloc([4, 256], BF16) for _ in range(2)]
            mvts = [A.alloc([2, 512], BF16) for _ in range(2)]
            ET = [A.alloc([2, 16], BF16) for _ in range(2)]
            rden = [A.alloc([16], F32) for _ in range(2)]
            b_Ks = [Buf("Ks0"), Buf("Ks1")]
            b_mk1 = [Buf("mk0"), Buf("mk1")]
            b_mv1 = [Buf("mv0"), Buf("mv1")]
            b_ET = [Buf("ET0"), Buf("ET1")]
            b_rden = [Buf("rd0"), Buf("rd1")]
            its = [(tok, hd) for tok in range(NS) for hd in range(4)]

            def dma_k(it):
                tok, hd = its[it]
                s2 = it % 2
                k.dma("pool", Ks[s2], ck_d[tok][:, hd * 512:(hd + 1) * 512].rearrange("(nh p) d -> p nh d", p=128), writes=[b_Ks[s2]])

            def dma_v(it):
                tok, hd = its[it]
                s2 = it % 2
                k.dma("pool", mvts[s2], cv_d[tok][:, hd * 512:(hd + 1) * 512].rearrange("(nh p) d -> p nh d", p=128), writes=[b_mv1[s2]])

            def st_a(it):
                tok, hd = its[it]
                s2 = it % 2
                b = tr_bank()
                pv = psb(b)
                for dc in range(4):
                    for nh in range(2):
                        tr(pv[:, (dc * 2 + nh) * 128:(dc * 2 + nh + 1) * 128], Ks[s2][:, nh, dc * 128:(dc + 1) * 128], idb, [b_Ks[s2], b_const], [pbuf[b]],
                           inc=(dc == 3 and nh == 1))
                if it % 2 == 0:
                    act(mkTs[s2], pv.rearrange("p (c n) -> p c n", c=4), AF.Copy, [pbuf[b]], [b_mk1[s2]])
                else:
                    dv(lambda e, pv=pv, s2=s2: e.tensor_copy(mkTs[s2], pv.rearrange("p (c n) -> p c n", c=4)), [pbuf[b]], [b_mk1[s2]])

            def st_b(it):
                tok, hd = its[it]
                s2 = it % 2
                attn_core(1, [hd],
                          lambda hd_, dc, nh: mkTs[s2][:, dc, nh * 128:(nh + 1) * 128], lambda hd_, dc: b_mk1[s2],
                          lambda hd_, dc, nh: mvts[s2][:, nh, dc * 128:(dc + 1) * 128], lambda hd_, dc, nh: b_mv1[s2],
                          qc_[:, :, tok:tok + 1], qcb_, oT_[:, :, tok:tok + 1], oTb_, ET[s2], b_ET[s2], rden[s2], b_rden[s2])
            dma_k(0)
            dma_k(1)
            dma_v(0)
            st_a(0)
            for it in range(len(its)):
                if it + 2 < len(its):
                    dma_k(it + 2)
                if it + 1 < len(its):
                    dma_v(it + 1)
                    st_a(it + 1)
                st_b(it)
            k.barrier()
            A.release(mk_)
        tinfo = [dict(xsrc=xm_d[t0:t0 + n], yT=yT[:, :, t0:t0 + n], yT_rb=[yTb[cc][ti] for cc in range(8)], yT_mb=[yTb[cc][ti] for cc in range(8, 16)],
                      ssum=ssum[:, t0:t0 + n], b_ssum=b_ssum, attn=attn_prompt, ydst=y_d[t0:t0 + n], gs=128) for ti, (t0, n) in enumerate(TILES)]
        tinfo.append(dict(xsrc=xs_d, yT=yTs, yT_rb=[b_yTs_r], yT_mb=[b_yTs_m], ssum=ssum_s, b_ssum=b_ssum_s, attn=attn_sample, ydst=ys_d, gs=NS))
        post_tile(TILES + [(1024, NS)], tinfo)
        assert k.dead or wstate["used"] == len(wsched), (wstate, len(wsched))
        k.finish()
        k.emit()
        print("instructions:", k.nins, "arena peak", A.peak, "of", NW, "A0 peak", A0.peak, "of", R0_HI)
    return nc


_CACHE = {}


def _consts():
    ident = np.eye(128, dtype=np.float32)
    s = np.arange(128)[:, None]
    t = np.arange(128)[None, :]
    maskneg = np.where(s <= t, 0.0, -30000.0).astype(np.float32)
    sel = np.zeros((4, 4, 128), np.float32)
    for h in range(4):
        sel[h, h, :] = 1.0
    return ident, maskneg, sel.reshape(4, 512)


def kernel(**inp):
    f = lambda a: np.ascontiguousarray(np.asarray(a, dtype=np.float32))
    if "nc" not in _CACHE:
        _CACHE["nc"] = build_program()
    nc = _CACHE["nc"]
    ident, maskneg, sel = _consts()
    prm = np.zeros((128, NPRM), np.float32)

    def colmajor(v, nch):
        return np.asarray(v, np.float32).reshape(nch, 128).T
    prm[:, P_GMIX:P_GMIX + 16] = colmajor(inp["g_mix"][0], 16)
    prm[:, P_GXA:P_GXA + 16] = colmajor(inp["g_xattn"][0], 16)
    prm[:, P_GMEM:P_GMEM + 16] = colmajor(inp["g_mem"][0], 16)
    prm[:, P_GFFN:P_GFFN + 16] = colmajor(inp["g_ffn"][0], 16)
    prm[:, P_GFIN:P_GFIN + 16] = colmajor(inp["g_final"], 16)
    for tap in range(4):
        prm[:, P_CRW + tap * 8:P_CRW + tap * 8 + 8] = colmajor(inp["conv_rnn_w"][0, tap], 8)
        prm[:, P_CMW + tap * 8:P_CMW + tap * 8 + 8] = colmajor(inp["conv_ml_w"][0, tap], 8)
    prm[:, P_CRB:P_CRB + 8] = colmajor(inp["conv_rnn_b"][0], 8)
    prm[:, P_CMB:P_CMB + 8] = colmajor(inp["conv_ml_b"][0], 8)
    prm[:, P_LBA:P_LBA + 8] = np.asarray(inp["lru_ba"][0], np.float32).T
    prm[:, P_LBX:P_LBX + 8] = np.asarray(inp["lru_bx"][0], np.float32).T
    prm[:, P_LAM:P_LAM + 8] = colmajor(inp["lru_lambda"][0], 8)
    prm[:, P_GRN:P_GRN + 8] = colmajor(inp["g_rnn_out"][0], 8)
    prm[:, P_GML2:P_GML2 + 2] = colmajor(inp["g_ml_out"][0], 2)
    prm[0:4, P_BI] = np.asarray(inp["ml_bi"][0], np.float32)
    prm[0:4, P_BF] = np.asarray(inp["ml_bf"][0], np.float32)
    gmlrep = np.ascontiguousarray(np.broadcast_to(np.asarray(inp["g_ml_out"][0], np.float32)[None, :], (128, 256)))
    shared = dict(
        prm=prm, gmlrep=gmlrep, ident=ident, maskneg=maskneg, sel=sel,
        w_in=f(inp["w_in"][0]), lru_wa=f(inp["lru_wa"][0]), lru_wx=f(inp["lru_wx"][0]),
        ml_wq=f(inp["ml_wq"][0]), ml_wk=f(inp["ml_wk"][0]), w_out=f(inp["w_out"][0]),
        w_cq=f(inp["w_cq"][0]), w_mk=f(inp["w_mk"][0]), w_mv=f(inp["w_mv"][0]), w_co=f(inp["w_co"][0]),
        w_up=f(inp["w_up"][0]), w_down=f(inp["w_down"][0]),
    )
    shared["cmw_rep"] = np.ascontiguousarray(np.broadcast_to(np.asarray(inp["conv_ml_w"][0], np.float32)[None], (16, 4, 1024)))
    st_ = np.zeros((16, 16, 128), np.float32)
    for t_ in range(16):
        st_[t_, t_, :] = 1.0
    shared["seltok"] = st_.reshape(16, 2048)
    shared["cmb_rep"] = np.ascontiguousarray(np.broadcast_to(np.asarray(inp["conv_ml_b"][0], np.float32)[None], (16, 1024)))
    shared["gb_rep"] = np.ascontiguousarray(np.broadcast_to(
        np.concatenate([np.asarray(inp["ml_bi"][0], np.float32), np.asarray(inp["ml_bf"][0], np.float32)])[None], (16, 8)))
    xsm = np.asarray(inp["x_sample"], np.float32)
    xpr = np.asarray(inp["x_prompt"], np.float32)
    memp = np.asarray(inp["mem_prompt"], np.float32)
    in_maps = []
    for c in range(8):
        b, hf = c // 2, c % 2
        d = dict(shared)
        d["xm"] = np.ascontiguousarray(xpr[b, hf * 1024:(hf + 1) * 1024])
        d["xp"] = np.ascontiguousarray(xpr[b, 0:1024])
        d["mem"] = np.ascontiguousarray(memp[b])
        d["mask"] = np.full((128, 1), float(hf), np.float32)
        sl = slice(c * 16, (c + 1) * 16)
        d["xs"] = np.ascontiguousarray(xsm[sl, 0])
        d["s_h"] = f(inp["state_rglru_h"][0, sl])
        d["s_rc"] = f(inp["state_rglru_conv"][0, sl])
        d["s_C"] = f(inp["state_mlstm_C"][0, sl])
        d["s_n"] = f(inp["state_mlstm_n"][0, sl])
        d["s_m"] = f(inp["state_mlstm_m"][0, sl])
        d["s_mc"] = f(inp["state_mlstm_conv"][0, sl])
        d["ck"] = f(inp["cache_mem_k"][0, sl]).reshape(16, 256, 2048)
        d["cv"] = f(inp["cache_mem_v"][0, sl]).reshape(16, 256, 2048)
        in_maps.append(d)
    res = run_bass_kernel_spmd(nc, in_maps, core_ids=list(range(8)))
    R = res.results
    B = 4
    y_prompt = np.zeros((B, 2048, 2048), np.float32)
    p_h = np.zeros((1, B, 1024), np.float32)
    p_rc = np.zeros((1, B, 3, 1024), np.float32)
    p_C = np.zeros((1, B, 4, 256, 256), np.float32)
    p_n = np.zeros((1, B, 4, 256), np.float32)
    p_m = np.zeros((1, B, 4), np.float32)
    p_mc = np.zeros((1, B, 3, 1024), np.float32)
    p_mk = np.zeros((1, B, 256, 4, 512), np.float32)
    p_mv = np.zeros((1, B, 256, 4, 512), np.float32)
    for c in range(8):
        b, hf = c // 2, c % 2
        r = R[c]
        y_prompt[b, hf * 1024:(hf + 1) * 1024] = r["o_y"]
        if hf == 1:
            p_h[0, b] = r["o_ph"].T.reshape(1024)
            p_rc[0, b] = r["o_prc"].transpose(2, 1, 0).reshape(3, 1024)
            p_mc[0, b] = r["o_pmc"].transpose(2, 1, 0).reshape(3, 1024)
            oc = r["o_pC"]
            p_C[0, b] = oc[:, :, :, 0:256].transpose(1, 3, 2, 0).reshape(4, 256, 256)
            p_n[0, b] = oc[:, :, :, 256].transpose(1, 2, 0).reshape(4, 256)
            p_m[0, b] = r["o_pm"].reshape(4)
            p_mk[0, b] = r["o_mkT"].transpose(2, 1, 0).reshape(256, 4, 512)
            p_mv[0, b] = r["o_mv"].reshape(256, 4, 512)
    y_s = np.zeros((128, 1, 2048), np.float32)
    s_h = np.zeros((1, 128, 1024), np.float32)
    s_rc = np.zeros((1, 128, 3, 1024), np.float32)
    s_C = np.zeros((1, 128, 4, 256, 256), np.float32)
    s_n = np.zeros((1, 128, 4, 256), np.float32)
    s_m = np.zeros((1, 128, 4), np.float32)
    s_mc = np.zeros((1, 128, 3, 1024), np.float32)
    for c in range(8):
        r = R[c]
        sl = slice(c * 16, (c + 1) * 16)
        y_s[sl, 0] = r["o_ys"]
        s_h[0, sl] = r["o_sh"].transpose(2, 1, 0).reshape(16, 1024)
        s_rc[0, sl] = r["o_src"].transpose(3, 2, 1, 0).reshape(16, 3, 1024)
        s_C[0, sl] = r["o_sC"]
        s_n[0, sl] = r["o_sn"]
        s_m[0, sl] = r["o_sm"]
        s_mc[0, sl] = r["o_smc"]
    return (y_prompt, y_s, p_h, p_rc, p_C, p_n, p_m, p_mc, p_mk, p_mv, s_h, s_rc, s_C, s_n, s_m, s_mc)
```

```python
import numpy as np
from contextlib import ExitStack
import concourse.bass as bass
import concourse.mybir as mybir
from concourse.bass_utils import run_bass_kernel_spmd

F32 = mybir.dt.float32
BF16 = mybir.dt.bfloat16
AF = mybir.ActivationFunctionType
ALU = mybir.AluOpType
AX = mybir.AxisListType

SAME_ENGINE_WAIT = True
EPS = 1e-6
NSLOT = 2

P_GMIX, P_GXA, P_GMEM, P_GFFN, P_GFIN = 0, 16, 32, 48, 64
P_CRW, P_CRB, P_LBA, P_LBX, P_LAM, P_GRN = 80, 112, 120, 128, 136, 144
P_CMW, P_CMB, P_GML2, P_BI, P_BF = 152, 184, 192, 194, 195
NPRM = 196


import os
STOP = int(os.environ.get("KSTOP", "0"))
DEBUG_SITES = bool(int(os.environ.get("KSITES", "0")))
DBG2 = int(os.environ.get("DBG2", "0"))
DBG3 = int(os.environ.get("DBG3", "0"))


class _Stop(Exception):
    pass


class Buf:
    __slots__ = ("name", "w", "r", "dsem", "dcount")

    def __init__(self, name="b"):
        self.name = name
        self.w = None
        self.r = {}
        self.dsem = None
        self.dcount = 0


class K:
    ENG = ("pe", "act", "dve", "pool", "sp")

    def __init__(self, nc, es):
        self.nc = nc
        self.es = es
        self.ops = {e: [] for e in self.ENG}
        self.sem = {e: es.enter_context(nc.semaphore("s_" + e)) for e in self.ENG}
        self.cnt = {e: 0 for e in self.ENG}
        self.known = {e: {} for e in self.ENG}
        self.semobj = {e: self.sem[e] for e in self.ENG}
        self.nd = 0
        self.dbufs = []
        self.nins = 0
        self.dead = False

    def _need(self, eng, reads, writes):
        need = {}

        def add(k, v):
            if need.get(k, 0) < v:
                need[k] = v
        for b in reads:
            if b.w:
                add(*b.w)
        for b in writes:
            if b.w:
                add(*b.w)
            for k, v in b.r.items():
                add(k, v)
        waits = []
        for k, v in need.items():
            if k == eng and (eng == "pe" or not SAME_ENGINE_WAIT):
                continue
            if self.known[eng].get(k, 0) >= v:
                continue
            self.known[eng][k] = v
            waits.append((self.semobj[k], v))
        return waits

    def op(self, eng, fn, reads=(), writes=(), inc=True):
        if self.dead:
            return
        waits = self._need(eng, reads, writes)
        val = self.cnt[eng] + 1
        if inc:
            self.cnt[eng] = val
        for b in reads:
            if b.r.get(eng, 0) < val:
                b.r[eng] = val
        for b in writes:
            b.w = (eng, val)
            b.r = {}
        sem = self.sem[eng]
        self.nins += 1
        if getattr(self, "trace", False):
            print("TRACE", eng, "val", val, "inc", inc, "waits", [(str(s_), v_) for s_, v_ in waits], "reads", [(b.name, b.w) for b in reads], "writes", [b.name for b in writes])
        import sys as _sys
        fr = _sys._getframe(1)
        site = []
        while fr is not None and len(site) < 3:
            site.append(fr.f_lineno)
            fr = fr.f_back
        site = "SITE" + "_".join(map(str, site)) + "_" + getattr(self, "tag", "")

        def run(e, waits=waits, fn=fn, inc=inc, sem=sem, site=site):
            for s, v in waits:
                e.wait_ge(s, v)
            ins = fn(e)
            if DEBUG_SITES:
                ins.annotate(site)
            if inc:
                ins.then_inc(sem, 1)
        self.ops[eng].append(run)

    def _dsem(self, b):
        if b.dsem is None:
            key = "d%d" % self.nd
            self.nd += 1
            b.dsem = key
            self.semobj[key] = self.es.enter_context(self.nc.semaphore(key))
            self.dbufs.append(b)
        return b.dsem

    def dma(self, q, out, in_, reads=(), writes=(), **kw):
        if self.dead:
            return
        waits = self._need(q, reads, writes)
        bl = list(reads) + list(writes)
        assert len(bl) == 1
        b = bl[0]
        kk = self._dsem(b)
        b.dcount += 16
        v = b.dcount
        if reads:
            b.r[kk] = v
        else:
            b.w = (kk, v)
            b.r = {}
        s = self.semobj[kk]
        self.nins += 1

        def run(e, waits=waits, s=s, out=out, in_=in_, kw=kw):
            for ws, wv in waits:
                e.wait_ge(ws, wv)
            e.dma_start(out=out, in_=in_, **kw).then_inc(s, 16)
        self.ops[q].append(run)

    def barrier(self):
        if self.dead:
            return
        tgt = [(e, self.cnt[e]) for e in self.ENG if self.cnt[e] > 0]
        tgt += [(b.dsem, b.dcount) for b in self.dbufs]
        for eng in self.ENG:
            waits = []
            for kk, v in tgt:
                if kk == eng:
                    continue
                if self.known[eng].get(kk, 0) >= v:
                    continue
                self.known[eng][kk] = v
                waits.append((self.semobj[kk], v))

            def run(e, waits=waits):
                for s, v in waits:
                    e.wait_ge(s, v)
            if waits:
                self.ops[eng].append(run)

    def finish(self):
        self.barrier()

    def emit(self):
        nc = self.nc
        with nc.Block() as block:
            @block.tensor
            def _(e):
                for f in self.ops["pe"]:
                    f(e)

            @block.scalar
            def _(e):
                for f in self.ops["act"]:
                    f(e)

            @block.vector
            def _(e):
                for f in self.ops["dve"]:
                    f(e)

            @block.gpsimd
            def _(e):
                for f in self.ops["pool"]:
                    f(e)

            @block.sync
            def _(e):
                for f in self.ops["sp"]:
                    f(e)


class Arena:
    def __init__(self, ap, lo, hi):
        self.ap = ap
        self.n = hi
        self.top = lo

    def alloc(self, shape, dt, parts=128):
        n = 1
        for s in shape:
            n *= s
        esz = 4 if dt == F32 else 2
        words = (n * esz + 3) // 4
        words = (words + 15) // 16 * 16
        off = self.top
        self.top += words
        assert self.top <= self.n, "arena overflow %d > %d" % (self.top, self.n)
        self.peak = max(getattr(self, "peak", 0), self.top)
        v = self.ap[:, off:off + words]
        if dt != F32:
            v = v.bitcast(dt)
        v = v[:, 0:n]
        if len(shape) == 2:
            v = v.rearrange("p (a b) -> p a b", a=shape[0])
        elif len(shape) == 3:
            v = v.rearrange("p (a b c) -> p a b c", a=shape[0], b=shape[1])
        elif len(shape) == 4:
            v = v.rearrange("p (a b c d) -> p a b c d", a=shape[0], b=shape[1], c=shape[2])
        if parts != 128:
            v = v[0:parts]
        return v

    def mark(self):
        return self.top

    def release(self, m):
        self.top = m


def build_program():
    nc = bass.Bass("TRN2", target_bir_lowering=False)

    def DI(name, shape):
        return nc.dram_tensor(name, list(shape), F32, kind="ExternalInput").ap()

    def DO(name, shape):
        return nc.dram_tensor(name, list(shape), F32, kind="ExternalOutput").ap()

    xm_d = DI("xm", [1024, 2048])
    xp_d = DI("xp", [1024, 2048])
    mem_d = DI("mem", [256, 2048])
    mask_d = DI("mask", [128, 1])
    prm_d = DI("prm", [128, NPRM])
    gml_d = DI("gmlrep", [128, 256])
    id_d = DI("ident", [128, 128])
    mneg_d = DI("maskneg", [128, 128])
    sel_d = DI("sel", [4, 4 * 128])
    w_in_d = DI("w_in", [2048, 5128])
    lwa_d = DI("lru_wa", [8, 128, 128])
    lwx_d = DI("lru_wx", [8, 128, 128])
    wq_d = DI("ml_wq", [4, 256, 256])
    wk_d = DI("ml_wk", [4, 256, 256])
    w_out_d = DI("w_out", [2048, 2048])
    w_cq_d = DI("w_cq", [2048, 2048])
    w_mk_d = DI("w_mk", [2048, 2048])
    w_mv_d = DI("w_mv", [2048, 2048])
    w_co_d = DI("w_co", [2048, 2048])
    w_up_d = DI("w_up", [2048, 8192])
    w_dn_d = DI("w_down", [8192, 2048])

    xs_d = DI("xs", [16, 2048])
    sh_d = DI("s_h", [16, 1024])
    src_d = DI("s_rc", [16, 3, 1024])
    sC_d = DI("s_C", [16, 4, 256, 256])
    sn_d = DI("s_n", [16, 4, 256])
    sm_d = DI("s_m", [16, 4])
    smc_d = DI("s_mc", [16, 3, 1024])
    ck_d = DI("ck", [16, 256, 2048])
    cv_d = DI("cv", [16, 256, 2048])
    seltok_d = DI("seltok", [16, 16 * 128])
    cmw_d = DI("cmw_rep", [16, 4, 1024])
    cmb_d = DI("cmb_rep", [16, 1024])
    gb_d = DI("gb_rep", [16, 8])
    ys_d = DO("o_ys", [16, 2048])
    osh_d = DO("o_sh", [128, 8, 16])
    osrc_d = DO("o_src", [128, 8, 3, 16])
    osC_d = DO("o_sC", [16, 4, 256, 256])
    osn_d = DO("o_sn", [16, 4, 256])
    osm_d = DO("o_sm", [16, 4])
    osmc_d = DO("o_smc", [16, 3, 1024])
    y_d = DO("o_y", [1024, 2048])
    oph_d = DO("o_ph", [128, 8])
    oprc_d = DO("o_prc", [128, 8, 3])
    opmc_d = DO("o_pmc", [128, 8, 3])
    opC_d = DO("o_pC", [128, 4, 2, 257])
    opm_d = DO("o_pm", [4, 1])
    omk_d = DO("o_mkT", [128, 16, 256])
    omv_d = DO("o_mv", [256, 2048])

    with ExitStack() as es:
        k = K(nc, es)
        NW = 52992
        ar_t = es.enter_context(nc.sbuf_tensor("arena", [128, NW], F32))
        A = Arena(ar_t, 0, NW)
        PS = es.enter_context(nc.psum_tensor("ps", [128, 8, 512], F32))

        def psb(b):
            return PS[:, b, :].bitcast(BF16)
        pbuf = [Buf("ps%d" % i) for i in range(8)]

        wslot = [A.alloc([16, 512], BF16) for _ in range(NSLOT)]
        wsb = [Buf("ws%d" % i) for i in range(NSLOT)]
        idf = A.alloc([128], F32)[:, :]
        idb = A.alloc([128], BF16)
        onesb = A.alloc([128], BF16)
        onesf = A.alloc([128], F32)
        mneg = A.alloc([128], F32)
        m01 = A.alloc([128], BF16)
        prm = A.alloc([NPRM], F32)
        gml = A.alloc([256], F32)
        maskc = A.alloc([1], F32)
        sel = A.alloc([4 * 128], F32, parts=4)
        off_wqb = A.top
        wqb = A.alloc([4, 2, 256], BF16)
        wkb = A.alloc([4, 2, 256], BF16)
        lwab = A.alloc([8, 128], BF16)
        lwxb = A.alloc([8, 128], BF16)
        C32 = A.alloc([4, 2, 257], F32)
        Cb = A.alloc([2, 257], BF16)
        hcar = A.alloc([8], F32)
        rtail = A.alloc([8, 3], F32)
        mtail = A.alloc([8, 3], F32)
        ccol = A.alloc([8], F32)
        ccol2 = A.alloc([8], F32)
        negbf = A.alloc([1], F32, parts=4)
        st0 = A.alloc([1], F32)
        gcar = A.alloc([4], F32, parts=4)
        b_const = Buf("const")
        yTs = A.alloc([16, 16], BF16)
        ssum_s = A.alloc([16], F32)
        PBASE = A.top
        R0_LO, R0_HI = PBASE, PBASE + 9216
        A0 = Arena(ar_t, R0_LO, R0_HI)
        A = Arena(ar_t, R0_HI, NW)
        print("persistent words", PBASE, "R12 words", NW - R0_HI)
        b_C32, b_Cb, b_hcar, b_rtail, b_mtail, b_gcar = [Buf(n) for n in "C32 Cb hcar rtail mtail gcar".split()]

        def act(out, in_, func, reads, writes, **kw):
            k.op("act", lambda e: e.activation(out, in_, func, **kw), reads=reads, writes=writes)

        def dve(fn, reads, writes):
            k.op("dve", fn, reads=reads, writes=writes)

        def mm(out, lhsT, rhs, start, stop, reads, writes, inc):
            k.op("pe", lambda e: e.matmul(out, lhsT, rhs, start=start, stop=stop), reads=reads, writes=writes, inc=inc)

        def tr(out, in_, ident, reads, writes, inc):
            k.op("pe", lambda e: e.transpose(out, in_, ident), reads=reads, writes=writes, inc=inc)

        def chk(n):
            if STOP == n:
                k.finish()
                k.dead = True
        for dst, src in ((idf, id_d), (mneg, mneg_d), (prm, prm_d), (gml, gml_d), (maskc, mask_d), (sel, sel_d)):
            k.dma("sp", dst, src, writes=[b_const])
        b_cw = Buf("constw")
        k.dma("pool", wqb, wq_d.rearrange("h (c p) n -> p h c n", p=128), writes=[b_cw])
        k.dma("pool", wkb, wk_d.rearrange("h (c p) n -> p h c n", p=128), writes=[b_cw])
        k.dma("pool", lwab, lwa_d.rearrange("h p n -> p h n"), writes=[b_cw])
        k.dma("pool", lwxb, lwx_d.rearrange("h p n -> p h n"), writes=[b_cw])
        k.op("dve", lambda e: e.memset(st0, 0.0), reads=[b_cw, b_const], writes=[b_const])
        dve(lambda e: e.tensor_copy(idb, idf), [b_const], [b_const])
        dve(lambda e: e.memset(onesb, 1.0), [], [b_const])
        dve(lambda e: e.tensor_scalar(m01, mneg, 0.0, None, op0=ALU.is_equal), [b_const], [b_const])
        dve(lambda e: e.memset(onesf, 1.0), [], [b_const])
        dve(lambda e: e.memset(C32, 0.0), [], [b_C32])
        dve(lambda e: e.memset(hcar, 0.0), [], [b_hcar])
        dve(lambda e: e.memset(gcar, 0.0), [], [b_gcar])
        dve(lambda e: e.memset(rtail, 0.0), [], [b_rtail])
        dve(lambda e: e.memset(mtail, 0.0), [], [b_mtail])
        act(ccol, prm[:, P_LAM:P_LAM + 8], AF.Exp, [b_const], [b_const], scale=-1.0)
        act(ccol, ccol, AF.Ln, [b_const], [b_const], bias=1.0)
        dve(lambda e: e.tensor_scalar(ccol2, ccol, -16.0, None, op0=ALU.mult), [b_const], [b_const])
        dve(lambda e: e.tensor_scalar(ccol, ccol, -8.0, None, op0=ALU.mult), [b_const], [b_const])
        dve(lambda e: e.tensor_scalar(negbf, prm[0:4, P_BF:P_BF + 1], -1.0, None, op0=ALU.mult), [b_const], [b_const])

        chk(1)
        wsched = []
        wstate = {"issued": 0, "used": 0, "cnt": [0, 0]}
        wassign = {}
        wflat = [w_.rearrange("p a b -> p (a b)") for w_ in wslot]
        hslot = [wflat[kk // 2][:, (kk % 2) * 4096:(kk % 2 + 1) * 4096].rearrange("p (a b) -> p a b", a=16) for kk in range(4)]
        hsb = [Buf("hs%d" % i) for i in range(4)]

        def wplan(ap, half=False):
            wsched.append((ap, half))

        def wplan256(wd, r0, c0):
            for hh_ in range(2):
                wplan(wd[r0:r0 + 2048, c0 + hh_ * 256:c0 + (hh_ + 1) * 256], True)

        def wissue():
            i = wstate["issued"]
            ap, half = wsched[i]
            nco = ap.shape[1]
            md = 1 if half else 0
            cidx = wstate["cnt"][md]
            wstate["cnt"][md] += 1
            if half:
                sl_, bf_ = hslot[cidx % 4], hsb[cidx % 4]
            else:
                sl_, bf_ = wslot[cidx % 2], wsb[cidx % 2]
            k.dma("pool", sl_[:, :, 0:nco], ap.rearrange("(c p) n -> p c n", p=128), writes=[bf_])
            wassign[i] = (sl_, bf_)
            wstate["issued"] = i + 1

        def wnext():
            i = wstate["used"]
            half = wsched[i][1]
            if wstate["issued"] <= i:
                if i > 0 and wsched[i - 1][1] != half:
                    k.barrier()
                wissue()
            depth = 4 if half else NSLOT
            while wstate["issued"] < min(len(wsched), i + depth) and wsched[wstate["issued"]][1] == half:
                wissue()
            wstate["used"] = i + 1
            return wassign.pop(i)

        def cols(wd, r0, c0, n):
            return wd[r0:r0 + 2048, c0:c0 + n]

        wplan(cols(w_in_d, 0, 5120, 8))
        for c0 in (3072, 3584, 4096, 4608, 2048, 2560, 1024, 1536, 0, 512):
            wplan(cols(w_in_d, 0, c0, 512))
        for ps_ in range(2):
            wplan(cols(w_in_d, 0, 5120, 8))
            for pr in range(2):
                wplan(cols(w_in_d, 0, 3072 + pr * 512, 512))
                if ps_ == 1:
                    wplan(cols(w_in_d, 0, 4096 + pr * 512, 512))
                wplan(cols(w_in_d, 0, 2048 + pr * 512, 512))
            for pr in range(2):
                if ps_ == 1:
                    wplan(cols(w_in_d, 0, 1024 + pr * 512, 512))
                wplan(cols(w_in_d, 0, 0 + pr * 512, 512))
        for j in range(4):
            wplan(cols(w_mk_d, 0, j * 512, 512))
        for j in range(4):
            wplan(cols(w_mv_d, 0, j * 512, 512))
        def plan_post():
            for j in range(4):
                wplan256(w_out_d, 0, j * 512)
            for j in range(4):
                wplan256(w_cq_d, 0, j * 512)
            for j in range(4):
                wplan256(w_co_d, 0, j * 512)
            for g in range(4):
                for j in range(4):
                    wplan256(w_up_d, 0, g * 2048 + j * 512)
                for j in range(4):
                    wplan256(w_dn_d, g * 2048, j * 512)
        plan_post()

        accn = {"i": 0, "banks": [0, 1]}

        def acc_bank():
            bl = accn["banks"]
            b = bl[accn["i"] % len(bl)]
            accn["i"] += 1
            return b

        trn = {"i": 0}

        def tr_bank():
            b = 2 + trn["i"] % 2
            trn["i"] += 1
            return b

        def load_norm(src, T, gcol0, xn, xnb, scratch):
            stg, stgb, xb2, xbb2, junk2, junkb2, st2, stb2 = scratch
            ng = T // 128

            def stage_a(i):
                s2 = i % 2
                junk, junkb, st, stb = junk2[s2], junkb2[s2], st2[s2], stb2[s2]
                k.dma("sp", stg[s2], src[i * 128:(i + 1) * 128, :], writes=[stgb[s2]])
                act(junk, stg[s2], AF.Square, [stgb[s2]], [junkb, stb], accum_out=st[:, 0:1])
                dve(lambda e, st=st: e.tensor_scalar(st[:, 1:2], st[:, 0:1], 1.0 / 2048, EPS, op0=ALU.mult, op1=ALU.add), [stb], [stb])
                act(st[:, 2:3], st[:, 1:2], AF.Sqrt, [stb], [stb])
                dve(lambda e, st=st: e.reciprocal(st[:, 3:4], st[:, 2:3]), [stb], [stb])

            def stage_b(i):
                s2 = i % 2
                xb, xbb, st, stb = xb2[s2], xbb2[s2], st2[s2], stb2[s2]
                dve(lambda e, s2=s2, xb=xb, st=st: e.tensor_scalar(xb, stg[s2], st[:, 3:4], None, op0=ALU.mult), [stb, stgb[s2]], [xbb])
                for hh in range(2):
                    b = tr_bank()
                    pv = psb(b).rearrange("p (a b) -> p a b", a=8)
                    for c in range(8):
                        cc = hh * 8 + c
                        tr(pv[:, c, :], xb[:, cc * 128:(cc + 1) * 128], idb, [xbb, b_const], [pbuf[b]], inc=(c == 7))
                    g = prm[:, gcol0 + hh * 8:gcol0 + hh * 8 + 8].unsqueeze(2).to_broadcast([128, 8, 128])
                    dve(lambda e, pv=pv, g=g, hh=hh, i=i: e.tensor_tensor(xn[:, hh * 8:hh * 8 + 8, i * 128:(i + 1) * 128], pv, g, ALU.mult),
                        [pbuf[b], b_const], [xnb[i]])
            stage_a(0)
            for i in range(ng):
                if i + 1 < ng:
                    stage_a(i + 1)
                stage_b(i)

        def fm_block(slot, sb_, nchunks, xin, xin_bufs, tiles, epi, kc=16):
            for j in range(nchunks):
                for ti, (t0, n) in enumerate(tiles):
                    b = acc_bank()
                    for c in range(kc):
                        mm(PS[:, b, 0:n], slot[:, c, j * 128:(j + 1) * 128], xin[:, c, t0:t0 + n], c == 0, c == kc - 1,
                           [sb_] + xin_bufs(t0, n), [pbuf[b]], inc=(c == kc - 1))
                    epi(j, ti, t0, n, PS[:, b, 0:n], pbuf[b])

        def tm_block(slot, sb_, ncols, xin, xin_bufs, nchunk_tok, epi, kc=16):
            for i in range(nchunk_tok):
                b = acc_bank()
                for c in range(kc):
                    mm(PS[:, b, 0:ncols], xin[:, c, i * 128:(i + 1) * 128], slot[:, c, 0:ncols], c == 0, c == kc - 1,
                       [sb_] + xin_bufs(i * 128, 128), [pbuf[b]], inc=(c == kc - 1))
                epi(i, PS[:, b, 0:ncols], pbuf[b])

        chk(12)
        NS = 16
        mS = A.mark()
        b_yTs_r, b_yTs_m = Buf("yTs_r"), Buf("yTs_m")
        b_ssum_s = Buf("ssum_s")
        xnS = A.alloc([16, NS], BF16)
        b_xnS = Buf("xnS")
        bc = lambda ap, shape: ap.to_broadcast(shape)

        def dv(fn, reads, writes):
            k.op("dve", fn, reads=reads, writes=writes)
        mS1 = A.mark()
        stgS = A.alloc([2048], F32)
        xbS = A.alloc([2048], BF16)
        junkS = A.alloc([2048], BF16)
        stS = A.alloc([4], F32)
        b_stgS, b_l = Buf("stgS"), Buf("l")
        k.dma("sp", stgS[0:16], xs_d, writes=[b_stgS])
        act(junkS[0:16], stgS[0:16], AF.Square, [b_stgS], [b_l], accum_out=stS[0:16, 0:1])
        dv(lambda e: e.tensor_scalar(stS[0:16, 1:2], stS[0:16, 0:1], 1.0 / 2048, EPS, op0=ALU.mult, op1=ALU.add), [b_l], [b_l])
        act(stS[0:16, 2:3], stS[0:16, 1:2], AF.Sqrt, [b_l], [b_l])
        dv(lambda e: e.reciprocal(stS[0:16, 3:4], stS[0:16, 2:3]), [b_l], [b_l])
        dv(lambda e: e.tensor_scalar(xbS[0:16], stgS[0:16], stS[0:16, 3:4], None, op0=ALU.mult), [b_l, b_stgS], [b_l])
        for hh in range(2):
            b = tr_bank()
            pv = psb(b)[:, 0:8 * 16].rearrange("p (a b) -> p a b", a=8)
            for c in range(8):
                cc = hh * 8 + c
                tr(pv[:, c, :], xbS[0:16, cc * 128:(cc + 1) * 128], idb[0:16, 0:16], [b_l, b_const], [pbuf[b]], inc=(c == 7))
            g = prm[:, P_GMIX + hh * 8:P_GMIX + hh * 8 + 8].unsqueeze(2).to_broadcast([128, 8, 16])
            dv(lambda e, pv=pv, g=g, hh=hh: e.tensor_tensor(xnS[:, hh * 8:hh * 8 + 8, :], pv, g, ALU.mult), [pbuf[b], b_const], [b_xnS])
        k.barrier()
        A.release(mS1)

        gz = A.alloc([8], F32)
        v_s = A.alloc([1024], F32)
        og_s = A.alloc([1024], F32)
        u_s = A.alloc([1024], F32)
        gel_s = A.alloc([8, NS], F32)
        xr_s = A.alloc([8, NS], F32)
        b_z = {n_: Buf(n_) for n_ in "gz v og u gel xr".split()}

        def tm_s(ncols, epi):
            slot, sb_ = wnext()
            b = acc_bank()
            for c in range(16):
                mm(PS[0:16, b, 0:ncols], xnS[:, c, :], slot[:, c, 0:ncols], c == 0, c == 15, [sb_, b_xnS], [pbuf[b]], inc=(c == 15))
            epi(PS[0:16, b, 0:ncols], pbuf[b])

        def fm_s(epi):
            slot, sb_ = wnext()
            b = acc_bank()
            for j in range(4):
                for c in range(16):
                    mm(PS[:, b, j * 16:(j + 1) * 16], slot[:, c, j * 128:(j + 1) * 128], xnS[:, c, :], c == 0, c == 15, [sb_, b_xnS], [pbuf[b]],
                       inc=(c == 15 and j == 3))
            epi(PS[:, b, 0:64].rearrange("p (j t) -> p j t", j=4), pbuf[b])
        tm_s(8, lambda acc, ab: act(gz[0:16], acc, AF.Copy, [ab], [b_z["gz"]]))
        for pr in range(2):
            tm_s(512, lambda acc, ab, pr=pr: act(v_s[0:16, pr * 512:(pr + 1) * 512], acc, AF.Copy, [ab], [b_z["v"]]))
        for pr in range(2):
            tm_s(512, lambda acc, ab, pr=pr: act(og_s[0:16, pr * 512:(pr + 1) * 512], acc, AF.Sigmoid, [ab], [b_z["og"]]))
        for pr in range(2):
            tm_s(512, lambda acc, ab, pr=pr: act(u_s[0:16, pr * 512:(pr + 1) * 512], acc, AF.Copy, [ab], [b_z["u"]]))
        for pr in range(2):
            fm_s(lambda acc, ab, pr=pr: act(gel_s[:, pr * 4:pr * 4 + 4, :], acc, AF.Gelu, [ab], [b_z["gel"]]))
        for pr in range(2):
            fm_s(lambda acc, ab, pr=pr: act(xr_s[:, pr * 4:pr * 4 + 4, :], acc, AF.Copy, [ab], [b_z["xr"]]))

        mS2 = A.mark()
        sh_tok = A.alloc([1024], F32)
        src_tok = A.alloc([3, 1024], F32)
        b_sh, b_src = Buf("sh"), Buf("src")
        k.dma("sp", sh_tok[0:16], sh_d, writes=[b_sh])
        k.dma("sp", src_tok[0:16], src_d, writes=[b_src])
        h0T = A.alloc([8, NS], F32)
        bufT = A.alloc([8, 3, NS], F32)
        b_h0T, b_bufT = Buf("h0T"), Buf("bufT")
        b = tr_bank()
        for c in range(8):
            tr(PS[:, b, c * 16:(c + 1) * 16], sh_tok[0:16, c * 128:(c + 1) * 128], idf[0:16, 0:16], [b_sh, b_const], [pbuf[b]], inc=(c == 7))
        dv(lambda e, b=b: e.tensor_copy(h0T, PS[:, b, 0:128].rearrange("p (c t) -> p c t", c=8)), [pbuf[b]], [b_h0T])
        b = tr_bank()
        for c in range(8):
            for j in range(3):
                tr(PS[:, b, (c * 3 + j) * 16:(c * 3 + j + 1) * 16], src_tok[0:16, j, c * 128:(c + 1) * 128], idf[0:16, 0:16], [b_src, b_const], [pbuf[b]],
                   inc=(c == 7 and j == 2))
        dv(lambda e, b=b: e.tensor_copy(bufT, PS[:, b, 0:384].rearrange("p (c j t) -> p c j t", c=8, j=3)), [pbuf[b]], [b_bufT])
        xcS = A.alloc([8, NS], F32)
        tS = A.alloc([8, NS], F32)
        xcbS = A.alloc([8, NS], BF16)
        rS = A.alloc([8, NS], F32)
        iS = A.alloc([8, NS], F32)
        aS = A.alloc([8, NS], F32)
        muS = A.alloc([8, NS], F32)
        hS = A.alloc([8, NS], F32)
        srcN = A.alloc([8, 3, NS], F32)
        b_r = [Buf("r%d" % i) for i in range(10)]
        Wt = lambda tap: prm[:, P_CRW + tap * 8:P_CRW + tap * 8 + 8].unsqueeze(2).to_broadcast([128, 8, NS])
        pbS = lambda col: prm[:, col:col + 8].unsqueeze(2).to_broadcast([128, 8, NS])
        dv(lambda e: e.tensor_tensor(xcS, bufT[:, :, 0, :], Wt(0), ALU.mult), [b_bufT, b_const], [b_r[0]])
        for j in (1, 2):
            dv(lambda e, j=j: e.tensor_tensor(tS, bufT[:, :, j, :], Wt(j), ALU.mult), [b_bufT, b_const], [b_r[1]])
            dv(lambda e: e.tensor_tensor(xcS, xcS, tS, ALU.add), [b_r[0], b_r[1]], [b_r[0]])
        dv(lambda e: e.tensor_tensor(tS, xr_s, Wt(3), ALU.mult), [b_z["xr"], b_const], [b_r[1]])
        dv(lambda e: e.tensor_tensor(xcS, xcS, tS, ALU.add), [b_r[0], b_r[1]], [b_r[0]])
        dv(lambda e: e.tensor_tensor(xcS, xcS, pbS(P_CRB), ALU.add), [b_r[0], b_const], [b_r[0]])
        dv(lambda e: e.tensor_copy(xcbS, xcS), [b_r[0]], [b_r[2]])
        for (W, dst, db, pcol) in ((lwab, rS, b_r[3], P_LBA), (lwxb, iS, b_r[4], P_LBX)):
            b = acc_bank()
            for c in range(8):
                mm(PS[:, b, c * 16:(c + 1) * 16], W[:, c, :], xcbS[:, c, :], True, True, [b_cw, b_r[2]], [pbuf[b]], inc=(c == 7))
            dv(lambda e, b=b, dst=dst, pcol=pcol: e.tensor_tensor(dst, PS[:, b, 0:128].rearrange("p (c t) -> p c t", c=8), pbS(pcol), ALU.add),
               [pbuf[b], b_const], [db])
            act(dst, dst, AF.Sigmoid, [db], [db])
        dv(lambda e: e.tensor_tensor(tS, rS, ccol[:, 0:8].unsqueeze(2).to_broadcast([128, 8, NS]), ALU.mult), [b_r[3], b_const], [b_r[1]])
        act(aS, tS, AF.Exp, [b_r[1]], [b_r[5]])
        act(muS, tS, AF.Exp, [b_r[1]], [b_r[6]], scale=2.0)
        act(muS, muS, AF.Sqrt, [b_r[6]], [b_r[6]], scale=-1.0, bias=1.0)
        dv(lambda e: e.tensor_tensor(iS, iS, xcS, ALU.mult), [b_r[4], b_r[0]], [b_r[4]])
        dv(lambda e: e.tensor_tensor(muS, muS, iS, ALU.mult), [b_r[6], b_r[4]], [b_r[6]])
        dv(lambda e: e.tensor_tensor(hS, aS, h0T, ALU.mult), [b_r[5], b_h0T], [b_r[7]])
        dv(lambda e: e.tensor_tensor(hS, hS, muS, ALU.add), [b_r[7], b_r[6]], [b_r[7]])
        k.dma("sp", osh_d, hS, reads=[b_r[7]])
        dv(lambda e: e.tensor_copy(srcN[:, :, 0:2, :], bufT[:, :, 1:3, :]), [b_bufT], [b_r[8]])
        dv(lambda e: e.tensor_copy(srcN[:, :, 2, :], xr_s), [b_z["xr"], b_r[8]], [b_r[8]])
        k.dma("sp", osrc_d, srcN, reads=[b_r[8]])
        dv(lambda e: e.tensor_tensor(tS, hS, gel_s, ALU.mult), [b_r[7], b_z["gel"]], [b_r[1]])
        dv(lambda e: e.tensor_tensor(yTs[:, 0:8, :], tS, pbS(P_GRN), ALU.mult), [b_r[1], b_const], [b_yTs_r])
        dv(lambda e: e.tensor_tensor(rS, tS, tS, ALU.mult), [b_r[1], b_r[3]], [b_r[3]])
        dv(lambda e: e.tensor_reduce(ssum_s, rS.rearrange("p c t -> p t c"), AX.X, ALU.add), [b_r[3]], [b_ssum_s])
        k.barrier()
        A.release(mS2)

        mS3 = A.mark()
        smc = A0.alloc([3, 1024], F32)
        snt = A.alloc([4, 256], F32)
        smt = A.alloc([4], F32)
        cmw = A0.alloc([4, 1024], F32)
        cmb = A0.alloc([1024], F32)
        gb = A.alloc([8], F32)
        b_in = Buf("sin")
        for dst, srcd in ((smc, smc_d), (snt, sn_d), (smt, sm_d), (cmw, cmw_d), (cmb, cmb_d), (gb, gb_d)):
            k.dma("sp", dst[0:16], srcd, writes=[b_in])
        P16 = slice(0, 16)
        ucp = A.alloc([1024], F32)
        t1 = A.alloc([1024], F32)
        ucbS = A.alloc([1024], BF16)
        ucT = A.alloc([8, NS], BF16)
        q_s = A.alloc([1024], F32)
        k_s = A.alloc([1024], F32)
        G = A.alloc([48], F32)
        b_m = [Buf("m%d" % i) for i in range(16)]
        dv(lambda e: e.tensor_tensor(ucp[P16], smc[P16, 0, :], cmw[P16, 0, :], ALU.mult), [b_in], [b_m[0]])
        for j in (1, 2):
            dv(lambda e, j=j: e.tensor_tensor(t1[P16], smc[P16, j, :], cmw[P16, j, :], ALU.mult), [b_in], [b_m[1]])
            dv(lambda e: e.tensor_tensor(ucp[P16], ucp[P16], t1[P16], ALU.add), [b_m[0], b_m[1]], [b_m[0]])
        dv(lambda e: e.tensor_tensor(t1[P16], u_s[P16], cmw[P16, 3, :], ALU.mult), [b_in, b_z["u"]], [b_m[1]])
        dv(lambda e: e.tensor_tensor(ucp[P16], ucp[P16], t1[P16], ALU.add), [b_m[0], b_m[1]], [b_m[0]])
        dv(lambda e: e.tensor_tensor(ucp[P16], ucp[P16], cmb[P16], ALU.add), [b_m[0], b_in], [b_m[0]])
        act(ucbS[P16], ucp[P16], AF.Silu, [b_m[0]], [b_m[2]])
        k.dma("sp", osmc_d[:, 0:2, :], smc[P16, 1:3, :], reads=[b_in])
        k.dma("sp", osmc_d[:, 2, :], u_s[P16], reads=[b_z["u"]])
        b = tr_bank()
        pv = psb(b)[:, 0:128].rearrange("p (a b) -> p a b", a=8)
        for c in range(8):
            tr(pv[:, c, :], ucbS[P16, c * 128:(c + 1) * 128], idb[0:16, 0:16], [b_m[2], b_const], [pbuf[b]], inc=(c == 7))
        dv(lambda e, pv=pv: e.tensor_copy(ucT, pv), [pbuf[b]], [b_m[4]])
        for h in range(4):
            for (W, dst, db, scl) in ((wqb, q_s, b_m[5], 1.0), (wkb, k_s, b_m[6], 1.0 / 16)):
                b = acc_bank()
                for ic in range(2):
                    mm(PS[0:16, b, 0:256], ucT[:, h * 2 + ic, :], W[:, h, ic, :], ic == 0, ic == 1, [b_cw, b_m[4]], [pbuf[b]], inc=(ic == 1))
                act(dst[P16, h * 256:(h + 1) * 256], PS[0:16, b, 0:256], AF.Copy, [pbuf[b]], [db], scale=scl)
        gG = lambda i: G[P16, i * 4:(i + 1) * 4]
        b_G = Buf("G")
        dv(lambda e: e.tensor_tensor(G[P16, 0:8], gz[P16], gb[P16], ALU.add), [b_z["gz"], b_in], [b_G])
        act(gG(1), gG(1), AF.Exp, [b_G], [b_G], scale=-1.0)
        act(gG(1), gG(1), AF.Ln, [b_G], [b_G], bias=1.0)
        dv(lambda e: e.tensor_tensor(gG(2), smt[P16], gG(1), ALU.subtract), [b_G, b_in], [b_G])
        dv(lambda e: e.tensor_tensor(gG(3), gG(2), gG(0), ALU.max), [b_G], [b_G])
        k.dma("sp", osm_d, gG(3), reads=[b_G])
        dv(lambda e: e.tensor_tensor(gG(4), gG(2), gG(3), ALU.subtract), [b_G], [b_G])
        act(gG(4), gG(4), AF.Exp, [b_G], [b_G])
        dv(lambda e: e.tensor_tensor(gG(5), gG(0), gG(3), ALU.subtract), [b_G], [b_G])
        act(gG(5), gG(5), AF.Exp, [b_G], [b_G])
        act(gG(6), gG(3), AF.Exp, [b_G], [b_G], scale=-1.0)
        v4 = lambda ap: ap.rearrange("p (h d) -> p h d", h=4)
        g4 = lambda i: gG(i).unsqueeze(2).to_broadcast([16, 4, 256])
        dv(lambda e: e.tensor_tensor(t1[P16], q_s[P16], k_s[P16], ALU.mult), [b_m[5], b_m[6]], [b_m[1]])
        dv(lambda e: e.tensor_reduce(gG(7), v4(t1[P16]), AX.X, ALU.add), [b_m[1], b_G], [b_G])
        dv(lambda e: e.tensor_tensor(v4(t1[P16]), v4(q_s[P16]), snt[P16], ALU.mult), [b_m[5], b_in, b_G], [b_m[1]])
        dv(lambda e: e.tensor_reduce(gG(8), v4(t1[P16]), AX.X, ALU.add), [b_m[1], b_G], [b_G])
        dv(lambda e: e.tensor_tensor(gG(9), gG(7), gG(5), ALU.mult), [b_G], [b_G])
        dv(lambda e: e.tensor_tensor(gG(10), gG(4), gG(8), ALU.mult), [b_G], [b_G])
        dv(lambda e: e.tensor_tensor(gG(10), gG(10), gG(9), ALU.add), [b_G], [b_G])
        act(gG(10), gG(10), AF.Abs, [b_G], [b_G])
        dv(lambda e: e.tensor_tensor(gG(10), gG(10), gG(6), ALU.max), [b_G], [b_G])
        dv(lambda e: e.reciprocal(gG(10), gG(10)), [b_G], [b_G])
        nN = A.alloc([4, 256], F32)
        gvS = A.alloc([4, 256], F32)
        dv(lambda e: e.tensor_tensor(nN[P16], snt[P16], g4(4), ALU.mult), [b_in, b_G], [b_m[7]])
        dv(lambda e: e.tensor_tensor(v4(t1[P16]), v4(k_s[P16]), g4(5), ALU.mult), [b_m[6], b_G, b_m[1]], [b_m[1]])
        dv(lambda e: e.tensor_tensor(nN[P16], nN[P16], v4(t1[P16]), ALU.add), [b_m[7], b_m[1]], [b_m[7]])
        k.dma("sp", osn_d, nN[P16], reads=[b_m[7]])
        dv(lambda e: e.tensor_tensor(gvS[P16], v4(v_s[P16]), g4(5), ALU.mult), [b_z["v"], b_G], [b_m[8]])
        Cq = A.alloc([4, 256], F32)
        b_Cq = Buf("Cq")
        selT = A.alloc([16 * 128], F32)
        b_selT = Buf("selT")
        k.dma("sp", selT[P16], seltok_d, writes=[b_selT])
        vT = A.alloc([8, NS], F32)
        CqT = A.alloc([8, NS], F32)
        wgR = A.alloc([NS, 8], F32)
        qR = [A.alloc([1024], F32) for _ in range(2)]
        kR = [A.alloc([1024], F32) for _ in range(2)]
        Ct = [A.alloc([8, 256], F32) for _ in range(2)]
        jk = A.alloc([256], F32)
        tT = [A.alloc([256], F32) for _ in range(2)]
        b_vT, b_CqT, b_wgR, b_jk = [Buf(x) for x in "vT CqT wgR jk".split()]
        b_tT = [Buf("tT0"), Buf("tT1")]
        b_qR, b_kR, b_Ct = [Buf("qR0"), Buf("qR1")], [Buf("kR0"), Buf("kR1")], [Buf("Ct0"), Buf("Ct1")]
        b = tr_bank()
        for c in range(8):
            tr(PS[:, b, c * 16:(c + 1) * 16], v_s[P16, c * 128:(c + 1) * 128], idf[0:16, 0:16], [b_z["v"], b_const], [pbuf[b]], inc=(c == 7))
        dv(lambda e, b=b: e.tensor_copy(vT, PS[:, b, 0:128].rearrange("p (c t) -> p c t", c=8)), [pbuf[b]], [b_vT])
        b = acc_bank()
        for tok in range(NS):
            mm(PS[:, b, tok * 8:(tok + 1) * 8], selT[P16, tok * 128:(tok + 1) * 128], G[P16, 16:24], True, True, [b_selT, b_G], [pbuf[b]], inc=(tok == NS - 1))
        dv(lambda e, b=b: e.tensor_copy(wgR, PS[:, b, 0:128].rearrange("p (t g) -> p t g", t=NS)), [pbuf[b]], [b_wgR])
        def c_prefetch(tok):
            s2 = tok % 2
            k.dma("sp", Ct[s2], sC_d[tok].rearrange("h (vh p) k -> p (h vh) k", p=128), writes=[b_Ct[s2]])
            for (src, dstR, dbR, sb1) in ((q_s, qR, b_qR, b_m[5]), (k_s, kR, b_kR, b_m[6])):
                for hh in range(2):
                    bb = 4 + (hh if src is q_s else 2 + hh)
                    mm(PS[:, bb, :], selT[P16, tok * 128:(tok + 1) * 128], src[P16, hh * 512:(hh + 1) * 512], True, True, [b_selT, sb1], [pbuf[bb]], inc=True)
                    act(dstR[s2][:, hh * 512:(hh + 1) * 512], PS[:, bb, :], AF.Copy, [pbuf[bb]], [dbR[s2]])
        c_prefetch(0)
        for tok in range(NS):
            s2 = tok % 2
            if tok + 1 < NS:
                c_prefetch(tok + 1)
            for hv in range(8):
                h = hv // 2
                dv(lambda e, s2=s2, hv=hv, h=h, tok=tok: e.scalar_tensor_tensor(jk, Ct[s2][:, hv, :], 1.0, qR[s2][:, h * 256:(h + 1) * 256], op0=ALU.mult, op1=ALU.mult,
                                                                               accum_out=CqT[:, hv, tok:tok + 1]),
                   [b_Ct[s2], b_qR[s2], b_CqT], [b_jk, b_CqT])
                k.op("pool", lambda e, s2=s2, hv=hv, h=h, tok=tok: e.tensor_scalar(tT[hv % 2], kR[s2][:, h * 256:(h + 1) * 256], vT[:, hv, tok:tok + 1], wgR[:, tok, 4 + h:5 + h],
                                                                                  op0=ALU.mult, op1=ALU.mult),
                     reads=[b_kR[s2], b_vT, b_wgR], writes=[b_tT[hv % 2]])
                dv(lambda e, s2=s2, hv=hv, h=h, tok=tok: e.scalar_tensor_tensor(Ct[s2][:, hv, :], Ct[s2][:, hv, :], wgR[:, tok, h:h + 1], tT[hv % 2], op0=ALU.mult, op1=ALU.add),
                   [b_Ct[s2], b_wgR, b_tT[hv % 2]], [b_Ct[s2]])
            k.dma("sp", osC_d[tok].rearrange("h (vh p) k -> p (h vh) k", p=128), Ct[s2], reads=[b_Ct[s2]])
        for q4 in range(2):
            b = tr_bank()
            for c in range(4):
                cc = q4 * 4 + c
                tr(PS[0:16, b, c * 128:(c + 1) * 128], CqT[:, cc, :], idf, [b_CqT, b_const], [pbuf[b]], inc=(c == 3))
            dv(lambda e, b=b, q4=q4: e.tensor_copy(Cq[P16, q4 * 2:q4 * 2 + 2, :], PS[0:16, b, :].rearrange("p (h d) -> p h d", h=2)), [pbuf[b]], [b_Cq])
        hN = A.alloc([4, 256], F32)
        dv(lambda e: e.tensor_tensor(hN[P16], Cq[P16], g4(4), ALU.mult), [b_Cq, b_G], [b_m[9]])
        dv(lambda e: e.tensor_tensor(v4(t1[P16]), v4(v_s[P16]), g4(9), ALU.mult), [b_z["v"], b_G, b_m[1]], [b_m[1]])
        dv(lambda e: e.tensor_tensor(hN[P16], hN[P16], v4(t1[P16]), ALU.add), [b_m[9], b_m[1]], [b_m[9]])
        dv(lambda e: e.tensor_tensor(hN[P16], hN[P16], g4(10), ALU.mult), [b_m[9], b_G], [b_m[9]])
        dv(lambda e: e.tensor_tensor(v4(t1[P16]), hN[P16], hN[P16], ALU.mult), [b_m[9], b_m[1]], [b_m[1]])
        dv(lambda e: e.tensor_reduce(gG(11), v4(t1[P16]), AX.X, ALU.add), [b_m[1], b_G], [b_G])
        dv(lambda e: e.tensor_scalar(gG(11), gG(11), 1.0 / 256, EPS, op0=ALU.mult, op1=ALU.add), [b_G], [b_G])
        act(gG(11), gG(11), AF.Sqrt, [b_G], [b_G])
        dv(lambda e: e.reciprocal(gG(11), gG(11)), [b_G], [b_G])
        dv(lambda e: e.tensor_tensor(hN[P16], hN[P16], g4(11), ALU.mult), [b_m[9], b_G], [b_m[9]])
        dv(lambda e: e.tensor_tensor(hN[P16], hN[P16], gml[P16].unsqueeze(1).to_broadcast([16, 4, 256]), ALU.mult), [b_m[9], b_const], [b_m[9]])
        dv(lambda e: e.tensor_tensor(v4(ucbS[P16]), hN[P16], v4(og_s[P16]), ALU.mult), [b_m[9], b_z["og"], b_m[2], b_m[4]], [b_m[2]])
        b = tr_bank()
        pv = psb(b)[:, 0:128].rearrange("p (a b) -> p a b", a=8)
        for c in range(8):
            tr(pv[:, c, :], ucbS[P16, c * 128:(c + 1) * 128], idb[0:16, 0:16], [b_m[2], b_const], [pbuf[b]], inc=(c == 7))
        dv(lambda e, pv=pv: e.tensor_copy(yTs[:, 8:16, :], pv), [pbuf[b]], [b_yTs_m])
        k.barrier()
        A.release(mS3)
        k.barrier()
        A.release(mS)
        A0.release(R0_LO)

        m_mix = A.mark()
        yT = A0.alloc([16, 1024], BF16)
        ssum = A0.alloc([1024], F32)
        xn = A.alloc([16, 1024], BF16)
        xnb = [Buf("xn%d" % i) for i in range(8)]
        yTb = [[Buf("yT%d_%d" % (c, t)) for t in range(2)] for c in range(16)]
        b_ssum = Buf("ssum")
        TILES = [(0, 512), (512, 512)]

        def xn_bufs(t0, n):
            return xnb[t0 // 128:(t0 + n + 127) // 128]

        for ps_ in range(2):
            main = ps_ == 1
            src = xm_d if main else xp_d
            m0 = A.mark()
            stg = [A.alloc([2048], F32) for _ in range(2)]
            scratch = (stg, [Buf("stg0"), Buf("stg1")], [A.alloc([2048], BF16) for _ in range(2)], [Buf("xb0"), Buf("xb1")],
                       [A.alloc([2048], BF16) for _ in range(2)], [Buf("jk0"), Buf("jk1")], [A.alloc([4], F32) for _ in range(2)], [Buf("st0"), Buf("st1")])
            load_norm(src, 1024, P_GMIX, xn, xnb, scratch)
            k.barrier()
            chk(2)
            A.release(m0)

            if main:
                dve(lambda e: e.tensor_scalar(C32, C32, maskc[:, 0:1], None, op0=ALU.mult), [b_C32, b_const], [b_C32])
                dve(lambda e: e.tensor_scalar(hcar, hcar, maskc[:, 0:1], None, op0=ALU.mult), [b_hcar, b_const], [b_hcar])
                dve(lambda e: e.tensor_scalar(gcar, gcar, maskc[0:4, 0:1], None, op0=ALU.mult), [b_gcar, b_const], [b_gcar])
                dve(lambda e: e.tensor_scalar(rtail, rtail, maskc[:, 0:1], None, op0=ALU.mult), [b_rtail, b_const], [b_rtail])
                dve(lambda e: e.tensor_scalar(mtail, mtail, maskc[:, 0:1], None, op0=ALU.mult), [b_mtail, b_const], [b_mtail])
                dve(lambda e: e.memset(ssum, 0.0), [], [b_ssum])
                chk(20)

            m1 = A.mark()
            R_B = A.alloc([1024], F32, parts=4)
            R_A = A.alloc([1024], F32, parts=4)
            R_ig = R_A
            R_M = A.alloc([1024], F32, parts=4)
            R_w = R_B
            R_g = A.alloc([1024], F32, parts=4)
            R_e = A.alloc([1024], F32, parts=4)
            R_s = A.alloc([16], F32, parts=4)
            gcols = A.alloc([8, 4, 4], F32)
            gsrep = A.alloc([4, 8], F32)
            b_rows = Buf("rows")
            b_gcols = Buf("gcols")
            b_gsrep = Buf("gsrep")

            slot, sb_ = wnext()
            for gi in range(2):
                for ti, (t0, n) in enumerate(TILES):
                    b = acc_bank()
                    for c in range(16):
                        mm(PS[0:4, b, 0:n], slot[:, c, gi * 4:gi * 4 + 4], xn[:, c, t0:t0 + n], c == 0, c == 15,
                           [sb_] + xn_bufs(t0, n), [pbuf[b]], inc=(c == 15))
                    if gi == 0:
                        act(R_ig[:, t0:t0 + n], PS[0:4, b, 0:n], AF.Identity, [pbuf[b], b_const], [b_rows], bias=prm[0:4, P_BI:P_BI + 1])
                    else:
                        act(R_e[:, t0:t0 + n], PS[0:4, b, 0:n], AF.Exp, [pbuf[b], b_const], [b_rows], scale=-1.0, bias=negbf[:, 0:1])
            act(R_e, R_e, AF.Ln, [b_rows], [b_rows], bias=1.0)
            dve(lambda e: e.tensor_tensor_scan(R_B, onesf[0:4, 0:1].to_broadcast([4, 1024]), R_e, gcar[:, 0:1], ALU.mult, ALU.subtract),
                [b_rows, b_gcar, b_const], [b_rows])
            dve(lambda e: e.tensor_tensor(R_A, R_ig, R_B, ALU.subtract), [b_rows], [b_rows])
            dve(lambda e: e.tensor_tensor_scan(R_M, onesf[0:4, 0:1].to_broadcast([4, 1024]), R_A, gcar[:, 1:2], ALU.mult, ALU.max),
                [b_rows, b_gcar], [b_rows])
            dve(lambda e: e.tensor_copy(R_s[:, 0:1], gcar[:, 1:2]), [b_gcar, b_rows], [b_rows])
            dve(lambda e: e.tensor_copy(R_s[:, 1:8], R_M[:, 127:896:128]), [b_rows], [b_rows])
            dve(lambda e: e.tensor_copy(R_s[:, 8:16], R_M[:, 127:1024:128]), [b_rows], [b_rows])
            v3 = lambda r: r.rearrange("p (c t) -> p c t", c=8)
            dve(lambda e: e.tensor_tensor(v3(R_g), v3(R_A), R_s[:, 8:16].unsqueeze(2).to_broadcast([4, 8, 128]), ALU.subtract), [b_rows], [b_rows])
            act(R_g, R_g, AF.Exp, [b_rows], [b_rows])
            dve(lambda e: e.tensor_tensor(R_e, R_B, R_M, ALU.add), [b_rows], [b_rows])
            dve(lambda e: e.tensor_copy(gcar[:, 2:3], R_e[:, 1023:1024]), [b_rows, b_gcar], [b_gcar])
            act(R_e, R_e, AF.Exp, [b_rows], [b_rows], scale=-1.0)
            dve(lambda e: e.tensor_copy(gcar[:, 0:1], R_B[:, 1023:1024]), [b_rows, b_gcar], [b_gcar])
            dve(lambda e: e.tensor_copy(gcar[:, 1:2], R_M[:, 1023:1024]), [b_rows, b_gcar], [b_gcar])
            dve(lambda e: e.tensor_tensor(v3(R_w), R_s[:, 0:8].unsqueeze(2).to_broadcast([4, 8, 128]), v3(R_M), ALU.subtract), [b_rows], [b_rows])
            act(R_w, R_w, AF.Exp, [b_rows], [b_rows])
            b = 4
            pgc = PS[:, b, 0:128].rearrange("p (c q h) -> p c q h", c=8, q=4)
            for c in range(8):
                for q, R in enumerate((R_A, R_w, R_e, R_g)):
                    tr(pgc[:, c, q, :], R[:, c * 128:(c + 1) * 128], idf[0:4, 0:4], [b_rows, b_const], [pbuf[b]], inc=(c == 7 and q == 3))
            dve(lambda e: e.tensor_copy(gcols, pgc), [pbuf[b]], [b_gcols])
            b = 5
            for h in range(4):
                mm(PS[:, b, h * 8:h * 8 + 8], sel[:, h * 128:(h + 1) * 128], R_w[:, 127:1024:128], True, True,
                   [b_rows, b_const], [pbuf[b]], inc=(h == 3))
            dve(lambda e: e.tensor_copy(gsrep, PS[:, 5, 0:32].rearrange("p (h c) -> p h c", h=4)), [pbuf[5]], [b_gsrep])
            chk(3)

            m2 = A.mark()
            vtok = A.alloc([8, 2, 257], BF16)
            b_vtok = [Buf("vtok%d" % i) for i in range(8)]
            ogt = A.alloc([8, 512], BF16)
            b_ogt = [Buf("og%d" % i) for i in range(8)]
            off_ub = A.top
            ub = A.alloc([4, 1028], BF16)
            ndv = ar_t[:, off_ub:off_ub + 2056].rearrange("p (a b) -> p a b", a=8)
            b_ub = [Buf("ub%d" % j) for j in range(4)]
            uc = A.alloc([4, 1024], BF16)
            b_uc = [[Buf("uc%d_%d" % (j, t)) for t in range(2)] for j in range(4)]
            diag = A.alloc([4, 128], BF16)
            b_diag = Buf("diag")
            qT = A.alloc([2, 1024], BF16)
            kT = A.alloc([2, 1024], BF16)
            ktok = A.alloc([8, 256], BF16)
            b_qT, b_kT, b_ktok = Buf("qT"), Buf("kT"), Buf("ktok")
            wk1 = A.alloc([257], F32)
            Eh = A.alloc([512], BF16)
            Pb = A.alloc([128], BF16)
            gv = A.alloc([257], BF16)
            ytk4 = A.alloc([4, 256], BF16)
            sm8 = A.alloc([32], F32)
            b_wk = [Buf("wk%d" % i) for i in range(8)]
            for pr in range(2):
                dve(lambda e: e.memset(vtok[:, :, :, 256:257], 1.0), [], b_vtok)
                slot, sb_ = wnext()

                def epi_v(i, acc, ab):
                    act(vtok[:, i, :, 0:256], acc.rearrange("p (h d) -> p h d", h=2), AF.Copy, [ab], [b_vtok[i]])
                tm_block(slot, sb_, 512, xn, xn_bufs, 8, epi_v)
                if main:
                    slot, sb_ = wnext()

                    def epi_og(i, acc, ab):
                        act(ogt[:, i, :], acc, AF.Sigmoid, [ab], [b_ogt[i]])
                    tm_block(slot, sb_, 512, xn, xn_bufs, 8, epi_og)
                slot, sb_ = wnext()
                dve(lambda e: e.memset(ub[:, :, 0:4], 0.0), [], b_ub)
                dve(lambda e, pr=pr: e.tensor_copy(ub[:, :, 1:4], mtail[:, pr * 4:pr * 4 + 4, :]), [b_mtail], b_ub)

                def epi_u(j, ti, t0, n, acc, ab, pr=pr):
                    act(ub[:, j, 4 + t0:4 + t0 + n], acc, AF.Copy, [ab], [b_ub[j]])
                    if ti == 1:
                        dve(lambda e: e.tensor_copy(mtail[:, pr * 4 + j, :], acc[:, n - 3:n]), [ab, b_ub[j]], [b_mtail])
                fm_block(slot, sb_, 4, xn, xn_bufs, TILES, epi_u)
                for j in range(4):
                    cg = pr * 4 + j
                    for tap in range(4):
                        dve(lambda e, tap=tap, cg=cg: e.tensor_scalar(diag[:, tap, :], idf, prm[:, P_CMW + tap * 8 + cg:P_CMW + tap * 8 + cg + 1], None, op0=ALU.mult),
                            [b_const], [b_diag])
                    for ti, (t0, n) in enumerate(TILES):
                        b = acc_bank()
                        for tap in range(4):
                            k.tag = "ps%dpr%dj%dti%dtap%d" % (ps_, pr, j, ti, tap)
                            if j > 0:
                                k.trace = False
                            mm(PS[:, b, 0:n], diag[:, tap, :], ub[:, j, t0 + tap + 1:t0 + tap + 1 + n], tap == 0, tap == 3,
                               [b_diag, b_ub[j]], [pbuf[b]], inc=(tap == 3))
                        act(uc[:, j, t0:t0 + n], PS[:, b, 0:n], AF.Silu, [pbuf[b], b_const], [b_uc[j][ti]], bias=prm[:, P_CMB + cg:P_CMB + cg + 1])
                if main and pr == 0:
                    chk(21)
                for hl in range(2):
                    h = pr * 2 + hl
                    ucb = lambda t0, n, hl=hl: [b_uc[hl * 2 + ic][t0 // 512] for ic in range(2)]
                    if main:
                        for (W, dstT, dbf, scl) in ((wqb, qT, b_qT, 1.0), (wkb, kT, b_kT, 1.0 / 16)):
                            for oc in range(2):
                                for ti, (t0, n) in enumerate(TILES):
                                    b = acc_bank()
                                    for ic in range(2):
                                        mm(PS[:, b, 0:n], W[:, h, ic, oc * 128:(oc + 1) * 128], uc[:, hl * 2 + ic, t0:t0 + n], ic == 0, ic == 1,
                                           [b_const] + ucb(t0, n), [pbuf[b]], inc=(ic == 1))
                                    act(dstT[:, oc, t0:t0 + n], PS[:, b, 0:n], AF.Copy, [pbuf[b]], [dbf], scale=scl)
                    for i in range(8):
                        b = acc_bank()
                        for ic in range(2):
                            mm(PS[:, b, 0:256], uc[:, hl * 2 + ic, i * 128:(i + 1) * 128], wkb[:, h, ic, :], ic == 0, ic == 1,
                               [b_const] + ucb(i * 128, 128), [pbuf[b]], inc=(ic == 1))
                        act(ktok[:, i, :], PS[:, b, 0:256], AF.Copy, [pbuf[b]], [b_ktok], scale=1.0 / 16)
                    if main and h == 0:
                        chk(22)
                    act(Cb, C32[:, h, :, :], AF.Copy, [b_C32], [b_Cb])
                    for i in range(8):
                        cs = slice(i * 128, (i + 1) * 128)
                        gc = lambda q, i=i, h=h: gcols[:, i, q, h:h + 1]
                        if main and i % 4 == 0:
                            mm(PS[:, 4, :], sel[:, h * 128:(h + 1) * 128], R_M[:, i * 128:i * 128 + 512], True, True, [b_rows, b_const], [pbuf[4]], inc=True)
                            for i4 in range(4):
                                act(Eh[:, i4 * 128:(i4 + 1) * 128], PS[:, 4, i4 * 128:(i4 + 1) * 128], AF.Exp, [pbuf[4], b_gcols], [b_wk[1]],
                                    scale=-1.0, bias=gcols[:, i + i4, 0, h:h + 1])
                            dve(lambda e: e.tensor_tensor(Eh.rearrange("p (c t) -> p c t", c=4), Eh.rearrange("p (c t) -> p c t", c=4),
                                                          m01.unsqueeze(1).to_broadcast([128, 4, 128]), ALU.mult), [b_wk[1], b_const], [b_wk[1]])
                        dve(lambda e, i=i, hl=hl, gc=gc: e.tensor_scalar(gv, vtok[:, i, hl, :], gc(3), None, op0=ALU.mult), [b_vtok[i], b_gcols], [b_wk[0]])
                        if main:
                            for dc in range(2):
                                mm(PS[:, 1, 0:128], kT[:, dc, cs], qT[:, dc, cs], dc == 0, dc == 1, [b_kT, b_qT], [pbuf[1]], inc=(dc == 1))
                        mm(PS[:, 7, 0:257], ktok[:, i, 0:128], gv, True, True, [b_ktok, b_wk[0]], [pbuf[7]], inc=True)
                        mm(PS[:, 0, 0:257], ktok[:, i, 128:256], gv, True, True, [b_ktok, b_wk[0]], [pbuf[0]], inc=True)
                        if main:
                            dve(lambda e, i=i: e.tensor_tensor(Pb, PS[:, 1, 0:128], Eh[:, (i % 4) * 128:(i % 4 + 1) * 128], ALU.mult), [pbuf[1], b_wk[1]], [b_wk[3]])
                            mm(PS[:, 5, 0:257], Pb, vtok[:, i, hl, :], True, True, [b_wk[3], b_vtok[i]], [pbuf[5]], inc=True)
                            for kc in range(2):
                                mm(PS[:, 6, 0:257], qT[:, kc, cs], Cb[:, kc, :], kc == 0, kc == 1, [b_qT, b_Cb], [pbuf[6]], inc=(kc == 1))
                            act(wk1, PS[:, 6, 0:257], AF.Identity, [pbuf[6], b_gcols], [b_wk[4]], scale=gc(1))
                            dve(lambda e, i=i: e.tensor_tensor(ndv[:, i, :], wk1, PS[:, 5, 0:257], ALU.add), [b_wk[4], pbuf[5]] + b_ub, b_ub)
                        if main and h == 0 and i == 0:
                            chk(23)
                        dve(lambda e, h=h, i=i: e.scalar_tensor_tensor(C32[:, h, 0, :], C32[:, h, 0, :], gsrep[:, h, i:i + 1], PS[:, 7, 0:257], op0=ALU.mult, op1=ALU.add),
                            [b_C32, b_gsrep, pbuf[7]], [b_C32])
                        dve(lambda e, h=h, i=i: e.scalar_tensor_tensor(C32[:, h, 1, :], C32[:, h, 1, :], gsrep[:, h, i:i + 1], PS[:, 0, 0:257], op0=ALU.mult, op1=ALU.add),
                            [b_C32, b_gsrep, pbuf[0]], [b_C32])
                        if main and i < 7:
                            act(Cb, C32[:, h, :, :], AF.Copy, [b_C32], [b_Cb])
                    if main:
                        ndh = ndv[:, :, 0:256]
                        act(sm8[:, 0:8], ndv[:, :, 256], AF.Abs, b_ub, [b_wk[6]])
                        dve(lambda e, h=h: e.tensor_tensor(sm8[:, 0:8], sm8[:, 0:8], gcols[:, :, 2, h], ALU.max), [b_wk[6], b_gcols], [b_wk[6]])
                        dve(lambda e: e.reciprocal(sm8[:, 8:16], sm8[:, 0:8]), [b_wk[6]], [b_wk[6]])
                        dve(lambda e: e.tensor_tensor(ndh, ndh, sm8[:, 8:16].unsqueeze(2).to_broadcast([128, 8, 256]), ALU.mult), [b_wk[6]] + b_ub, b_ub)
                        for i in range(8):
                            act(wk1[:, 0:256], ndv[:, i, 0:256], AF.Square, b_ub + [b_wk[6]], [b_wk[4], b_wk[6]], accum_out=sm8[:, 16 + i:17 + i])
                        dve(lambda e: e.tensor_scalar(sm8[:, 24:32], sm8[:, 16:24], 1.0 / 256, EPS, op0=ALU.mult, op1=ALU.add), [b_wk[6]], [b_wk[6]])
                        act(sm8[:, 24:32], sm8[:, 24:32], AF.Sqrt, [b_wk[6]], [b_wk[6]])
                        dve(lambda e: e.reciprocal(sm8[:, 24:32], sm8[:, 24:32]), [b_wk[6]], [b_wk[6]])
                        dve(lambda e: e.tensor_tensor(ndh, ndh, sm8[:, 24:32].unsqueeze(2).to_broadcast([128, 8, 256]), ALU.mult), [b_wk[6]] + b_ub, b_ub)
                        dve(lambda e: e.tensor_tensor(ndh, ndh, gml.unsqueeze(1).to_broadcast([128, 8, 256]), ALU.mult), [b_const] + b_ub, b_ub)
                        for half in range(2):
                            dve(lambda e, half=half, hl=hl: e.tensor_tensor(ytk4, ndv[:, half * 4:half * 4 + 4, 0:256], ogt[:, half * 4:half * 4 + 4, hl * 256:(hl + 1) * 256], ALU.mult),
                                b_ub + b_ogt[half * 4:half * 4 + 4], [b_wk[7]])
                            bT = tr_bank()
                            pv = psb(bT)
                            for i4 in range(4):
                                for hf in range(2):
                                    tr(pv[:, (hf * 4 + i4) * 128:(hf * 4 + i4 + 1) * 128], ytk4[:, i4, hf * 128:(hf + 1) * 128], idb, [b_wk[7], b_const], [pbuf[bT]],
                                       inc=(i4 == 3 and hf == 1))
                            for hf in range(2):
                                cgl = 8 + h * 2 + hf
                                act(yT[:, cgl, half * 512:(half + 1) * 512], pv[:, hf * 512:(hf + 1) * 512], AF.Copy, [pbuf[bT]], [yTb[cgl][half]])
                if main and pr == 0:
                    chk(24)
            k.barrier()
            A.release(m2)
            if main:
                chk(25)
                k.dma("sp", opC_d, C32, reads=[b_C32])
                k.dma("sp", opm_d, gcar[:, 2:3], reads=[b_gcar])
                k.dma("sp", opmc_d, mtail, reads=[b_mtail])
            k.barrier()
            A.release(m1)

            chk(4 if not main else 6)
            m2 = A.mark()
            xrb = A.alloc([4, 1028], BF16)
            b_xrb = [Buf("xrb%d" % j) for j in range(4)]
            gel = A.alloc([4, 1024], BF16)
            b_gel = [[Buf("gel") for t in range(2)] for j in range(4)]
            dgr = A.alloc([4, 4, 128], BF16)
            b_dgr = Buf("dgr")
            RW = [dict(xc=A.alloc([1024], F32), xcb=A.alloc([1024], BF16), rr=A.alloc([1024], F32), ii=A.alloc([1024], F32),
                       aa=A.alloc([1024], F32), mu=A.alloc([1024], F32), hh_=A.alloc([1024], F32), bw=[Buf("rw%d" % i) for i in range(8)]) for _ in range(2)]
            for pr in range(2):
                if main:
                    slot, sb_ = wnext()

                    def epi_gr(j, ti, t0, n, acc, ab):
                        act(gel[:, j, t0:t0 + n], acc, AF.Gelu, [ab], [b_gel[j][ti]])
                    fm_block(slot, sb_, 4, xn, xn_bufs, TILES, epi_gr)
                slot, sb_ = wnext()
                dve(lambda e: e.memset(xrb[:, :, 0:4], 0.0), [], b_xrb)
                dve(lambda e, pr=pr: e.tensor_copy(xrb[:, :, 1:4], rtail[:, pr * 4:pr * 4 + 4, :]), [b_rtail], b_xrb)

                def epi_xr(j, ti, t0, n, acc, ab, pr=pr):
                    act(xrb[:, j, 4 + t0:4 + t0 + n], acc, AF.Copy, [ab], [b_xrb[j]])
                    if ti == 1:
                        dve(lambda e: e.tensor_copy(rtail[:, pr * 4 + j, :], acc[:, n - 3:n]), [ab, b_xrb[j]], [b_rtail])
                fm_block(slot, sb_, 4, xn, xn_bufs, TILES, epi_xr)
                def rg_front(j, pr=pr):
                    cg = pr * 4 + j
                    rw_ = RW[j % 2]
                    xc, xcb, rr, ii, aa, mu, hh_, bw = rw_['xc'], rw_['xcb'], rw_['rr'], rw_['ii'], rw_['aa'], rw_['mu'], rw_['hh_'], rw_['bw']
                    for ti, (t0, n) in enumerate(TILES):
                        b = acc_bank()
                        for tap in range(4):
                            mm(PS[:, b, 0:n], dgr[:, j, tap, :], xrb[:, j, t0 + tap + 1:t0 + tap + 1 + n], tap == 0, tap == 3,
                               [b_dgr, b_xrb[j]], [pbuf[b]], inc=(tap == 3))
                        act(xc[:, t0:t0 + n], PS[:, b, 0:n], AF.Identity, [pbuf[b], b_const], [bw[0]], bias=prm[:, P_CRB + cg:P_CRB + cg + 1])
                        act(xcb[:, t0:t0 + n], PS[:, b, 0:n], AF.Identity, [pbuf[b], b_const], [bw[1]], bias=prm[:, P_CRB + cg:P_CRB + cg + 1])
                    for (W, dst, db, pb) in ((lwab, rr, bw[2], P_LBA), (lwxb, ii, bw[3], P_LBX)):
                        for ti, (t0, n) in enumerate(TILES):
                            b = acc_bank()
                            mm(PS[:, b, 0:n], W[:, cg, :], xcb[:, t0:t0 + n], True, True, [b_const, bw[1]], [pbuf[b]], inc=True)
                            act(dst[:, t0:t0 + n], PS[:, b, 0:n], AF.Sigmoid, [pbuf[b], b_const], [db], bias=prm[:, pb + cg:pb + cg + 1])
                def rg_back(j, pr=pr):
                    cg = pr * 4 + j
                    rw_ = RW[j % 2]
                    xc, xcb, rr, ii, aa, mu, hh_, bw = rw_['xc'], rw_['xcb'], rw_['rr'], rw_['ii'], rw_['aa'], rw_['mu'], rw_['hh_'], rw_['bw']
                    act(aa, rr, AF.Exp, [bw[2], b_const], [bw[4]], scale=ccol[:, cg:cg + 1])
                    act(mu, rr, AF.Exp, [bw[2], b_const], [bw[5]], scale=ccol2[:, cg:cg + 1])
                    act(mu, mu, AF.Sqrt, [bw[5]], [bw[5]], scale=-1.0, bias=1.0)
                    dve(lambda e, xc=xc, xcb=xcb, rr=rr, ii=ii, aa=aa, mu=mu, hh_=hh_: e.tensor_tensor(ii, ii, xc, ALU.mult), [bw[3], bw[0]], [bw[3]])
                    dve(lambda e, xc=xc, xcb=xcb, rr=rr, ii=ii, aa=aa, mu=mu, hh_=hh_: e.tensor_tensor(mu, mu, ii, ALU.mult), [bw[5], bw[3]], [bw[5]])
                    dve(lambda e, cg=cg, xc=xc, xcb=xcb, rr=rr, ii=ii, aa=aa, mu=mu, hh_=hh_: e.tensor_tensor_scan(hh_, aa, mu, hcar[:, cg:cg + 1], ALU.mult, ALU.add), [bw[4], bw[5], b_hcar], [bw[6]])
                    dve(lambda e, cg=cg, xc=xc, xcb=xcb, rr=rr, ii=ii, aa=aa, mu=mu, hh_=hh_: e.tensor_copy(hcar[:, cg:cg + 1], hh_[:, 1023:1024]), [bw[6], b_hcar], [b_hcar])
                    if main:
                        dve(lambda e, j=j, xc=xc, xcb=xcb, rr=rr, ii=ii, aa=aa, mu=mu, hh_=hh_: e.tensor_tensor(hh_, hh_, gel[:, j, :], ALU.mult), [bw[6]] + b_gel[j], [bw[6]])
                        dve(lambda e, cg=cg, xc=xc, xcb=xcb, rr=rr, ii=ii, aa=aa, mu=mu, hh_=hh_: e.tensor_scalar(yT[:, cg, :], hh_, prm[:, P_GRN + cg:P_GRN + cg + 1], None, op0=ALU.mult),
                            [bw[6], b_const], yTb[cg])
                        dve(lambda e, xc=xc, xcb=xcb, rr=rr, ii=ii, aa=aa, mu=mu, hh_=hh_: e.tensor_tensor(rr, hh_, hh_, ALU.mult), [bw[6], bw[2]], [bw[2]])
                        dve(lambda e, xc=xc, xcb=xcb, rr=rr, ii=ii, aa=aa, mu=mu, hh_=hh_: e.tensor_tensor(ssum, ssum, rr, ALU.add), [bw[2], b_ssum], [b_ssum])
                for j in range(4):
                    for tap in range(4):
                        dve(lambda e, tap=tap, j=j, cg=pr * 4 + j: e.tensor_scalar(dgr[:, j, tap, :], idf, prm[:, P_CRW + tap * 8 + cg:P_CRW + tap * 8 + cg + 1], None, op0=ALU.mult),
                            [b_const], [b_dgr])
                rg_front(0)
                for j in range(4):
                    if j + 1 < 4:
                        rg_front(j + 1)
                    rg_back(j)
            k.barrier()
            A.release(m2)
            chk(5 if not main else 7)
            if main:
                k.dma("sp", oph_d, hcar, reads=[b_hcar])
                k.dma("sp", oprc_d, rtail, reads=[b_rtail])

        k.barrier()
        A.release(m_mix)
        AM = Arena(ar_t, off_wqb, off_wqb + 4096)
        mkT = AM.alloc([16, 256], BF16)
        b_mkT = [Buf("mkT%d" % c) for c in range(16)]
        mvt = AM.alloc([2, 2048], BF16)
        b_mvt = [[Buf("mvt") for jb in range(4)] for nh in range(2)]
        m4 = A.mark()
        mn = A.alloc([16, 256], BF16)
        mnb = [Buf("mn0"), Buf("mn1")]
        m5 = A.mark()
        stg = [A.alloc([2048], F32) for _ in range(2)]
        scratch = (stg, [Buf("stg0"), Buf("stg1")], [A.alloc([2048], BF16) for _ in range(2)], [Buf("xb0"), Buf("xb1")],
                   [A.alloc([2048], BF16) for _ in range(2)], [Buf("jk0"), Buf("jk1")], [A.alloc([4], F32) for _ in range(2)], [Buf("st0"), Buf("st1")])
        chk(30)
        load_norm(mem_d, 256, P_GMEM, mn, mnb, scratch)
        k.barrier()
        chk(31)
        A.release(m5)
        ost = [A.alloc([4, 256], F32) for _ in range(2)]
        ostb = [Buf("ost0"), Buf("ost1")]
        mn_bufs = lambda t0, n: mnb[t0 // 128:(t0 + n + 127) // 128]
        for jb in range(4):
            slot, sb_ = wnext()
            s2 = jb % 2

            def epi_mk(j, ti, t0, n, acc, ab, jb=jb, s2=s2):
                act(ost[s2][:, j, :], acc, AF.Copy, [ab], [ostb[s2]])
                dve(lambda e: e.tensor_copy(mkT[:, jb * 4 + j, :], ost[s2][:, j, :]), [ostb[s2]], [b_mkT[jb * 4 + j]])
            fm_block(slot, sb_, 4, mn, mn_bufs, [(0, 256)], epi_mk)
            k.dma("sp", omk_d[:, jb * 4:jb * 4 + 4, :], ost[s2], reads=[ostb[s2]])
        chk(32)
        ost2 = [ost[0].rearrange("p a b -> p (a b)")[:, 0:512], ost[1].rearrange("p a b -> p (a b)")[:, 0:512]]
        cnt2 = 0
        for jb in range(4):
            slot, sb_ = wnext()

            def epi_mv(i, acc, ab, jb=jb):
                nonlocal cnt2
                s2 = cnt2 % 2
                cnt2 += 1
                act(ost2[s2], acc, AF.Copy, [ab], [ostb[s2]])
                dve(lambda e: e.tensor_copy(mvt[:, i, jb * 512:(jb + 1) * 512], ost2[s2]), [ostb[s2]], [b_mvt[i][jb]])
                k.dma("sp", omv_d[i * 128:(i + 1) * 128, jb * 512:(jb + 1) * 512], ost2[s2], reads=[ostb[s2]])
            tm_block(slot, sb_, 512, mn, mn_bufs, 2, epi_mv)
        k.barrier()
        A.release(m4)

        chk(8)

        def attn_core(n, heads, kfn, kbuf, vfn, vbuf, qc_, qcb_, oT_, oTb_, ET, b_ET, rden, b_rden):
            for hd in heads:
                for nh in range(2):
                    b = acc_bank()
                    for dc in range(4):
                        c = hd * 4 + dc
                        mm(PS[:, b, 0:n], kfn(hd, dc, nh), qc_[:, c, :], dc == 0, dc == 3,
                           [kbuf(hd, dc), qcb_[c]], [pbuf[b]], inc=(dc == 3))
                    act(ET[:, nh, 0:n], PS[:, b, 0:n], AF.Exp, [pbuf[b]], [b_ET], scale=float(512 ** -0.5))
                b = acc_bank()
                for nh in range(2):
                    mm(PS[:, b, 0:n], onesb, ET[:, nh, 0:n], nh == 0, nh == 1, [b_const, b_ET], [pbuf[b]], inc=(nh == 1))
                dve(lambda e, b=b: e.reciprocal(rden[:, 0:n], PS[:, b, 0:n]), [pbuf[b]], [b_rden])
                for dc in range(4):
                    c = hd * 4 + dc
                    b = acc_bank()
                    for nh in range(2):
                        mm(PS[:, b, 0:n], vfn(hd, dc, nh), ET[:, nh, 0:n], nh == 0, nh == 1,
                           [vbuf(hd, dc, nh), b_ET], [pbuf[b]], inc=(nh == 1))
                    dve(lambda e, b=b, c=c: e.tensor_tensor(oT_[:, c, :], PS[:, b, 0:n], rden[:, 0:n], ALU.mult), [pbuf[b], b_rden], [oTb_[c]])

        def post_tile(tiles, tinfo):
            NT = sum(n_ for _, n_ in tiles)
            nmax = max(n_ for _, n_ in tiles)
            accn["banks"] = [0, 1, 4, 5, 6, 7]
            m_tile = A.mark()
            X = A.alloc([16, NT], F32)
            off_hq = A.top
            hq = A.alloc([16, NT], BF16)
            Xb = [[Buf("X%d_%d" % (c, ti)) for ti in range(len(tiles))] for c in range(16)]
            m3 = A.mark()
            if NT >= 512:
                AH = Arena(ar_t, off_hq, off_hq + 16 * NT // 2)
                stg = [AH.alloc([2048], F32) for _ in range(2)]
            else:
                stg = [A.alloc([2048], F32) for _ in range(2)]
            stgb = [Buf("stg0"), Buf("stg1")]
            rstd = A.alloc([NT], F32)
            b_rstd = Buf("rstd")
            tmp = A.alloc([nmax], F32)
            b_tmp = Buf("tmp")
            gi = 0
            for ti, (t0, n) in enumerate(tiles):
                gs = tinfo[ti]["gs"]
                for i in range(n // gs):
                    s2 = gi % 2
                    gi += 1
                    r0 = t0 + i * gs
                    k.dma("sp", stg[s2][0:gs], tinfo[ti]["xsrc"][i * gs:(i + 1) * gs, :], writes=[stgb[s2]])
                    for q4 in range(4):
                        b = tr_bank()
                        for c in range(4):
                            cc = q4 * 4 + c
                            tr(PS[:, b, c * gs:(c + 1) * gs], stg[s2][0:gs, cc * 128:(cc + 1) * 128], idf[0:gs, 0:gs], [stgb[s2], b_const], [pbuf[b]], inc=(c == 3))
                        act(X[:, q4 * 4:q4 * 4 + 4, r0:r0 + gs], PS[:, b, 0:4 * gs].rearrange("p (c t) -> p c t", c=4), AF.Copy,
                            [pbuf[b]], [Xb[q4 * 4 + c][ti] for c in range(4)])
                b = acc_bank()
                mm(PS[:, b, 0:n], onesf, tinfo[ti]["ssum"], True, True, [b_const, tinfo[ti]["b_ssum"]], [pbuf[b]], inc=True)
                act(rstd[:, t0:t0 + n], PS[:, b, 0:n], AF.Sqrt, [pbuf[b]], [b_rstd], scale=1.0 / 1024, bias=EPS)
            dve(lambda e: e.reciprocal(rstd, rstd), [b_rstd], [b_rstd])
            for jb in range(8):
                slot, sb_ = wnext()
                for j in range(2):
                    m = jb * 2 + j
                    for ti, (t0, n) in enumerate(tiles):
                        b1 = acc_bank()
                        for c in range(8):
                            mm(PS[:, b1, 0:n], slot[:, c, j * 128:(j + 1) * 128], tinfo[ti]["yT"][:, c, :], c == 0, c == 7,
                               [sb_] + tinfo[ti]["yT_rb"], [pbuf[b1]], inc=(c == 7))
                        b2 = acc_bank()
                        for c in range(8, 16):
                            mm(PS[:, b2, 0:n], slot[:, c, j * 128:(j + 1) * 128], tinfo[ti]["yT"][:, c, :], c == 8, c == 15,
                               [sb_] + tinfo[ti]["yT_mb"], [pbuf[b2]], inc=(c == 15))
                        dve(lambda e, b1=b1, t0=t0, n=n: e.tensor_tensor(tmp[:, 0:n], PS[:, b1, 0:n], rstd[:, t0:t0 + n], ALU.mult), [pbuf[b1], b_rstd], [b_tmp])
                        dve(lambda e, m=m, b2=b2, t0=t0, n=n: e.tensor_tensor(X[:, m, t0:t0 + n], X[:, m, t0:t0 + n], PS[:, b2, 0:n], ALU.add), [pbuf[b2], Xb[m][ti]], [Xb[m][ti]])
                        dve(lambda e, m=m, t0=t0, n=n: e.tensor_tensor(X[:, m, t0:t0 + n], X[:, m, t0:t0 + n], tmp[:, 0:n], ALU.add), [b_tmp, Xb[m][ti]], [Xb[m][ti]])
            k.barrier()
            A.release(m3)
            A0.release(R0_LO)

            def rmsnorm_fm(gcol0, out, outb):
                mk_ = A.mark()
                mk0 = A0.mark()
                sq = A0.alloc([16, nmax], BF16)
                rs = A.alloc([nmax], F32)
                b_sq, b_rs = Buf("sq"), Buf("rs")
                for ti, (t0, n) in enumerate(tiles):
                    for c in range(16):
                        act(sq[:, c, 0:n], X[:, c, t0:t0 + n], AF.Square, [Xb[c][ti]], [b_sq])
                    b = acc_bank()
                    for c in range(16):
                        mm(PS[:, b, 0:n], onesb, sq[:, c, 0:n], c == 0, c == 15, [b_const, b_sq], [pbuf[b]], inc=(c == 15))
                    act(rs[:, 0:n], PS[:, b, 0:n], AF.Sqrt, [pbuf[b]], [b_rs], scale=1.0 / 2048, bias=EPS)
                    dve(lambda e, n=n: e.reciprocal(rs[:, 0:n], rs[:, 0:n]), [b_rs], [b_rs])
                    for c in range(16):
                        dve(lambda e, c=c, t0=t0, n=n: e.scalar_tensor_tensor(out[:, c, t0:t0 + n], X[:, c, t0:t0 + n], prm[:, gcol0 + c:gcol0 + c + 1], rs[:, 0:n],
                                                                               op0=ALU.mult, op1=ALU.mult),
                            [Xb[c][ti], b_const, b_rs], [outb[c][ti]])
                k.barrier()
                A.release(mk_)
                A0.release(mk0)

            def tb(bl):
                return lambda t0_, n_: [bl[c][[t for t, _ in tiles].index(t0_)] for c in range(16)]
            chk(9)
            hqb = [[Buf("hq") for _ in tiles] for c in range(16)]
            rmsnorm_fm(P_GXA, hq, hqb)
            m6 = A.mark()
            mk0 = A0.mark()
            qc = A0.alloc([16, NT], BF16)
            qcb = [[Buf("qc") for _ in tiles] for c in range(16)]
            for jb in range(8):
                slot, sb_ = wnext()

                def epi_q(j, ti_, t0_, n_, acc, ab, jb=jb):
                    act(qc[:, jb * 2 + j, t0_:t0_ + n_], acc, AF.Copy, [ab], [qcb[jb * 2 + j][ti_]])
                fm_block(slot, sb_, 2, hq, tb(hqb), tiles, epi_q)
            k.barrier()
            oT = hq
            oTb = [[Buf("oT") for _ in tiles] for c in range(16)]
            for ti, (t0, n) in enumerate(tiles):
                tinfo[ti]["attn"](n, qc[:, :, t0:t0 + n], [qcb[c][ti] for c in range(16)], oT[:, :, t0:t0 + n], [oTb[c][ti] for c in range(16)])
            for jb in range(8):
                slot, sb_ = wnext()

                def epi_co(j, ti_, t0_, n_, acc, ab, jb=jb):
                    m = jb * 2 + j
                    dve(lambda e: e.tensor_tensor(X[:, m, t0_:t0_ + n_], X[:, m, t0_:t0_ + n_], acc, ALU.add), [ab, Xb[m][ti_]], [Xb[m][ti_]])
                fm_block(slot, sb_, 2, oT, tb(oTb), tiles, epi_co)
            k.barrier()
            A.release(m6)
            A0.release(mk0)

            chk(10)
            hn = hq
            hnb = [[Buf("hn") for _ in tiles] for c in range(16)]
            rmsnorm_fm(P_GFFN, hn, hnb)
            m6 = A.mark()
            mk0 = A0.mark()
            hG = A0.alloc([16, NT], BF16)
            hGb = [[Buf("hG") for _ in tiles] for c in range(16)]
            rl = [A.alloc([nmax], F32) for _ in range(2)]
            rlb = [Buf("rl0"), Buf("rl1")]
            cnt3 = [0]
            for g in range(4):
                for jb in range(8):
                    slot, sb_ = wnext()

                    def epi_up(j, ti_, t0_, n_, acc, ab, jb=jb):
                        s2 = cnt3[0] % 2
                        cnt3[0] += 1
                        act(rl[s2][:, 0:n_], acc, AF.Relu, [ab], [rlb[s2]])
                        dve(lambda e: e.tensor_tensor(hG[:, jb * 2 + j, t0_:t0_ + n_], rl[s2][:, 0:n_], rl[s2][:, 0:n_], ALU.mult), [rlb[s2]], [hGb[jb * 2 + j][ti_]])
                    fm_block(slot, sb_, 2, hn, tb(hnb), tiles, epi_up)
                for jb in range(8):
                    slot, sb_ = wnext()

                    def epi_dn(j, ti_, t0_, n_, acc, ab, jb=jb):
                        m = jb * 2 + j
                        dve(lambda e: e.tensor_tensor(X[:, m, t0_:t0_ + n_], X[:, m, t0_:t0_ + n_], acc, ALU.add), [ab, Xb[m][ti_]], [Xb[m][ti_]])
                    fm_block(slot, sb_, 2, hG, tb(hGb), tiles, epi_dn)
            k.barrier()
            A.release(m6)
            A0.release(mk0)

            chk(11)
            m7 = A.mark()
            mk0 = A0.mark()
            sq = hq
            rs = A.alloc([nmax], F32)
            yn = [A0.alloc([16, 128], F32) for _ in range(2)]
            ob = [A0.alloc([2048], F32) for _ in range(2)]
            b_sq, b_rs = Buf("sq"), Buf("rs")
            b_yn = [Buf("yn0"), Buf("yn1")]
            obb = [Buf("ob0"), Buf("ob1")]
            gi = 0
            for ti, (t0, n) in enumerate(tiles):
                for c in range(16):
                    act(sq[:, c, t0:t0 + n], X[:, c, t0:t0 + n], AF.Square, [Xb[c][ti]], [b_sq])
                b = acc_bank()
                for c in range(16):
                    mm(PS[:, b, 0:n], onesb, sq[:, c, t0:t0 + n], c == 0, c == 15, [b_const, b_sq], [pbuf[b]], inc=(c == 15))
                act(rs[:, 0:n], PS[:, b, 0:n], AF.Sqrt, [pbuf[b]], [b_rs], scale=1.0 / 2048, bias=EPS)
                dve(lambda e, n=n: e.reciprocal(rs[:, 0:n], rs[:, 0:n]), [b_rs], [b_rs])
                gs = tinfo[ti]["gs"]
                for i in range(n // gs):
                    s2 = gi % 2
                    gi += 1
                    r0 = t0 + i * gs
                    for c in range(16):
                        dve(lambda e, c=c, i=i, s2=s2, r0=r0, gs=gs: e.scalar_tensor_tensor(yn[s2][:, c, 0:gs], X[:, c, r0:r0 + gs], prm[:, P_GFIN + c:P_GFIN + c + 1],
                                                                                             rs[:, i * gs:(i + 1) * gs], op0=ALU.mult, op1=ALU.mult),
                            [Xb[c][ti], b_const, b_rs], [b_yn[s2]])
                    for q4 in range(4):
                        b = tr_bank()
                        for c in range(4):
                            cc = q4 * 4 + c
                            tr(PS[0:gs, b, c * 128:(c + 1) * 128], yn[s2][:, cc, 0:gs], idf, [b_yn[s2], b_const], [pbuf[b]], inc=(c == 3))
                        act(ob[s2][0:gs, q4 * 512:(q4 + 1) * 512], PS[0:gs, b, :], AF.Copy, [pbuf[b]], [obb[s2]])
                    k.dma("sp", tinfo[ti]["ydst"][i * gs:(i + 1) * gs, :], ob[s2][0:gs], reads=[obb[s2]])
            k.barrier()
            A.release(m_tile)
            A0.release(mk0)
            accn["banks"] = [0, 1]

        def attn_prompt(n_, qc_, qcb_, oT_, oTb_):
            mk_ = A.mark()
            ET = A.alloc([2, 512], BF16)
            rden = A.alloc([512], F32)
            attn_core(n_, range(4),
                      lambda hd, dc, nh: mkT[:, hd * 4 + dc, nh * 128:(nh + 1) * 128], lambda hd, dc: b_mkT[hd * 4 + dc],
                      lambda hd, dc, nh: mvt[:, nh, (hd * 4 + dc) * 128:(hd * 4 + dc + 1) * 128], lambda hd, dc, nh: b_mvt[nh][hd],
                      qc_, qcb_, oT_, oTb_, ET, Buf("ET"), rden, Buf("rden"))
            k.barrier()
            A.release(mk_)
        def attn_sample(n_, qc_, qcb_, oT_, oTb_):
            mk_ = A.mark()
            Ks = [A.alloc([2, 512], BF16) for _ in range(2)]
            mkTs = [A.alloc([4, 256], BF16) for _ in range(2)]
            mvts = [A.alloc([2, 512], BF16) for _ in range(2)]
            ET = [A.alloc([2, 16], BF16) for _ in range(2)]
            rden = [A.alloc([16], F32) for _ in range(2)]
            b_Ks = [Buf("Ks0"), Buf("Ks1")]
            b_mk1 = [Buf("mk0"), Buf("mk1")]
            b_mv1 = [Buf("mv0"), Buf("mv1")]
            b_ET = [Buf("ET0"), Buf("ET1")]
            b_rden = [Buf("rd0"), Buf("rd1")]
            its = [(tok, hd) for tok in range(NS) for hd in range(4)]

            def dma_k(it):
                tok, hd = its[it]
                s2 = it % 2
                k.dma("pool", Ks[s2], ck_d[tok][:, hd * 512:(hd + 1) * 512].rearrange("(nh p) d -> p nh d", p=128), writes=[b_Ks[s2]])

            def dma_v(it):
                tok, hd = its[it]
                s2 = it % 2
                k.dma("pool", mvts[s2], cv_d[tok][:, hd * 512:(hd + 1) * 512].rearrange("(nh p) d -> p nh d", p=128), writes=[b_mv1[s2]])

            def st_a(it):
                tok, hd = its[it]
                s2 = it % 2
                b = tr_bank()
                pv = psb(b)
                for dc in range(4):
                    for nh in range(2):
                        tr(pv[:, (dc * 2 + nh) * 128:(dc * 2 + nh + 1) * 128], Ks[s2][:, nh, dc * 128:(dc + 1) * 128], idb, [b_Ks[s2], b_const], [pbuf[b]],
                           inc=(dc == 3 and nh == 1))
                if it % 2 == 0:
                    act(mkTs[s2], pv.rearrange("p (c n) -> p c n", c=4), AF.Copy, [pbuf[b]], [b_mk1[s2]])
                else:
                    dv(lambda e, pv=pv, s2=s2: e.tensor_copy(mkTs[s2], pv.rearrange("p (c n) -> p c n", c=4)), [pbuf[b]], [b_mk1[s2]])

            def st_b(it):
                tok, hd = its[it]
                s2 = it % 2
                attn_core(1, [hd],
                          lambda hd_, dc, nh: mkTs[s2][:, dc, nh * 128:(nh + 1) * 128], lambda hd_, dc: b_mk1[s2],
                          lambda hd_, dc, nh: mvts[s2][:, nh, dc * 128:(dc + 1) * 128], lambda hd_, dc, nh: b_mv1[s2],
                          qc_[:, :, tok:tok + 1], qcb_, oT_[:, :, tok:tok + 1], oTb_, ET[s2], b_ET[s2], rden[s2], b_rden[s2])
            dma_k(0)
            dma_k(1)
            dma_v(0)
            st_a(0)
            for it in range(len(its)):
                if it + 2 < len(its):
                    dma_k(it + 2)
                if it + 1 < len(its):
                    dma_v(it + 1)
                    st_a(it + 1)
                st_b(it)
            k.barrier()
            A.release(mk_)
        tinfo = [dict(xsrc=xm_d[t0:t0 + n], yT=yT[:, :, t0:t0 + n], yT_rb=[yTb[cc][ti] for cc in range(8)], yT_mb=[yTb[cc][ti] for cc in range(8, 16)],
                      ssum=ssum[:, t0:t0 + n], b_ssum=b_ssum, attn=attn_prompt, ydst=y_d[t0:t0 + n], gs=128) for ti, (t0, n) in enumerate(TILES)]
        tinfo.append(dict(xsrc=xs_d, yT=yTs, yT_rb=[b_yTs_r], yT_mb=[b_yTs_m], ssum=ssum_s, b_ssum=b_ssum_s, attn=attn_sample, ydst=ys_d, gs=NS))
        post_tile(TILES + [(1024, NS)], tinfo)
        assert k.dead or wstate["used"] == len(wsched), (wstate, len(wsched))
        k.finish()
        k.emit()
        print("instructions:", k.nins, "arena peak", A.peak, "of", NW, "A0 peak", A0.peak, "of", R0_HI)
    return nc


_CACHE = {}


def _consts():
    ident = np.eye(128, dtype=np.float32)
    s = np.arange(128)[:, None]
    t = np.arange(128)[None, :]
    maskneg = np.where(s <= t, 0.0, -30000.0).astype(np.float32)
    sel = np.zeros((4, 4, 128), np.float32)
    for h in range(4):
        sel[h, h, :] = 1.0
    return ident, maskneg, sel.reshape(4, 512)


def kernel(**inp):
    f = lambda a: np.ascontiguousarray(np.asarray(a, dtype=np.float32))
    if "nc" not in _CACHE:
        _CACHE["nc"] = build_program()
    nc = _CACHE["nc"]
    ident, maskneg, sel = _consts()
    prm = np.zeros((128, NPRM), np.float32)

    def colmajor(v, nch):
        return np.asarray(v, np.float32).reshape(nch, 128).T
    prm[:, P_GMIX:P_GMIX + 16] = colmajor(inp["g_mix"][0], 16)
    prm[:, P_GXA:P_GXA + 16] = colmajor(inp["g_xattn"][0], 16)
    prm[:, P_GMEM:P_GMEM + 16] = colmajor(inp["g_mem"][0], 16)
    prm[:, P_GFFN:P_GFFN + 16] = colmajor(inp["g_ffn"][0], 16)
    prm[:, P_GFIN:P_GFIN + 16] = colmajor(inp["g_final"], 16)
    for tap in range(4):
        prm[:, P_CRW + tap * 8:P_CRW + tap * 8 + 8] = colmajor(inp["conv_rnn_w"][0, tap], 8)
        prm[:, P_CMW + tap * 8:P_CMW + tap * 8 + 8] = colmajor(inp["conv_ml_w"][0, tap], 8)
    prm[:, P_CRB:P_CRB + 8] = colmajor(inp["conv_rnn_b"][0], 8)
    prm[:, P_CMB:P_CMB + 8] = colmajor(inp["conv_ml_b"][0], 8)
    prm[:, P_LBA:P_LBA + 8] = np.asarray(inp["lru_ba"][0], np.float32).T
    prm[:, P_LBX:P_LBX + 8] = np.asarray(inp["lru_bx"][0], np.float32).T
    prm[:, P_LAM:P_LAM + 8] = colmajor(inp["lru_lambda"][0], 8)
    prm[:, P_GRN:P_GRN + 8] = colmajor(inp["g_rnn_out"][0], 8)
    prm[:, P_GML2:P_GML2 + 2] = colmajor(inp["g_ml_out"][0], 2)
    prm[0:4, P_BI] = np.asarray(inp["ml_bi"][0], np.float32)
    prm[0:4, P_BF] = np.asarray(inp["ml_bf"][0], np.float32)
    gmlrep = np.ascontiguousarray(np.broadcast_to(np.asarray(inp["g_ml_out"][0], np.float32)[None, :], (128, 256)))
    shared = dict(
        prm=prm, gmlrep=gmlrep, ident=ident, maskneg=maskneg, sel=sel,
        w_in=f(inp["w_in"][0]), lru_wa=f(inp["lru_wa"][0]), lru_wx=f(inp["lru_wx"][0]),
        ml_wq=f(inp["ml_wq"][0]), ml_wk=f(inp["ml_wk"][0]), w_out=f(inp["w_out"][0]),
        w_cq=f(inp["w_cq"][0]), w_mk=f(inp["w_mk"][0]), w_mv=f(inp["w_mv"][0]), w_co=f(inp["w_co"][0]),
        w_up=f(inp["w_up"][0]), w_down=f(inp["w_down"][0]),
    )
    shared["cmw_rep"] = np.ascontiguousarray(np.broadcast_to(np.asarray(inp["conv_ml_w"][0], np.float32)[None], (16, 4, 1024)))
    st_ = np.zeros((16, 16, 128), np.float32)
    for t_ in range(16):
        st_[t_, t_, :] = 1.0
    shared["seltok"] = st_.reshape(16, 2048)
    shared["cmb_rep"] = np.ascontiguousarray(np.broadcast_to(np.asarray(inp["conv_ml_b"][0], np.float32)[None], (16, 1024)))
    shared["gb_rep"] = np.ascontiguousarray(np.broadcast_to(
        np.concatenate([np.asarray(inp["ml_bi"][0], np.float32), np.asarray(inp["ml_bf"][0], np.float32)])[None], (16, 8)))
    xsm = np.asarray(inp["x_sample"], np.float32)
    xpr = np.asarray(inp["x_prompt"], np.float32)
    memp = np.asarray(inp["mem_prompt"], np.float32)
    in_maps = []
    for c in range(8):
        b, hf = c // 2, c % 2
        d = dict(shared)
        d["xm"] = np.ascontiguousarray(xpr[b, hf * 1024:(hf + 1) * 1024])
        d["xp"] = np.ascontiguousarray(xpr[b, 0:1024])
        d["mem"] = np.ascontiguousarray(memp[b])
        d["mask"] = np.full((128, 1), float(hf), np.float32)
        sl = slice(c * 16, (c + 1) * 16)
        d["xs"] = np.ascontiguousarray(xsm[sl, 0])
        d["s_h"] = f(inp["state_rglru_h"][0, sl])
        d["s_rc"] = f(inp["state_rglru_conv"][0, sl])
        d["s_C"] = f(inp["state_mlstm_C"][0, sl])
        d["s_n"] = f(inp["state_mlstm_n"][0, sl])
        d["s_m"] = f(inp["state_mlstm_m"][0, sl])
        d["s_mc"] = f(inp["state_mlstm_conv"][0, sl])
        d["ck"] = f(inp["cache_mem_k"][0, sl]).reshape(16, 256, 2048)
        d["cv"] = f(inp["cache_mem_v"][0, sl]).reshape(16, 256, 2048)
        in_maps.append(d)
    res = run_bass_kernel_spmd(nc, in_maps, core_ids=list(range(8)))
    R = res.results
    B = 4
    y_prompt = np.zeros((B, 2048, 2048), np.float32)
    p_h = np.zeros((1, B, 1024), np.float32)
    p_rc = np.zeros((1, B, 3, 1024), np.float32)
    p_C = np.zeros((1, B, 4, 256, 256), np.float32)
    p_n = np.zeros((1, B, 4, 256), np.float32)
    p_m = np.zeros((1, B, 4), np.float32)
    p_mc = np.zeros((1, B, 3, 1024), np.float32)
    p_mk = np.zeros((1, B, 256, 4, 512), np.float32)
    p_mv = np.zeros((1, B, 256, 4, 512), np.float32)
    for c in range(8):
        b, hf = c // 2, c % 2
        r = R[c]
        y_prompt[b, hf * 1024:(hf + 1) * 1024] = r["o_y"]
        if hf == 1:
            p_h[0, b] = r["o_ph"].T.reshape(1024)
            p_rc[0, b] = r["o_prc"].transpose(2, 1, 0).reshape(3, 1024)
            p_mc[0, b] = r["o_pmc"].transpose(2, 1, 0).reshape(3, 1024)
            oc = r["o_pC"]
            p_C[0, b] = oc[:, :, :, 0:256].transpose(1, 3, 2, 0).reshape(4, 256, 256)
            p_n[0, b] = oc[:, :, :, 256].transpose(1, 2, 0).reshape(4, 256)
            p_m[0, b] = r["o_pm"].reshape(4)
            p_mk[0, b] = r["o_mkT"].transpose(2, 1, 0).reshape(256, 4, 512)
            p_mv[0, b] = r["o_mv"].reshape(256, 4, 512)
    y_s = np.zeros((128, 1, 2048), np.float32)
    s_h = np.zeros((1, 128, 1024), np.float32)
    s_rc = np.zeros((1, 128, 3, 1024), np.float32)
    s_C = np.zeros((1, 128, 4, 256, 256), np.float32)
    s_n = np.zeros((1, 128, 4, 256), np.float32)
    s_m = np.zeros((1, 128, 4), np.float32)
    s_mc = np.zeros((1, 128, 3, 1024), np.float32)
    for c in range(8):
        r = R[c]
        sl = slice(c * 16, (c + 1) * 16)
        y_s[sl, 0] = r["o_ys"]
        s_h[0, sl] = r["o_sh"].transpose(2, 1, 0).reshape(16, 1024)
        s_rc[0, sl] = r["o_src"].transpose(3, 2, 1, 0).reshape(16, 3, 1024)
        s_C[0, sl] = r["o_sC"]
        s_n[0, sl] = r["o_sn"]
        s_m[0, sl] = r["o_sm"]
        s_mc[0, sl] = r["o_smc"]
    return (y_prompt, y_s, p_h, p_rc, p_C, p_n, p_m, p_mc, p_mk, p_mv, s_h, s_rc, s_C, s_n, s_m, s_mc)
```

```python
import numpy as np
from contextlib import ExitStack
import concourse.bass as bass
import concourse.mybir as mybir
from concourse.bass_utils import run_bass_kernel_spmd

F32 = mybir.dt.float32
BF16 = mybir.dt.bfloat16
AF = mybir.ActivationFunctionType
ALU = mybir.AluOpType
AX = mybir.AxisListType

SAME_ENGINE_WAIT = True
EPS = 1e-6
NSLOT = 2

P_GMIX, P_GXA, P_GMEM, P_GFFN, P_GFIN = 0, 16, 32, 48, 64
P_CRW, P_CRB, P_LBA, P_LBX, P_LAM, P_GRN = 80, 112, 120, 128, 136, 144
P_CMW, P_CMB, P_GML2, P_BI, P_BF = 152, 184, 192, 194, 195
NPRM = 196


import os
STOP = int(os.environ.get("KSTOP", "0"))
DEBUG_SITES = bool(int(os.environ.get("KSITES", "0")))
DBG2 = int(os.environ.get("DBG2", "0"))
DBG3 = int(os.environ.get("DBG3", "0"))


class _Stop(Exception):
    pass


class Buf:
    __slots__ = ("name", "w", "r", "dsem", "dcount")

    def __init__(self, name="b"):
        self.name = name
        self.w = None
        self.r = {}
        self.dsem = None
        self.dcount = 0


class K:
    ENG = ("pe", "act", "dve", "pool", "sp")

    def __init__(self, nc, es):
        self.nc = nc
        self.es = es
        self.ops = {e: [] for e in self.ENG}
        self.sem = {e: es.enter_context(nc.semaphore("s_" + e)) for e in self.ENG}
        self.cnt = {e: 0 for e in self.ENG}
        self.known = {e: {} for e in self.ENG}
        self.semobj = {e: self.sem[e] for e in self.ENG}
        self.nd = 0
        self.dbufs = []
        self.nins = 0
        self.dead = False

    def _need(self, eng, reads, writes):
        need = {}

        def add(k, v):
            if need.get(k, 0) < v:
                need[k] = v
        for b in reads:
            if b.w:
                add(*b.w)
        for b in writes:
            if b.w:
                add(*b.w)
            for k, v in b.r.items():
                add(k, v)
        waits = []
        for k, v in need.items():
            if k == eng and (eng == "pe" or not SAME_ENGINE_WAIT):
                continue
            if self.known[eng].get(k, 0) >= v:
                continue
            self.known[eng][k] = v
            waits.append((self.semobj[k], v))
        return waits

    def op(self, eng, fn, reads=(), writes=(), inc=True):
        if self.dead:
            return
        waits = self._need(eng, reads, writes)
        val = self.cnt[eng] + 1
        if inc:
            self.cnt[eng] = val
        for b in reads:
            if b.r.get(eng, 0) < val:
                b.r[eng] = val
        for b in writes:
            b.w = (eng, val)
            b.r = {}
        sem = self.sem[eng]
        self.nins += 1
        if getattr(self, "trace", False):
            print("TRACE", eng, "val", val, "inc", inc, "waits", [(str(s_), v_) for s_, v_ in waits], "reads", [(b.name, b.w) for b in reads], "writes", [b.name for b in writes])
        import sys as _sys
        fr = _sys._getframe(1)
        site = []
        while fr is not None and len(site) < 3:
            site.append(fr.f_lineno)
            fr = fr.f_back
        site = "SITE" + "_".join(map(str, site)) + "_" + getattr(self, "tag", "")

        def run(e, waits=waits, fn=fn, inc=inc, sem=sem, site=site):
            for s, v in waits:
                e.wait_ge(s, v)
            ins = fn(e)
            if DEBUG_SITES:
                ins.annotate(site)
            if inc:
                ins.then_inc(sem, 1)
        self.ops[eng].append(run)

    def _dsem(self, b):
        if b.dsem is None:
            key = "d%d" % self.nd
            self.nd += 1
            b.dsem = key
            self.semobj[key] = self.es.enter_context(self.nc.semaphore(key))
            self.dbufs.append(b)
        return b.dsem

    def dma(self, q, out, in_, reads=(), writes=(), **kw):
        if self.dead:
            return
        waits = self._need(q, reads, writes)
        bl = list(reads) + list(writes)
        assert len(bl) == 1
        b = bl[0]
        kk = self._dsem(b)
        b.dcount += 16
        v = b.dcount
        if reads:
            b.r[kk] = v
        else:
            b.w = (kk, v)
            b.r = {}
        s = self.semobj[kk]
        self.nins += 1

        def run(e, waits=waits, s=s, out=out, in_=in_, kw=kw):
            for ws, wv in waits:
                e.wait_ge(ws, wv)
            e.dma_start(out=out, in_=in_, **kw).then_inc(s, 16)
        self.ops[q].append(run)

    def barrier(self):
        if self.dead:
            return
        tgt = [(e, self.cnt[e]) for e in self.ENG if self.cnt[e] > 0]
        tgt += [(b.dsem, b.dcount) for b in self.dbufs]
        for eng in self.ENG:
            waits = []
            for kk, v in tgt:
                if kk == eng:
                    continue
                if self.known[eng].get(kk, 0) >= v:
                    continue
                self.known[eng][kk] = v
                waits.append((self.semobj[kk], v))

            def run(e, waits=waits):
                for s, v in waits:
                    e.wait_ge(s, v)
            if waits:
                self.ops[eng].append(run)

    def finish(self):
        self.barrier()

    def emit(self):
        nc = self.nc
        with nc.Block() as block:
            @block.tensor
            def _(e):
                for f in self.ops["pe"]:
                    f(e)

            @block.scalar
            def _(e):
                for f in self.ops["act"]:
                    f(e)

            @block.vector
            def _(e):
                for f in self.ops["dve"]:
                    f(e)

            @block.gpsimd
            def _(e):
                for f in self.ops["pool"]:
                    f(e)

            @block.sync
            def _(e):
                for f in self.ops["sp"]:
                    f(e)


class Arena:
    def __init__(self, ap, lo, hi):
        self.ap = ap
        self.n = hi
        self.top = lo

    def alloc(self, shape, dt, parts=128):
        n = 1
        for s in shape:
            n *= s
        esz = 4 if dt == F32 else 2
        words = (n * esz + 3) // 4
        words = (words + 15) // 16 * 16
        off = self.top
        self.top += words
        assert self.top <= self.n, "arena overflow %d > %d" % (self.top, self.n)
        self.peak = max(getattr(self, "peak", 0), self.top)
        v = self.ap[:, off:off + words]
        if dt != F32:
            v = v.bitcast(dt)
        v = v[:, 0:n]
        if len(shape) == 2:
            v = v.rearrange("p (a b) -> p a b", a=shape[0])
        elif len(shape) == 3:
            v = v.rearrange("p (a b c) -> p a b c", a=shape[0], b=shape[1])
        elif len(shape) == 4:
            v = v.rearrange("p (a b c d) -> p a b c d", a=shape[0], b=shape[1], c=shape[2])
        if parts != 128:
            v = v[0:parts]
        return v

    def mark(self):
        return self.top

    def release(self, m):
        self.top = m


def build_program():
    nc = bass.Bass("TRN2", target_bir_lowering=False)

    def DI(name, shape):
        return nc.dram_tensor(name, list(shape), F32, kind="ExternalInput").ap()

    def DO(name, shape):
        return nc.dram_tensor(name, list(shape), F32, kind="ExternalOutput").ap()

    xm_d = DI("xm", [1024, 2048])
    xp_d = DI("xp", [1024, 2048])
    mem_d = DI("mem", [256, 2048])
    mask_d = DI("mask", [128, 1])
    prm_d = DI("prm", [128, NPRM])
    gml_d = DI("gmlrep", [128, 256])
    id_d = DI("ident", [128, 128])
    mneg_d = DI("maskneg", [128, 128])
    sel_d = DI("sel", [4, 4 * 128])
    w_in_d = DI("w_in", [2048, 5128])
    lwa_d = DI("lru_wa", [8, 128, 128])
    lwx_d = DI("lru_wx", [8, 128, 128])
    wq_d = DI("ml_wq", [4, 256, 256])
    wk_d = DI("ml_wk", [4, 256, 256])
    w_out_d = DI("w_out", [2048, 2048])
    w_cq_d = DI("w_cq", [2048, 2048])
    w_mk_d = DI("w_mk", [2048, 2048])
    w_mv_d = DI("w_mv", [2048, 2048])
    w_co_d = DI("w_co", [2048, 2048])
    w_up_d = DI("w_up", [2048, 8192])
    w_dn_d = DI("w_down", [8192, 2048])

    xs_d = DI("xs", [16, 2048])
    sh_d = DI("s_h", [16, 1024])
    src_d = DI("s_rc", [16, 3, 1024])
    sC_d = DI("s_C", [16, 4, 256, 256])
    sn_d = DI("s_n", [16, 4, 256])
    sm_d = DI("s_m", [16, 4])
    smc_d = DI("s_mc", [16, 3, 1024])
    ck_d = DI("ck", [16, 256, 2048])
    cv_d = DI("cv", [16, 256, 2048])
    seltok_d = DI("seltok", [16, 16 * 128])
    cmw_d = DI("cmw_rep", [16, 4, 1024])
    cmb_d = DI("cmb_rep", [16, 1024])
    gb_d = DI("gb_rep", [16, 8])
    ys_d = DO("o_ys", [16, 2048])
    osh_d = DO("o_sh", [128, 8, 16])
    osrc_d = DO("o_src", [128, 8, 3, 16])
    osC_d = DO("o_sC", [16, 4, 256, 256])
    osn_d = DO("o_sn", [16, 4, 256])
    osm_d = DO("o_sm", [16, 4])
    osmc_d = DO("o_smc", [16, 3, 1024])
    y_d = DO("o_y", [1024, 2048])
    oph_d = DO("o_ph", [128, 8])
    oprc_d = DO("o_prc", [128, 8, 3])
    opmc_d = DO("o_pmc", [128, 8, 3])
    opC_d = DO("o_pC", [128, 4, 2, 257])
    opm_d = DO("o_pm", [4, 1])
    omk_d = DO("o_mkT", [128, 16, 256])
    omv_d = DO("o_mv", [256, 2048])

    with ExitStack() as es:
        k = K(nc, es)
        NW = 52992
        ar_t = es.enter_context(nc.sbuf_tensor("arena", [128, NW], F32))
        A = Arena(ar_t, 0, NW)
        PS = es.enter_context(nc.psum_tensor("ps", [128, 8, 512], F32))

        def psb(b):
            return PS[:, b, :].bitcast(BF16)
        pbuf = [Buf("ps%d" % i) for i in range(8)]

        wslot = [A.alloc([16, 512], BF16) for _ in range(NSLOT)]
        wsb = [Buf("ws%d" % i) for i in range(NSLOT)]
        idf = A.alloc([128], F32)[:, :]
        idb = A.alloc([128], BF16)
        onesb = A.alloc([128], BF16)
        onesf = A.alloc([128], F32)
        mneg = A.alloc([128], F32)
        m01 = A.alloc([128], BF16)
        prm = A.alloc([NPRM], F32)
        gml = A.alloc([256], F32)
        maskc = A.alloc([1], F32)
        sel = A.alloc([4 * 128], F32, parts=4)
        off_wqb = A.top
        wqb = A.alloc([4, 2, 256], BF16)
        wkb = A.alloc([4, 2, 256], BF16)
        lwab = A.alloc([8, 128], BF16)
        lwxb = A.alloc([8, 128], BF16)
        C32 = A.alloc([4, 2, 257], F32)
        Cb = A.alloc([2, 257], BF16)
        hcar = A.alloc([8], F32)
        rtail = A.alloc([8, 3], F32)
        mtail = A.alloc([8, 3], F32)
        ccol = A.alloc([8], F32)
        ccol2 = A.alloc([8], F32)
        negbf = A.alloc([1], F32, parts=4)
        st0 = A.alloc([1], F32)
        gcar = A.alloc([4], F32, parts=4)
        b_const = Buf("const")
        yTs = A.alloc([16, 16], BF16)
        ssum_s = A.alloc([16], F32)
        PBASE = A.top
        R0_LO, R0_HI = PBASE, PBASE + 9216
        A0 = Arena(ar_t, R0_LO, R0_HI)
        A = Arena(ar_t, R0_HI, NW)
        print("persistent words", PBASE, "R12 words", NW - R0_HI)
        b_C32, b_Cb, b_hcar, b_rtail, b_mtail, b_gcar = [Buf(n) for n in "C32 Cb hcar rtail mtail gcar".split()]

        def act(out, in_, func, reads, writes, **kw):
            k.op("act", lambda e: e.activation(out, in_, func, **kw), reads=reads, writes=writes)

        def dve(fn, reads, writes):
            k.op("dve", fn, reads=reads, writes=writes)

        def mm(out, lhsT, rhs, start, stop, reads, writes, inc):
            k.op("pe", lambda e: e.matmul(out, lhsT, rhs, start=start, stop=stop), reads=reads, writes=writes, inc=inc)

        def tr(out, in_, ident, reads, writes, inc):
            k.op("pe", lambda e: e.transpose(out, in_, ident), reads=reads, writes=writes, inc=inc)

        def chk(n):
            if STOP == n:
                k.finish()
                k.dead = True
        for dst, src in ((idf, id_d), (mneg, mneg_d), (prm, prm_d), (gml, gml_d), (maskc, mask_d), (sel, sel_d)):
            k.dma("sp", dst, src, writes=[b_const])
        b_cw = Buf("constw")
        k.dma("pool", wqb, wq_d.rearrange("h (c p) n -> p h c n", p=128), writes=[b_cw])
        k.dma("pool", wkb, wk_d.rearrange("h (c p) n -> p h c n", p=128), writes=[b_cw])
        k.dma("pool", lwab, lwa_d.rearrange("h p n -> p h n"), writes=[b_cw])
        k.dma("pool", lwxb, lwx_d.rearrange("h p n -> p h n"), writes=[b_cw])
        k.op("dve", lambda e: e.memset(st0, 0.0), reads=[b_cw, b_const], writes=[b_const])
        dve(lambda e: e.tensor_copy(idb, idf), [b_const], [b_const])
        dve(lambda e: e.memset(onesb, 1.0), [], [b_const])
        dve(lambda e: e.tensor_scalar(m01, mneg, 0.0, None, op0=ALU.is_equal), [b_const], [b_const])
        dve(lambda e: e.memset(onesf, 1.0), [], [b_const])
        dve(lambda e: e.memset(C32, 0.0), [], [b_C32])
        dve(lambda e: e.memset(hcar, 0.0), [], [b_hcar])
        dve(lambda e: e.memset(gcar, 0.0), [], [b_gcar])
        dve(lambda e: e.memset(rtail, 0.0), [], [b_rtail])
        dve(lambda e: e.memset(mtail, 0.0), [], [b_mtail])
        act(ccol, prm[:, P_LAM:P_LAM + 8], AF.Exp, [b_const], [b_const], scale=-1.0)
        act(ccol, ccol, AF.Ln, [b_const], [b_const], bias=1.0)
        dve(lambda e: e.tensor_scalar(ccol2, ccol, -16.0, None, op0=ALU.mult), [b_const], [b_const])
        dve(lambda e: e.tensor_scalar(ccol, ccol, -8.0, None, op0=ALU.mult), [b_const], [b_const])
        dve(lambda e: e.tensor_scalar(negbf, prm[0:4, P_BF:P_BF + 1], -1.0, None, op0=ALU.mult), [b_const], [b_const])

        chk(1)
        wsched = []
        wstate = {"issued": 0, "used": 0, "cnt": [0, 0]}
        wassign = {}
        wflat = [w_.rearrange("p a b -> p (a b)") for w_ in wslot]
        hslot = [wflat[kk // 2][:, (kk % 2) * 4096:(kk % 2 + 1) * 4096].rearrange("p (a b) -> p a b", a=16) for kk in range(4)]
        hsb = [Buf("hs%d" % i) for i in range(4)]

        def wplan(ap, half=False):
            wsched.append((ap, half))

        def wplan256(wd, r0, c0):
            for hh_ in range(2):
                wplan(wd[r0:r0 + 2048, c0 + hh_ * 256:c0 + (hh_ + 1) * 256], True)

        def wissue():
            i = wstate["issued"]
            ap, half = wsched[i]
            nco = ap.shape[1]
            md = 1 if half else 0
            cidx = wstate["cnt"][md]
            wstate["cnt"][md] += 1
            if half:
                sl_, bf_ = hslot[cidx % 4], hsb[cidx % 4]
            else:
                sl_, bf_ = wslot[cidx % 2], wsb[cidx % 2]
            k.dma("pool", sl_[:, :, 0:nco], ap.rearrange("(c p) n -> p c n", p=128), writes=[bf_])
            wassign[i] = (sl_, bf_)
            wstate["issued"] = i + 1

        def wnext():
            i = wstate["used"]
            half = wsched[i][1]
            if wstate["issued"] <= i:
                if i > 0 and wsched[i - 1][1] != half:
                    k.barrier()
                wissue()
            depth = 4 if half else NSLOT
            while wstate["issued"] < min(len(wsched), i + depth) and wsched[wstate["issued"]][1] == half:
                wissue()
            wstate["used"] = i + 1
            return wassign.pop(i)

        def cols(wd, r0, c0, n):
            return wd[r0:r0 + 2048, c0:c0 + n]

        wplan(cols(w_in_d, 0, 5120, 8))
        for c0 in (3072, 3584, 4096, 4608, 2048, 2560, 1024, 1536, 0, 512):
            wplan(cols(w_in_d, 0, c0, 512))
        for ps_ in range(2):
            wplan(cols(w_in_d, 0, 5120, 8))
            for pr in range(2):
                wplan(cols(w_in_d, 0, 3072 + pr * 512, 512))
                if ps_ == 1:
                    wplan(cols(w_in_d, 0, 4096 + pr * 512, 512))
                wplan(cols(w_in_d, 0, 2048 + pr * 512, 512))
            for pr in range(2):
                if ps_ == 1:
                    wplan(cols(w_in_d, 0, 1024 + pr * 512, 512))
                wplan(cols(w_in_d, 0, 0 + pr * 512, 512))
        for j in range(4):
            wplan(cols(w_mk_d, 0, j * 512, 512))
        for j in range(4):
            wplan(cols(w_mv_d, 0, j * 512, 512))
        def plan_post():
            for j in range(4):
                wplan256(w_out_d, 0, j * 512)
            for j in range(4):
                wplan256(w_cq_d, 0, j * 512)
            for j in range(4):
                wplan256(w_co_d, 0, j * 512)
            for g in range(4):
                for j in range(4):
                    wplan256(w_up_d, 0, g * 2048 + j * 512)
                for j in range(4):
                    wplan256(w_dn_d, g * 2048, j * 512)
        plan_post()

        accn = {"i": 0, "banks": [0, 1]}

        def acc_bank():
            bl = accn["banks"]
            b = bl[accn["i"] % len(bl)]
            accn["i"] += 1
            return b

        trn = {"i": 0}

        def tr_bank():
            b = 2 + trn["i"] % 2
            trn["i"] += 1
            return b

        def load_norm(src, T, gcol0, xn, xnb, scratch):
            stg, stgb, xb2, xbb2, junk2, junkb2, st2, stb2 = scratch
            ng = T // 128

            def stage_a(i):
                s2 = i % 2
                junk, junkb, st, stb = junk2[s2], junkb2[s2], st2[s2], stb2[s2]
                k.dma("sp", stg[s2], src[i * 128:(i + 1) * 128, :], writes=[stgb[s2]])
                act(junk, stg[s2], AF.Square, [stgb[s2]], [junkb, stb], accum_out=st[:, 0:1])
                dve(lambda e, st=st: e.tensor_scalar(st[:, 1:2], st[:, 0:1], 1.0 / 2048, EPS, op0=ALU.mult, op1=ALU.add), [stb], [stb])
                act(st[:, 2:3], st[:, 1:2], AF.Sqrt, [stb], [stb])
                dve(lambda e, st=st: e.reciprocal(st[:, 3:4], st[:, 2:3]), [stb], [stb])

            def stage_b(i):
                s2 = i % 2
                xb, xbb, st, stb = xb2[s2], xbb2[s2], st2[s2], stb2[s2]
                dve(lambda e, s2=s2, xb=xb, st=st: e.tensor_scalar(xb, stg[s2], st[:, 3:4], None, op0=ALU.mult), [stb, stgb[s2]], [xbb])
                for hh in range(2):
                    b = tr_bank()
                    pv = psb(b).rearrange("p (a b) -> p a b", a=8)
                    for c in range(8):
                        cc = hh * 8 + c
                        tr(pv[:, c, :], xb[:, cc * 128:(cc + 1) * 128], idb, [xbb, b_const], [pbuf[b]], inc=(c == 7))
                    g = prm[:, gcol0 + hh * 8:gcol0 + hh * 8 + 8].unsqueeze(2).to_broadcast([128, 8, 128])
                    dve(lambda e, pv=pv, g=g, hh=hh, i=i: e.tensor_tensor(xn[:, hh * 8:hh * 8 + 8, i * 128:(i + 1) * 128], pv, g, ALU.mult),
                        [pbuf[b], b_const], [xnb[i]])
            stage_a(0)
            for i in range(ng):
                if i + 1 < ng:
                    stage_a(i + 1)
                stage_b(i)

        def fm_block(slot, sb_, nchunks, xin, xin_bufs, tiles, epi, kc=16):
            for j in range(nchunks):
                for ti, (t0, n) in enumerate(tiles):
                    b = acc_bank()
                    for c in range(kc):
                        mm(PS[:, b, 0:n], slot[:, c, j * 128:(j + 1) * 128], xin[:, c, t0:t0 + n], c == 0, c == kc - 1,
                           [sb_] + xin_bufs(t0, n), [pbuf[b]], inc=(c == kc - 1))
                    epi(j, ti, t0, n, PS[:, b, 0:n], pbuf[b])

        def tm_block(slot, sb_, ncols, xin, xin_bufs, nchunk_tok, epi, kc=16):
            for i in range(nchunk_tok):
                b = acc_bank()
                for c in range(kc):
                    mm(PS[:, b, 0:ncols], xin[:, c, i * 128:(i + 1) * 128], slot[:, c, 0:ncols], c == 0, c == kc - 1,
                       [sb_] + xin_bufs(i * 128, 128), [pbuf[b]], inc=(c == kc - 1))
                epi(i, PS[:, b, 0:ncols], pbuf[b])

        chk(12)
        NS = 16
        mS = A.mark()
        b_yTs_r, b_yTs_m = Buf("yTs_r"), Buf("yTs_m")
        b_ssum_s = Buf("ssum_s")
        xnS = A.alloc([16, NS], BF16)
        b_xnS = Buf("xnS")
        bc = lambda ap, shape: ap.to_broadcast(shape)

        def dv(fn, reads, writes):
            k.op("dve", fn, reads=reads, writes=writes)
        mS1 = A.mark()
        stgS = A.alloc([2048], F32)
        xbS = A.alloc([2048], BF16)
        junkS = A.alloc([2048], BF16)
        stS = A.alloc([4], F32)
        b_stgS, b_l = Buf("stgS"), Buf("l")
        k.dma("sp", stgS[0:16], xs_d, writes=[b_stgS])
        act(junkS[0:16], stgS[0:16], AF.Square, [b_stgS], [b_l], accum_out=stS[0:16, 0:1])
        dv(lambda e: e.tensor_scalar(stS[0:16, 1:2], stS[0:16, 0:1], 1.0 / 2048, EPS, op0=ALU.mult, op1=ALU.add), [b_l], [b_l])
        act(stS[0:16, 2:3], stS[0:16, 1:2], AF.Sqrt, [b_l], [b_l])
        dv(lambda e: e.reciprocal(stS[0:16, 3:4], stS[0:16, 2:3]), [b_l], [b_l])
        dv(lambda e: e.tensor_scalar(xbS[0:16], stgS[0:16], stS[0:16, 3:4], None, op0=ALU.mult), [b_l, b_stgS], [b_l])
        for hh in range(2):
            b = tr_bank()
            pv = psb(b)[:, 0:8 * 16].rearrange("p (a b) -> p a b", a=8)
            for c in range(8):
                cc = hh * 8 + c
                tr(pv[:, c, :], xbS[0:16, cc * 128:(cc + 1) * 128], idb[0:16, 0:16], [b_l, b_const], [pbuf[b]], inc=(c == 7))
            g = prm[:, P_GMIX + hh * 8:P_GMIX + hh * 8 + 8].unsqueeze(2).to_broadcast([128, 8, 16])
            dv(lambda e, pv=pv, g=g, hh=hh: e.tensor_tensor(xnS[:, hh * 8:hh * 8 + 8, :], pv, g, ALU.mult), [pbuf[b], b_const], [b_xnS])
        k.barrier()
        A.release(mS1)

        gz = A.alloc([8], F32)
        v_s = A.alloc([1024], F32)
        og_s = A.alloc([1024], F32)
        u_s = A.alloc([1024], F32)
        gel_s = A.alloc([8, NS], F32)
        xr_s = A.alloc([8, NS], F32)
        b_z = {n_: Buf(n_) for n_ in "gz v og u gel xr".split()}

        def tm_s(ncols, epi):
            slot, sb_ = wnext()
            b = acc_bank()
            for c in range(16):
                mm(PS[0:16, b, 0:ncols], xnS[:, c, :], slot[:, c, 0:ncols], c == 0, c == 15, [sb_, b_xnS], [pbuf[b]], inc=(c == 15))
            epi(PS[0:16, b, 0:ncols], pbuf[b])

        def fm_s(epi):
            slot, sb_ = wnext()
            b = acc_bank()
            for j in range(4):
                for c in range(16):
                    mm(PS[:, b, j * 16:(j + 1) * 16], slot[:, c, j * 128:(j + 1) * 128], xnS[:, c, :], c == 0, c == 15, [sb_, b_xnS], [pbuf[b]],
                       inc=(c == 15 and j == 3))
            epi(PS[:, b, 0:64].rearrange("p (j t) -> p j t", j=4), pbuf[b])
        tm_s(8, lambda acc, ab: act(gz[0:16], acc, AF.Copy, [ab], [b_z["gz"]]))
        for pr in range(2):
            tm_s(512, lambda acc, ab, pr=pr: act(v_s[0:16, pr * 512:(pr + 1) * 512], acc, AF.Copy, [ab], [b_z["v"]]))
        for pr in range(2):
            tm_s(512, lambda acc, ab, pr=pr: act(og_s[0:16, pr * 512:(pr + 1) * 512], acc, AF.Sigmoid, [ab], [b_z["og"]]))
        for pr in range(2):
            tm_s(512, lambda acc, ab, pr=pr: act(u_s[0:16, pr * 512:(pr + 1) * 512], acc, AF.Copy, [ab], [b_z["u"]]))
        for pr in range(2):
            fm_s(lambda acc, ab, pr=pr: act(gel_s[:, pr * 4:pr * 4 + 4, :], acc, AF.Gelu, [ab], [b_z["gel"]]))
        for pr in range(2):
            fm_s(lambda acc, ab, pr=pr: act(xr_s[:, pr * 4:pr * 4 + 4, :], acc, AF.Copy, [ab], [b_z["xr"]]))

        mS2 = A.mark()
        sh_tok = A.alloc([1024], F32)
        src_tok = A.alloc([3, 1024], F32)
        b_sh, b_src = Buf("sh"), Buf("src")
        k.dma("sp", sh_tok[0:16], sh_d, writes=[b_sh])
        k.dma("sp", src_tok[0:16], src_d, writes=[b_src])
        h0T = A.alloc([8, NS], F32)
        bufT = A.alloc([8, 3, NS], F32)
        b_h0T, b_bufT = Buf("h0T"), Buf("bufT")
        b = tr_bank()
        for c in range(8):
            tr(PS[:, b, c * 16:(c + 1) * 16], sh_tok[0:16, c * 128:(c + 1) * 128], idf[0:16, 0:16], [b_sh, b_const], [pbuf[b]], inc=(c == 7))
        dv(lambda e, b=b: e.tensor_copy(h0T, PS[:, b, 0:128].rearrange("p (c t) -> p c t", c=8)), [pbuf[b]], [b_h0T])
        b = tr_bank()
        for c in range(8):
            for j in range(3):
                tr(PS[:, b, (c * 3 + j) * 16:(c * 3 + j + 1) * 16], src_tok[0:16, j, c * 128:(c + 1) * 128], idf[0:16, 0:16], [b_src, b_const], [pbuf[b]],
                   inc=(c == 7 and j == 2))
        dv(lambda e, b=b: e.tensor_copy(bufT, PS[:, b, 0:384].rearrange("p (c j t) -> p c j t", c=8, j=3)), [pbuf[b]], [b_bufT])
        xcS = A.alloc([8, NS], F32)
        tS = A.alloc([8, NS], F32)
        xcbS = A.alloc([8, NS], BF16)
        rS = A.alloc([8, NS], F32)
        iS = A.alloc([8, NS], F32)
        aS = A.alloc([8, NS], F32)
        muS = A.alloc([8, NS], F32)
        hS = A.alloc([8, NS], F32)
        srcN = A.alloc([8, 3, NS], F32)
        b_r = [Buf("r%d" % i) for i in range(10)]
        Wt = lambda tap: prm[:, P_CRW + tap * 8:P_CRW + tap * 8 + 8].unsqueeze(2).to_broadcast([128, 8, NS])
        pbS = lambda col: prm[:, col:col + 8].unsqueeze(2).to_broadcast([128, 8, NS])
        dv(lambda e: e.tensor_tensor(xcS, bufT[:, :, 0, :], Wt(0), ALU.mult), [b_bufT, b_const], [b_r[0]])
        for j in (1, 2):
            dv(lambda e, j=j: e.tensor_tensor(tS, bufT[:, :, j, :], Wt(j), ALU.mult), [b_bufT, b_const], [b_r[1]])
            dv(lambda e: e.tensor_tensor(xcS, xcS, tS, ALU.add), [b_r[0], b_r[1]], [b_r[0]])
        dv(lambda e: e.tensor_tensor(tS, xr_s, Wt(3), ALU.mult), [b_z["xr"], b_const], [b_r[1]])
        dv(lambda e: e.tensor_tensor(xcS, xcS, tS, ALU.add), [b_r[0], b_r[1]], [b_r[0]])
        dv(lambda e: e.tensor_tensor(xcS, xcS, pbS(P_CRB), ALU.add), [b_r[0], b_const], [b_r[0]])
        dv(lambda e: e.tensor_copy(xcbS, xcS), [b_r[0]], [b_r[2]])
        for (W, dst, db, pcol) in ((lwab, rS, b_r[3], P_LBA), (lwxb, iS, b_r[4], P_LBX)):
            b = acc_bank()
            for c in range(8):
                mm(PS[:, b, c * 16:(c + 1) * 16], W[:, c, :], xcbS[:, c, :], True, True, [b_cw, b_r[2]], [pbuf[b]], inc=(c == 7))
            dv(lambda e, b=b, dst=dst, pcol=pcol: e.tensor_tensor(dst, PS[:, b, 0:128].rearrange("p (c t) -> p c t", c=8), pbS(pcol), ALU.add),
               [pbuf[b], b_const], [db])
            act(dst, dst, AF.Sigmoid, [db], [db])
        dv(lambda e: e.tensor_tensor(tS, rS, ccol[:, 0:8].unsqueeze(2).to_broadcast([128, 8, NS]), ALU.mult), [b_r[3], b_const], [b_r[1]])
        act(aS, tS, AF.Exp, [b_r[1]], [b_r[5]])
        act(muS, tS, AF.Exp, [b_r[1]], [b_r[6]], scale=2.0)
        act(muS, muS, AF.Sqrt, [b_r[6]], [b_r[6]], scale=-1.0, bias=1.0)
        dv(lambda e: e.tensor_tensor(iS, iS, xcS, ALU.mult), [b_r[4], b_r[0]], [b_r[4]])
        dv(lambda e: e.tensor_tensor(muS, muS, iS, ALU.mult), [b_r[6], b_r[4]], [b_r[6]])
        dv(lambda e: e.tensor_tensor(hS, aS, h0T, ALU.mult), [b_r[5], b_h0T], [b_r[7]])
        dv(lambda e: e.tensor_tensor(hS, hS, muS, ALU.add), [b_r[7], b_r[6]], [b_r[7]])
        k.dma("sp", osh_d, hS, reads=[b_r[7]])
        dv(lambda e: e.tensor_copy(srcN[:, :, 0:2, :], bufT[:, :, 1:3, :]), [b_bufT], [b_r[8]])
        dv(lambda e: e.tensor_copy(srcN[:, :, 2, :], xr_s), [b_z["xr"], b_r[8]], [b_r[8]])
        k.dma("sp", osrc_d, srcN, reads=[b_r[8]])
        dv(lambda e: e.tensor_tensor(tS, hS, gel_s, ALU.mult), [b_r[7], b_z["gel"]], [b_r[1]])
        dv(lambda e: e.tensor_tensor(yTs[:, 0:8, :], tS, pbS(P_GRN), ALU.mult), [b_r[1], b_const], [b_yTs_r])
        dv(lambda e: e.tensor_tensor(rS, tS, tS, ALU.mult), [b_r[1], b_r[3]], [b_r[3]])
        dv(lambda e: e.tensor_reduce(ssum_s, rS.rearrange("p c t -> p t c"), AX.X, ALU.add), [b_r[3]], [b_ssum_s])
        k.barrier()
        A.release(mS2)

        mS3 = A.mark()
        smc = A0.alloc([3, 1024], F32)
        snt = A.alloc([4, 256], F32)
        smt = A.alloc([4], F32)
        cmw = A0.alloc([4, 1024], F32)
        cmb = A0.alloc([1024], F32)
        gb = A.alloc([8], F32)
        b_in = Buf("sin")
        for dst, srcd in ((smc, smc_d), (snt, sn_d), (smt, sm_d), (cmw, cmw_d), (cmb, cmb_d), (gb, gb_d)):
            k.dma("sp", dst[0:16], srcd, writes=[b_in])
        P16 = slice(0, 16)
        ucp = A.alloc([1024], F32)
        t1 = A.alloc([1024], F32)
        ucbS = A.alloc([1024], BF16)
        ucT = A.alloc([8, NS], BF16)
        q_s = A.alloc([1024], F32)
        k_s = A.alloc([1024], F32)
        G = A.alloc([48], F32)
        b_m = [Buf("m%d" % i) for i in range(16)]
        dv(lambda e: e.tensor_tensor(ucp[P16], smc[P16, 0, :], cmw[P16, 0, :], ALU.mult), [b_in], [b_m[0]])
        for j in (1, 2):
            dv(lambda e, j=j: e.tensor_tensor(t1[P16], smc[P16, j, :], cmw[P16, j, :], ALU.mult), [b_in], [b_m[1]])
            dv(lambda e: e.tensor_tensor(ucp[P16], ucp[P16], t1[P16], ALU.add), [b_m[0], b_m[1]], [b_m[0]])
        dv(lambda e: e.tensor_tensor(t1[P16], u_s[P16], cmw[P16, 3, :], ALU.mult), [b_in, b_z["u"]], [b_m[1]])
        dv(lambda e: e.tensor_tensor(ucp[P16], ucp[P16], t1[P16], ALU.add), [b_m[0], b_m[1]], [b_m[0]])
        dv(lambda e: e.tensor_tensor(ucp[P16], ucp[P16], cmb[P16], ALU.add), [b_m[0], b_in], [b_m[0]])
        act(ucbS[P16], ucp[P16], AF.Silu, [b_m[0]], [b_m[2]])
        k.dma("sp", osmc_d[:, 0:2, :], smc[P16, 1:3, :], reads=[b_in])
        k.dma("sp", osmc_d[:, 2, :], u_s[P16], reads=[b_z["u"]])
        b = tr_bank()
        pv = psb(b)[:, 0:128].rearrange("p (a b) -> p a b", a=8)
        for c in range(8):
            tr(pv[:, c, :], ucbS[P16, c * 128:(c + 1) * 128], idb[0:16, 0:16], [b_m[2], b_const], [pbuf[b]], inc=(c == 7))
        dv(lambda e, pv=pv: e.tensor_copy(ucT, pv), [pbuf[b]], [b_m[4]])
        for h in range(4):
            for (W, dst, db, scl) in ((wqb, q_s, b_m[5], 1.0), (wkb, k_s, b_m[6], 1.0 / 16)):
                b = acc_bank()
                for ic in range(2):
                    mm(PS[0:16, b, 0:256], ucT[:, h * 2 + ic, :], W[:, h, ic, :], ic == 0, ic == 1, [b_cw, b_m[4]], [pbuf[b]], inc=(ic == 1))
                act(dst[P16, h * 256:(h + 1) * 256], PS[0:16, b, 0:256], AF.Copy, [pbuf[b]], [db], scale=scl)
        gG = lambda i: G[P16, i * 4:(i + 1) * 4]
        b_G = Buf("G")
        dv(lambda e: e.tensor_tensor(G[P16, 0:8], gz[P16], gb[P16], ALU.add), [b_z["gz"], b_in], [b_G])
        act(gG(1), gG(1), AF.Exp, [b_G], [b_G], scale=-1.0)
        act(gG(1), gG(1), AF.Ln, [b_G], [b_G], bias=1.0)
        dv(lambda e: e.tensor_tensor(gG(2), smt[P16], gG(1), ALU.subtract), [b_G, b_in], [b_G])
        dv(lambda e: e.tensor_tensor(gG(3), gG(2), gG(0), ALU.max), [b_G], [b_G])
        k.dma("sp", osm_d, gG(3), reads=[b_G])
        dv(lambda e: e.tensor_tensor(gG(4), gG(2), gG(3), ALU.subtract), [b_G], [b_G])
        act(gG(4), gG(4), AF.Exp, [b_G], [b_G])
        dv(lambda e: e.tensor_tensor(gG(5), gG(0), gG(3), ALU.subtract), [b_G], [b_G])
        act(gG(5), gG(5), AF.Exp, [b_G], [b_G])
        act(gG(6), gG(3), AF.Exp, [b_G], [b_G], scale=-1.0)
        v4 = lambda ap: ap.rearrange("p (h d) -> p h d", h=4)
        g4 = lambda i: gG(i).unsqueeze(2).to_broadcast([16, 4, 256])
        dv(lambda e: e.tensor_tensor(t1[P16], q_s[P16], k_s[P16], ALU.mult), [b_m[5], b_m[6]], [b_m[1]])
        dv(lambda e: e.tensor_reduce(gG(7), v4(t1[P16]), AX.X, ALU.add), [b_m[1], b_G], [b_G])
        dv(lambda e: e.tensor_tensor(v4(t1[P16]), v4(q_s[P16]), snt[P16], ALU.mult), [b_m[5], b_in, b_G], [b_m[1]])
        dv(lambda e: e.tensor_reduce(gG(8), v4(t1[P16]), AX.X, ALU.add), [b_m[1], b_G], [b_G])
        dv(lambda e: e.tensor_tensor(gG(9), gG(7), gG(5), ALU.mult), [b_G], [b_G])
        dv(lambda e: e.tensor_tensor(gG(10), gG(4), gG(8), ALU.mult), [b_G], [b_G])
        dv(lambda e: e.tensor_tensor(gG(10), gG(10), gG(9), ALU.add), [b_G], [b_G])
        act(gG(10), gG(10), AF.Abs, [b_G], [b_G])
        dv(lambda e: e.tensor_tensor(gG(10), gG(10), gG(6), ALU.max), [b_G], [b_G])
        dv(lambda e: e.reciprocal(gG(10), gG(10)), [b_G], [b_G])
        nN = A.alloc([4, 256], F32)
        gvS = A.alloc([4, 256], F32)
        dv(lambda e: e.tensor_tensor(nN[P16], snt[P16], g4(4), ALU.mult), [b_in, b_G], [b_m[7]])
        dv(lambda e: e.tensor_tensor(v4(t1[P16]), v4(k_s[P16]), g4(5), ALU.mult), [b_m[6], b_G, b_m[1]], [b_m[1]])
        dv(lambda e: e.tensor_tensor(nN[P16], nN[P16], v4(t1[P16]), ALU.add), [b_m[7], b_m[1]], [b_m[7]])
        k.dma("sp", osn_d, nN[P16], reads=[b_m[7]])
        dv(lambda e: e.tensor_tensor(gvS[P16], v4(v_s[P16]), g4(5), ALU.mult), [b_z["v"], b_G], [b_m[8]])
        Cq = A.alloc([4, 256], F32)
        b_Cq = Buf("Cq")
        selT = A.alloc([16 * 128], F32)
        b_selT = Buf("selT")
        k.dma("sp", selT[P16], seltok_d, writes=[b_selT])
        vT = A.alloc([8, NS], F32)
        CqT = A.alloc([8, NS], F32)
        wgR = A.alloc([NS, 8], F32)
        qR = [A.alloc([1024], F32) for _ in range(2)]
        kR = [A.alloc([1024], F32) for _ in range(2)]
        Ct = [A.alloc([8, 256], F32) for _ in range(2)]
        jk = A.alloc([256], F32)
        tT = [A.alloc([256], F32) for _ in range(2)]
        b_vT, b_CqT, b_wgR, b_jk = [Buf(x) for x in "vT CqT wgR jk".split()]
        b_tT = [Buf("tT0"), Buf("tT1")]
        b_qR, b_kR, b_Ct = [Buf("qR0"), Buf("qR1")], [Buf("kR0"), Buf("kR1")], [Buf("Ct0"), Buf("Ct1")]
        b = tr_bank()
        for c in range(8):
            tr(PS[:, b, c * 16:(c + 1) * 16], v_s[P16, c * 128:(c + 1) * 128], idf[0:16, 0:16], [b_z["v"], b_const], [pbuf[b]], inc=(c == 7))
        dv(lambda e, b=b: e.tensor_copy(vT, PS[:, b, 0:128].rearrange("p (c t) -> p c t", c=8)), [pbuf[b]], [b_vT])
        b = acc_bank()
        for tok in range(NS):
            mm(PS[:, b, tok * 8:(tok + 1) * 8], selT[P16, tok * 128:(tok + 1) * 128], G[P16, 16:24], True, True, [b_selT, b_G], [pbuf[b]], inc=(tok == NS - 1))
        dv(lambda e, b=b: e.tensor_copy(wgR, PS[:, b, 0:128].rearrange("p (t g) -> p t g", t=NS)), [pbuf[b]], [b_wgR])
        def c_prefetch(tok):
            s2 = tok % 2
            k.dma("sp", Ct[s2], sC_d[tok].rearrange("h (vh p) k -> p (h vh) k", p=128), writes=[b_Ct[s2]])
            for (src, dstR, dbR, sb1) in ((q_s, qR, b_qR, b_m[5]), (k_s, kR, b_kR, b_m[6])):
                for hh in range(2):
                    bb = 4 + (hh if src is q_s else 2 + hh)
                    mm(PS[:, bb, :], selT[P16, tok * 128:(tok + 1) * 128], src[P16, hh * 512:(hh + 1) * 512], True, True, [b_selT, sb1], [pbuf[bb]], inc=True)
                    act(dstR[s2][:, hh * 512:(hh + 1) * 512], PS[:, bb, :], AF.Copy, [pbuf[bb]], [dbR[s2]])
        c_prefetch(0)
        for tok in range(NS):
            s2 = tok % 2
            if tok + 1 < NS:
                c_prefetch(tok + 1)
            for hv in range(8):
                h = hv // 2
                dv(lambda e, s2=s2, hv=hv, h=h, tok=tok: e.scalar_tensor_tensor(jk, Ct[s2][:, hv, :], 1.0, qR[s2][:, h * 256:(h + 1) * 256], op0=ALU.mult, op1=ALU.mult,
                                                                               accum_out=CqT[:, hv, tok:tok + 1]),
                   [b_Ct[s2], b_qR[s2], b_CqT], [b_jk, b_CqT])
                k.op("pool", lambda e, s2=s2, hv=hv, h=h, tok=tok: e.tensor_scalar(tT[hv % 2], kR[s2][:, h * 256:(h + 1) * 256], vT[:, hv, tok:tok + 1], wgR[:, tok, 4 + h:5 + h],
                                                                                  op0=ALU.mult, op1=ALU.mult),
                     reads=[b_kR[s2], b_vT, b_wgR], writes=[b_tT[hv % 2]])
                dv(lambda e, s2=s2, hv=hv, h=h, tok=tok: e.scalar_tensor_tensor(Ct[s2][:, hv, :], Ct[s2][:, hv, :], wgR[:, tok, h:h + 1], tT[hv % 2], op0=ALU.mult, op1=ALU.add),
                   [b_Ct[s2], b_wgR, b_tT[hv % 2]], [b_Ct[s2]])
            k.dma("sp", osC_d[tok].rearrange("h (vh p) k -> p (h vh) k", p=128), Ct[s2], reads=[b_Ct[s2]])
        for q4 in range(2):
            b = tr_bank()
            for c in range(4):
                cc = q4 * 4 + c
                tr(PS[0:16, b, c * 128:(c + 1) * 128], CqT[:, cc, :], idf, [b_CqT, b_const], [pbuf[b]], inc=(c == 3))
            dv(lambda e, b=b, q4=q4: e.tensor_copy(Cq[P16, q4 * 2:q4 * 2 + 2, :], PS[0:16, b, :].rearrange("p (h d) -> p h d", h=2)), [pbuf[b]], [b_Cq])
        hN = A.alloc([4, 256], F32)
        dv(lambda e: e.tensor_tensor(hN[P16], Cq[P16], g4(4), ALU.mult), [b_Cq, b_G], [b_m[9]])
        dv(lambda e: e.tensor_tensor(v4(t1[P16]), v4(v_s[P16]), g4(9), ALU.mult), [b_z["v"], b_G, b_m[1]], [b_m[1]])
        dv(lambda e: e.tensor_tensor(hN[P16], hN[P16], v4(t1[P16]), ALU.add), [b_m[9], b_m[1]], [b_m[9]])
        dv(lambda e: e.tensor_tensor(hN[P16], hN[P16], g4(10), ALU.mult), [b_m[9], b_G], [b_m[9]])
        dv(lambda e: e.tensor_tensor(v4(t1[P16]), hN[P16], hN[P16], ALU.mult), [b_m[9], b_m[1]], [b_m[1]])
        dv(lambda e: e.tensor_reduce(gG(11), v4(t1[P16]), AX.X, ALU.add), [b_m[1], b_G], [b_G])
        dv(lambda e: e.tensor_scalar(gG(11), gG(11), 1.0 / 256, EPS, op0=ALU.mult, op1=ALU.add), [b_G], [b_G])
        act(gG(11), gG(11), AF.Sqrt, [b_G], [b_G])
        dv(lambda e: e.reciprocal(gG(11), gG(11)), [b_G], [b_G])
        dv(lambda e: e.tensor_tensor(hN[P16], hN[P16], g4(11), ALU.mult), [b_m[9], b_G], [b_m[9]])
        dv(lambda e: e.tensor_tensor(hN[P16], hN[P16], gml[P16].unsqueeze(1).to_broadcast([16, 4, 256]), ALU.mult), [b_m[9], b_const], [b_m[9]])
        dv(lambda e: e.tensor_tensor(v4(ucbS[P16]), hN[P16], v4(og_s[P16]), ALU.mult), [b_m[9], b_z["og"], b_m[2], b_m[4]], [b_m[2]])
        b = tr_bank()
        pv = psb(b)[:, 0:128].rearrange("p (a b) -> p a b", a=8)
        for c in range(8):
            tr(pv[:, c, :], ucbS[P16, c * 128:(c + 1) * 128], idb[0:16, 0:16], [b_m[2], b_const], [pbuf[b]], inc=(c == 7))
        dv(lambda e, pv=pv: e.tensor_copy(yTs[:, 8:16, :], pv), [pbuf[b]], [b_yTs_m])
        k.barrier()
        A.release(mS3)
        k.barrier()
        A.release(mS)
        A0.release(R0_LO)

        m_mix = A.mark()
        yT = A0.alloc([16, 1024], BF16)
        ssum = A0.alloc([1024], F32)
        xn = A.alloc([16, 1024], BF16)
        xnb = [Buf("xn%d" % i) for i in range(8)]
        yTb = [[Buf("yT%d_%d" % (c, t)) for t in range(2)] for c in range(16)]
        b_ssum = Buf("ssum")
        TILES = [(0, 512), (512, 512)]

        def xn_bufs(t0, n):
            return xnb[t0 // 128:(t0 + n + 127) // 128]

        for ps_ in range(2):
            main = ps_ == 1
            src = xm_d if main else xp_d
            m0 = A.mark()
            stg = [A.alloc([2048], F32) for _ in range(2)]
            scratch = (stg, [Buf("stg0"), Buf("stg1")], [A.alloc([2048], BF16) for _ in range(2)], [Buf("xb0"), Buf("xb1")],
                       [A.alloc([2048], BF16) for _ in range(2)], [Buf("jk0"), Buf("jk1")], [A.alloc([4], F32) for _ in range(2)], [Buf("st0"), Buf("st1")])
            load_norm(src, 1024, P_GMIX, xn, xnb, scratch)
            k.barrier()
            chk(2)
            A.release(m0)

            if main:
                dve(lambda e: e.tensor_scalar(C32, C32, maskc[:, 0:1], None, op0=ALU.mult), [b_C32, b_const], [b_C32])
                dve(lambda e: e.tensor_scalar(hcar, hcar, maskc[:, 0:1], None, op0=ALU.mult), [b_hcar, b_const], [b_hcar])
                dve(lambda e: e.tensor_scalar(gcar, gcar, maskc[0:4, 0:1], None, op0=ALU.mult), [b_gcar, b_const], [b_gcar])
                dve(lambda e: e.tensor_scalar(rtail, rtail, maskc[:, 0:1], None, op0=ALU.mult), [b_rtail, b_const], [b_rtail])
                dve(lambda e: e.tensor_scalar(mtail, mtail, maskc[:, 0:1], None, op0=ALU.mult), [b_mtail, b_const], [b_mtail])
                dve(lambda e: e.memset(ssum, 0.0), [], [b_ssum])
                chk(20)

            m1 = A.mark()
            R_B = A.alloc([1024], F32, parts=4)
            R_A = A.alloc([1024], F32, parts=4)
            R_ig = R_A
            R_M = A.alloc([1024], F32, parts=4)
            R_w = R_B
            R_g = A.alloc([1024], F32, parts=4)
            R_e = A.alloc([1024], F32, parts=4)
            R_s = A.alloc([16], F32, parts=4)
            gcols = A.alloc([8, 4, 4], F32)
            gsrep = A.alloc([4, 8], F32)
            b_rows = Buf("rows")
            b_gcols = Buf("gcols")
            b_gsrep = Buf("gsrep")

            slot, sb_ = wnext()
            for gi in range(2):
                for ti, (t0, n) in enumerate(TILES):
                    b = acc_bank()
                    for c in range(16):
                        mm(PS[0:4, b, 0:n], slot[:, c, gi * 4:gi * 4 + 4], xn[:, c, t0:t0 + n], c == 0, c == 15,
                           [sb_] + xn_bufs(t0, n), [pbuf[b]], inc=(c == 15))
                    if gi == 0:
                        act(R_ig[:, t0:t0 + n], PS[0:4, b, 0:n], AF.Identity, [pbuf[b], b_const], [b_rows], bias=prm[0:4, P_BI:P_BI + 1])
                    else:
                        act(R_e[:, t0:t0 + n], PS[0:4, b, 0:n], AF.Exp, [pbuf[b], b_const], [b_rows], scale=-1.0, bias=negbf[:, 0:1])
            act(R_e, R_e, AF.Ln, [b_rows], [b_rows], bias=1.0)
            dve(lambda e: e.tensor_tensor_scan(R_B, onesf[0:4, 0:1].to_broadcast([4, 1024]), R_e, gcar[:, 0:1], ALU.mult, ALU.subtract),
                [b_rows, b_gcar, b_const], [b_rows])
            dve(lambda e: e.tensor_tensor(R_A, R_ig, R_B, ALU.subtract), [b_rows], [b_rows])
            dve(lambda e: e.tensor_tensor_scan(R_M, onesf[0:4, 0:1].to_broadcast([4, 1024]), R_A, gcar[:, 1:2], ALU.mult, ALU.max),
                [b_rows, b_gcar], [b_rows])
            dve(lambda e: e.tensor_copy(R_s[:, 0:1], gcar[:, 1:2]), [b_gcar, b_rows], [b_rows])
            dve(lambda e: e.tensor_copy(R_s[:, 1:8], R_M[:, 127:896:128]), [b_rows], [b_rows])
            dve(lambda e: e.tensor_copy(R_s[:, 8:16], R_M[:, 127:1024:128]), [b_rows], [b_rows])
            v3 = lambda r: r.rearrange("p (c t) -> p c t", c=8)
            dve(lambda e: e.tensor_tensor(v3(R_g), v3(R_A), R_s[:, 8:16].unsqueeze(2).to_broadcast([4, 8, 128]), ALU.subtract), [b_rows], [b_rows])
            act(R_g, R_g, AF.Exp, [b_rows], [b_rows])
            dve(lambda e: e.tensor_tensor(R_e, R_B, R_M, ALU.add), [b_rows], [b_rows])
            dve(lambda e: e.tensor_copy(gcar[:, 2:3], R_e[:, 1023:1024]), [b_rows, b_gcar], [b_gcar])
            act(R_e, R_e, AF.Exp, [b_rows], [b_rows], scale=-1.0)
            dve(lambda e: e.tensor_copy(gcar[:, 0:1], R_B[:, 1023:1024]), [b_rows, b_gcar], [b_gcar])
            dve(lambda e: e.tensor_copy(gcar[:, 1:2], R_M[:, 1023:1024]), [b_rows, b_gcar], [b_gcar])
            dve(lambda e: e.tensor_tensor(v3(R_w), R_s[:, 0:8].unsqueeze(2).to_broadcast([4, 8, 128]), v3(R_M), ALU.subtract), [b_rows], [b_rows])
            act(R_w, R_w, AF.Exp, [b_rows], [b_rows])
            b = 4
            pgc = PS[:, b, 0:128].rearrange("p (c q h) -> p c q h", c=8, q=4)
            for c in range(8):
                for q, R in enumerate((R_A, R_w, R_e, R_g)):
                    tr(pgc[:, c, q, :], R[:, c * 128:(c + 1) * 128], idf[0:4, 0:4], [b_rows, b_const], [pbuf[b]], inc=(c == 7 and q == 3))
            dve(lambda e: e.tensor_copy(gcols, pgc), [pbuf[b]], [b_gcols])
            b = 5
            for h in range(4):
                mm(PS[:, b, h * 8:h * 8 + 8], sel[:, h * 128:(h + 1) * 128], R_w[:, 127:1024:128], True, True,
                   [b_rows, b_const], [pbuf[b]], inc=(h == 3))
            dve(lambda e: e.tensor_copy(gsrep, PS[:, 5, 0:32].rearrange("p (h c) -> p h c", h=4)), [pbuf[5]], [b_gsrep])
            chk(3)

            m2 = A.mark()
            vtok = A.alloc([8, 2, 257], BF16)
            b_vtok = [Buf("vtok%d" % i) for i in range(8)]
            ogt = A.alloc([8, 512], BF16)
            b_ogt = [Buf("og%d" % i) for i in range(8)]
            off_ub = A.top
            ub = A.alloc([4, 1028], BF16)
            ndv = ar_t[:, off_ub:off_ub + 2056].rearrange("p (a b) -> p a b", a=8)
            b_ub = [Buf("ub%d" % j) for j in range(4)]
            uc = A.alloc([4, 1024], BF16)
            b_uc = [[Buf("uc%d_%d" % (j, t)) for t in range(2)] for j in range(4)]
            diag = A.alloc([4, 128], BF16)
            b_diag = Buf("diag")
            qT = A.alloc([2, 1024], BF16)
            kT = A.alloc([2, 1024], BF16)
            ktok = A.alloc([8, 256], BF16)
            b_qT, b_kT, b_ktok = Buf("qT"), Buf("kT"), Buf("ktok")
            wk1 = A.alloc([257], F32)
            Eh = A.alloc([512], BF16)
            Pb = A.alloc([128], BF16)
            gv = A.alloc([257], BF16)
            ytk4 = A.alloc([4, 256], BF16)
            sm8 = A.alloc([32], F32)
            b_wk = [Buf("wk%d" % i) for i in range(8)]
            for pr in range(2):
                dve(lambda e: e.memset(vtok[:, :, :, 256:257], 1.0), [], b_vtok)
                slot, sb_ = wnext()

                def epi_v(i, acc, ab):
                    act(vtok[:, i, :, 0:256], acc.rearrange("p (h d) -> p h d", h=2), AF.Copy, [ab], [b_vtok[i]])
                tm_block(slot, sb_, 512, xn, xn_bufs, 8, epi_v)
                if main:
                    slot, sb_ = wnext()

                    def epi_og(i, acc, ab):
                        act(ogt[:, i, :], acc, AF.Sigmoid, [ab], [b_ogt[i]])
                    tm_block(slot, sb_, 512, xn, xn_bufs, 8, epi_og)
                slot, sb_ = wnext()
                dve(lambda e: e.memset(ub[:, :, 0:4], 0.0), [], b_ub)
                dve(lambda e, pr=pr: e.tensor_copy(ub[:, :, 1:4], mtail[:, pr * 4:pr * 4 + 4, :]), [b_mtail], b_ub)

                def epi_u(j, ti, t0, n, acc, ab, pr=pr):
                    act(ub[:, j, 4 + t0:4 + t0 + n], acc, AF.Copy, [ab], [b_ub[j]])
                    if ti == 1:
                        dve(lambda e: e.tensor_copy(mtail[:, pr * 4 + j, :], acc[:, n - 3:n]), [ab, b_ub[j]], [b_mtail])
                fm_block(slot, sb_, 4, xn, xn_bufs, TILES, epi_u)
                for j in range(4):
                    cg = pr * 4 + j
                    for tap in range(4):
                        dve(lambda e, tap=tap, cg=cg: e.tensor_scalar(diag[:, tap, :], idf, prm[:, P_CMW + tap * 8 + cg:P_CMW + tap * 8 + cg + 1], None, op0=ALU.mult),
                            [b_const], [b_diag])
                    for ti, (t0, n) in enumerate(TILES):
                        b = acc_bank()
                        for tap in range(4):
                            k.tag = "ps%dpr%dj%dti%dtap%d" % (ps_, pr, j, ti, tap)
                            if j > 0:
                                k.trace = False
                            mm(PS[:, b, 0:n], diag[:, tap, :], ub[:, j, t0 + tap + 1:t0 + tap + 1 + n], tap == 0, tap == 3,
                               [b_diag, b_ub[j]], [pbuf[b]], inc=(tap == 3))
                        act(uc[:, j, t0:t0 + n], PS[:, b, 0:n], AF.Silu, [pbuf[b], b_const], [b_uc[j][ti]], bias=prm[:, P_CMB + cg:P_CMB + cg + 1])
                if main and pr == 0:
                    chk(21)
                for hl in range(2):
                    h = pr * 2 + hl
                    ucb = lambda t0, n, hl=hl: [b_uc[hl * 2 + ic][t0 // 512] for ic in range(2)]
                    if main:
                        for (W, dstT, dbf, scl) in ((wqb, qT, b_qT, 1.0), (wkb, kT, b_kT, 1.0 / 16)):
                            for oc in range(2):
                                for ti, (t0, n) in enumerate(TILES):
                                    b = acc_bank()
                                    for ic in range(2):
                                        mm(PS[:, b, 0:n], W[:, h, ic, oc * 128:(oc + 1) * 128], uc[:, hl * 2 + ic, t0:t0 + n], ic == 0, ic == 1,
                                           [b_const] + ucb(t0, n), [pbuf[b]], inc=(ic == 1))
                                    act(dstT[:, oc, t0:t0 + n], PS[:, b, 0:n], AF.Copy, [pbuf[b]], [dbf], scale=scl)
                    for i in range(8):
                        b = acc_bank()
                        for ic in range(2):
                            mm(PS[:, b, 0:256], uc[:, hl * 2 + ic, i * 128:(i + 1) * 128], wkb[:, h, ic, :], ic == 0, ic == 1,
                               [b_const] + ucb(i * 128, 128), [pbuf[b]], inc=(ic == 1))
                        act(ktok[:, i, :], PS[:, b, 0:256], AF.Copy, [pbuf[b]], [b_ktok], scale=1.0 / 16)
                    if main and h == 0:
                        chk(22)
                    act(Cb, C32[:, h, :, :], AF.Copy, [b_C32], [b_Cb])
                    for i in range(8):
                        cs = slice(i * 128, (i + 1) * 128)
                        gc = lambda q, i=i, h=h: gcols[:, i, q, h:h + 1]
                        if main and i % 4 == 0:
                            mm(PS[:, 4, :], sel[:, h * 128:(h + 1) * 128], R_M[:, i * 128:i * 128 + 512], True, True, [b_rows, b_const], [pbuf[4]], inc=True)
                            for i4 in range(4):
                                act(Eh[:, i4 * 128:(i4 + 1) * 128], PS[:, 4, i4 * 128:(i4 + 1) * 128], AF.Exp, [pbuf[4], b_gcols], [b_wk[1]],
                                    scale=-1.0, bias=gcols[:, i + i4, 0, h:h + 1])
                            dve(lambda e: e.tensor_tensor(Eh.rearrange("p (c t) -> p c t", c=4), Eh.rearrange("p (c t) -> p c t", c=4),
                                                          m01.unsqueeze(1).to_broadcast([128, 4, 128]), ALU.mult), [b_wk[1], b_const], [b_wk[1]])
                        dve(lambda e, i=i, hl=hl, gc=gc: e.tensor_scalar(gv, vtok[:, i, hl, :], gc(3), None, op0=ALU.mult), [b_vtok[i], b_gcols], [b_wk[0]])
                        if main:
                            for dc in range(2):
                                mm(PS[:, 1, 0:128], kT[:, dc, cs], qT[:, dc, cs], dc == 0, dc == 1, [b_kT, b_qT], [pbuf[1]], inc=(dc == 1))
                        mm(PS[:, 7, 0:257], ktok[:, i, 0:128], gv, True, True, [b_ktok, b_wk[0]], [pbuf[7]], inc=True)
                        mm(PS[:, 0, 0:257], ktok[:, i, 128:256], gv, True, True, [b_ktok, b_wk[0]], [pbuf[0]], inc=True)
                        if main:
                            dve(lambda e, i=i: e.tensor_tensor(Pb, PS[:, 1, 0:128], Eh[:, (i % 4) * 128:(i % 4 + 1) * 128], ALU.mult), [pbuf[1], b_wk[1]], [b_wk[3]])
                            mm(PS[:, 5, 0:257], Pb, vtok[:, i, hl, :], True, True, [b_wk[3], b_vtok[i]], [pbuf[5]], inc=True)
                            for kc in range(2):
                                mm(PS[:, 6, 0:257], qT[:, kc, cs], Cb[:, kc, :], kc == 0, kc == 1, [b_qT, b_Cb], [pbuf[6]], inc=(kc == 1))
                            act(wk1, PS[:, 6, 0:257], AF.Identity, [pbuf[6], b_gcols], [b_wk[4]], scale=gc(1))
                            dve(lambda e, i=i: e.tensor_tensor(ndv[:, i, :], wk1, PS[:, 5, 0:257], ALU.add), [b_wk[4], pbuf[5]] + b_ub, b_ub)
                        if main and h == 0 and i == 0:
                            chk(23)
                        dve(lambda e, h=h, i=i: e.scalar_tensor_tensor(C32[:, h, 0, :], C32[:, h, 0, :], gsrep[:, h, i:i + 1], PS[:, 7, 0:257], op0=ALU.mult, op1=ALU.add),
                            [b_C32, b_gsrep, pbuf[7]], [b_C32])
                        dve(lambda e, h=h, i=i: e.scalar_tensor_tensor(C32[:, h, 1, :], C32[:, h, 1, :], gsrep[:, h, i:i + 1], PS[:, 0, 0:257], op0=ALU.mult, op1=ALU.add),
                            [b_C32, b_gsrep, pbuf[0]], [b_C32])
                        if main and i < 7:
                            act(Cb, C32[:, h, :, :], AF.Copy, [b_C32], [b_Cb])
                    if main:
                        ndh = ndv[:, :, 0:256]
                        act(sm8[:, 0:8], ndv[:, :, 256], AF.Abs, b_ub, [b_wk[6]])
                        dve(lambda e, h=h: e.tensor_tensor(sm8[:, 0:8], sm8[:, 0:8], gcols[:, :, 2, h], ALU.max), [b_wk[6], b_gcols], [b_wk[6]])
                        dve(lambda e: e.reciprocal(sm8[:, 8:16], sm8[:, 0:8]), [b_wk[6]], [b_wk[6]])
                        dve(lambda e: e.tensor_tensor(ndh, ndh, sm8[:, 8:16].unsqueeze(2).to_broadcast([128, 8, 256]), ALU.mult), [b_wk[6]] + b_ub, b_ub)
                        for i in range(8):
                            act(wk1[:, 0:256], ndv[:, i, 0:256], AF.Square, b_ub + [b_wk[6]], [b_wk[4], b_wk[6]], accum_out=sm8[:, 16 + i:17 + i])
                        dve(lambda e: e.tensor_scalar(sm8[:, 24:32], sm8[:, 16:24], 1.0 / 256, EPS, op0=ALU.mult, op1=ALU.add), [b_wk[6]], [b_wk[6]])
                        act(sm8[:, 24:32], sm8[:, 24:32], AF.Sqrt, [b_wk[6]], [b_wk[6]])
                        dve(lambda e: e.reciprocal(sm8[:, 24:32], sm8[:, 24:32]), [b_wk[6]], [b_wk[6]])
                        dve(lambda e: e.tensor_tensor(ndh, ndh, sm8[:, 24:32].unsqueeze(2).to_broadcast([128, 8, 256]), ALU.mult), [b_wk[6]] + b_ub, b_ub)
                        dve(lambda e: e.tensor_tensor(ndh, ndh, gml.unsqueeze(1).to_broadcast([128, 8, 256]), ALU.mult), [b_const] + b_ub, b_ub)
                        for half in range(2):
                            dve(lambda e, half=half, hl=hl: e.tensor_tensor(ytk4, ndv[:, half * 4:half * 4 + 4, 0:256], ogt[:, half * 4:half * 4 + 4, hl * 256:(hl + 1) * 256], ALU.mult),
                                b_ub + b_ogt[half * 4:half * 4 + 4], [b_wk[7]])
                            bT = tr_bank()
                            pv = psb(bT)
                            for i4 in range(4):
                                for hf in range(2):
                                    tr(pv[:, (hf * 4 + i4) * 128:(hf * 4 + i4 + 1) * 128], ytk4[:, i4, hf * 128:(hf + 1) * 128], idb, [b_wk[7], b_const], [pbuf[bT]],
                                       inc=(i4 == 3 and hf == 1))
                            for hf in range(2):
                                cgl = 8 + h * 2 + hf
                                act(yT[:, cgl, half * 512:(half + 1) * 512], pv[:, hf * 512:(hf + 1) * 512], AF.Copy, [pbuf[bT]], [yTb[cgl][half]])
                if main and pr == 0:
                    chk(24)
            k.barrier()
            A.release(m2)
            if main:
                chk(25)
                k.dma("sp", opC_d, C32, reads=[b_C32])
                k.dma("sp", opm_d, gcar[:, 2:3], reads=[b_gcar])
                k.dma("sp", opmc_d, mtail, reads=[b_mtail])
            k.barrier()
            A.release(m1)

            chk(4 if not main else 6)
            m2 = A.mark()
            xrb = A.alloc([4, 1028], BF16)
            b_xrb = [Buf("xrb%d" % j) for j in range(4)]
            gel = A.alloc([4, 1024], BF16)
            b_gel = [[Buf("gel") for t in range(2)] for j in range(4)]
            dgr = A.alloc([4, 4, 128], BF16)
            b_dgr = Buf("dgr")
            RW = [dict(xc=A.alloc([1024], F32), xcb=A.alloc([1024], BF16), rr=A.alloc([1024], F32), ii=A.alloc([1024], F32),
                       aa=A.alloc([1024], F32), mu=A.alloc([1024], F32), hh_=A.alloc([1024], F32), bw=[Buf("rw%d" % i) for i in range(8)]) for _ in range(2)]
            for pr in range(2):
                if main:
                    slot, sb_ = wnext()

                    def epi_gr(j, ti, t0, n, acc, ab):
                        act(gel[:, j, t0:t0 + n], acc, AF.Gelu, [ab], [b_gel[j][ti]])
                    fm_block(slot, sb_, 4, xn, xn_bufs, TILES, epi_gr)
                slot, sb_ = wnext()
                dve(lambda e: e.memset(xrb[:, :, 0:4], 0.0), [], b_xrb)
                dve(lambda e, pr=pr: e.tensor_copy(xrb[:, :, 1:4], rtail[:, pr * 4:pr * 4 + 4, :]), [b_rtail], b_xrb)

                def epi_xr(j, ti, t0, n, acc, ab, pr=pr):
                    act(xrb[:, j, 4 + t0:4 + t0 + n], acc, AF.Copy, [ab], [b_xrb[j]])
                    if ti == 1:
                        dve(lambda e: e.tensor_copy(rtail[:, pr * 4 + j, :], acc[:, n - 3:n]), [ab, b_xrb[j]], [b_rtail])
                fm_block(slot, sb_, 4, xn, xn_bufs, TILES, epi_xr)
                def rg_front(j, pr=pr):
                    cg = pr * 4 + j
                    rw_ = RW[j % 2]
                    xc, xcb, rr, ii, aa, mu, hh_, bw = rw_['xc'], rw_['xcb'], rw_['rr'], rw_['ii'], rw_['aa'], rw_['mu'], rw_['hh_'], rw_['bw']
                    for ti, (t0, n) in enumerate(TILES):
                        b = acc_bank()
                        for tap in range(4):
                            mm(PS[:, b, 0:n], dgr[:, j, tap, :], xrb[:, j, t0 + tap + 1:t0 + tap + 1 + n], tap == 0, tap == 3,
                               [b_dgr, b_xrb[j]], [pbuf[b]], inc=(tap == 3))
                        act(xc[:, t0:t0 + n], PS[:, b, 0:n], AF.Identity, [pbuf[b], b_const], [bw[0]], bias=prm[:, P_CRB + cg:P_CRB + cg + 1])
                        act(xcb[:, t0:t0 + n], PS[:, b, 0:n], AF.Identity, [pbuf[b], b_const], [bw[1]], bias=prm[:, P_CRB + cg:P_CRB + cg + 1])
                    for (W, dst, db, pb) in ((lwab, rr, bw[2], P_LBA), (lwxb, ii, bw[3], P_LBX)):
                        for ti, (t0, n) in enumerate(TILES):
                            b = acc_bank()
                            mm(PS[:, b, 0:n], W[:, cg, :], xcb[:, t0:t0 + n], True, True, [b_const, bw[1]], [pbuf[b]], inc=True)
                            act(dst[:, t0:t0 + n], PS[:, b, 0:n], AF.Sigmoid, [pbuf[b], b_const], [db], bias=prm[:, pb + cg:pb + cg + 1])
                def rg_back(j, pr=pr):
                    cg = pr * 4 + j
                    rw_ = RW[j % 2]
                    xc, xcb, rr, ii, aa, mu, hh_, bw = rw_['xc'], rw_['xcb'], rw_['rr'], rw_['ii'], rw_['aa'], rw_['mu'], rw_['hh_'], rw_['bw']
                    act(aa, rr, AF.Exp, [bw[2], b_const], [bw[4]], scale=ccol[:, cg:cg + 1])
                    act(mu, rr, AF.Exp, [bw[2], b_const], [bw[5]], scale=ccol2[:, cg:cg + 1])
                    act(mu, mu, AF.Sqrt, [bw[5]], [bw[5]], scale=-1.0, bias=1.0)
                    dve(lambda e, xc=xc, xcb=xcb, rr=rr, ii=ii, aa=aa, mu=mu, hh_=hh_: e.tensor_tensor(ii, ii, xc, ALU.mult), [bw[3], bw[0]], [bw[3]])
                    dve(lambda e, xc=xc, xcb=xcb, rr=rr, ii=ii, aa=aa, mu=mu, hh_=hh_: e.tensor_tensor(mu, mu, ii, ALU.mult), [bw[5], bw[3]], [bw[5]])
                    dve(lambda e, cg=cg, xc=xc, xcb=xcb, rr=rr, ii=ii, aa=aa, mu=mu, hh_=hh_: e.tensor_tensor_scan(hh_, aa, mu, hcar[:, cg:cg + 1], ALU.mult, ALU.add), [bw[4], bw[5], b_hcar], [bw[6]])
                    dve(lambda e, cg=cg, xc=xc, xcb=xcb, rr=rr, ii=ii, aa=aa, mu=mu, hh_=hh_: e.tensor_copy(hcar[:, cg:cg + 1], hh_[:, 1023:1024]), [bw[6], b_hcar], [b_hcar])
                    if main:
                        dve(lambda e, j=j, xc=xc, xcb=xcb, rr=rr, ii=ii, aa=aa, mu=mu, hh_=hh_: e.tensor_tensor(hh_, hh_, gel[:, j, :], ALU.mult), [bw[6]] + b_gel[j], [bw[6]])
                        dve(lambda e, cg=cg, xc=xc, xcb=xcb, rr=rr, ii=ii, aa=aa, mu=mu, hh_=hh_: e.tensor_scalar(yT[:, cg, :], hh_, prm[:, P_GRN + cg:P_GRN + cg + 1], None, op0=ALU.mult),
                            [bw[6], b_const], yTb[cg])
                        dve(lambda e, xc=xc, xcb=xcb, rr=rr, ii=ii, aa=aa, mu=mu, hh_=hh_: e.tensor_tensor(aa, hh_, hh_, ALU.mult), [bw[6], bw[4]], [bw[4]])
                        dve(lambda e, xc=xc, xcb=xcb, rr=rr, ii=ii, aa=aa, mu=mu, hh_=hh_: e.tensor_tensor(ssum, ssum, aa, ALU.add), [bw[4], b_ssum], [b_ssum])
                for j in range(4):
                    for tap in range(4):
                        dve(lambda e, tap=tap, j=j, cg=pr * 4 + j: e.tensor_scalar(dgr[:, j, tap, :], idf, prm[:, P_CRW + tap * 8 + cg:P_CRW + tap * 8 + cg + 1], None, op0=ALU.mult),
                            [b_const], [b_dgr])
                rg_front(0)
                for j in range(4):
                    if j + 1 < 4:
                        rg_front(j + 1)
                    rg_back(j)
            k.barrier()
            A.release(m2)
            chk(5 if not main else 7)
            if main:
                k.dma("sp", oph_d, hcar, reads=[b_hcar])
                k.dma("sp", oprc_d, rtail, reads=[b_rtail])

        k.barrier()
        A.release(m_mix)
        AM = Arena(ar_t, off_wqb, off_wqb + 4096)
        mkT = AM.alloc([16, 256], BF16)
        b_mkT = [Buf("mkT%d" % c) for c in range(16)]
        mvt = AM.alloc([2, 2048], BF16)
        b_mvt = [[Buf("mvt") for jb in range(4)] for nh in range(2)]
        m4 = A.mark()
        mn = A.alloc([16, 256], BF16)
        mnb = [Buf("mn0"), Buf("mn1")]
        m5 = A.mark()
        stg = [A.alloc([2048], F32) for _ in range(2)]
        scratch = (stg, [Buf("stg0"), Buf("stg1")], [A.alloc([2048], BF16) for _ in range(2)], [Buf("xb0"), Buf("xb1")],
                   [A.alloc([2048], BF16) for _ in range(2)], [Buf("jk0"), Buf("jk1")], [A.alloc([4], F32) for _ in range(2)], [Buf("st0"), Buf("st1")])
        chk(30)
        load_norm(mem_d, 256, P_GMEM, mn, mnb, scratch)
        k.barrier()
        chk(31)
        A.release(m5)
        ost = [A.alloc([4, 256], F32) for _ in range(2)]
        ostb = [Buf("ost0"), Buf("ost1")]
        mn_bufs = lambda t0, n: mnb[t0 // 128:(t0 + n + 127) // 128]
        for jb in range(4):
            slot, sb_ = wnext()
            s2 = jb % 2

            def epi_mk(j, ti, t0, n, acc, ab, jb=jb, s2=s2):
                act(ost[s2][:, j, :], acc, AF.Copy, [ab], [ostb[s2]])
                dve(lambda e: e.tensor_copy(mkT[:, jb * 4 + j, :], ost[s2][:, j, :]), [ostb[s2]], [b_mkT[jb * 4 + j]])
            fm_block(slot, sb_, 4, mn, mn_bufs, [(0, 256)], epi_mk)
            k.dma("sp", omk_d[:, jb * 4:jb * 4 + 4, :], ost[s2], reads=[ostb[s2]])
        chk(32)
        ost2 = [ost[0].rearrange("p a b -> p (a b)")[:, 0:512], ost[1].rearrange("p a b -> p (a b)")[:, 0:512]]
        cnt2 = 0
        for jb in range(4):
            slot, sb_ = wnext()

            def epi_mv(i, acc, ab, jb=jb):
                nonlocal cnt2
                s2 = cnt2 % 2
                cnt2 += 1
                act(ost2[s2], acc, AF.Copy, [ab], [ostb[s2]])
                dve(lambda e: e.tensor_copy(mvt[:, i, jb * 512:(jb + 1) * 512], ost2[s2]), [ostb[s2]], [b_mvt[i][jb]])
                k.dma("sp", omv_d[i * 128:(i + 1) * 128, jb * 512:(jb + 1) * 512], ost2[s2], reads=[ostb[s2]])
            tm_block(slot, sb_, 512, mn, mn_bufs, 2, epi_mv)
        k.barrier()
        A.release(m4)

        chk(8)

        def attn_core(n, heads, kfn, kbuf, vfn, vbuf, qc_, qcb_, oT_, oTb_, ET, b_ET, rden, b_rden):
            for hd in heads:
                for nh in range(2):
                    b = acc_bank()
                    for dc in range(4):
                        c = hd * 4 + dc
                        mm(PS[:, b, 0:n], kfn(hd, dc, nh), qc_[:, c, :], dc == 0, dc == 3,
                           [kbuf(hd, dc), qcb_[c]], [pbuf[b]], inc=(dc == 3))
                    act(ET[:, nh, 0:n], PS[:, b, 0:n], AF.Exp, [pbuf[b]], [b_ET], scale=float(512 ** -0.5))
                b = acc_bank()
                for nh in range(2):
                    mm(PS[:, b, 0:n], onesb, ET[:, nh, 0:n], nh == 0, nh == 1, [b_const, b_ET], [pbuf[b]], inc=(nh == 1))
                dve(lambda e, b=b: e.reciprocal(rden[:, 0:n], PS[:, b, 0:n]), [pbuf[b]], [b_rden])
                for dc in range(4):
                    c = hd * 4 + dc
                    b = acc_bank()
                    for nh in range(2):
                        mm(PS[:, b, 0:n], vfn(hd, dc, nh), ET[:, nh, 0:n], nh == 0, nh == 1,
                           [vbuf(hd, dc, nh), b_ET], [pbuf[b]], inc=(nh == 1))
                    dve(lambda e, b=b, c=c: e.tensor_tensor(oT_[:, c, :], PS[:, b, 0:n], rden[:, 0:n], ALU.mult), [pbuf[b], b_rden], [oTb_[c]])

        def post_tile(tiles, tinfo):
            NT = sum(n_ for _, n_ in tiles)
            nmax = max(n_ for _, n_ in tiles)
            accn["banks"] = [0, 1, 4, 5, 6, 7]
            m_tile = A.mark()
            X = A.alloc([16, NT], F32)
            off_hq = A.top
            hq = A.alloc([16, NT], BF16)
            Xb = [[Buf("X%d_%d" % (c, ti)) for ti in range(len(tiles))] for c in range(16)]
            m3 = A.mark()
            if NT >= 512:
                AH = Arena(ar_t, off_hq, off_hq + 16 * NT // 2)
                stg = [AH.alloc([2048], F32) for _ in range(2)]
            else:
                stg = [A.alloc([2048], F32) for _ in range(2)]
            stgb = [Buf("stg0"), Buf("stg1")]
            rstd = A.alloc([NT], F32)
            b_rstd = Buf("rstd")
            tmp = A.alloc([nmax], F32)
            b_tmp = Buf("tmp")
            gi = 0
            for ti, (t0, n) in enumerate(tiles):
                gs = tinfo[ti]["gs"]
                for i in range(n // gs):
                    s2 = gi % 2
                    gi += 1
                    r0 = t0 + i * gs
                    k.dma("sp", stg[s2][0:gs], tinfo[ti]["xsrc"][i * gs:(i + 1) * gs, :], writes=[stgb[s2]])
                    for q4 in range(4):
                        b = tr_bank()
                        for c in range(4):
                            cc = q4 * 4 + c
                            tr(PS[:, b, c * gs:(c + 1) * gs], stg[s2][0:gs, cc * 128:(cc + 1) * 128], idf[0:gs, 0:gs], [stgb[s2], b_const], [pbuf[b]], inc=(c == 3))
                        act(X[:, q4 * 4:q4 * 4 + 4, r0:r0 + gs], PS[:, b, 0:4 * gs].rearrange("p (c t) -> p c t", c=4), AF.Copy,
                            [pbuf[b]], [Xb[q4 * 4 + c][ti] for c in range(4)])
                b = acc_bank()
                mm(PS[:, b, 0:n], onesf, tinfo[ti]["ssum"], True, True, [b_const, tinfo[ti]["b_ssum"]], [pbuf[b]], inc=True)
                act(rstd[:, t0:t0 + n], PS[:, b, 0:n], AF.Sqrt, [pbuf[b]], [b_rstd], scale=1.0 / 1024, bias=EPS)
            dve(lambda e: e.reciprocal(rstd, rstd), [b_rstd], [b_rstd])
            for jb in range(8):
                slot, sb_ = wnext()
                for j in range(2):
                    m = jb * 2 + j
                    for ti, (t0, n) in enumerate(tiles):
                        b1 = acc_bank()
                        for c in range(8):
                            mm(PS[:, b1, 0:n], slot[:, c, j * 128:(j + 1) * 128], tinfo[ti]["yT"][:, c, :], c == 0, c == 7,
                               [sb_] + tinfo[ti]["yT_rb"], [pbuf[b1]], inc=(c == 7))
                        b2 = acc_bank()
                        for c in range(8, 16):
                            mm(PS[:, b2, 0:n], slot[:, c, j * 128:(j + 1) * 128], tinfo[ti]["yT"][:, c, :], c == 8, c == 15,
                               [sb_] + tinfo[ti]["yT_mb"], [pbuf[b2]], inc=(c == 15))
                        dve(lambda e, b1=b1, t0=t0, n=n: e.tensor_tensor(tmp[:, 0:n], PS[:, b1, 0:n], rstd[:, t0:t0 + n], ALU.mult), [pbuf[b1], b_rstd], [b_tmp])
                        dve(lambda e, m=m, b2=b2, t0=t0, n=n: e.tensor_tensor(X[:, m, t0:t0 + n], X[:, m, t0:t0 + n], PS[:, b2, 0:n], ALU.add), [pbuf[b2], Xb[m][ti]], [Xb[m][ti]])
                        dve(lambda e, m=m, t0=t0, n=n: e.tensor_tensor(X[:, m, t0:t0 + n], X[:, m, t0:t0 + n], tmp[:, 0:n], ALU.add), [b_tmp, Xb[m][ti]], [Xb[m][ti]])
            k.barrier()
            A.release(m3)
            A0.release(R0_LO)

            def rmsnorm_fm(gcol0, out, outb):
                mk_ = A.mark()
                mk0 = A0.mark()
                sq = A0.alloc([16, nmax], BF16)
                rs = A.alloc([nmax], F32)
                b_sq, b_rs = Buf("sq"), Buf("rs")
                for ti, (t0, n) in enumerate(tiles):
                    for c in range(16):
                        act(sq[:, c, 0:n], X[:, c, t0:t0 + n], AF.Square, [Xb[c][ti]], [b_sq])
                    b = acc_bank()
                    for c in range(16):
                        mm(PS[:, b, 0:n], onesb, sq[:, c, 0:n], c == 0, c == 15, [b_const, b_sq], [pbuf[b]], inc=(c == 15))
                    act(rs[:, 0:n], PS[:, b, 0:n], AF.Sqrt, [pbuf[b]], [b_rs], scale=1.0 / 2048, bias=EPS)
                    dve(lambda e, n=n: e.reciprocal(rs[:, 0:n], rs[:, 0:n]), [b_rs], [b_rs])
                    for c in range(16):
                        dve(lambda e, c=c, t0=t0, n=n: e.scalar_tensor_tensor(out[:, c, t0:t0 + n], X[:, c, t0:t0 + n], prm[:, gcol0 + c:gcol0 + c + 1], rs[:, 0:n],
                                                                               op0=ALU.mult, op1=ALU.mult),
                            [Xb[c][ti], b_const, b_rs], [outb[c][ti]])
                k.barrier()
                A.release(mk_)
                A0.release(mk0)

            def tb(bl):
                return lambda t0_, n_: [bl[c][[t for t, _ in tiles].index(t0_)] for c in range(16)]
            chk(9)
            hqb = [[Buf("hq") for _ in tiles] for c in range(16)]
            rmsnorm_fm(P_GXA, hq, hqb)
            m6 = A.mark()
            mk0 = A0.mark()
            qc = A0.alloc([16, NT], BF16)
            qcb = [[Buf("qc") for _ in tiles] for c in range(16)]
            for jb in range(8):
                slot, sb_ = wnext()

                def epi_q(j, ti_, t0_, n_, acc, ab, jb=jb):
                    act(qc[:, jb * 2 + j, t0_:t0_ + n_], acc, AF.Copy, [ab], [qcb[jb * 2 + j][ti_]])
                fm_block(slot, sb_, 2, hq, tb(hqb), tiles, epi_q)
            k.barrier()
            oT = hq
            oTb = [[Buf("oT") for _ in tiles] for c in range(16)]
            for ti, (t0, n) in enumerate(tiles):
                tinfo[ti]["attn"](n, qc[:, :, t0:t0 + n], [qcb[c][ti] for c in range(16)], oT[:, :, t0:t0 + n], [oTb[c][ti] for c in range(16)])
            for jb in range(8):
                slot, sb_ = wnext()

                def epi_co(j, ti_, t0_, n_, acc, ab, jb=jb):
                    m = jb * 2 + j
                    dve(lambda e: e.tensor_tensor(X[:, m, t0_:t0_ + n_], X[:, m, t0_:t0_ + n_], acc, ALU.add), [ab, Xb[m][ti_]], [Xb[m][ti_]])
                fm_block(slot, sb_, 2, oT, tb(oTb), tiles, epi_co)
            k.barrier()
            A.release(m6)
            A0.release(mk0)

            chk(10)
            hn = hq
            hnb = [[Buf("hn") for _ in tiles] for c in range(16)]
            rmsnorm_fm(P_GFFN, hn, hnb)
            m6 = A.mark()
            mk0 = A0.mark()
            hG = A0.alloc([16, NT], BF16)
            hGb = [[Buf("hG") for _ in tiles] for c in range(16)]
            rl = [A.alloc([nmax], F32) for _ in range(2)]
            rlb = [Buf("rl0"), Buf("rl1")]
            cnt3 = [0]
            for g in range(4):
                for jb in range(8):
                    slot, sb_ = wnext()

                    def epi_up(j, ti_, t0_, n_, acc, ab, jb=jb):
                        s2 = cnt3[0] % 2
                        cnt3[0] += 1
                        act(rl[s2][:, 0:n_], acc, AF.Relu, [ab], [rlb[s2]])
                        dve(lambda e: e.tensor_tensor(hG[:, jb * 2 + j, t0_:t0_ + n_], rl[s2][:, 0:n_], rl[s2][:, 0:n_], ALU.mult), [rlb[s2]], [hGb[jb * 2 + j][ti_]])
                    fm_block(slot, sb_, 2, hn, tb(hnb), tiles, epi_up)
                for jb in range(8):
                    slot, sb_ = wnext()

                    def epi_dn(j, ti_, t0_, n_, acc, ab, jb=jb):
                        m = jb * 2 + j
                        dve(lambda e: e.tensor_tensor(X[:, m, t0_:t0_ + n_], X[:, m, t0_:t0_ + n_], acc, ALU.add), [ab, Xb[m][ti_]], [Xb[m][ti_]])
                    fm_block(slot, sb_, 2, hG, tb(hGb), tiles, epi_dn)
            k.barrier()
            A.release(m6)
            A0.release(mk0)

            chk(11)
            m7 = A.mark()
            mk0 = A0.mark()
            sq = hq
            rs = A.alloc([nmax], F32)
            yn = [A0.alloc([16, 128], F32) for _ in range(2)]
            ob = [A0.alloc([2048], F32) for _ in range(2)]
            b_sq, b_rs = Buf("sq"), Buf("rs")
            b_yn = [Buf("yn0"), Buf("yn1")]
            obb = [Buf("ob0"), Buf("ob1")]
            gi = 0
            for ti, (t0, n) in enumerate(tiles):
                for c in range(16):
                    act(sq[:, c, t0:t0 + n], X[:, c, t0:t0 + n], AF.Square, [Xb[c][ti]], [b_sq])
                b = acc_bank()
                for c in range(16):
                    mm(PS[:, b, 0:n], onesb, sq[:, c, t0:t0 + n], c == 0, c == 15, [b_const, b_sq], [pbuf[b]], inc=(c == 15))
                act(rs[:, 0:n], PS[:, b, 0:n], AF.Sqrt, [pbuf[b]], [b_rs], scale=1.0 / 2048, bias=EPS)
                dve(lambda e, n=n: e.reciprocal(rs[:, 0:n], rs[:, 0:n]), [b_rs], [b_rs])
                gs = tinfo[ti]["gs"]
                for i in range(n // gs):
                    s2 = gi % 2
                    gi += 1
                    r0 = t0 + i * gs
                    for c in range(16):
                        dve(lambda e, c=c, i=i, s2=s2, r0=r0, gs=gs: e.scalar_tensor_tensor(yn[s2][:, c, 0:gs], X[:, c, r0:r0 + gs], prm[:, P_GFIN + c:P_GFIN + c + 1],
                                                                                             rs[:, i * gs:(i + 1) * gs], op0=ALU.mult, op1=ALU.mult),
                            [Xb[c][ti], b_const, b_rs], [b_yn[s2]])
                    for q4 in range(4):
                        b = tr_bank()
                        for c in range(4):
                            cc = q4 * 4 + c
                            tr(PS[0:gs, b, c * 128:(c + 1) * 128], yn[s2][:, cc, 0:gs], idf, [b_yn[s2], b_const], [pbuf[b]], inc=(c == 3))
                        act(ob[s2][0:gs, q4 * 512:(q4 + 1) * 512], PS[0:gs, b, :], AF.Copy, [pbuf[b]], [obb[s2]])
                    k.dma("sp", tinfo[ti]["ydst"][i * gs:(i + 1) * gs, :], ob[s2][0:gs], reads=[obb[s2]])
            k.barrier()
            A.release(m_tile)
            A0.release(mk0)
            accn["banks"] = [0, 1]

        def attn_prompt(n_, qc_, qcb_, oT_, oTb_):
            mk_ = A.mark()
            ET = A.alloc([2, 512], BF16)
            rden = A.alloc([512], F32)
            attn_core(n_, range(4),
                      lambda hd, dc, nh: mkT[:, hd * 4 + dc, nh * 128:(nh + 1) * 128], lambda hd, dc: b_mkT[hd * 4 + dc],
                      lambda hd, dc, nh: mvt[:, nh, (hd * 4 + dc) * 128:(hd * 4 + dc + 1) * 128], lambda hd, dc, nh: b_mvt[nh][hd],
                      qc_, qcb_, oT_, oTb_, ET, Buf("ET"), rden, Buf("rden"))
            k.barrier()
            A.release(mk_)
        def attn_sample(n_, qc_, qcb_, oT_, oTb_):
            mk_ = A.mark()
            Ks = [A.alloc([2, 512], BF16) for _ in range(2)]
            mkTs = [A.alloc([4, 256], BF16) for _ in range(2)]
            mvts = [A.alloc([2, 512], BF16) for _ in range(2)]
            ET = [A.alloc([2, 16], BF16) for _ in range(2)]
            rden = [A.alloc([16], F32) for _ in range(2)]
            b_Ks = [Buf("Ks0"), Buf("Ks1")]
            b_mk1 = [Buf("mk0"), Buf("mk1")]
            b_mv1 = [Buf("mv0"), Buf("mv1")]
            b_ET = [Buf("ET0"), Buf("ET1")]
            b_rden = [Buf("rd0"), Buf("rd1")]
            its = [(tok, hd) for tok in range(NS) for hd in range(4)]

            def dma_k(it):
                tok, hd = its[it]
                s2 = it % 2
                k.dma("pool", Ks[s2], ck_d[tok][:, hd * 512:(hd + 1) * 512].rearrange("(nh p) d -> p nh d", p=128), writes=[b_Ks[s2]])

            def dma_v(it):
                tok, hd = its[it]
                s2 = it % 2
                k.dma("pool", mvts[s2], cv_d[tok][:, hd * 512:(hd + 1) * 512].rearrange("(nh p) d -> p nh d", p=128), writes=[b_mv1[s2]])

            def st_a(it):
                tok, hd = its[it]
                s2 = it % 2
                b = tr_bank()
                pv = psb(b)
                for dc in range(4):
                    for nh in range(2):
                        tr(pv[:, (dc * 2 + nh) * 128:(dc * 2 + nh + 1) * 128], Ks[s2][:, nh, dc * 128:(dc + 1) * 128], idb, [b_Ks[s2], b_const], [pbuf[b]],
                           inc=(dc == 3 and nh == 1))
                if it % 2 == 0:
                    act(mkTs[s2], pv.rearrange("p (c n) -> p c n", c=4), AF.Copy, [pbuf[b]], [b_mk1[s2]])
                else:
                    dv(lambda e, pv=pv, s2=s2: e.tensor_copy(mkTs[s2], pv.rearrange("p (c n) -> p c n", c=4)), [pbuf[b]], [b_mk1[s2]])

            def st_b(it):
                tok, hd = its[it]
                s2 = it % 2
                attn_core(1, [hd],
                          lambda hd_, dc, nh: mkTs[s2][:, dc, nh * 128:(nh + 1) * 128], lambda hd_, dc: b_mk1[s2],
                          lambda hd_, dc, nh: mvts[s2][:, nh, dc * 128:(dc + 1) * 128], lambda hd_, dc, nh: b_mv1[s2],
                          qc_[:, :, tok:tok + 1], qcb_, oT_[:, :, tok:tok + 1], oTb_, ET[s2], b_ET[s2], rden[s2], b_rden[s2])
            dma_k(0)
            dma_k(1)
            dma_v(0)
            st_a(0)
            for it in range(len(its)):
                if it + 2 < len(its):
                    dma_k(it + 2)
                if it + 1 < len(its):
                    dma_v(it + 1)
                    st_a(it + 1)
                st_b(it)
            k.barrier()
            A.release(mk_)
        tinfo = [dict(xsrc=xm_d[t0:t0 + n], yT=yT[:, :, t0:t0 + n], yT_rb=[yTb[cc][ti] for cc in range(8)], yT_mb=[yTb[cc][ti] for cc in range(8, 16)],
                      ssum=ssum[:, t0:t0 + n], b_ssum=b_ssum, attn=attn_prompt, ydst=y_d[t0:t0 + n], gs=128) for ti, (t0, n) in enumerate(TILES)]
        tinfo.append(dict(xsrc=xs_d, yT=yTs, yT_rb=[b_yTs_r], yT_mb=[b_yTs_m], ssum=ssum_s, b_ssum=b_ssum_s, attn=attn_sample, ydst=ys_d, gs=NS))
        post_tile(TILES + [(1024, NS)], tinfo)
        assert k.dead or wstate["used"] == len(wsched), (wstate, len(wsched))
        k.finish()
        k.emit()
        print("instructions:", k.nins, "arena peak", A.peak, "of", NW, "A0 peak", A0.peak, "of", R0_HI)
    return nc


_CACHE = {}


def _consts():
    ident = np.eye(128, dtype=np.float32)
    s = np.arange(128)[:, None]
    t = np.arange(128)[None, :]
    maskneg = np.where(s <= t, 0.0, -30000.0).astype(np.float32)
    sel = np.zeros((4, 4, 128), np.float32)
    for h in range(4):
        sel[h, h, :] = 1.0
    return ident, maskneg, sel.reshape(4, 512)


def kernel(**inp):
    f = lambda a: np.ascontiguousarray(np.asarray(a, dtype=np.float32))
    if "nc" not in _CACHE:
        _CACHE["nc"] = build_program()
    nc = _CACHE["nc"]
    ident, maskneg, sel = _consts()
    prm = np.zeros((128, NPRM), np.float32)

    def colmajor(v, nch):
        return np.asarray(v, np.float32).reshape(nch, 128).T
    prm[:, P_GMIX:P_GMIX + 16] = colmajor(inp["g_mix"][0], 16)
    prm[:, P_GXA:P_GXA + 16] = colmajor(inp["g_xattn"][0], 16)
    prm[:, P_GMEM:P_GMEM + 16] = colmajor(inp["g_mem"][0], 16)
    prm[:, P_GFFN:P_GFFN + 16] = colmajor(inp["g_ffn"][0], 16)
    prm[:, P_GFIN:P_GFIN + 16] = colmajor(inp["g_final"], 16)
    for tap in range(4):
        prm[:, P_CRW + tap * 8:P_CRW + tap * 8 + 8] = colmajor(inp["conv_rnn_w"][0, tap], 8)
        prm[:, P_CMW + tap * 8:P_CMW + tap * 8 + 8] = colmajor(inp["conv_ml_w"][0, tap], 8)
    prm[:, P_CRB:P_CRB + 8] = colmajor(inp["conv_rnn_b"][0], 8)
    prm[:, P_CMB:P_CMB + 8] = colmajor(inp["conv_ml_b"][0], 8)
    prm[:, P_LBA:P_LBA + 8] = np.asarray(inp["lru_ba"][0], np.float32).T
    prm[:, P_LBX:P_LBX + 8] = np.asarray(inp["lru_bx"][0], np.float32).T
    prm[:, P_LAM:P_LAM + 8] = colmajor(inp["lru_lambda"][0], 8)
    prm[:, P_GRN:P_GRN + 8] = colmajor(inp["g_rnn_out"][0], 8)
    prm[:, P_GML2:P_GML2 + 2] = colmajor(inp["g_ml_out"][0], 2)
    prm[0:4, P_BI] = np.asarray(inp["ml_bi"][0], np.float32)
    prm[0:4, P_BF] = np.asarray(inp["ml_bf"][0], np.float32)
    gmlrep = np.ascontiguousarray(np.broadcast_to(np.asarray(inp["g_ml_out"][0], np.float32)[None, :], (128, 256)))
    shared = dict(
        prm=prm, gmlrep=gmlrep, ident=ident, maskneg=maskneg, sel=sel,
        w_in=f(inp["w_in"][0]), lru_wa=f(inp["lru_wa"][0]), lru_wx=f(inp["lru_wx"][0]),
        ml_wq=f(inp["ml_wq"][0]), ml_wk=f(inp["ml_wk"][0]), w_out=f(inp["w_out"][0]),
        w_cq=f(inp["w_cq"][0]), w_mk=f(inp["w_mk"][0]), w_mv=f(inp["w_mv"][0]), w_co=f(inp["w_co"][0]),
        w_up=f(inp["w_up"][0]), w_down=f(inp["w_down"][0]),
    )
    shared["cmw_rep"] = np.ascontiguousarray(np.broadcast_to(np.asarray(inp["conv_ml_w"][0], np.float32)[None], (16, 4, 1024)))
    st_ = np.zeros((16, 16, 128), np.float32)
    for t_ in range(16):
        st_[t_, t_, :] = 1.0
    shared["seltok"] = st_.reshape(16, 2048)
    shared["cmb_rep"] = np.ascontiguousarray(np.broadcast_to(np.asarray(inp["conv_ml_b"][0], np.float32)[None], (16, 1024)))
    shared["gb_rep"] = np.ascontiguousarray(np.broadcast_to(
        np.concatenate([np.asarray(inp["ml_bi"][0], np.float32), np.asarray(inp["ml_bf"][0], np.float32)])[None], (16, 8)))
    xsm = np.asarray(inp["x_sample"], np.float32)
    xpr = np.asarray(inp["x_prompt"], np.float32)
    memp = np.asarray(inp["mem_prompt"], np.float32)
    in_maps = []
    for c in range(8):
        b, hf = c // 2, c % 2
        d = dict(shared)
        d["xm"] = np.ascontiguousarray(xpr[b, hf * 1024:(hf + 1) * 1024])
        d["xp"] = np.ascontiguousarray(xpr[b, 0:1024])
        d["mem"] = np.ascontiguousarray(memp[b])
        d["mask"] = np.full((128, 1), float(hf), np.float32)
        sl = slice(c * 16, (c + 1) * 16)
        d["xs"] = np.ascontiguousarray(xsm[sl, 0])
        d["s_h"] = f(inp["state_rglru_h"][0, sl])
        d["s_rc"] = f(inp["state_rglru_conv"][0, sl])
        d["s_C"] = f(inp["state_mlstm_C"][0, sl])
        d["s_n"] = f(inp["state_mlstm_n"][0, sl])
        d["s_m"] = f(inp["state_mlstm_m"][0, sl])
        d["s_mc"] = f(inp["state_mlstm_conv"][0, sl])
        d["ck"] = f(inp["cache_mem_k"][0, sl]).reshape(16, 256, 2048)
        d["cv"] = f(inp["cache_mem_v"][0, sl]).reshape(16, 256, 2048)
        in_maps.append(d)
    res = run_bass_kernel_spmd(nc, in_maps, core_ids=list(range(8)))
    R = res.results
    B = 4
    y_prompt = np.zeros((B, 2048, 2048), np.float32)
    p_h = np.zeros((1, B, 1024), np.float32)
    p_rc = np.zeros((1, B, 3, 1024), np.float32)
    p_C = np.zeros((1, B, 4, 256, 256), np.float32)
    p_n = np.zeros((1, B, 4, 256), np.float32)
    p_m = np.zeros((1, B, 4), np.float32)
    p_mc = np.zeros((1, B, 3, 1024), np.float32)
    p_mk = np.zeros((1, B, 256, 4, 512), np.float32)
    p_mv = np.zeros((1, B, 256, 4, 512), np.float32)
    for c in range(8):
        b, hf = c // 2, c % 2
        r = R[c]
        y_prompt[b, hf * 1024:(hf + 1) * 1024] = r["o_y"]
        if hf == 1:
            p_h[0, b] = r["o_ph"].T.reshape(1024)
            p_rc[0, b] = r["o_prc"].transpose(2, 1, 0).reshape(3, 1024)
            p_mc[0, b] = r["o_pmc"].transpose(2, 1, 0).reshape(3, 1024)
            oc = r["o_pC"]
            p_C[0, b] = oc[:, :, :, 0:256].transpose(1, 3, 2, 0).reshape(4, 256, 256)
            p_n[0, b] = oc[:, :, :, 256].transpose(1, 2, 0).reshape(4, 256)
            p_m[0, b] = r["o_pm"].reshape(4)
            p_mk[0, b] = r["o_mkT"].transpose(2, 1, 0).reshape(256, 4, 512)
            p_mv[0, b] = r["o_mv"].reshape(256, 4, 512)
    y_s = np.zeros((128, 1, 2048), np.float32)
    s_h = np.zeros((1, 128, 1024), np.float32)
    s_rc = np.zeros((1, 128, 3, 1024), np.float32)
    s_C = np.zeros((1, 128, 4, 256, 256), np.float32)
    s_n = np.zeros((1, 128, 4, 256), np.float32)
    s_m = np.zeros((1, 128, 4), np.float32)
    s_mc = np.zeros((1, 128, 3, 1024), np.float32)
    for c in range(8):
        r = R[c]
        sl = slice(c * 16, (c + 1) * 16)
        y_s[sl, 0] = r["o_ys"]
        s_h[0, sl] = r["o_sh"].transpose(2, 1, 0).reshape(16, 1024)
        s_rc[0, sl] = r["o_src"].transpose(3, 2, 1, 0).reshape(16, 3, 1024)
        s_C[0, sl] = r["o_sC"]
        s_n[0, sl] = r["o_sn"]
        s_m[0, sl] = r["o_sm"]
        s_mc[0, sl] = r["o_smc"]
    return (y_prompt, y_s, p_h, p_rc, p_C, p_n, p_m, p_mc, p_mk, p_mv, s_h, s_rc, s_C, s_n, s_m, s_mc)
```
